# Optimizing a Trainium2 kernel written in Bass

```python
import math
import jax
import jax.numpy as jnp
from jax import lax
import numpy as np

D_MODEL = 1024
BATCH = 16
SEQ = 2048
DEPTH = 2

N_MIXERS = 4
MIX_WIDTH = 256
HEAD_DIM = 64
N_HEADS = 4
NORM_EPS = 1e-6
NEG_INF = -1e30

GDN_CONV = 4
GDN_CHUNK = 64

SG_CHUNK = 128
SG_GROUPS = 4
SG_GROUP_DIM = MIX_WIDTH // SG_GROUPS

RWKV_DECAY_RANK = 64
RWKV_A_RANK = 64
RWKV_GN_EPS = 64e-5
RW_WIDTH = 3 * MIX_WIDTH + RWKV_DECAY_RANK + RWKV_A_RANK

NSA_KV_DIM = HEAD_DIM
NSA_CMP_LEN = 32
NSA_CMP_STRIDE = 16
NSA_CMP_RATIO = NSA_CMP_LEN // NSA_CMP_STRIDE
NSA_CMP_HIDDEN = 64
NSA_SEL_BLOCK = 64
NSA_N_SEL = 8
NSA_FORCE_BONUS = 1e4
NSA_WINDOW = 256
NSA_Q_BLOCK = 128
NSA_SEL_Q_BLOCK = 64

IN_SPLITS = (
    3 * MIX_WIDTH,
    N_HEADS,
    N_HEADS,
    MIX_WIDTH,
    MIX_WIDTH,
    MIX_WIDTH,
    MIX_WIDTH,
    RW_WIDTH,
    MIX_WIDTH,
    MIX_WIDTH,
    6 * NSA_KV_DIM,
    3 * N_HEADS,
    MIX_WIDTH,
    N_MIXERS * D_MODEL,
)
IN_OFFSETS = tuple(sum(IN_SPLITS[: i + 1]) for i in range(len(IN_SPLITS) - 1))
D_IN = sum(IN_SPLITS)
ALIBI_SLOPES = tuple(2.0 ** (-8.0 * (h + 1) / N_HEADS) for h in range(N_HEADS))

kernel_name = 'hybrid_gdn_gmlp_rwkv7_nsa_gated_merge'


def rmsnorm(x, gain, eps=NORM_EPS):
    x32 = x.astype(jnp.float32)
    y = x32 * lax.rsqrt(jnp.mean(x32 * x32, axis=-1, keepdims=True) + eps)
    return (y * gain.astype(jnp.float32)).astype(x.dtype)


def layernorm(x, gain, bias, eps):
    x32 = x.astype(jnp.float32)
    xc = x32 - jnp.mean(x32, axis=-1, keepdims=True)
    y = xc * lax.rsqrt(jnp.mean(xc * xc, axis=-1, keepdims=True) + eps)
    return (y * gain.astype(jnp.float32) + bias.astype(jnp.float32)).astype(x.dtype)


def l2norm(x, eps=1e-6):
    x32 = x.astype(jnp.float32)
    return (x32 * lax.rsqrt(jnp.sum(x32 * x32, axis=-1, keepdims=True) + eps)).astype(x.dtype)


def shift_prev(x):
    return jnp.pad(x, ((0, 0), (1, 0), (0, 0)))[:, :-1]


def causal_depthwise_conv(x, w):
    width, ch = w.shape
    return lax.conv_general_dilated(
        x, w[:, None, :].astype(x.dtype), window_strides=(1,),
        padding=((width - 1, 0),), dimension_numbers=('NWC', 'WIO', 'NWC'),
        feature_group_count=ch)


def masked_softmax(s, mask):
    p = jax.nn.softmax(jnp.where(mask, s, NEG_INF), axis=-1)
    return jnp.where(mask, p, 0.0)


def chunked_gated_delta_rule(q, k, v, g, beta):
    B, T, H, E = q.shape
    C = GDN_CHUNK
    N = T // C

    def chunks(t):
        t = t.reshape((B, N, C, H) + t.shape[3:])
        return jnp.moveaxis(t, (1, 3), (0, 2))

    q, k, v, g, beta = (chunks(t) for t in (q, k, v, g, beta))
    gc = jnp.cumsum(g, axis=-1)
    pos = jnp.arange(C)
    causal = pos[:, None] >= pos[None, :]
    strict = pos[:, None] > pos[None, :]
    decay = jnp.where(causal, jnp.exp(jnp.where(causal, gc[..., :, None] - gc[..., None, :], 0.0)), 0.0)
    kb = k * beta[..., None]
    a_mat = jnp.where(strict, jnp.einsum('nbhcd,nbhed->nbhce', kb, k) * decay, 0.0)
    eye = jnp.eye(C, dtype=q.dtype)
    t_inv = lax.linalg.triangular_solve(eye + a_mat, jnp.broadcast_to(eye, a_mat.shape),
                                        left_side=True, lower=True, unit_diagonal=True)
    u = t_inv @ (v * beta[..., None])
    w = t_inv @ (kb * jnp.exp(gc)[..., None])
    attn = jnp.einsum('nbhcd,nbhed->nbhce', q, k) * decay
    g_last = gc[..., -1]
    q_in = q * jnp.exp(gc)[..., None]
    k_out = k * jnp.exp(g_last[..., None] - gc)[..., None]

    def step(state, xs):
        u_n, w_n, attn_n, q_n, k_n, gl_n = xs
        v_new = u_n - w_n @ state
        o_n = q_n @ state + attn_n @ v_new
        state = state * jnp.exp(gl_n)[..., None, None] + jnp.swapaxes(k_n, -1, -2) @ v_new
        return state, o_n

    state0 = jnp.zeros((B, H, E, v.shape[-1]), q.dtype)
    _, o = lax.scan(step, state0, (u, w, attn, q_in, k_out, g_last))
    return jnp.moveaxis(o, (0, 2), (1, 3)).reshape(B, T, H, -1)


def gated_deltanet(qkv, a_logit, b_logit, z, conv_w, a_log, dt_bias, norm_w):
    B, T, _ = qkv.shape
    f32 = jnp.float32
    qkv = jax.nn.silu(causal_depthwise_conv(qkv, conv_w))
    q, k, v = jnp.split(qkv, 3, axis=-1)
    heads = lambda t: t.reshape(B, T, N_HEADS, HEAD_DIM).astype(f32)
    q = l2norm(heads(q)) * (HEAD_DIM ** -0.5)
    k = l2norm(heads(k))
    v = heads(v)
    beta = jax.nn.sigmoid(b_logit.astype(f32))
    g = -jnp.exp(a_log.astype(f32)) * jax.nn.softplus(a_logit.astype(f32) + dt_bias.astype(f32))
    o = chunked_gated_delta_rule(q, k, v, g, beta)
    o = rmsnorm(o, norm_w) * jax.nn.silu(heads(z))
    return o.reshape(B, T, MIX_WIDTH).astype(qkv.dtype)


def spatial_gating(u, v, z, ln_gain, ln_bias, w_s, b_s):
    B, T, _ = u.shape
    u = jax.nn.gelu(u)
    v = jax.nn.gelu(v).reshape(B, T, SG_GROUPS, SG_GROUP_DIM)
    v = layernorm(v, ln_gain.reshape(SG_GROUPS, SG_GROUP_DIM), ln_bias.reshape(SG_GROUPS, SG_GROUP_DIM), 1e-5)
    v = v.reshape(B, T // SG_CHUNK, SG_CHUNK, SG_GROUPS, SG_GROUP_DIM)
    tril = jnp.tril(jnp.ones((SG_CHUNK, SG_CHUNK), dtype=bool))
    w_c = jnp.where(tril, w_s, 0.0).astype(v.dtype)
    mixed = jnp.einsum('gts,bnsgc->bntgc', w_c, v) + b_s.T[:, :, None].astype(v.dtype)
    s = u * mixed.reshape(B, T, MIX_WIDTH)
    return s * jax.nn.silu(z)


def rwkv7_scan(r, w, k, v, kk, a):
    def step(S, xs):
        r_t, w_t, k_t, v_t, kk_t, a_t = xs
        sa = jnp.einsum('bhvk,bhk->bhv', S, -kk_t)
        S = (S * w_t[:, :, None, :] + sa[..., None] * (kk_t * a_t)[:, :, None, :]
             + v_t[..., None] * k_t[:, :, None, :])
        return S, jnp.einsum('bhvk,bhk->bhv', S, r_t)

    B, T, H, E = r.shape
    xs = tuple(jnp.moveaxis(t, 1, 0) for t in (r, w, k, v, kk, a))
    _, o = lax.scan(step, jnp.zeros((B, H, E, E), jnp.float32), xs)
    return jnp.moveaxis(o, 0, 1)


def rwkv7_time_mix(rw, z, mu, w_up, w0, a_up, a0, k_k, k_a, r_k, gn_gain, gn_bias):
    B, T, _ = rw.shape
    f32 = jnp.float32
    rw = (rw + mu * (shift_prev(rw) - rw)).astype(f32)
    r, k, v, wd, ad = jnp.split(
        rw, (MIX_WIDTH, 2 * MIX_WIDTH, 3 * MIX_WIDTH, 3 * MIX_WIDTH + RWKV_DECAY_RANK), axis=-1)
    w_log = -jax.nn.softplus(-(w0.astype(f32) + jnp.tanh(wd) @ w_up.astype(f32))) - 0.5
    decay = jnp.exp(-jnp.exp(w_log))
    a = jax.nn.sigmoid(a0.astype(f32) + ad @ a_up.astype(f32))
    heads = lambda t: t.reshape(B, T, N_HEADS, HEAD_DIM)
    kk = l2norm(heads(k * k_k.astype(f32)))
    k = k * (1.0 + (a - 1.0) * k_a.astype(f32))
    r_h, k_h, v_h = heads(r), heads(k), heads(v)
    o = rwkv7_scan(r_h, heads(decay), k_h, v_h, kk, heads(a))
    o = layernorm(o, gn_gain.reshape(N_HEADS, HEAD_DIM), gn_bias.reshape(N_HEADS, HEAD_DIM), RWKV_GN_EPS)
    o = o + jnp.sum(r_h * k_h * r_k.astype(f32), axis=-1, keepdims=True) * v_h
    return (o.reshape(B, T, MIX_WIDTH) * jax.nn.silu(z.astype(f32))).astype(z.dtype)


def compress_blocks(x, pos_emb, w1, w2):
    B, T, E = x.shape
    ns = T // NSA_CMP_STRIDE
    nc = ns - NSA_CMP_RATIO + 1
    c = x.reshape(B, ns, NSA_CMP_STRIDE, E)
    blocks = jnp.concatenate([c[:, j:j + nc] for j in range(NSA_CMP_RATIO)], axis=2)
    blocks = (blocks + pos_emb.astype(x.dtype)).reshape(B, nc, NSA_CMP_LEN * E)
    return jax.nn.gelu(blocks @ w1) @ w2


def nsa_compressed(q, k, v, pos_emb, w1, w2, slopes):
    B, T, H, E = q.shape
    kc = compress_blocks(k, pos_emb[0], w1[0], w2[0])
    vc = compress_blocks(v, pos_emb[1], w1[1], w2[1])
    nc = kc.shape[1]
    end = jnp.arange(nc) * NSA_CMP_STRIDE + NSA_CMP_LEN - 1
    dist = (jnp.arange(T)[:, None] - end[None, :]).astype(jnp.float32)
    s = (jnp.einsum('bthd,bnd->bhtn', q, kc).astype(jnp.float32) * (E ** -0.5)
         - slopes[:, None, None] * dist)
    p = masked_softmax(s, dist >= 0)
    return jnp.einsum('bhtn,bnd->bthd', p.astype(q.dtype), vc), p


def nsa_selected(q, k, v, p_cmp, slopes):
    B, T, H, E = q.shape
    nc = p_cmp.shape[-1]
    nsb = T // NSA_SEL_BLOCK
    n_sel = min(NSA_N_SEL, nsb)
    cs = np.arange(nc) * NSA_CMP_STRIDE
    bs = np.arange(nsb) * NSA_SEL_BLOCK
    inter = np.clip(np.minimum(cs[:, None] + NSA_CMP_LEN, bs[None, :] + NSA_SEL_BLOCK)
                    - np.maximum(cs[:, None], bs[None, :]), 0, None).astype(np.float32) / NSA_CMP_LEN
    imp = jnp.einsum('bhtn,nj->btj', p_cmp, jnp.asarray(inter))
    cur = jnp.arange(T)[:, None] // NSA_SEL_BLOCK
    blk = jnp.arange(nsb)[None, :]
    forced = (blk == 0) | (blk == cur) | (blk == cur - 1)
    score = jnp.where(blk <= cur, imp + jnp.where(forced, NSA_FORCE_BONUS, 0.0), NEG_INF)
    _, idx = lax.top_k(score, n_sel)

    kb = k.reshape(B, nsb, NSA_SEL_BLOCK, E)
    vb = v.reshape(B, nsb, NSA_SEL_BLOCK, E)
    QB = NSA_SEL_Q_BLOCK
    nqb = T // QB
    qs = jnp.moveaxis(q.reshape(B, nqb, QB, H, E), 1, 0)
    idxs = jnp.moveaxis(idx.reshape(B, nqb, QB, n_sel), 1, 0)
    t0s = jnp.arange(nqb) * QB
    bidx = jnp.arange(B)[:, None, None]

    def one_block(args):
        q_blk, idx_blk, t0 = args
        kg = kb[bidx, idx_blk]
        vg = vb[bidx, idx_blk]
        tq = t0 + jnp.arange(QB)
        pos = idx_blk[..., None] * NSA_SEL_BLOCK + jnp.arange(NSA_SEL_BLOCK)
        dist = (tq[None, :, None, None] - pos).astype(jnp.float32)[:, :, None]
        s = (jnp.einsum('bqhd,bqnsd->bqhns', q_blk, kg).astype(jnp.float32) * (E ** -0.5)
             - slopes[None, None, :, None, None] * dist)
        mask = jnp.broadcast_to(dist >= 0, s.shape)
        p = masked_softmax(s.reshape(B, QB, H, -1), mask.reshape(B, QB, H, -1)).reshape(s.shape)
        return jnp.einsum('bqhns,bqnsd->bqhd', p.astype(q_blk.dtype), vg)

    o = lax.map(one_block, (qs, idxs, t0s))
    return jnp.moveaxis(o, 0, 1).reshape(B, T, H, E)


def nsa_window(q, k, v, slopes):
    B, T, H, E = q.shape
    QB = NSA_Q_BLOCK
    nb = T // QB
    nprev = NSA_WINDOW // QB

    def band(t):
        tb = jnp.pad(t.reshape(B, nb, QB, E), ((0, 0), (nprev, 0), (0, 0), (0, 0)))
        return jnp.concatenate([tb[:, j:j + nb] for j in range(nprev + 1)], axis=2)

    kband, vband = band(k), band(v)
    qb = q.reshape(B, nb, QB, H, E)
    tq = jnp.arange(nb)[:, None] * QB + jnp.arange(QB)
    pos = jnp.arange(nb)[:, None] * QB - nprev * QB + jnp.arange((nprev + 1) * QB)
    dist = tq[:, :, None] - pos[:, None, :]
    mask = (dist >= 0) & (dist < NSA_WINDOW) & (pos[:, None, :] >= 0)
    s = (jnp.einsum('bnqhd,bnkd->bnhqk', qb, kband).astype(jnp.float32) * (E ** -0.5)
         - slopes[None, None, :, None, None] * dist.astype(jnp.float32)[None, :, None])
    p = masked_softmax(s, mask[None, :, None])
    o = jnp.einsum('bnhqk,bnkd->bnqhd', p.astype(q.dtype), vband)
    return o.reshape(B, T, H, E)


def native_sparse_attention(q, kv, gate_logit, z, cmp_pos, cmp_w1, cmp_w2):
    B, T, _ = q.shape
    q = q.reshape(B, T, N_HEADS, HEAD_DIM)
    k_c, v_c, k_s, v_s, k_w, v_w = jnp.split(kv, 6, axis=-1)
    slopes = jnp.asarray(ALIBI_SLOPES, jnp.float32)
    o_cmp, p_cmp = nsa_compressed(q, k_c, v_c, cmp_pos, cmp_w1, cmp_w2, slopes)
    o_sel = nsa_selected(q, k_s, v_s, p_cmp, slopes)
    o_win = nsa_window(q, k_w, v_w, slopes)
    g = jax.nn.sigmoid(gate_logit.reshape(B, T, N_HEADS, 3))
    o = g[..., 0:1] * o_cmp + g[..., 1:2] * o_sel + g[..., 2:3] * o_win
    return o.reshape(B, T, MIX_WIDTH) * jax.nn.silu(z)


def hybrid_layer(x, norm_gain, w_in, gdn_conv, gdn_a_log, gdn_dt_bias, gdn_norm,
                 sg_ln_gain, sg_ln_bias, sg_w_s, sg_b_s,
                 rw_mu, rw_w_up, rw_w0, rw_a_up, rw_a0, rw_k_k, rw_k_a, rw_r_k, rw_gn_gain, rw_gn_bias,
                 nsa_cmp_pos, nsa_cmp_w1, nsa_cmp_w2, w_branch, w_out):
    h = rmsnorm(x, norm_gain)
    proj = h @ w_in
    (gdn_qkv, gdn_a, gdn_b, gdn_z, sg_u, sg_v, sg_z, rw, rw_z,
     nsa_q, nsa_kv, nsa_g, nsa_z, merge) = jnp.split(proj, IN_OFFSETS, axis=-1)
    ys = (
        gated_deltanet(gdn_qkv, gdn_a, gdn_b, gdn_z, gdn_conv, gdn_a_log, gdn_dt_bias, gdn_norm),
        spatial_gating(sg_u, sg_v, sg_z, sg_ln_gain, sg_ln_bias, sg_w_s, sg_b_s),
        rwkv7_time_mix(rw, rw_z, rw_mu, rw_w_up, rw_w0, rw_a_up, rw_a0, rw_k_k, rw_k_a, rw_r_k,
                       rw_gn_gain, rw_gn_bias),
        native_sparse_attention(nsa_q, nsa_kv, nsa_g, nsa_z, nsa_cmp_pos, nsa_cmp_w1, nsa_cmp_w2),
    )
    gates = jnp.split(jax.nn.sigmoid(merge), N_MIXERS, axis=-1)
    mixed = gates[0] * (ys[0] @ w_branch[0])
    for i in range(1, N_MIXERS):
        mixed = mixed + gates[i] * (ys[i] @ w_branch[i])
    return x + mixed @ w_out


def setup_inputs(seed: int = 0) -> dict:
    key = jax.random.key(seed)
    ks = jax.random.split(key, 32)
    f32 = jnp.float32
    L, D, W, H, E = DEPTH, D_MODEL, MIX_WIDTH, N_HEADS, HEAD_DIM
    nrm = lambda k, shape, scale: scale * jax.random.normal(k, shape, f32)
    dt = jnp.exp(jax.random.uniform(ks[5], (L, H), f32, math.log(1e-3), math.log(1e-1)))
    return {
        'x': nrm(ks[0], (BATCH, SEQ, D), 1.0),
        'norm_gain': 1.0 + nrm(ks[1], (L, D), 0.1),
        'w_in': nrm(ks[2], (L, D, D_IN), D ** -0.5),
        'gdn_conv': nrm(ks[3], (L, GDN_CONV, 3 * W), 0.5),
        'gdn_a_log': jnp.log(jax.random.uniform(ks[4], (L, H), f32, 1.0, 16.0)),
        'gdn_dt_bias': jnp.log(jnp.expm1(dt)),
        'gdn_norm': 1.0 + nrm(ks[6], (L, E), 0.1),
        'sg_ln_gain': 1.0 + nrm(ks[7], (L, W), 0.1),
        'sg_ln_bias': nrm(ks[8], (L, W), 0.02),
        'sg_w_s': nrm(ks[9], (L, SG_GROUPS, SG_CHUNK, SG_CHUNK), SG_CHUNK ** -0.5),
        'sg_b_s': 1.0 + nrm(ks[10], (L, SG_GROUPS, SG_CHUNK), 0.1),
        'rw_mu': jax.random.uniform(ks[11], (L, RW_WIDTH), f32, 0.0, 1.0),
        'rw_w_up': nrm(ks[12], (L, RWKV_DECAY_RANK, W), 0.1),
        'rw_w0': jax.random.uniform(ks[13], (L, W), f32, -6.0, -1.0),
        'rw_a_up': nrm(ks[14], (L, RWKV_A_RANK, W), 0.1),
        'rw_a0': nrm(ks[15], (L, W), 0.1),
        'rw_k_k': 0.85 + nrm(ks[16], (L, W), 0.05),
        'rw_k_a': 1.0 + nrm(ks[17], (L, W), 0.05),
        'rw_r_k': nrm(ks[18], (L, H, E), 0.1),
        'rw_gn_gain': 1.0 + nrm(ks[19], (L, W), 0.1),
        'rw_gn_bias': nrm(ks[20], (L, W), 0.02),
        'nsa_cmp_pos': nrm(ks[21], (L, 2, NSA_CMP_LEN, NSA_KV_DIM), 0.1),
        'nsa_cmp_w1': nrm(ks[22], (L, 2, NSA_CMP_LEN * NSA_KV_DIM, NSA_CMP_HIDDEN), (NSA_CMP_LEN * NSA_KV_DIM) ** -0.5),
        'nsa_cmp_w2': nrm(ks[23], (L, 2, NSA_CMP_HIDDEN, NSA_KV_DIM), NSA_CMP_HIDDEN ** -0.5),
        'w_branch': nrm(ks[24], (L, N_MIXERS, W, D), W ** -0.5),
        'w_out': nrm(ks[25], (L, D, D), D ** -0.5),
        'final_gain': 1.0 + nrm(ks[26], (D,), 0.1),
    }


def reference(x, norm_gain, w_in, gdn_conv, gdn_a_log, gdn_dt_bias, gdn_norm,
              sg_ln_gain, sg_ln_bias, sg_w_s, sg_b_s,
              rw_mu, rw_w_up, rw_w0, rw_a_up, rw_a0, rw_k_k, rw_k_a, rw_r_k, rw_gn_gain, rw_gn_bias,
              nsa_cmp_pos, nsa_cmp_w1, nsa_cmp_w2, w_branch, w_out, final_gain):
    for l in range(DEPTH):
        x = hybrid_layer(
            x, norm_gain[l], w_in[l], gdn_conv[l], gdn_a_log[l], gdn_dt_bias[l], gdn_norm[l],
            sg_ln_gain[l], sg_ln_bias[l], sg_w_s[l], sg_b_s[l],
            rw_mu[l], rw_w_up[l], rw_w0[l], rw_a_up[l], rw_a0[l], rw_k_k[l], rw_k_a[l], rw_r_k[l],
            rw_gn_gain[l], rw_gn_bias[l],
            nsa_cmp_pos[l], nsa_cmp_w1[l], nsa_cmp_w2[l], w_branch[l], w_out[l])
    return rmsnorm(x, final_gain)
```

```python
import math
from contextlib import ExitStack

import numpy as np
import concourse.bass as bass
import concourse.mybir as mybir
from concourse.bass_utils import run_bass_kernel_spmd

F32 = mybir.dt.float32
BF16 = mybir.dt.bfloat16
AF = mybir.ActivationFunctionType
ALU = mybir.AluOpType
AX = mybir.AxisListType

NCORES = 8
DEPTH = 2
D = 1024
T = 2048
NB = 2
NTT = T // 128
D_IN = 7956
NF = 2176
NT = 1684

F_GQ, F_GK, F_GV = 0, 256, 512
F_RW = 768
F_NQ = 1664
F_KC, F_VC, F_KS, F_KW = 1920, 1984, 2048, 2112
T_GZ, T_SU, T_SV, T_SZ, T_RZ, T_NZ = 0, 256, 512, 768, 1024, 1280
T_VS, T_VW, T_GA, T_GB, T_NG = 1536, 1600, 1664, 1668, 1672


def _col_perm():
    o = {}
    off = 0
    names = ["gdn_qkv", "gdn_a", "gdn_b", "gdn_z", "sg_u", "sg_v", "sg_z", "rw", "rw_z",
             "nsa_q", "nsa_kv", "nsa_g", "nsa_z", "merge"]
    sizes = [768, 4, 4, 256, 256, 256, 256, 896, 256, 256, 384, 12, 256, 4096]
    for n, s in zip(names, sizes):
        o[n] = off
        off += s
    assert off == D_IN
    r = lambda a, n: list(range(a, a + n))
    kv = o["nsa_kv"]
    fcols = (r(o["gdn_qkv"], 768) + r(o["rw"], 896) + r(o["nsa_q"], 256)
             + r(kv, 64) + r(kv + 64, 64) + r(kv + 128, 64) + r(kv + 256, 64))
    tcols = (r(o["gdn_z"], 256) + r(o["sg_u"], 256) + r(o["sg_v"], 256) + r(o["sg_z"], 256)
             + r(o["rw_z"], 256) + r(o["nsa_z"], 256) + r(kv + 192, 64) + r(kv + 320, 64)
             + r(o["gdn_a"], 4) + r(o["gdn_b"], 4) + r(o["nsa_g"], 12))
    assert len(fcols) == NF and len(tcols) == NT
    return np.array(fcols), np.array(tcols), o["merge"]


class Res:
    __slots__ = ("last_w", "readers")

    def __init__(self):
        self.last_w = None
        self.readers = []


class DramRes(Res):
    __slots__ = ()


class Tile:
    def __init__(self, t):
        self.t = t
        self._r = {}

    def r(self, key=0):
        x = self._r.get(key)
        if x is None:
            x = self._r[key] = Res()
        return x

    def __getitem__(self, idx):
        return self.t[idx]


class Sched:
    ENGS = ("pe", "act", "dve", "pool", "sp")
    ROT = 30000

    def __init__(self, nc, stack, n_dma_ring=8):
        self.nc = nc
        self.stack = stack
        self.sems = []
        self.eng_sem = {}
        self.eng_cnt = {}
        self.ring = {}
        for e in self.ENGS:
            self.eng_sem[e] = self._new_sem(f"s_{e}")
            self.eng_cnt[e] = 0
            self.ring[e] = [[self._new_sem(f"d_{e}{i}"), 0] for i in range(n_dma_ring)]
        self.ring_pos = {e: 0 for e in self.ENGS}
        self.waited = {e: {} for e in self.ENGS}
        self.ops = {e: [] for e in self.ENGS}
        self.nops = 0
        self.defer = None
        self.last_ev = {e: None for e in self.ENGS}

    def _new_sem(self, name):
        s = self.stack.enter_context(self.nc.semaphore(name))
        self.sems.append(s)
        return len(self.sems) - 1

    def _need(self, eng, ev, waits):
        if ev is None:
            return
        si, val = ev[1], ev[2]
        if self.waited[eng].get(si, 0) >= val:
            return
        if val > waits.get(si, 0):
            waits[si] = val

    def op(self, eng, fn, reads=(), writes=(), dma=False, pe_acc=False):
        if self.defer is not None:
            self.defer.append((eng, fn, list(reads), list(writes), dma, pe_acc))
            return None
        reads = [r for r in reads if not isinstance(r, DramRes)]
        writes = [w for w in writes if not isinstance(w, DramRes)]
        waits = {}
        for r in reads:
            self._need(eng, r.last_w, waits)
        for w in writes:
            lw = w.last_w
            if not (pe_acc and lw is not None and lw[0] == "pe" and eng == "pe"):
                self._need(eng, lw, waits)
            for ev in w.readers:
                self._need(eng, ev, waits)
        if dma:
            pos = self.ring_pos[eng]
            self.ring_pos[eng] = (pos + 1) % len(self.ring[eng])
            slot = self.ring[eng][pos]
            if slot[1] > 0:
                self._need(eng, (eng, slot[0], slot[1]), waits)
            slot[1] += 16
            ev = (eng, slot[0], slot[1])
            inc = (slot[0], 16)
        else:
            if self.eng_cnt[eng] >= self.ROT:
                self.eng_sem[eng] = self._new_sem(f"s_{eng}_{len(self.sems)}")
                self.eng_cnt[eng] = 0
            self.eng_cnt[eng] += 1
            ev = (eng, self.eng_sem[eng], self.eng_cnt[eng])
            inc = (self.eng_sem[eng], 1)
        for si, val in waits.items():
            self.waited[eng][si] = val
        self.ops[eng].append((list(waits.items()), fn, inc))
        for r in reads:
            r.readers.append(ev)
        for w in writes:
            w.last_w = ev
            w.readers = []
        self.nops += 1
        self.last_ev[eng] = ev
        return ev

    def replay(self, *lists):
        assert self.defer is None
        pos = [0] * len(lists)
        total = sum(len(x) for x in lists)
        for _ in range(total):
            best, bf = None, None
            for i, x in enumerate(lists):
                if pos[i] < len(x):
                    f = pos[i] / len(x)
                    if bf is None or f < bf:
                        best, bf = i, f
            a = lists[best][pos[best]]
            pos[best] += 1
            self.op(a[0], a[1], a[2], a[3], a[4], a[5])

    def emit(self, final_events=()):
        nc = self.nc
        sems = self.sems
        ops = self.ops
        self.ops = {e: [] for e in self.ENGS}
        tail = [ev for ev in self.last_ev.values() if ev is not None]
        for e in self.ENGS:
            for slot in self.ring[e]:
                if slot[1] > 0:
                    tail.append((e, slot[0], slot[1]))
        tail += list(final_events)
        tw = {}
        for ev in tail:
            tw[ev[1]] = max(tw.get(ev[1], 0), ev[2])

        with nc.Block() as block:
            def run(engname, e):
                for waits, fn, inc in ops[engname]:
                    for si, val in waits:
                        e.wait_ge(sems[si], val)
                    ins = fn(e)
                    ins.then_inc(sems[inc[0]], inc[1])
                for si, val in tw.items():
                    if self.waited[engname].get(si, 0) < val:
                        e.wait_ge(sems[si], val)
                        self.waited[engname][si] = val

            @block.tensor
            def _(e):
                run("pe", e)

            @block.scalar
            def _(e):
                run("act", e)

            @block.vector
            def _(e):
                run("dve", e)

            @block.gpsimd
            def _(e):
                run("pool", e)

            @block.sync
            def _(e):
                run("sp", e)


class Ctx:
    pass


_UID = [0]


def _alloc(nc, st, name, shape, dtype, psum=False):
    _UID[0] += 1
    name = f"{name}_{_UID[0]}"
    if psum:
        return Tile(st.enter_context(nc.psum_tensor(name, shape, dtype)))
    return Tile(st.enter_context(nc.sbuf_tensor(name, shape, dtype)))


def stage1(c, l, b):
    nc, S = c.nc, c.S
    xin = c.xres[l]
    with ExitStack() as st:
        A = lambda n, s, d, psum=False: _alloc(nc, st, n, s, d, psum)
        xt = [A(f"s1_x{i}", [128, D], F32) for i in range(2)]
        sq = A("s1_sq", [128, D], F32)
        ssum = [A(f"s1_ss{i}", [128, 1], F32) for i in range(2)]
        hb = [A(f"s1_hb{i}", [128, D], BF16) for i in range(2)]
        pT = [A(f"s1_pT{i}", [128, 8, 128], BF16, psum=True) for i in range(2)]
        wf32 = [A(f"s1_wf{i}", [128, 8, 512], F32) for i in range(2)]
        wbf = [A(f"s1_wb{i}", [128, 8, 512], BF16) for i in range(2)]
        po = [A(f"s1_po{i}", [128, 512], F32, psum=True) for i in range(4)]
        ot = [A(f"s1_ot{i}", [128, 512], F32) for i in range(4)]

        def a_front(tt):
            i = tt % 2
            x_, ss_, hb_, pT_ = xt[i], ssum[i], hb[i], pT[i]
            S.op("sp", lambda e: e.dma_start(out=x_[:], in_=xin[b, tt * 128:(tt + 1) * 128, :]),
                 reads=[c.xres_r[l]], writes=[x_.r()], dma=True)
            S.op("act", lambda e: e.activation(out=sq[:], in_=x_[:], func=AF.Square, accum_out=ss_[:]),
                 reads=[x_.r()], writes=[sq.r(), ss_.r()])
            S.op("act", lambda e: e.activation(out=ss_[:], in_=ss_[:], func=AF.Sqrt, scale=1.0 / D, bias=c.eps6[:, 0:1]),
                 reads=[ss_.r()], writes=[ss_.r()])
            S.op("dve", lambda e: e.reciprocal(out=ss_[:], in_=ss_[:]), reads=[ss_.r()], writes=[ss_.r()])
            S.op("dve", lambda e: e.tensor_scalar(out=hb_[:], in0=x_[:], scalar1=ss_[:, 0:1], scalar2=None, op0=ALU.mult),
                 reads=[x_.r(), ss_.r()], writes=[hb_.r()])

            def tr(e):
                for k in range(8):
                    ins = e.transpose(out=pT_[:, k, :], in_=hb_[:, k * 128:(k + 1) * 128], identity=c.identb[:])
                return ins
            S.op("pe", tr, reads=[hb_.r()], writes=[pT_.r()])

        def a_back(tt):
            pT_ = pT[tt % 2]
            eng = "dve" if tt % 2 == 0 else "act"
            if eng == "dve":
                S.op("dve", lambda e: e.tensor_copy(out=c.hT[:, :, tt * 128:(tt + 1) * 128], in_=pT_[:]), reads=[pT_.r()], writes=[c.hT.r(tt)])
            else:
                S.op("act", lambda e: e.copy(out=c.hT[:, :, tt * 128:(tt + 1) * 128], in_=pT_[:]), reads=[pT_.r()], writes=[c.hT.r(tt)])
        a_front(0)
        for tt in range(NTT):
            if tt + 1 < NTT:
                a_front(tt + 1)
            a_back(tt)

        hT_all = [c.hT.r(tt) for tt in range(NTT)]

        def load_w(src, c0, n, j):
            wf_, wb_ = wf32[j], wbf[j]
            S.op("sp", lambda e: e.dma_start(out=wf_[:, :, 0:n], in_=src[l, :, c0:c0 + n].rearrange("(k p) n -> p k n", p=128)),
                 writes=[wf_.r()], dma=True)
            for k in range(8):
                eng = "pool" if k % 2 == 0 else "act"
                if eng == "pool":
                    S.op("dve", lambda e, k=k: e.tensor_scalar(out=wb_[:, k, 0:n], in0=wf_[:, k, 0:n], scalar1=c.gainT[:, l * 8 + k:l * 8 + k + 1], scalar2=None, op0=ALU.mult),
                         reads=[wf_.r()], writes=[wb_.r(k)])
                else:
                    S.op("act", lambda e, k=k: e.activation(out=wb_[:, k, 0:n], in_=wf_[:, k, 0:n], func=AF.Copy, scale=c.gainT[:, l * 8 + k:l * 8 + k + 1]),
                         reads=[wf_.r()], writes=[wb_.r(k)])
            return wb_, [wb_.r(k) for k in range(8)]

        cnt = 0
        for cc in range(0, NF, 512):
            n = min(512, NF - cc)
            wb_, wres = load_w(c.wf, cc, n, (cc // 512) % 2)
            for c1 in range(0, n, 128):
                for tq in range(4):
                    j = cnt % 4
                    cnt += 1
                    po_, ot_ = po[j], ot[j]

                    def mm(e, po_=po_, wb_=wb_, c1=c1, tq=tq):
                        for k in range(8):
                            ins = e.matmul(po_[:], lhsT=wb_[:, k, c1:c1 + 128], rhs=c.hT[:, k, tq * 512:(tq + 1) * 512],
                                           start=(k == 0), stop=(k == 7))
                        return ins
                    S.op("pe", mm, reads=wres + hT_all[tq * 4:tq * 4 + 4], writes=[po_.r()])
                    ev_eng = "dve" if cnt % 2 == 0 else "act"
                    if ev_eng == "dve":
                        S.op("dve", lambda e, po_=po_, ot_=ot_: e.tensor_copy(out=ot_[:], in_=po_[:]), reads=[po_.r()], writes=[ot_.r()])
                    else:
                        S.op("act", lambda e, po_=po_, ot_=ot_: e.copy(out=ot_[:], in_=po_[:]), reads=[po_.r()], writes=[ot_.r()])
                    row = cc + c1
                    S.op("pool", lambda e, ot_=ot_, row=row, tq=tq: e.dma_start(out=c.Fs[b][row:row + 128, tq * 512:(tq + 1) * 512], in_=ot_[:]),
                         reads=[ot_.r()], writes=[c.Fs_r[b]], dma=True)

        for ci, cc in enumerate(range(0, NT, 512)):
            n = min(512, NT - cc)
            wb_, wres = load_w(c.wt, cc, n, (ci + 1) % 2)
            for tt in range(NTT):
                j = cnt % 4
                cnt += 1
                po_, ot_ = po[j], ot[j]

                def mm(e, po_=po_, wb_=wb_, tt=tt, n=n):
                    for k in range(8):
                        ins = e.matmul(po_[:, 0:n], lhsT=c.hT[:, k, tt * 128:(tt + 1) * 128], rhs=wb_[:, k, 0:n],
                                       start=(k == 0), stop=(k == 7))
                    return ins
                S.op("pe", mm, reads=wres + [hT_all[tt]], writes=[po_.r()])
                if cnt % 2 == 0:
                    S.op("dve", lambda e, po_=po_, ot_=ot_, n=n: e.tensor_copy(out=ot_[:, 0:n], in_=po_[:, 0:n]), reads=[po_.r()], writes=[ot_.r()])
                else:
                    S.op("act", lambda e, po_=po_, ot_=ot_, n=n: e.copy(out=ot_[:, 0:n], in_=po_[:, 0:n]), reads=[po_.r()], writes=[ot_.r()])
                S.op("pool", lambda e, ot_=ot_, tt=tt, cc=cc, n=n: e.dma_start(out=c.Ts[b][tt * 128:(tt + 1) * 128, cc:cc + n], in_=ot_[:, 0:n]),
                     reads=[ot_.r()], writes=[c.Ts_r[b]], dma=True)
        S.emit()


def _tmp(nc, st, name, shape, dtype, n=2, psum=False):
    return [_alloc(nc, st, f"{name}{i}", shape, dtype, psum) for i in range(n)]


def _dbg_y(c, b, tt, col, yt, res, width=256):
    if c.ydbg is None:
        return
    c.S.op("pool", lambda e: e.dma_start(out=c.ydbg[b, tt * 128:(tt + 1) * 128, col:col + width], in_=yt),
           reads=[res], writes=[c.ydbg_r], dma=True)


def _y_to_yT(c, st_tiles, yb, yb_r, mixer, tt, i):
    S = c.S
    pT = st_tiles[i]

    def tr(e):
        for j in range(2):
            ins = e.transpose(out=pT[:, j, :], in_=yb[:, j * 128:(j + 1) * 128], identity=c.identb[:])
        return ins
    S.op("pe", tr, reads=[yb_r], writes=[pT.r()])
    S.op("act", lambda e: e.copy(out=c.yT[:, 2 * mixer:2 * mixer + 2, tt * 128:(tt + 1) * 128], in_=pT[:]),
         reads=[pT.r()], writes=[c.yT.r((mixer, tt))])


def mixerB(c, l, b, ext_st=None):
    nc, S = c.nc, c.S
    with ExitStack() as own_st:
        st = ext_st if ext_st is not None else own_st
        A = lambda n, s, d, psum=False: _alloc(nc, st, n, s, d, psum)
        nbuf = 2 if ext_st is None else 1
        TM = lambda n, s, d, k=2, psum=False: _tmp(nc, st, n, s, d, (k if psum else nbuf), psum)
        ws32 = A("mb_ws32", [128, 4, 128], F32)
        ws = A("mb_ws", [128, 4, 128], BF16)
        lng = A("mb_lng", [128, 256], F32)
        lnb = A("mb_lnb", [128, 256], F32)
        bsT = A("mb_bs", [128, 4], F32)
        triT = A("mb_triT", [128, 128], F32)
        S.op("sp", lambda e: e.dma_start(out=triT[:], in_=c.p["triT"][:, :]), writes=[triT.r()], dma=True)
        S.op("sp", lambda e: e.dma_start(out=ws32[:], in_=c.p["sg_wsT"][l]), writes=[ws32.r()], dma=True)
        S.op("sp", lambda e: e.dma_start(out=lng[:], in_=c.p["sg_ln_gain"][l:l + 1, :].partition_broadcast(128)), writes=[lng.r()], dma=True)
        S.op("sp", lambda e: e.dma_start(out=lnb[:], in_=c.p["sg_ln_bias"][l:l + 1, :].partition_broadcast(128)), writes=[lnb.r()], dma=True)
        S.op("sp", lambda e: e.dma_start(out=bsT[:], in_=c.p["sg_bsT"][l]), writes=[bsT.r()], dma=True)
        S.op("dve", lambda e: e.tensor_tensor(out=ws[:], in0=ws32[:], in1=triT[:].unsqueeze(1).to_broadcast([128, 4, 128]), op=ALU.mult),
             reads=[ws32.r(), triT.r()], writes=[ws.r()])
        xin = TM("mb_in", [128, 768], F32)
        gl = TM("mb_gl", [128, 512], F32)
        sz = TM("mb_sz", [128, 256], F32)
        m4 = TM("mb_m4", [128, 4], F32)
        v4 = TM("mb_v4", [128, 4], F32)
        xc = TM("mb_xc", [128, 256], F32)
        t1 = TM("mb_t1", [128, 256], F32)
        vb = TM("mb_vb", [128, 256], BF16)
        pm = TM("mb_pm", [128, 256], F32, psum=True)
        y32 = TM("mb_y", [128, 256], F32)
        yb = _tmp(nc, st, "mb_yb", [128, 256], BF16, 2)
        pT = TM("mb_pT", [128, 2, 128], BF16, psum=True)
        g3 = lambda ap: ap.rearrange("p (g c) -> p g c", g=4)
        bc = lambda ap: ap.unsqueeze(2).to_broadcast([128, 4, 64])
        for tt in range(NTT):
            i = tt % nbuf
            xi, gl_, sz_, m_, v_, xc_, t_, vb_, pm_, y_, yb_ = xin[i], gl[i], sz[i], m4[i], v4[i], xc[i], t1[i], vb[i], pm[i], y32[i], yb[tt % 2]
            S.op("sp", lambda e, xi=xi, tt=tt: e.dma_start(out=xi[:], in_=c.Ts[b][tt * 128:(tt + 1) * 128, T_SU:T_SU + 768]),
                 reads=[c.Ts_r[b]], writes=[xi.r()], dma=True)
            S.op("act", lambda e, xi=xi, gl_=gl_: e.activation(out=gl_[:], in_=xi[:, 0:512], func=AF.Gelu_apprx_tanh), reads=[xi.r()], writes=[gl_.r()])
            S.op("act", lambda e, xi=xi, sz_=sz_: e.activation(out=sz_[:], in_=xi[:, 512:768], func=AF.Silu), reads=[xi.r()], writes=[sz_.r()])
            S.op("dve", lambda e, gl_=gl_, m_=m_: e.tensor_reduce(out=m_[:], in_=g3(gl_[:, 256:512]), axis=AX.X, op=ALU.add), reads=[gl_.r()], writes=[m_.r()])
            S.op("dve", lambda e, gl_=gl_, m_=m_, xc_=xc_: e.scalar_tensor_tensor(out=g3(xc_[:]), in0=bc(m_[:]), scalar=-1.0 / 64, in1=g3(gl_[:, 256:512]), op0=ALU.mult, op1=ALU.add),
                 reads=[gl_.r(), m_.r()], writes=[xc_.r()])
            S.op("dve", lambda e, xc_=xc_, t_=t_: e.tensor_tensor(out=t_[:], in0=xc_[:], in1=xc_[:], op=ALU.mult), reads=[xc_.r()], writes=[t_.r()])
            S.op("dve", lambda e, t_=t_, v_=v_: e.tensor_reduce(out=v_[:], in_=g3(t_[:]), axis=AX.X, op=ALU.add), reads=[t_.r()], writes=[v_.r()])
            S.op("act", lambda e, v_=v_: e.activation(out=v_[:], in_=v_[:], func=AF.Sqrt, scale=1.0 / 64, bias=c.eps5[:, 0:1]), reads=[v_.r()], writes=[v_.r()])
            S.op("dve", lambda e, v_=v_: e.reciprocal(out=v_[:], in_=v_[:]), reads=[v_.r()], writes=[v_.r()])
            S.op("dve", lambda e, xc_=xc_, v_=v_, t_=t_: e.tensor_tensor(out=g3(t_[:]), in0=g3(xc_[:]), in1=bc(v_[:]), op=ALU.mult), reads=[xc_.r(), v_.r()], writes=[t_.r()])
            S.op("dve", lambda e, t_=t_: e.tensor_tensor(out=t_[:], in0=t_[:], in1=lng[:], op=ALU.mult), reads=[t_.r(), lng.r()], writes=[t_.r()])
            S.op("dve", lambda e, t_=t_, vb_=vb_: e.tensor_tensor(out=vb_[:], in0=t_[:], in1=lnb[:], op=ALU.add), reads=[t_.r(), lnb.r()], writes=[vb_.r()])

            def mm(e, vb_=vb_, pm_=pm_):
                for g in range(4):
                    ins = e.matmul(pm_[:, g * 64:(g + 1) * 64], lhsT=ws[:, g, :], rhs=vb_[:, g * 64:(g + 1) * 64], start=True, stop=True)
                return ins
            S.op("pe", mm, reads=[vb_.r(), ws.r()], writes=[pm_.r()])
            S.op("dve", lambda e, pm_=pm_, t_=t_: e.tensor_tensor(out=g3(t_[:]), in0=g3(pm_[:]), in1=bc(bsT[:]), op=ALU.add), reads=[pm_.r(), bsT.r()], writes=[t_.r()])
            S.op("dve", lambda e, t_=t_, gl_=gl_: e.tensor_tensor(out=t_[:], in0=t_[:], in1=gl_[:, 0:256], op=ALU.mult), reads=[t_.r(), gl_.r()], writes=[t_.r()])
            S.op("dve", lambda e, t_=t_, sz_=sz_, y_=y_: e.tensor_tensor(out=y_[:], in0=t_[:], in1=sz_[:], op=ALU.mult), reads=[t_.r(), sz_.r()], writes=[y_.r()])
            S.op("dve", lambda e, y_=y_, yb_=yb_: e.tensor_copy(out=yb_[:], in_=y_[:]), reads=[y_.r()], writes=[yb_.r()])
            _dbg_y(c, b, tt, 256, y_[:], y_.r())
            if tt >= 1:
                _y_to_yT(c, pT, yb[(tt - 1) % 2], yb[(tt - 1) % 2].r(), 1, tt - 1, (tt - 1) % 2)
        _y_to_yT(c, pT, yb[(NTT - 1) % 2], yb[(NTT - 1) % 2].r(), 1, NTT - 1, (NTT - 1) % 2)
        if ext_st is None:
            S.emit()


SLOPES = [2.0 ** (-8.0 * (h + 1) / 4) for h in range(4)]
NEG = -1e30


def nsa_consts():
    p = np.arange(128)[:, None, None]
    h_sl = np.array(SLOPES, dtype=np.float64)[None, :, None]
    m = (np.arange(248) - 120)[None, None, :]
    dist = p - 16 * m - 31
    wc = np.where(dist >= 0, -h_sl * dist, NEG).astype(np.float32)
    jp = (np.arange(62) - 30)[None, :]
    cur = (np.arange(128) >= 64).astype(np.int64)[:, None]
    sb = np.where(jp > cur, NEG, np.where((jp == cur) | (jp == cur - 1), 1e4, 0.0)).astype(np.float32)
    tq = np.arange(128)[None, None, None, :]
    pk = np.arange(128)[:, None, None, None]
    dl = np.arange(16)[None, :, None, None]
    hs = np.array(SLOPES, dtype=np.float64)[None, None, :, None]
    d4 = 128 * dl + tq - pk
    bb = np.where(d4 >= 0, -hs * d4, NEG).astype(np.float32)
    bw2 = np.where(d4[:, 2:3] < 256, bb[:, 2:3], NEG).astype(np.float32)[:, 0]
    j = np.arange(32)[:, None, None]
    kb = np.arange(16)[None, :, None]
    pp = (np.arange(128) >= 64).astype(np.int64)[None, None, :]
    esel = (j == 2 * kb + pp).astype(np.float32)
    return {"nsa_wc": wc, "nsa_sb": sb, "nsa_bb": bb.reshape(128, 16, 512), "nsa_bw2": bw2.reshape(128, 512), "nsa_esel": esel}


def mixerD(c, l, b):
    nc, S = c.nc, c.S
    P = c.p
    with ExitStack() as st:
        A = lambda n, s, d, psum=False: _alloc(nc, st, n, s, d, psum)
        TM = lambda n, s, d, k=2, psum=False: _tmp(nc, st, n, s, d, k, psum)
        wc = A("md_wc", [128, 4, 248], F32)
        sbt = A("md_sb", [128, 62], F32)
        bb = A("md_bb", [128, 16, 512], F32)
        bw2 = A("md_bw2", [128, 512], F32)
        esel = A("md_esel", [32, 16, 128], BF16)
        for tl, nm in ((wc, "nsa_wc"), (sbt, "nsa_sb"), (bb, "nsa_bb"), (bw2, "nsa_bw2")):
            S.op("sp", lambda e, tl=tl, nm=nm: e.dma_start(out=tl[:], in_=P[nm]), writes=[tl.r()], dma=True)
        qT = A("md_qT", [64, 4, T], BF16)
        ksT = A("md_ksT", [64, T], BF16)
        kwT = A("md_kwT", [64, T], BF16)
        vs = A("md_vs", [128, NTT, 65], BF16)
        vw = A("md_vw", [128, NTT, 65], BF16)
        selT = A("md_selT", [32, T], BF16)
        kcmpT = A("md_kcmpT", [64, 128], BF16)
        vcmp = A("md_vcmp", [128, 64], BF16)
        ps_c = A("md_ps_c", [128, 4, 128], F32, psum=True)
        pmisc = A("md_pmisc", [128, 1024], BF16, psum=True)
        ps_s = TM("md_ps_s", [128, 512], F32, 2, psum=True)
        pmask4 = A("md_pmask", [128, 4, 128], F32, psum=True)
        o_s = A("md_o_s", [128, 4, 65], F32, psum=True)
        o_w = A("md_o_w", [128, 4, 65], F32, psum=True)
        o_c = A("md_o_c", [128, 4, 64], F32, psum=True)

        pst = ExitStack()
        A0, TM0 = A, TM
        A = lambda n, s, d, psum=False: _alloc(nc, pst, n, s, d, psum)
        TM = lambda n, s, d, k=2, psum=False: _tmp(nc, pst, n, s, d, k, psum)
        stg = TM("md_stg", [64, T], F32)
        esel32 = A("md_esel32", [32, 16, 128], F32)
        S.op("sp", lambda e: e.dma_start(out=esel32[:], in_=P["nsa_esel"]), writes=[esel32.r()], dma=True)
        S.op("act", lambda e: e.copy(out=esel[:], in_=esel32[:]), reads=[esel32.r()], writes=[esel.r()])
        cl = [A(f"md_cl{i}", [64, T], BF16) for i in range(2)]
        ch = [A(f"md_ch{i}", [64, T], BF16) for i in range(2)]
        vstg = A("md_vstg", [128, NTT, 128], F32)
        posT = A("md_posT", [64, 2, 32], F32)
        w1s = A("md_w1s", [64, 32, 64], F32)
        w1 = [A(f"md_w1{i}", [64, 32, 64], BF16) for i in range(2)]
        w2s = A("md_w2s", [64, 2, 64], F32)
        w2 = A("md_w2", [64, 2, 64], BF16)
        h1T = [A(f"md_h1T{i}", [64, 128], BF16) for i in range(2)]
        Fsb, Tsb = c.Fs[b], c.Ts[b]
        for h in range(4):
            sg = stg[h % 2]
            S.op("sp", lambda e, sg=sg, h=h: e.dma_start(out=sg[:], in_=Fsb[F_NQ + h * 64:F_NQ + (h + 1) * 64, :]),
                 reads=[c.Fs_r[b]], writes=[sg.r()], dma=True)
            S.op("act", lambda e, sg=sg, h=h: e.activation(out=qT[:, h, :], in_=sg[:], func=AF.Copy, scale=0.125), reads=[sg.r()], writes=[qT.r(h)])
        qT_all = [qT.r(h) for h in range(4)]
        for i, (row, dst) in enumerate(((F_KS, ksT), (F_KW, kwT))):
            sg = stg[i % 2]
            S.op("sp", lambda e, sg=sg, row=row: e.dma_start(out=sg[:], in_=Fsb[row:row + 64, :]), reads=[c.Fs_r[b]], writes=[sg.r()], dma=True)
            S.op("act", lambda e, sg=sg, dst=dst: e.copy(out=dst[:], in_=sg[:]), reads=[sg.r()], writes=[dst.r()])
        S.op("sp", lambda e: e.dma_start(out=posT[:], in_=P["cmp_posT"][l]), writes=[posT.r()], dma=True)
        v3 = lambda ap: ap.rearrange("p (n s) -> p n s", s=16)
        for i, row in enumerate((F_KC, F_VC)):
            sg = stg[i % 2]
            S.op("sp", lambda e, sg=sg, row=row: e.dma_start(out=sg[:], in_=Fsb[row:row + 64, :]), reads=[c.Fs_r[b]], writes=[sg.r()], dma=True)
            S.op("dve", lambda e, sg=sg, i=i: e.tensor_tensor(out=v3(cl[i][:]), in0=v3(sg[:]), in1=posT[:, i, 0:16].unsqueeze(1).to_broadcast([64, 128, 16]), op=ALU.add),
                 reads=[sg.r(), posT.r()], writes=[cl[i].r()])
            S.op("dve", lambda e, sg=sg, i=i: e.tensor_tensor(out=v3(ch[i][:]), in0=v3(sg[:]), in1=posT[:, i, 16:32].unsqueeze(1).to_broadcast([64, 128, 16]), op=ALU.add),
                 reads=[sg.r(), posT.r()], writes=[ch[i].r()])
        S.op("sp", lambda e: e.dma_start(out=vstg[:], in_=Tsb[:, T_VS:T_VS + 128].rearrange("(n p) c -> p n c", p=128)),
             reads=[c.Ts_r[b]], writes=[vstg.r()], dma=True)
        for i, dst in enumerate((vs, vw)):
            S.op("pool", lambda e, dst=dst: e.memset(dst[:, :, 64:65], 1.0), writes=[dst.r()])
            S.op("act", lambda e, dst=dst, i=i: e.copy(out=dst[:, :, 0:64], in_=vstg[:, :, i * 64:(i + 1) * 64]), reads=[vstg.r()], writes=[dst.r()])
        S.op("sp", lambda e: e.dma_start(out=w2s[:], in_=P["nsa_cmp_w2"][l].rearrange("i h e -> h i e")), writes=[w2s.r()], dma=True)
        S.op("act", lambda e: e.copy(out=w2[:], in_=w2s[:]), reads=[w2s.r()], writes=[w2.r()])
        S.op("pool", lambda e: e.memset(kcmpT[:], 0.0), writes=[kcmpT.r()])
        S.op("pool", lambda e: e.memset(vcmp[:], 0.0), writes=[vcmp.r()])
        for i in range(2):
            S.op("sp", lambda e, i=i: e.dma_start(out=w1s[:], in_=P["nsa_cmp_w1"][l, i].rearrange("(p e) h -> e p h", e=64)), writes=[w1s.r()], dma=True)
            S.op("act", lambda e, i=i: e.copy(out=w1[i][:], in_=w1s[:]), reads=[w1s.r()], writes=[w1[i].r()])
            ph = ps_s[i]

            def mm1(e, i=i, ph=ph):
                for pp in range(32):
                    src = cl[i] if pp < 16 else ch[i]
                    if pp < 16:
                        rhs = v3(src[:])[:, 0:127, pp]
                    else:
                        rhs = v3(src[:])[:, 1:128, pp - 16]
                    ins = e.matmul(ph[0:64, 0:127], lhsT=w1[i][:, pp, :], rhs=rhs, start=(pp == 0), stop=(pp == 31))
                return ins
            S.op("pe", mm1, reads=[w1[i].r(), cl[i].r(), ch[i].r()], writes=[ph.r()])
            S.op("act", lambda e, i=i, ph=ph: e.activation(out=h1T[i][:, 0:127], in_=ph[0:64, 0:127], func=AF.Gelu_apprx_tanh), reads=[ph.r()], writes=[h1T[i].r()])
        S.op("pe", lambda e: e.matmul(ps_s[0][0:64, 0:127], lhsT=w2[:, 0, :], rhs=h1T[0][:, 0:127], start=True, stop=True),
             reads=[w2.r(), h1T[0].r()], writes=[ps_s[0].r()])
        S.op("dve", lambda e: e.tensor_copy(out=kcmpT[:, 0:127], in_=ps_s[0][0:64, 0:127]), reads=[ps_s[0].r()], writes=[kcmpT.r()])
        S.op("pe", lambda e: e.matmul(ps_s[1][0:127, 0:64], lhsT=h1T[1][:, 0:127], rhs=w2[:, 1, :], start=True, stop=True),
             reads=[w2.r(), h1T[1].r()], writes=[ps_s[1].r()])
        S.op("dve", lambda e: e.tensor_copy(out=vcmp[0:127, :], in_=ps_s[1][0:127, 0:64]), reads=[ps_s[1].r()], writes=[vcmp.r()])

        S.emit()
        pst.close()
        A, TM = A0, TM0
        sbuf2 = TM("md_sbuf", [128, 4, 128], F32)
        mx = TM("md_mx", [128, 4], F32)
        sm = TM("md_sm", [128, 4], F32)
        p32 = TM("md_p32", [128, 4, 128], F32)
        pb = TM("md_pb", [128, 4, 128], BF16)
        pTc = TM("md_pTc", [128, 4, 128], BF16)
        psum_h = TM("md_psh", [128, 128], F32)
        a32 = TM("md_a32", [128, 32], F32)
        imp = TM("md_imp", [128, 32], F32)
        top8 = TM("md_top8", [128, 8], F32)
        selb = TM("md_selb", [128, 32], BF16)
        tmp = TM("md_tmp", [128, 512], F32, 3)
        maskb = TM("md_maskb", [128, NTT, 128], BF16)
        eb = TM("md_eb", [128, 512], BF16, 3)
        pm = TM("md_pm", [128, 512], BF16, 3)
        oc = TM("md_oc", [128, 256], F32, 3)
        osn = TM("md_osn", [128, 256], F32)
        own = TM("md_own", [128, 256], F32)
        rsw = TM("md_rsw", [128, 8], F32)
        gz = TM("md_gz", [128, 12 + 256], F32)
        gsg = TM("md_gsg", [128, 12], F32)
        szz = TM("md_szz", [128, 256], F32)
        yy = TM("md_yy", [128, 256], F32)
        yb = TM("md_yb", [128, 256], BF16)
        pT_y = [Tile(pmisc.t) for _ in range(2)]
        g4 = lambda ap: ap.rearrange("p (h c) -> p h c", h=4)
        bc4 = lambda ap, n: ap.unsqueeze(2).to_broadcast([128, 4, n])
        cntl = [0]

        def cmp_part(tt):
            i = tt % 2
            tsl = slice(tt * 128, (tt + 1) * 128)
            def mmc(e):
                for h in range(4):
                    ins = e.matmul(ps_c[:, h, :], lhsT=qT[:, h, tsl], rhs=kcmpT[:], start=True, stop=True)
                return ins
            S.op("pe", mmc, reads=qT_all + [kcmpT.r()], writes=[ps_c.r()])
            sb_, mx_, sm_, p32_, pb_, pTc_ = sbuf2[i], mx[i], sm[i], p32[i], pb[i], pTc[i]
            o0 = 120 - 8 * tt
            S.op("dve", lambda e: e.tensor_tensor(out=sb_[:], in0=ps_c[:], in1=wc[:, :, o0:o0 + 128], op=ALU.add), reads=[ps_c.r(), wc.r()], writes=[sb_.r()])
            S.op("dve", lambda e: e.tensor_reduce(out=mx_[:], in_=sb_[:], axis=AX.X, op=ALU.max), reads=[sb_.r()], writes=[mx_.r()])
            S.op("dve", lambda e: e.tensor_scalar(out=mx_[:], in0=mx_[:], scalar1=-1e4, scalar2=-1.0, op0=ALU.max, op1=ALU.mult), reads=[mx_.r()], writes=[mx_.r()])
            S.op("dve", lambda e: e.tensor_tensor(out=sb_[:], in0=sb_[:], in1=bc4(mx_[:], 128), op=ALU.add), reads=[sb_.r(), mx_.r()], writes=[sb_.r()])
            S.op("act", lambda e: e.activation(out=sb_[:], in_=sb_[:], func=AF.Exp), reads=[sb_.r()], writes=[sb_.r()])
            S.op("dve", lambda e: e.tensor_reduce(out=sm_[:], in_=sb_[:], axis=AX.X, op=ALU.add), reads=[sb_.r()], writes=[sm_.r()])
            S.op("dve", lambda e: e.tensor_scalar(out=sm_[:], in0=sm_[:], scalar1=1e-30, scalar2=None, op0=ALU.add), reads=[sm_.r()], writes=[sm_.r()])
            S.op("dve", lambda e: e.reciprocal(out=sm_[:], in_=sm_[:]), reads=[sm_.r()], writes=[sm_.r()])
            S.op("dve", lambda e: e.tensor_tensor(out=p32_[:], in0=sb_[:], in1=bc4(sm_[:], 128), op=ALU.mult), reads=[sb_.r(), sm_.r()], writes=[p32_.r()])
            S.op("act", lambda e: e.copy(out=pb_[:], in_=p32_[:]), reads=[p32_.r()], writes=[pb_.r()])
            def trc(e):
                for h in range(4):
                    ins = e.transpose(out=pmisc[:, h * 128:(h + 1) * 128], in_=pb_[:, h, :], identity=c.identb[:])
                return ins
            S.op("pe", trc, reads=[pb_.r()], writes=[pmisc.r()])
            S.op("act", lambda e: e.copy(out=pTc_[:].rearrange("p h n -> p (h n)"), in_=pmisc[:, 0:512]), reads=[pmisc.r()], writes=[pTc_.r()])

            def mmoc(e):
                for h in range(4):
                    ins = e.matmul(o_c[:, h, :], lhsT=pTc_[:, h, :], rhs=vcmp[:], start=True, stop=True)
                return ins
            S.op("pe", mmoc, reads=[pTc_.r(), vcmp.r()], writes=[o_c.r()])
            oc_ = oc[tt % 3]
            S.op("act", lambda e: e.copy(out=oc_[:], in_=o_c[:].rearrange("p h c -> p (h c)")), reads=[o_c.r()], writes=[oc_.r()])
            ph_, a_, imp_, t8_, selb_ = psum_h[i], a32[i], imp[i], top8[i], selb[i]
            S.op("dve", lambda e: e.tensor_reduce(out=ph_[:], in_=p32_[:].rearrange("p h n -> p n h"), axis=AX.X, op=ALU.add), reads=[p32_.r()], writes=[ph_.r()])
            pv = lambda ap: ap.rearrange("p (j m) -> p j m", m=4)
            S.op("dve", lambda e: e.tensor_reduce(out=a_[:], in_=pv(ph_[:]), axis=AX.X, op=ALU.add), reads=[ph_.r()], writes=[a_.r()])
            S.op("dve", lambda e: e.scalar_tensor_tensor(out=imp_[:], in0=pv(ph_[:])[:, :, 3], scalar=-0.5, in1=a_[:], op0=ALU.mult, op1=ALU.add),
                 reads=[ph_.r(), a_.r()], writes=[imp_.r()])
            S.op("dve", lambda e: e.scalar_tensor_tensor(out=imp_[:, 1:32], in0=pv(ph_[:])[:, 0:31, 3], scalar=0.5, in1=imp_[:, 1:32], op0=ALU.mult, op1=ALU.add),
                 reads=[ph_.r(), imp_.r()], writes=[imp_.r()])
            j0 = 30 - 2 * tt
            S.op("dve", lambda e: e.tensor_tensor(out=imp_[:], in0=imp_[:], in1=sbt[:, j0:j0 + 32], op=ALU.add), reads=[imp_.r(), sbt.r()], writes=[imp_.r()])
            S.op("dve", lambda e: e.tensor_scalar(out=imp_[:, 0:1], in0=imp_[:, 0:1], scalar1=1e4, scalar2=None, op0=ALU.add), reads=[imp_.r()], writes=[imp_.r()])
            S.op("dve", lambda e: e.max(out=t8_[:], in_=imp_[:]), reads=[imp_.r()], writes=[t8_.r()])
            S.op("dve", lambda e: e.tensor_scalar(out=selb_[:], in0=imp_[:], scalar1=t8_[:, 7:8], scalar2=None, op0=ALU.is_ge),
                 reads=[imp_.r(), t8_.r()], writes=[selb_.r()])
            S.op("pe", lambda e: e.transpose(out=pmisc[0:32, 512:640], in_=selb_[:], identity=c.identb[:]), reads=[selb_.r()], writes=[pmisc.r()])
            S.op("act", lambda e: e.copy(out=selT[:, tsl], in_=pmisc[0:32, 512:640]), reads=[pmisc.r()], writes=[selT.r(tt)])

        def att_part(tt):
            i = tt % 2
            tsl = slice(tt * 128, (tt + 1) * 128)
            oc_, osn_, own_, rs_ = oc[tt % 3], osn[i], own[i], rsw[i]
            mk_ = maskb[i]
            for k0 in range(0, tt + 1, 4):
                k1 = min(k0 + 4, tt + 1)

                def mmm(e, k0=k0, k1=k1):
                    for kb in range(k0, k1):
                        ins = e.matmul(pmask4[:, kb - k0, :], lhsT=esel[:, kb, :], rhs=selT[:, tsl], start=True, stop=True)
                    return ins
                S.op("pe", mmm, reads=[esel.r(), selT.r(tt)], writes=[pmask4.r()])
                S.op("act", lambda e, k0=k0, k1=k1: e.copy(out=mk_[:, k0:k1, :], in_=pmask4[:, 0:k1 - k0, :]), reads=[pmask4.r()], writes=[mk_.r()])
            kbs = [kb for kb in (tt - 2, tt - 1, tt) if kb >= 0]
            its = [("s", kb) for kb in range(tt + 1)] + [("w", kb) for kb in kbs]
            bufs = []
            for _ in its:
                bufs.append((cntl[0] % 3, cntl[0] % 2))
                cntl[0] += 1

            def front(k):
                kind, kb = its[k]
                j, jj = bufs[k]
                pss, tmp_, eb_, pm_ = ps_s[jj], tmp[j], eb[j], pm[j]
                ksl = slice(kb * 128, (kb + 1) * 128)
                kT = ksT if kind == "s" else kwT
                S.op("pe", lambda e: e.matmul(pss[:], lhsT=kT[:, ksl], rhs=qT[:, :, tsl], start=True, stop=True), reads=qT_all + [kT.r()], writes=[pss.r()])
                dlt = tt - kb
                btab = bw2[:] if (kind == "w" and dlt == 2) else bb[:, dlt, :]
                S.op("dve", lambda e: e.tensor_tensor(out=tmp_[:], in0=pss[:], in1=btab, op=ALU.add), reads=[pss.r(), bb.r(), bw2.r()], writes=[tmp_.r()])
                S.op("act", lambda e: e.activation(out=eb_[:], in_=tmp_[:], func=AF.Exp), reads=[tmp_.r()], writes=[eb_.r()])
                if kind == "s":
                    S.op("pool", lambda e: e.tensor_tensor(out=g4(pm_[:]), in0=g4(eb_[:]), in1=mk_[:, kb, :].unsqueeze(1).to_broadcast([128, 4, 128]), op=ALU.mult),
                         reads=[eb_.r(), mk_.r()], writes=[pm_.r()])

            def back(k):
                kind, kb = its[k]
                j, jj = bufs[k]
                eb_, pm_ = eb[j], pm[j]
                if kind == "s":
                    def mmpv(e):
                        for h in range(4):
                            ins = e.matmul(o_s[:, h, :], lhsT=pm_[:, h * 128:(h + 1) * 128], rhs=vs[:, kb, :], start=(kb == 0 and h == 0), stop=(kb == tt and h == 3), skip_group_check=True)
                        return ins
                    S.op("pe", mmpv, reads=[pm_.r(), vs.r()], writes=[o_s.r()], pe_acc=(kb > 0))
                else:
                    first, last = (kb == kbs[0]), (kb == kbs[-1])

                    def mmpw(e):
                        for h in range(4):
                            ins = e.matmul(o_w[:, h, :], lhsT=eb_[:, h * 128:(h + 1) * 128], rhs=vw[:, kb, :], start=(first and h == 0), stop=(last and h == 3), skip_group_check=True)
                        return ins
                    S.op("pe", mmpw, reads=[eb_.r(), vw.r()], writes=[o_w.r()], pe_acc=(not first))
            nit = len(its)
            front(0)
            if nit > 1:
                front(1)
            for k in range(nit):
                back(k)
                if k + 2 < nit:
                    front(k + 2)

        def att_tail(tt):
            i = tt % 2
            tsl = slice(tt * 128, (tt + 1) * 128)
            oc_, osn_, own_, rs_ = oc[tt % 3], osn[i], own[i], rsw[i]
            S.op("dve", lambda e: e.reciprocal(out=rs_[:, 0:4], in_=o_s[:, :, 64]), reads=[o_s.r()], writes=[rs_.r()])
            S.op("dve", lambda e: e.tensor_tensor(out=g4(osn_[:]), in0=o_s[:, :, 0:64], in1=bc4(rs_[:, 0:4], 64), op=ALU.mult), reads=[o_s.r(), rs_.r()], writes=[osn_.r()])
            S.op("dve", lambda e: e.reciprocal(out=rs_[:, 4:8], in_=o_w[:, :, 64]), reads=[o_w.r()], writes=[rs_.r()])
            S.op("dve", lambda e: e.tensor_tensor(out=g4(own_[:]), in0=o_w[:, :, 0:64], in1=bc4(rs_[:, 4:8], 64), op=ALU.mult), reads=[o_w.r(), rs_.r()], writes=[own_.r()])
            gz_, gs_, sz_, y_, yb_ = gz[i], gsg[i], szz[i], yy[i], yb[i]
            S.op("sp", lambda e: e.dma_start(out=gz_[:, 0:12], in_=Tsb[tsl, T_NG:T_NG + 12]), reads=[c.Ts_r[b]], writes=[gz_.r()], dma=True)
            S.op("sp", lambda e: e.dma_start(out=gz_[:, 12:268], in_=Tsb[tsl, T_NZ:T_NZ + 256]), reads=[c.Ts_r[b]], writes=[gz_.r(1)], dma=True)
            S.op("act", lambda e: e.activation(out=gs_[:], in_=gz_[:, 0:12], func=AF.Sigmoid), reads=[gz_.r()], writes=[gs_.r()])
            S.op("act", lambda e: e.activation(out=sz_[:], in_=gz_[:, 12:268], func=AF.Silu), reads=[gz_.r(1)], writes=[sz_.r()])
            gv = lambda k: gs_[:].rearrange("p (h k) -> p h k", k=3)[:, :, k].unsqueeze(2).to_broadcast([128, 4, 64])
            S.op("dve", lambda e: e.tensor_tensor(out=g4(oc_[:]), in0=g4(oc_[:]), in1=gv(0), op=ALU.mult), reads=[oc_.r(), gs_.r()], writes=[oc_.r()])
            S.op("dve", lambda e: e.tensor_tensor(out=g4(osn_[:]), in0=g4(osn_[:]), in1=gv(1), op=ALU.mult), reads=[osn_.r(), gs_.r()], writes=[osn_.r()])
            S.op("dve", lambda e: e.tensor_tensor(out=g4(own_[:]), in0=g4(own_[:]), in1=gv(2), op=ALU.mult), reads=[own_.r(), gs_.r()], writes=[own_.r()])
            S.op("dve", lambda e: e.tensor_tensor(out=oc_[:], in0=oc_[:], in1=osn_[:], op=ALU.add), reads=[oc_.r(), osn_.r()], writes=[oc_.r()])
            S.op("dve", lambda e: e.tensor_tensor(out=oc_[:], in0=oc_[:], in1=own_[:], op=ALU.add), reads=[oc_.r(), own_.r()], writes=[oc_.r()])
            S.op("dve", lambda e: e.tensor_tensor(out=y_[:], in0=oc_[:], in1=sz_[:], op=ALU.mult), reads=[oc_.r(), sz_.r()], writes=[y_.r()])
            S.op("act", lambda e: e.copy(out=yb_[:], in_=y_[:]), reads=[y_.r()], writes=[yb_.r()])
            _dbg_y(c, b, tt, 768, y_[:], y_.r())

            def tr(e):
                for jx in range(2):
                    ins = e.transpose(out=pmisc[:, 640 + jx * 128:640 + (jx + 1) * 128], in_=yb_[:, jx * 128:(jx + 1) * 128], identity=c.identb[:])
                return ins
            S.op("pe", tr, reads=[yb_.r()], writes=[pmisc.r()])
            S.op("act", lambda e: e.copy(out=c.yT[:, 6:8, tsl], in_=pmisc[:, 640:896].rearrange("p (j n) -> p j n", j=2)),
                 reads=[pmisc.r()], writes=[c.yT.r((3, tt))])

        def rec(fn, t_):
            S.defer = []
            fn(t_)
            lst, S.defer = S.defer, None
            return lst
        S.replay(rec(cmp_part, 0))
        for tt in range(NTT + 1):
            lists = []
            if tt < NTT:
                lists.append(rec(att_part, tt))
            if tt >= 1:
                lists.append(rec(att_tail, tt - 1))
            if tt + 1 < NTT:
                lists.append(rec(cmp_part, tt + 1))
            S.replay(*lists)
        S.emit()


def _dplr_loop(c, st, b, pfx, AQ, Bt, Kt, BKV, Pend, bonT, l):
    nc, S = c.nc, c.S
    P = c.p
    Tsb = c.Ts[b]
    NCH = T // 64
    NPR = NCH // 2
    A = lambda n, s, d, psum=False: _alloc(nc, st, pfx + n, s, d, psum)
    TM = lambda n, s, d, k=2, psum=False: _tmp(nc, st, pfx + n, s, d, k, psum)
    pTr = A("_pTr", [128, 1024], BF16, psum=True)
    pA = A("_pA", [64, 8, 128], F32, psum=True)
    pB = A("_pB", [64, 8, 128], F32, psum=True)
    pXU = A("_pXU", [64, 2, 256], F32, psum=True)
    pO = A("_pO", [128, 512], F32, psum=True)
    pHd = A("_pHd", [128, 2, 256], F32, psum=True)
    H = A("_H", [128, 2, 64], F32)
    Hbs = TM("_Hb", [128, 2, 64], BF16)
    Hs = A("_Hs", [128, 2, 64], F32)
    tokm = TM("_tokm", [64, 2, 6, 128], BF16)
    NY = TM("_NY", [64, 8, 128], BF16, 2)
    R = TM("_R", [64, 8, 64], BF16, 2)
    TTs = TM("_TT", [64, 8, 64], BF16)
    MQ = TM("_MQ", [64, 8, 64], BF16)
    LM = TM("_LM", [64, 8, 128], BF16)
    XS = TM("_XS", [64, 256], BF16)
    Ub = TM("_Ub", [64, 256], BF16)
    gng = A("_gng", [128, 256], F32)
    gnb = A("_gnb", [128, 256], F32)
    S.op("sp", lambda e: e.dma_start(out=gng[:], in_=P["rw_gn_gain"][l:l + 1, :].partition_broadcast(128)), writes=[gng.r()], dma=True)
    S.op("sp", lambda e: e.dma_start(out=gnb[:], in_=P["rw_gn_bias"][l:l + 1, :].partition_broadcast(128)), writes=[gnb.r()], dma=True)
    S.op("dve", lambda e: e.memset(H[:], 0.0), writes=[H.r()])
    S.op("dve", lambda e: e.memset(Hbs[0][:], 0.0), writes=[Hbs[0].r()])
    o32 = TM("_o32", [128, 256], F32)
    m4 = TM("_m4", [128, 4], F32)
    v4 = TM("_v4", [128, 4], F32)
    xc = TM("_xc", [128, 256], F32)
    t1 = TM("_t1", [128, 256], F32)
    zin = TM("_zin", [128, 256], F32)
    bon = TM("_bon", [128, 256], BF16)
    y32 = TM("_y32", [128, 256], F32)
    yb = TM("_yb", [128, 256], BF16)
    g3 = lambda ap: ap.rearrange("p (g c) -> p g c", g=4)
    bc = lambda ap: ap.unsqueeze(2).to_broadcast([128, 4, 64])
    M1 = c.m1cat
    ML = c.maskL
    I64 = c.ident[0:64, 0:64]
    bc8 = lambda ap, n: ap.unsqueeze(1).to_broadcast([64, 8, n])
    AQm = AQ
    AQr = [a.r(k) for a in AQm for k in ((0, 0), (0, 1), (1, 0), (1, 1))]
    BKVr = [BKV.r((cc, i)) for cc in range(2) for i in range(3)]

    def phase1(m):
        i2 = m % 2
        tk, mq, lm, tts = tokm[i2], MQ[i2], LM[i2], TTs[i2]
        for ci in range(2):
            n = 2 * m + ci

            def tra(e, n=n):
                for cc in range(2):
                    for it in range(3):
                        ins = e.transpose(out=pTr[0:64, (cc * 3 + it) * 128:(cc * 3 + it + 1) * 128], in_=BKV[:, cc, n, it, :], identity=c.identb[:])
                return ins
            S.op("pe", tra, reads=BKVr, writes=[pTr.r()])
            S.op("act", lambda e, ci=ci: e.copy(out=tk[:, ci, :, :].rearrange("p a b -> p (a b)"), in_=pTr[0:64, 0:768]), reads=[pTr.r()], writes=[tk.r()])

        def mmx2(e):
            for ci in range(2):
                n = 2 * m + ci
                nsl = slice(n * 64, (n + 1) * 64)
                for h in range(4):
                    cc = h // 2
                    ins = e.matmul(pB[:, ci * 4 + h, :], lhsT=Kt[:, cc, nsl], rhs=AQm[h % 2][:, cc, n, :], start=True, stop=True)
            return ins
        S.op("pe", mmx2, reads=AQr + [Kt.r(0), Kt.r(1)], writes=[pB.r()])
        S.op("dve", lambda e: e.tensor_tensor(out=lm[:], in0=pB[:], in1=bc8(M1[:], 128), op=ALU.mult), reads=[pB.r(), M1.r()], writes=[lm.r()])

        def mmx1(e):
            for ci in range(2):
                n = 2 * m + ci
                nsl = slice(n * 64, (n + 1) * 64)
                for h in range(4):
                    cc = h // 2
                    aq = AQm[h % 2]
                    e.matmul(pA[:, ci * 4 + h, :], lhsT=Bt[:, cc, nsl], rhs=aq[:, cc, n, :], start=True, stop=True)
                    ins = e.matmul(pB[:, ci * 4 + h, 0:64], lhsT=aq[:, cc, n, 0:64], rhs=Bt[:, cc, nsl], start=True, stop=True)
            return ins
        S.op("pe", mmx1, reads=AQr + [Bt.r(0), Bt.r(1)], writes=[pA.r(), pB.r()])
        ny, r_ = NY[0], R[0]
        S.op("dve", lambda e: e.tensor_tensor(out=ny[:, :, 0:64], in0=pA[:, :, 0:64], in1=bc8(M1[:, 0:64], 64), op=ALU.mult), reads=[pA.r(), M1.r()], writes=[ny.r()])
        S.op("dve", lambda e: e.tensor_tensor(out=ny[:, :, 64:128], in0=ny[:, :, 0:64], in1=bc8(I64, 64), op=ALU.add), reads=[ny.r(), c.ident.r()], writes=[ny.r()])
        S.op("dve", lambda e: e.tensor_tensor(out=mq[:], in0=pA[:, :, 64:128], in1=bc8(M1[:, 64:128], 64), op=ALU.mult), reads=[pA.r(), M1.r()], writes=[mq.r()])
        S.op("dve", lambda e: e.tensor_tensor(out=r_[:], in0=pB[:, :, 0:64], in1=bc8(ML[:], 64), op=ALU.mult), reads=[pB.r(), ML.r()], writes=[r_.r()])
        _neumann(S, NY, R, pA, pB, 8, tts)

    def phase2(m):
        i2 = m % 2
        tk, mq, lm, tts = tokm[i2], MQ[i2], LM[i2], TTs[i2]
        for ci in range(2):
            n = 2 * m + ci
            p0 = ci * 4
            xs_, ub_ = XS[ci], Ub[ci]
            Hb, Hbn = Hbs[n % 2], Hbs[(n + 1) % 2]
            S.op("dve", lambda e, n=n: e.tensor_tensor(out=Hs[:], in0=H[:], in1=Pend[:, :, n].unsqueeze(2).to_broadcast([128, 2, 64]), op=ALU.mult),
                 reads=[H.r(), Pend.r(0), Pend.r(1)], writes=[Hs.r()])

            def mmd(e, n=n, ci=ci, p0=p0, Hb=Hb):
                for h in range(4):
                    cc, hp = h // 2, h % 2
                    e.matmul(pXU[:, 0, h * 64:(h + 1) * 64], lhsT=lm[:, p0 + h, 0:64], rhs=tk[:, ci, cc * 3 + 2, hp * 64:(hp + 1) * 64], start=True, stop=False)
                    ins = e.matmul(pXU[:, 0, h * 64:(h + 1) * 64], lhsT=AQm[hp][:, cc, n, 0:64], rhs=Hb[:, cc, :], start=False, stop=True)
                return ins
            S.op("pe", mmd, reads=[lm.r(), tk.r(), Hb.r()] + AQr, writes=[pXU.r()])
            S.op("act", lambda e, xs_=xs_: e.copy(out=xs_[:], in_=pXU[:, 0, :]), reads=[pXU.r()], writes=[xs_.r()])

            def mme(e, xs_=xs_, p0=p0):
                for h in range(4):
                    ins = e.matmul(pXU[:, 1, h * 64:(h + 1) * 64], lhsT=tts[:, p0 + h, :], rhs=xs_[:, h * 64:(h + 1) * 64], start=True, stop=True)
                return ins
            S.op("pe", mme, reads=[tts.r(), xs_.r()], writes=[pXU.r()])
            S.op("dve", lambda e, ub_=ub_: e.tensor_copy(out=ub_[:], in_=pXU[:, 1, :]), reads=[pXU.r()], writes=[ub_.r()])
            half = ci * 64

            def mmg(e, ci=ci, ub_=ub_):
                for h in range(4):
                    cc, hp = h // 2, h % 2
                    hs_ = slice(hp * 64, (hp + 1) * 64)
                    o_ = pHd[hs_, cc, 0:64]
                    e.matmul(o_, lhsT=tk[:, ci, cc * 3 + 0, hs_], rhs=ub_[:, h * 64:(h + 1) * 64], start=True, stop=False)
                    ins = e.matmul(o_, lhsT=tk[:, ci, cc * 3 + 1, hs_], rhs=tk[:, ci, cc * 3 + 2, hs_], start=False, stop=True)
                return ins
            S.op("pe", mmg, reads=[tk.r(), ub_.r()], writes=[pHd.r()])
            S.op("dve", lambda e, Hbn=Hbn: e.tensor_tensor(out=Hbn[:], in0=Hs[:], in1=pHd[:, :, 0:64], op=ALU.add), reads=[Hs.r(), pHd.r()], writes=[Hbn.r()])

            def mmf(e, n=n, ci=ci, p0=p0, ub_=ub_, half=half, Hb=Hb):
                for h in range(4):
                    cc, hp = h // 2, h % 2
                    o_ = pO[half:half + 64, h * 64:(h + 1) * 64]
                    e.matmul(o_, lhsT=AQm[hp][:, cc, n, 64:128], rhs=Hb[:, cc, :], start=True, stop=False)
                    e.matmul(o_, lhsT=mq[:, p0 + h, :], rhs=ub_[:, h * 64:(h + 1) * 64], start=False, stop=False)
                    ins = e.matmul(o_, lhsT=lm[:, p0 + h, 64:128], rhs=tk[:, ci, cc * 3 + 2, hp * 64:(hp + 1) * 64], start=False, stop=True)
                return ins
            S.op("pe", mmf, reads=[lm.r(), mq.r(), tk.r(), Hb.r(), ub_.r()] + AQr, writes=[pO.r(half)])
            S.op("dve", lambda e: e.tensor_tensor(out=H[:], in0=Hs[:], in1=pHd[:, :, 0:64], op=ALU.add), reads=[Hs.r(), pHd.r()], writes=[H.r()])

    def post(m):
        tt = m
        i = tt % 2
        tsl = slice(tt * 128, (tt + 1) * 128)
        o_, m_, v_, xc_, t_, z_, bo_, y_, yb_ = o32[i], m4[i], v4[i], xc[i], t1[i], zin[i], bon[i], y32[i], yb[i]
        S.op("sp", lambda e: e.dma_start(out=z_[:], in_=Tsb[tsl, T_RZ:T_RZ + 256]), reads=[c.Ts_r[b]], writes=[z_.r()], dma=True)
        S.op("act", lambda e: e.activation(out=z_[:], in_=z_[:], func=AF.Silu), reads=[z_.r()], writes=[z_.r()])
        S.op("act", lambda e: e.copy(out=o_[:], in_=pO[:, 0:256]), reads=[pO.r(0), pO.r(64)], writes=[o_.r()])

        def trb(e):
            for cc in range(2):
                ins = e.transpose(out=pTr[:, 768 + cc * 128:768 + (cc + 1) * 128], in_=bonT[:, cc, tsl], identity=c.identb[:])
            return ins
        S.op("pe", trb, reads=[bonT.r((cc, tq)) for cc in range(2) for tq in range(4)], writes=[pTr.r()])
        S.op("act", lambda e: e.copy(out=bo_[:], in_=pTr[:, 768:1024]), reads=[pTr.r()], writes=[bo_.r()])
        S.op("dve", lambda e: e.tensor_reduce(out=m_[:], in_=g3(o_[:]), axis=AX.X, op=ALU.add), reads=[o_.r()], writes=[m_.r()])
        S.op("dve", lambda e: e.scalar_tensor_tensor(out=g3(xc_[:]), in0=bc(m_[:]), scalar=-1.0 / 64, in1=g3(o_[:]), op0=ALU.mult, op1=ALU.add),
             reads=[o_.r(), m_.r()], writes=[xc_.r()])
        S.op("dve", lambda e: e.tensor_tensor(out=t_[:], in0=xc_[:], in1=xc_[:], op=ALU.mult), reads=[xc_.r()], writes=[t_.r()])
        S.op("dve", lambda e: e.tensor_reduce(out=v_[:], in_=g3(t_[:]), axis=AX.X, op=ALU.add), reads=[t_.r()], writes=[v_.r()])
        S.op("act", lambda e: e.activation(out=v_[:], in_=v_[:], func=AF.Sqrt, scale=1.0 / 64, bias=c.epsgn[:, 0:1]), reads=[v_.r()], writes=[v_.r()])
        S.op("dve", lambda e: e.reciprocal(out=v_[:], in_=v_[:]), reads=[v_.r()], writes=[v_.r()])
        S.op("dve", lambda e: e.tensor_tensor(out=g3(t_[:]), in0=g3(xc_[:]), in1=bc(v_[:]), op=ALU.mult), reads=[xc_.r(), v_.r()], writes=[t_.r()])
        S.op("dve", lambda e: e.tensor_tensor(out=t_[:], in0=t_[:], in1=gng[:], op=ALU.mult), reads=[t_.r(), gng.r()], writes=[t_.r()])
        S.op("dve", lambda e: e.tensor_tensor(out=t_[:], in0=t_[:], in1=gnb[:], op=ALU.add), reads=[t_.r(), gnb.r()], writes=[t_.r()])
        S.op("dve", lambda e: e.tensor_tensor(out=t_[:], in0=t_[:], in1=bo_[:], op=ALU.add), reads=[t_.r(), bo_.r()], writes=[t_.r()])
        S.op("dve", lambda e: e.tensor_tensor(out=y_[:], in0=t_[:], in1=z_[:], op=ALU.mult), reads=[t_.r(), z_.r()], writes=[y_.r()])
        S.op("act", lambda e: e.copy(out=yb_[:], in_=y_[:]), reads=[y_.r()], writes=[yb_.r()])
        _dbg_y(c, b, tt, 512, y_[:], y_.r())

        def tr(e):
            for jx in range(2):
                ins = e.transpose(out=pTr[:, 768 + jx * 128:768 + (jx + 1) * 128], in_=yb_[:, jx * 128:(jx + 1) * 128], identity=c.identb[:])
            return ins
        S.op("pe", tr, reads=[yb_.r()], writes=[pTr.r()])
        S.op("act", lambda e: e.copy(out=c.yT[:, 4:6, tsl], in_=pTr[:, 768:1024].rearrange("p (j n) -> p j n", j=2)),
             reads=[pTr.r()], writes=[c.yT.r((2, tt))])

    def rec(fn, m):
        S.defer = []
        fn(m)
        lst, S.defer = S.defer, None
        return lst

    S.replay(rec(phase1, 0))
    for m in range(NPR + 1):
        lists = []
        if m < NPR:
            lists.append(rec(phase2, m))
        if m >= 1:
            lists.append(rec(post, m - 1))
        if m + 1 < NPR:
            lists.append(rec(phase1, m + 1))
        S.replay(*lists)


def mixerC(c, l, b):
    nc, S = c.nc, c.S
    P = c.p
    Fsb, Tsb = c.Fs[b], c.Ts[b]
    NCH = T // 64
    with ExitStack() as st:
        A = lambda n, s, d, psum=False: _alloc(nc, st, n, s, d, psum)
        TM = lambda n, s, d, k=2, psum=False: _tmp(nc, st, n, s, d, k, psum)
        AQ0 = A("mc_AQ0", [128, 2, NCH, 128], BF16)
        AQ1 = A("mc_AQ1", [128, 2, NCH, 128], BF16)
        AQ = AQ0
        Bt = A("mc_Bt", [128, 2, T], BF16)
        Kt = A("mc_Kt", [128, 2, T], BF16)
        BKV = A("mc_BKV", [128, 2, NCH, 3, 64], BF16)
        bonT = A("mc_bonT", [128, 2, T], BF16)
        Pend = A("mc_Pend", [128, 2, NCH], F32)
        cols = A("mc_cols", [128, 2, 5], F32)
        S.op("sp", lambda e: e.dma_start(out=cols[:], in_=P["rw_cols"][l]), writes=[cols.r()], dma=True)
        pst = ExitStack()
        A1 = lambda n, s, d, psum=False: _alloc(nc, pst, n, s, d, psum)
        muT = A1("mc_muT", [128, 7], F32)
        lora = A1("mc_lora", [128, 256], F32)
        rst = A1("mc_rst", [128, T], BF16)
        wdad = A1("mc_wdad", [128, T], F32)
        t_lw = A1("mc_tlw", [128, T], F32)
        t_a = A1("mc_ta", [128, T], F32)
        t_gc = A1("mc_tgc", [128, T], F32)
        t0 = A1("mc_t0", [128, T], F32)
        t1 = A1("mc_t1", [128, T], F32)
        t_x = A1("mc_tx", [128, T], F32)
        xr = t1
        pp = [A1(f"mc_pp{i}", [128, 512], F32, psum=True) for i in range(4)]
        S.op("sp", lambda e: e.dma_start(out=muT[:], in_=P["rw_muT"][l]), writes=[muT.r()], dma=True)
        S.op("sp", lambda e: e.dma_start(out=lora[:], in_=P["rw_lora"][l]), writes=[lora.r()], dma=True)
        S.op("pool", lambda e: e.memset(rst[:], 1.0), writes=[rst.r()])
        S.op("pool", lambda e: e.memset(rst[:].rearrange("p (n s) -> p n s", s=64)[:, :, 0:1], 0.0), writes=[rst.r()])
        c3 = lambda ap: ap.rearrange("p (n s) -> p n s", s=64)

        def load_shift(dst, row, mcol, tmp):
            S.op("sp", lambda e: e.dma_start(out=dst[:], in_=Fsb[row:row + 128, :]), reads=[c.Fs_r[b]], writes=[dst.r()], dma=True)
            S.op("dve", lambda e: e.tensor_tensor(out=tmp[:, 1:T], in0=dst[:, 0:T - 1], in1=dst[:, 1:T], op=ALU.subtract), reads=[dst.r()], writes=[tmp.r()])
            S.op("dve", lambda e: e.tensor_scalar(out=tmp[:, 0:1], in0=dst[:, 0:1], scalar1=-1.0, scalar2=None, op0=ALU.mult), reads=[dst.r()], writes=[tmp.r()])
            S.op("dve", lambda e: e.scalar_tensor_tensor(out=dst[:], in0=tmp[:], scalar=muT[:, mcol:mcol + 1], in1=dst[:], op0=ALU.mult, op1=ALU.add),
                 reads=[tmp.r(), dst.r(), muT.r()], writes=[dst.r()])

        load_shift(wdad, F_RW + 768, 6, t0)
        S.op("act", lambda e: e.activation(out=wdad[0:64, :], in_=wdad[0:64, :], func=AF.Tanh), reads=[wdad.r()], writes=[wdad.r()])
        def prep_cc(cc):
            csl = slice(cc * 128, (cc + 1) * 128)
            for tq in range(4):
                qs = slice(tq * 512, (tq + 1) * 512)
                S.op("pe", lambda e, tq=tq, qs=qs: e.matmul(pp[tq][:], lhsT=lora[0:64, csl], rhs=wdad[0:64, qs], start=True, stop=True),
                     reads=[lora.r(), wdad.r()], writes=[pp[tq].r()])
                S.op("act", lambda e, tq=tq, qs=qs: e.activation(out=t_lw[:, qs], in_=pp[tq][:], func=AF.Sigmoid, bias=cols[:, cc, 0:1]),
                     reads=[pp[tq].r(), cols.r()], writes=[t_lw.r()])
            S.op("dve", lambda e: e.tensor_scalar(out=t_lw[:], in0=t_lw[:], scalar1=-0.6065306597126334, scalar2=None, op0=ALU.mult),
                 reads=[t_lw.r()], writes=[t_lw.r()])
            for tq in range(4):
                qs = slice(tq * 512, (tq + 1) * 512)
                S.op("pe", lambda e, tq=tq, qs=qs: e.matmul(pp[tq][:], lhsT=lora[64:128, csl], rhs=wdad[64:128, qs], start=True, stop=True),
                     reads=[lora.r(), wdad.r()], writes=[pp[tq].r()])
                S.op("act", lambda e, tq=tq, qs=qs: e.activation(out=t_a[:, qs], in_=pp[tq][:], func=AF.Sigmoid, bias=cols[:, cc, 1:2]),
                     reads=[pp[tq].r(), cols.r()], writes=[t_a.r()])
            ta_all = [t_a.r()]
            S.op("dve", lambda e: e.tensor_tensor_scan(out=t_gc[:], data0=rst[:], data1=t_lw[:], initial=0.0, op0=ALU.mult, op1=ALU.add),
                 reads=[rst.r(), t_lw.r()], writes=[t_gc.r()])
            S.op("dve", lambda e: e.tensor_tensor(out=t_lw[:], in0=t_gc[:], in1=t_lw[:], op=ALU.subtract), reads=[t_gc.r(), t_lw.r()], writes=[t_lw.r()])
            xk = t_x
            load_shift(xk, F_RW + 256 + cc * 128, 2 + cc, t0)
            S.op("dve", lambda e: e.tensor_scalar(out=t0[:], in0=xk[:], scalar1=cols[:, cc, 2:3], scalar2=None, op0=ALU.mult), reads=[xk.r(), cols.r()], writes=[t0.r()])
            S.op("dve", lambda e: e.tensor_tensor(out=t1[:], in0=t0[:], in1=t0[:], op=ALU.mult), reads=[t0.r()], writes=[t1.r()])
            for tq in range(4):
                qs = slice(tq * 512, (tq + 1) * 512)
                S.op("pe", lambda e, tq=tq, qs=qs: e.matmul(pp[tq][:], lhsT=c.blk[:], rhs=t1[:, qs], start=True, stop=True),
                     reads=[c.blk.r(), t1.r()], writes=[pp[tq].r()])
            for tq in range(4):
                qs = slice(tq * 512, (tq + 1) * 512)
                S.op("act", lambda e, tq=tq, qs=qs: e.activation(out=t1[:, qs], in_=pp[tq][:], func=AF.Sqrt, bias=c.eps6[:, 0:1]),
                     reads=[pp[tq].r()] + [pp[q].r() for q in range(4)], writes=[t1.r()])
            S.op("dve", lambda e: e.reciprocal(out=t1[:], in_=t1[:]), reads=[t1.r()], writes=[t1.r()])
            S.op("dve", lambda e: e.tensor_tensor(out=t0[:], in0=t0[:], in1=t1[:], op=ALU.mult), reads=[t0.r(), t1.r()], writes=[t0.r()])
            S.op("dve", lambda e: e.tensor_tensor(out=t1[:], in0=t0[:], in1=t_a[:], op=ALU.mult), reads=[t0.r()] + ta_all, writes=[t1.r()])
            S.op("dve", lambda e: e.tensor_scalar(out=t_a[:], in0=t_a[:], scalar1=-1.0, scalar2=cols[:, cc, 3:4], op0=ALU.add, op1=ALU.mult),
                 reads=ta_all + [t1.r(), cols.r()], writes=[t_a.r()])
            S.op("dve", lambda e: e.tensor_tensor(out=t_a[:], in0=t_a[:], in1=xk[:], op=ALU.mult), reads=[t_a.r(), xk.r()], writes=[t_a.r()])
            S.op("dve", lambda e: e.tensor_tensor(out=t_a[:], in0=t_a[:], in1=xk[:], op=ALU.add), reads=[t_a.r(), xk.r()], writes=[t_a.r()])
            S.op("act", lambda e: e.activation(out=t_x[:], in_=t_lw[:], func=AF.Exp), reads=[t_lw.r(), t_a.r()], writes=[t_x.r()])
            S.op("dve", lambda e: e.scalar_tensor_tensor(out=AQ[:, cc, :, 0:64], in0=c3(t0[:]), scalar=-1.0, in1=c3(t_x[:]), op0=ALU.mult, op1=ALU.mult),
                 reads=[t0.r(), t_x.r()], writes=[AQ.r((cc, 0))])
            S.op("dve", lambda e: e.tensor_scalar(out=AQ1[:, cc, :, 0:64], in0=AQ0[:, cc, :, 0:64], scalar1=c.hmask[:, 1:2], scalar2=None, op0=ALU.mult),
                 reads=[AQ.r((cc, 0)), c.hmask.r()], writes=[AQ1.r((cc, 0))])
            S.op("dve", lambda e: e.tensor_scalar(out=AQ0[:, cc, :, 0:64], in0=AQ0[:, cc, :, 0:64], scalar1=c.hmask[:, 0:1], scalar2=None, op0=ALU.mult),
                 reads=[AQ.r((cc, 0)), AQ1.r((cc, 0)), c.hmask.r()], writes=[AQ.r((cc, 0))])
            S.op("act", lambda e: e.activation(out=t_x[:], in_=t_gc[:], func=AF.Exp, scale=-1.0), reads=[t_gc.r(), AQ.r((cc, 0))], writes=[t_x.r()])
            S.op("dve", lambda e: e.tensor_tensor(out=Bt[:, cc, :], in0=t1[:], in1=t_x[:], op=ALU.mult), reads=[t1.r(), t_x.r()], writes=[Bt.r(cc)])
            S.op("dve", lambda e: e.tensor_tensor(out=Kt[:, cc, :], in0=t_a[:], in1=t_x[:], op=ALU.mult), reads=[t_a.r(), t_x.r()], writes=[Kt.r(cc)])
            S.op("dve", lambda e: e.tensor_tensor(out=c3(t_x[:]), in0=c3(t_gc[:])[:, :, 63:64].to_broadcast([128, NCH, 64]), in1=c3(t_gc[:]), op=ALU.subtract),
                 reads=[t_gc.r(), Bt.r(cc), Kt.r(cc)], writes=[t_x.r()])
            S.op("act", lambda e: e.activation(out=t_x[:], in_=t_x[:], func=AF.Exp), reads=[t_x.r()], writes=[t_x.r()])
            S.op("dve", lambda e: e.tensor_tensor(out=BKV[:, cc, :, 0, :], in0=c3(t1[:]), in1=c3(t_x[:]), op=ALU.mult), reads=[t1.r(), t_x.r()], writes=[BKV.r((cc, 0))])
            S.op("dve", lambda e: e.tensor_tensor(out=BKV[:, cc, :, 1, :], in0=c3(t_a[:]), in1=c3(t_x[:]), op=ALU.mult), reads=[t_a.r(), t_x.r()], writes=[BKV.r((cc, 1))])
            S.op("act", lambda e: e.activation(out=Pend[:, cc, :], in_=c3(t_gc[:])[:, :, 63], func=AF.Exp), reads=[t_gc.r()], writes=[Pend.r(cc)])
            S.op("act", lambda e: e.activation(out=t_x[:], in_=t_gc[:], func=AF.Exp), reads=[t_gc.r(), BKV.r((cc, 0)), BKV.r((cc, 1))], writes=[t_x.r()])
            load_shift(xr, F_RW + cc * 128, cc, t0)
            S.op("dve", lambda e: e.tensor_tensor(out=AQ[:, cc, :, 64:128], in0=c3(xr[:]), in1=c3(t_x[:]), op=ALU.mult), reads=[xr.r(), t_x.r()], writes=[AQ.r((cc, 1))])
            S.op("dve", lambda e: e.tensor_scalar(out=AQ1[:, cc, :, 64:128], in0=AQ0[:, cc, :, 64:128], scalar1=c.hmask[:, 1:2], scalar2=None, op0=ALU.mult),
                 reads=[AQ.r((cc, 1)), c.hmask.r()], writes=[AQ1.r((cc, 1))])
            S.op("dve", lambda e: e.tensor_scalar(out=AQ0[:, cc, :, 64:128], in0=AQ0[:, cc, :, 64:128], scalar1=c.hmask[:, 0:1], scalar2=None, op0=ALU.mult),
                 reads=[AQ.r((cc, 1)), AQ1.r((cc, 1)), c.hmask.r()], writes=[AQ.r((cc, 1))])
            S.op("dve", lambda e: e.scalar_tensor_tensor(out=t0[:], in0=xr[:], scalar=cols[:, cc, 4:5], in1=t_a[:], op0=ALU.mult, op1=ALU.mult),
                 reads=[xr.r(), t_a.r(), cols.r()], writes=[t0.r()])
            for tq in range(4):
                qs = slice(tq * 512, (tq + 1) * 512)
                S.op("pe", lambda e, tq=tq, qs=qs: e.matmul(pp[tq][:], lhsT=c.blk[:], rhs=t0[:, qs], start=True, stop=True),
                     reads=[c.blk.r(), t0.r()], writes=[pp[tq].r()])
            xv = t_x
            load_shift(xv, F_RW + 512 + cc * 128, 4 + cc, t1)
            for tq in range(4):
                qs = slice(tq * 512, (tq + 1) * 512)
                S.op("dve", lambda e, tq=tq, qs=qs: e.tensor_tensor(out=bonT[:, cc, qs], in0=pp[tq][:], in1=xv[:, qs], op=ALU.mult),
                     reads=[pp[tq].r(), xv.r()], writes=[bonT.r((cc, tq))])
            S.op("act", lambda e: e.copy(out=BKV[:, cc, :, 2, :], in_=c3(xv[:])), reads=[xv.r()], writes=[BKV.r((cc, 2))])
        for cc in range(2):
            prep_cc(cc)
        S.emit()
        pst.close()

        _dplr_loop(c, st, b, "mc", (AQ0, AQ1), Bt, Kt, BKV, Pend, bonT, l)
        S.emit()


def mixerA(c, l, b):
    nc, S = c.nc, c.S
    P = c.p
    Fsb, Tsb = c.Fs[b], c.Ts[b]
    NCH = T // 64
    with ExitStack() as st:
        A = lambda n, s, d, psum=False: _alloc(nc, st, "ma_" + n, s, d, psum)
        TM = lambda n, s, d, k=2, psum=False: _tmp(nc, st, "ma_" + n, s, d, k, psum)
        knT = A("knT", [128, 2, T], BF16)
        vT = A("vT", [128, 2, T], BF16)
        KQ0 = A("KQ0", [128, 2, NCH, 128], BF16)
        KQ1 = A("KQ1", [128, 2, NCH, 128], BF16)
        KQm = (KQ0, KQ1)
        szA = A("sz", [64, NCH, 256], F32)
        gA = A("g", [64, NCH, 4], F32)
        bA = A("b", [64, NCH, 4], F32)
        gcA = A("gc", [64, NCH, 4], F32)
        c3 = lambda ap: ap.rearrange("p (n s) -> p n s", s=64)
        pst = ExitStack()
        A1 = lambda n, s, d, psum=False: _alloc(nc, pst, "ma_" + n, s, d, psum)
        convT = A1("convT", [128, 6, 4], F32)
        xs2 = [A1(f"x{i}", [128, T], F32) for i in range(2)]
        accs2 = [A1(f"acc{i}", [128, T], F32) for i in range(2)]
        sqs2 = [A1(f"sq{i}", [128, T], BF16) for i in range(2)]
        blkb = A1("blkb", [128, 128], BF16)
        S.op("dve", lambda e: e.tensor_copy(out=blkb[:], in_=c.blk[:]), reads=[c.blk.r()], writes=[blkb.r()])
        ab = A1("ab", [64, NCH, 8], F32)
        dtb = A1("dtb", [64, 4], F32)
        nA = A1("nA", [64, 4], F32)
        pp = [A1(f"pp{i}", [128, 512], F32, psum=True) for i in range(4)]
        S.defer = []
        S.op("sp", lambda e: e.dma_start(out=convT[:], in_=P["gdn_convT"][l]), writes=[convT.r()], dma=True)
        S.op("sp", lambda e: e.dma_start(out=dtb[:], in_=P["gdn_dt_bias"][l:l + 1, :].partition_broadcast(64)), writes=[dtb.r()], dma=True)
        S.op("sp", lambda e: e.dma_start(out=nA[:], in_=P["gdn_a_log"][l:l + 1, :].partition_broadcast(64)), writes=[nA.r()], dma=True)
        S.op("sp", lambda e: e.dma_start(out=ab[:], in_=Tsb[:, T_GA:T_GA + 8].rearrange("(n p) c -> p n c", p=64)), reads=[c.Ts_r[b]], writes=[ab.r()], dma=True)
        for q4 in range(4):
            S.op("sp", lambda e, q4=q4: e.dma_start(out=szA[:, q4 * 8:(q4 + 1) * 8, :], in_=Tsb[q4 * 512:(q4 + 1) * 512, T_GZ:T_GZ + 256].rearrange("(n p) c -> p n c", p=64)),
                 reads=[c.Ts_r[b]], writes=[szA.r()], dma=True)
        S.op("act", lambda e: e.activation(out=szA[:], in_=szA[:], func=AF.Silu), reads=[szA.r()], writes=[szA.r()])
        S.op("act", lambda e: e.activation(out=nA[:], in_=nA[:], func=AF.Exp), reads=[nA.r()], writes=[nA.r()])
        S.op("dve", lambda e: e.tensor_tensor(out=gA[:], in0=ab[:, :, 0:4], in1=dtb[:].unsqueeze(1).to_broadcast([64, NCH, 4]), op=ALU.add), reads=[ab.r(), dtb.r()], writes=[gA.r()])
        S.op("act", lambda e: e.activation(out=gA[:], in_=gA[:], func=AF.Exp), reads=[gA.r()], writes=[gA.r()])
        S.op("act", lambda e: e.activation(out=gA[:], in_=gA[:], func=AF.Ln, bias=c.one[0:64, 0:1]), reads=[gA.r()], writes=[gA.r()])
        S.op("dve", lambda e: e.scalar_tensor_tensor(out=gA[:], in0=gA[:], scalar=-1.0, in1=nA[:].unsqueeze(1).to_broadcast([64, NCH, 4]), op0=ALU.mult, op1=ALU.mult),
             reads=[gA.r(), nA.r()], writes=[gA.r()])
        S.op("act", lambda e: e.activation(out=bA[:], in_=ab[:, :, 4:8], func=AF.Exp, scale=-1.0), reads=[ab.r()], writes=[bA.r()])
        S.op("dve", lambda e: e.tensor_scalar(out=bA[:], in0=bA[:], scalar1=1.0, scalar2=None, op0=ALU.add), reads=[bA.r()], writes=[bA.r()])
        S.op("dve", lambda e: e.reciprocal(out=bA[:], in_=bA[:]), reads=[bA.r()], writes=[bA.r()])
        S.op("pe", lambda e: e.matmul(pp[0][0:64, 0:NCH * 4], lhsT=c.m1cat[:, 64:128], rhs=gA[:].rearrange("p n h -> p (n h)"), start=True, stop=True),
             reads=[gA.r(), c.m1cat.r()], writes=[pp[0].r()])
        S.op("dve", lambda e: e.tensor_copy(out=gcA[:].rearrange("p n h -> p (n h)"), in_=pp[0][0:64, 0:NCH * 4]), reads=[pp[0].r()], writes=[gcA.r()])

        def conv_tile(ti, x, acc):
            S.op("sp", lambda e: e.dma_start(out=x[:], in_=Fsb[F_GQ + ti * 128:F_GQ + (ti + 1) * 128, :]), reads=[c.Fs_r[b]], writes=[x.r()], dma=True)
            S.op("dve", lambda e: e.tensor_scalar(out=acc[:], in0=x[:], scalar1=convT[:, ti, 3:4], scalar2=None, op0=ALU.mult), reads=[x.r(), convT.r()], writes=[acc.r()])
            for sh in (1, 2, 3):
                S.op("dve", lambda e, sh=sh: e.scalar_tensor_tensor(out=acc[:, sh:T], in0=x[:, 0:T - sh], scalar=convT[:, ti, 3 - sh:4 - sh], in1=acc[:, sh:T], op0=ALU.mult, op1=ALU.add),
                     reads=[x.r(), acc.r(), convT.r()], writes=[acc.r()])
            S.op("act", lambda e: e.activation(out=x[:], in_=acc[:], func=AF.Silu), reads=[acc.r()], writes=[x.r()])

        def l2n(scale, x, sq, pp, rt):
            S.op("dve", lambda e: e.tensor_tensor(out=sq[:], in0=x[:], in1=x[:], op=ALU.mult), reads=[x.r()], writes=[sq.r()])
            for tq in range(4):
                qs = slice(tq * 512, (tq + 1) * 512)
                S.op("pe", lambda e, tq=tq, qs=qs: e.matmul(pp[tq % 2][:], lhsT=blkb[:], rhs=sq[:, qs], start=True, stop=True), reads=[blkb.r(), sq.r()], writes=[pp[tq % 2].r()])
                S.op("act", lambda e, tq=tq, qs=qs: e.activation(out=rt[:, qs], in_=pp[tq % 2][:], func=AF.Sqrt, bias=c.eps6[:, 0:1]), reads=[pp[tq % 2].r()], writes=[rt.r()])
            S.op("dve", lambda e: e.reciprocal(out=rt[:], in_=rt[:]), reads=[rt.r()], writes=[rt.r()])
            S.op("dve", lambda e: e.scalar_tensor_tensor(out=x[:], in0=x[:], scalar=scale, in1=rt[:], op0=ALU.mult, op1=ALU.mult), reads=[x.r(), rt.r()], writes=[x.r()])

        def prep_cc(cc):
            x, acc, sq, pp2 = xs2[cc], accs2[cc], sqs2[cc], pp[2 * cc:2 * cc + 2]
            conv_tile(2 + cc, x, acc)
            l2n(1.0, x, sq, pp2, acc)
            S.op("act", lambda e: e.copy(out=knT[:, cc, :], in_=x[:]), reads=[x.r()], writes=[knT.r(cc)])
            for hp in range(2):
                S.op("dve", lambda e, hp=hp: e.tensor_scalar(out=KQm[hp][:, cc, :, 0:64], in0=c3(x[:]), scalar1=c.hmask[:, hp:hp + 1], scalar2=None, op0=ALU.mult),
                     reads=[x.r(), c.hmask.r()], writes=[KQm[hp].r((cc, 0))])
            conv_tile(cc, x, acc)
            l2n(0.125, x, sq, pp2, acc)
            for hp in range(2):
                S.op("dve", lambda e, hp=hp: e.tensor_scalar(out=KQm[hp][:, cc, :, 64:128], in0=c3(x[:]), scalar1=c.hmask[:, hp:hp + 1], scalar2=None, op0=ALU.mult),
                     reads=[x.r(), c.hmask.r()], writes=[KQm[hp].r((cc, 1))])
            conv_tile(4 + cc, x, acc)
            S.op("act", lambda e: e.copy(out=vT[:, cc, :], in_=x[:]), reads=[x.r()], writes=[vT.r(cc)])
        lst0, S.defer = S.defer, []
        prep_cc(0)
        lstc0, S.defer = S.defer, []
        prep_cc(1)
        lstc1, S.defer = S.defer, None
        S.replay(lst0)
        lists = [lstc0, lstc1]
        if c.merge_B:
            S.defer = []
            mixerB(c, l, b, ext_st=pst)
            lstB, S.defer = S.defer, None
            lists.append(lstB)
        S.replay(*lists)
        S.emit()
        pst.close()

        NPR = NCH // 2
        pTr = A("pTr", [128, 1024], BF16, psum=True)
        pX1 = A("pX1", [64, 8, 128], F32, psum=True)
        pNB = A("pNB", [64, 8, 64], F32, psum=True)
        pGB = A("pGB", [128, 2, 256], F32, psum=True)
        pXab = A("pXab", [64, 2, 256], F32, psum=True)
        pO = A("pO", [64, 2, 256], F32, psum=True)
        pHU = A("pHU", [128, 512], F32, psum=True)
        H = A("H", [128, 2, 64], F32)
        Hbs = TM("Hb", [128, 2, 64], BF16)
        Hs = A("Hs", [128, 2, 64], F32)
        ones = A("ones", [64, 128], F32)
        nw = A("nw", [64, 64], F32)
        S.op("sp", lambda e: e.dma_start(out=nw[:], in_=P["gdn_norm"][l:l + 1, :].partition_broadcast(64)), writes=[nw.r()], dma=True)
        S.op("dve", lambda e: e.memset(ones[:], 1.0), writes=[ones.r()])
        S.op("dve", lambda e: e.memset(H[:], 0.0), writes=[H.r()])
        S.op("dve", lambda e: e.memset(Hbs[0][:], 0.0), writes=[Hbs[0].r()])
        tokm = TM("tokm", [64, 2, 4, 128], BF16)
        MQ = TM("MQ", [64, 8, 64], BF16)
        NL = TM("NL", [64, 8, 64], BF16)
        TTp = TM("TTp", [64, 8, 64], BF16)
        scp = TM("scp", [64, 2, 8], F32)
        PCr = TM("PCr", [64, 2, 4], F32)
        PeT = TM("PeT", [128, 2, 2], F32)
        Vb = TM("Vb", [64, 8, 64], F32)
        Vbb = TM("Vbb", [64, 8, 64], BF16)
        R2 = TM("R2", [64, 2, 4, 64], F32)
        Dm = TM("Dm", [64, 4, 64], F32)
        E1 = TM("E1", [64, 4, 64], F32)
        E2 = TM("E2", [64, 4, 64], F32)
        G1 = TM("G1", [64, 4, 64], F32)
        G2 = TM("G2", [64, 4, 64], F32)
        GL = TM("GL", [64, 4, 64], F32)
        NY = TM("NY", [64, 8, 128], BF16, 2)
        R = TM("R", [64, 8, 64], BF16, 2)
        XS = TM("XS", [64, 256], BF16)
        XSf = TM("XSf", [64, 256], F32)
        Wb = TM("Wb", [64, 256], BF16)
        Wp = TM("Wp", [64, 256], BF16)
        o32 = TM("o32", [64, 256], F32, 4)
        t1 = TM("t1", [64, 256], F32)
        s4 = TM("s4", [64, 4], F32)
        y32 = TM("y32", [64, 256], F32)
        yb = TM("yb", [64, 256], BF16)
        M1 = c.m1cat
        ML = c.maskL
        I64 = c.ident[0:64, 0:64]
        b4 = lambda ap, n: ap.unsqueeze(1).to_broadcast([64, 4, n])
        b8 = lambda ap, n: ap.unsqueeze(1).to_broadcast([64, 8, n])
        s4b = lambda ap, n: ap.unsqueeze(2).to_broadcast([64, 4, n])
        g4 = lambda ap: ap.rearrange("p (h c) -> p h c", h=4)
        KQr = [a.r(k) for a in KQm for k in ((0, 0), (0, 1), (1, 0), (1, 1))]

        def phase1(m):
            i2 = m % 2
            tk, mq, nl, tts, sc_, pcr, pe_, vb, vbb = tokm[i2], MQ[i2], NL[i2], TTp[i2], scp[i2], PCr[i2], PeT[i2], Vb[i2], Vbb[i2]
            ny, r_ = NY[0], R[0]
            for ci in range(2):
                n = 2 * m + ci
                nsl = slice(n * 64, (n + 1) * 64)
                p0 = ci * 4
                r2, dm, e1, e2, g1, g2, gl = R2[ci], Dm[ci], E1[ci], E2[ci], G1[ci], G2[ci], GL[ci]
                def tra(e, nsl=nsl):
                    for cc in range(2):
                        e.transpose(out=pTr[0:64, cc * 128:(cc + 1) * 128], in_=knT[:, cc, nsl], identity=c.identb[:])
                        ins = e.transpose(out=pTr[0:64, (2 + cc) * 128:(3 + cc) * 128], in_=vT[:, cc, nsl], identity=c.identb[:])
                    return ins
                S.op("pe", tra, reads=[knT.r(0), knT.r(1), vT.r(0), vT.r(1)], writes=[pTr.r()])
                S.op("act", lambda e, ci=ci: e.copy(out=tk[:, ci, :, :].rearrange("p a b -> p (a b)"), in_=pTr[0:64, 0:512]), reads=[pTr.r()], writes=[tk.r()])
                S.op("dve", lambda e, r2=r2, n=n: e.tensor_tensor(out=r2[:, 0, :, :], in0=s4b(gA[:, n, :], 64), in1=b4(M1[:, 64:128], 64), op=ALU.mult), reads=[gA.r(), M1.r()], writes=[r2.r()])
                S.op("dve", lambda e, r2=r2, n=n: e.tensor_tensor(out=r2[:, 1, :, :], in0=s4b(bA[:, n, :], 64), in1=b4(I64, 64), op=ALU.mult), reads=[bA.r(), c.ident.r()], writes=[r2.r()])
                S.op("pe", lambda e, r2=r2: e.matmul(pGB[:].rearrange("p a b -> p (a b)"), lhsT=ones[:], rhs=r2[:].rearrange("p a h i -> p (a h i)"), start=True, stop=True),
                     reads=[ones.r(), r2.r()], writes=[pGB.r()])
                GR = lambda: pGB[0:64, 0, :].rearrange("p (h i) -> p h i", h=4)
                BR = lambda: pGB[0:64, 1, :].rearrange("p (h i) -> p h i", h=4)
                S.op("dve", lambda e, dm=dm, n=n: e.tensor_tensor(out=dm[:], in0=GR(), in1=s4b(gcA[:, n, :], 64), op=ALU.subtract), reads=[pGB.r(), gcA.r()], writes=[dm.r()])
                S.op("dve", lambda e, dm=dm, e1=e1: e.tensor_scalar(out=e1[:], in0=dm[:], scalar1=0.0, scalar2=None, op0=ALU.min), reads=[dm.r()], writes=[e1.r()])
                S.op("dve", lambda e, dm=dm, e2=e2: e.tensor_scalar(out=e2[:], in0=dm[:], scalar1=-1.0, scalar2=0.0, op0=ALU.mult, op1=ALU.min), reads=[dm.r()], writes=[e2.r()])
                S.op("act", lambda e, e1=e1: e.activation(out=e1[:], in_=e1[:], func=AF.Exp), reads=[e1.r()], writes=[e1.r()])
                S.op("act", lambda e, e2=e2: e.activation(out=e2[:], in_=e2[:], func=AF.Exp), reads=[e2.r()], writes=[e2.r()])
                S.op("dve", lambda e, e1=e1, g2=g2: e.tensor_tensor(out=g2[:], in0=e1[:], in1=b4(M1[:, 64:128], 64), op=ALU.mult), reads=[e1.r(), M1.r()], writes=[g2.r()])
                S.op("dve", lambda e, e1=e1, g1=g1: e.tensor_tensor(out=g1[:], in0=e1[:], in1=b4(M1[:, 0:64], 64), op=ALU.mult), reads=[e1.r(), M1.r()], writes=[g1.r()])
                S.op("dve", lambda e, g1=g1: e.scalar_tensor_tensor(out=g1[:], in0=g1[:], scalar=-1.0, in1=BR(), op0=ALU.mult, op1=ALU.mult), reads=[g1.r(), pGB.r()], writes=[g1.r()])
                S.op("dve", lambda e, e2=e2, gl=gl: e.tensor_tensor(out=gl[:], in0=e2[:], in1=b4(ML[:], 64), op=ALU.mult), reads=[e2.r(), ML.r()], writes=[gl.r()])
                S.op("dve", lambda e, gl=gl, n=n: e.scalar_tensor_tensor(out=gl[:], in0=gl[:], scalar=-1.0, in1=s4b(bA[:, n, :], 64), op0=ALU.mult, op1=ALU.mult), reads=[gl.r(), bA.r()], writes=[gl.r()])
                S.op("act", lambda e, ci=ci, n=n: e.activation(out=sc_[:, ci, 0:4], in_=gcA[:, n, :], func=AF.Exp), reads=[gcA.r()], writes=[sc_.r()])
                S.op("dve", lambda e, ci=ci, n=n: e.scalar_tensor_tensor(out=sc_[:, ci, 4:8], in0=sc_[:, ci, 0:4], scalar=-1.0, in1=bA[:, n, :], op0=ALU.mult, op1=ALU.mult), reads=[sc_.r(), bA.r()], writes=[sc_.r()])
                S.op("dve", lambda e, ci=ci, e1=e1: e.tensor_copy(out=pcr[:, ci, :], in_=e1[:, :, 63]), reads=[e1.r()], writes=[pcr.r()])
                for hp in range(2):
                    S.op("act", lambda e, ci=ci, hp=hp: e.activation(out=pe_[hp * 64:(hp + 1) * 64, ci, :], in_=pGB[hp * 64:(hp + 1) * 64, 0, :].rearrange("p (cc hp i) -> p cc hp i", hp=2, i=64)[:, :, hp, 63], func=AF.Exp),
                         reads=[pGB.r()], writes=[pe_.r()])
                S.op("dve", lambda e, ci=ci, p0=p0, n=n: e.tensor_tensor(out=vb[:, p0:p0 + 4, :], in0=tk[:, ci, 2:4, :].rearrange("p a (hp v) -> p (a hp) v", hp=2), in1=s4b(bA[:, n, :], 64), op=ALU.mult),
                     reads=[tk.r(), bA.r()], writes=[vb.r()])
                S.op("act", lambda e, p0=p0: e.copy(out=vbb[:, p0:p0 + 4, :], in_=vb[:, p0:p0 + 4, :]), reads=[vb.r()], writes=[vbb.r()])
                def mmx(e, n=n, nsl=nsl, p0=p0):
                    for h in range(4):
                        cc = h // 2
                        ins = e.matmul(pX1[:, p0 + h, :], lhsT=knT[:, cc, nsl], rhs=KQm[h % 2][:, cc, n, :], start=True, stop=True)
                    return ins
                S.op("pe", mmx, reads=KQr + [knT.r(0), knT.r(1)], writes=[pX1.r()])
                S.op("dve", lambda e, g1=g1, p0=p0: e.tensor_tensor(out=ny[:, p0:p0 + 4, 0:64], in0=pX1[:, p0:p0 + 4, 0:64], in1=g1[:], op=ALU.mult), reads=[pX1.r(), g1.r()], writes=[ny.r()])
                S.op("dve", lambda e, g2=g2, p0=p0: e.tensor_tensor(out=mq[:, p0:p0 + 4, :], in0=pX1[:, p0:p0 + 4, 64:128], in1=g2[:], op=ALU.mult), reads=[pX1.r(), g2.r()], writes=[mq.r()])
                S.op("dve", lambda e, gl=gl, p0=p0: e.tensor_tensor(out=r_[:, p0:p0 + 4, :], in0=pX1[:, p0:p0 + 4, 0:64], in1=gl[:], op=ALU.mult), reads=[pX1.r(), gl.r()], writes=[r_.r()])
            S.op("dve", lambda e: e.tensor_tensor(out=ny[:, :, 64:128], in0=ny[:, :, 0:64], in1=b8(I64, 64), op=ALU.add), reads=[ny.r(), c.ident.r()], writes=[ny.r()])
            S.op("act", lambda e: e.copy(out=nl[:], in_=ny[:, :, 0:64]), reads=[ny.r()], writes=[nl.r()])
            _neumann(S, NY, R, pX1, pNB, 8, tts)

        def phase2(m):
            i2 = m % 2
            tk, mq, nl, tts, sc_, pcr, pe_, vb, vbb = tokm[i2], MQ[i2], NL[i2], TTp[i2], scp[i2], PCr[i2], PeT[i2], Vb[i2], Vbb[i2]
            for ci in range(2):
                n = 2 * m + ci
                nsl = slice(n * 64, (n + 1) * 64)
                p0 = ci * 4
                xs_, xf_, wb, wp = XS[ci], XSf[ci], Wb[ci], Wp[ci]
                Hb, Hbn = Hbs[n % 2], Hbs[(n + 1) % 2]
                S.op("dve", lambda e, ci=ci: e.tensor_tensor(out=Hs[:], in0=H[:], in1=pe_[:, ci, :].unsqueeze(2).to_broadcast([128, 2, 64]), op=ALU.mult), reads=[H.r(), pe_.r()], writes=[Hs.r()])
                def mmd(e, n=n, p0=p0, Hb=Hb):
                    for h in range(4):
                        cc = h // 2
                        e.matmul(pXab[:, 0, h * 64:(h + 1) * 64], lhsT=KQm[h % 2][:, cc, n, 0:64], rhs=Hb[:, cc, :], start=True, stop=True)
                        ins = e.matmul(pXab[:, 1, h * 64:(h + 1) * 64], lhsT=nl[:, p0 + h, :], rhs=vbb[:, p0 + h, :], start=True, stop=True)
                    return ins
                S.op("pe", mmd, reads=KQr + [Hb.r(), nl.r(), vbb.r()], writes=[pXab.r()])
                S.op("dve", lambda e, xf_=xf_, ci=ci: e.tensor_tensor(out=g4(xf_[:]), in0=g4(pXab[:, 0, :]), in1=s4b(sc_[:, ci, 4:8], 64), op=ALU.mult), reads=[pXab.r(), sc_.r()], writes=[xf_.r()])
                S.op("dve", lambda e, xs_=xs_, xf_=xf_: e.tensor_tensor(out=xs_[:], in0=xf_[:], in1=pXab[:, 1, :], op=ALU.add), reads=[xf_.r(), pXab.r()], writes=[xs_.r()])
                def mme(e, xs_=xs_, p0=p0):
                    for h in range(4):
                        ins = e.matmul(pHU[0:64, 256 + h * 64:256 + (h + 1) * 64], lhsT=tts[:, p0 + h, :], rhs=xs_[:, h * 64:(h + 1) * 64], start=True, stop=True)
                    return ins
                S.op("pe", mme, reads=[tts.r(), xs_.r()], writes=[pHU.r()])
                S.op("dve", lambda e, wb=wb, p0=p0: e.tensor_tensor(out=g4(wb[:]), in0=g4(pHU[0:64, 256:512]), in1=vb[:, p0:p0 + 4, :], op=ALU.add), reads=[pHU.r(), vb.r()], writes=[wb.r()])
                S.op("dve", lambda e, wb=wb, wp=wp, ci=ci: e.tensor_tensor(out=g4(wp[:]), in0=g4(wb[:]), in1=s4b(pcr[:, ci, :], 64), op=ALU.mult), reads=[wb.r(), pcr.r()], writes=[wp.r()])
                def mmg(e, ci=ci, wp=wp):
                    for h in range(4):
                        cc, hp = h // 2, h % 2
                        hs_ = slice(hp * 64, (hp + 1) * 64)
                        ins = e.matmul(pHU[hs_, cc * 64:(cc + 1) * 64], lhsT=tk[:, ci, cc, hs_], rhs=wp[:, h * 64:(h + 1) * 64], start=True, stop=True)
                    return ins
                S.op("pe", mmg, reads=[tk.r(), wp.r()], writes=[pHU.r()])
                S.op("dve", lambda e, Hbn=Hbn: e.tensor_tensor(out=Hbn[:], in0=Hs[:], in1=pHU[:, 0:128].rearrange("p (cc v) -> p cc v", cc=2), op=ALU.add), reads=[Hs.r(), pHU.r()], writes=[Hbn.r()])
                def mmf(e, n=n, p0=p0, wb=wb, Hb=Hb):
                    for h in range(4):
                        cc = h // 2
                        e.matmul(pO[:, 0, h * 64:(h + 1) * 64], lhsT=KQm[h % 2][:, cc, n, 64:128], rhs=Hb[:, cc, :], start=True, stop=True)
                        ins = e.matmul(pO[:, 1, h * 64:(h + 1) * 64], lhsT=mq[:, p0 + h, :], rhs=wb[:, h * 64:(h + 1) * 64], start=True, stop=True)
                    return ins
                S.op("pe", mmf, reads=KQr + [Hb.r(), mq.r(), wb.r()], writes=[pO.r()])
                S.op("dve", lambda e: e.tensor_tensor(out=H[:], in0=Hs[:], in1=pHU[:, 0:128].rearrange("p (cc v) -> p cc v", cc=2), op=ALU.add), reads=[Hs.r(), pHU.r()], writes=[H.r()])
                o_ = o32[(m % 2) * 2 + ci]
                S.op("dve", lambda e, o_=o_, ci=ci: e.tensor_tensor(out=g4(o_[:]), in0=g4(pO[:, 0, :]), in1=s4b(sc_[:, ci, 0:4], 64), op=ALU.mult), reads=[pO.r(), sc_.r()], writes=[o_.r()])
                S.op("dve", lambda e, o_=o_: e.tensor_tensor(out=o_[:], in0=o_[:], in1=pO[:, 1, :], op=ALU.add), reads=[o_.r(), pO.r()], writes=[o_.r()])

        def post(m):
            for ci in range(2):
                n = 2 * m + ci
                nsl = slice(n * 64, (n + 1) * 64)
                o_, t_, s_, y_, yb_ = o32[(m % 2) * 2 + ci], t1[ci], s4[ci], y32[ci], yb[ci]
                S.op("dve", lambda e, o_=o_, t_=t_: e.tensor_tensor(out=t_[:], in0=o_[:], in1=o_[:], op=ALU.mult), reads=[o_.r(), t_.r()], writes=[t_.r()])
                S.op("dve", lambda e, t_=t_, s_=s_: e.tensor_reduce(out=s_[:], in_=g4(t_[:]), axis=AX.X, op=ALU.add), reads=[t_.r()], writes=[s_.r()])
                S.op("act", lambda e, s_=s_: e.activation(out=s_[:], in_=s_[:], func=AF.Ln, scale=1.0 / 64, bias=c.eps6[0:64, 0:1]), reads=[s_.r()], writes=[s_.r()])
                S.op("act", lambda e, s_=s_: e.activation(out=s_[:], in_=s_[:], func=AF.Exp, scale=-0.5), reads=[s_.r()], writes=[s_.r()])
                S.op("dve", lambda e, o_=o_, s_=s_, t_=t_: e.tensor_tensor(out=g4(t_[:]), in0=g4(o_[:]), in1=s4b(s_[:], 64), op=ALU.mult), reads=[o_.r(), s_.r()], writes=[t_.r()])
                S.op("dve", lambda e, t_=t_: e.tensor_tensor(out=g4(t_[:]), in0=g4(t_[:]), in1=b4(nw[:], 64), op=ALU.mult), reads=[t_.r(), nw.r()], writes=[t_.r()])
                S.op("dve", lambda e, t_=t_, y_=y_, n=n: e.tensor_tensor(out=y_[:], in0=t_[:], in1=szA[:, n, :], op=ALU.mult), reads=[t_.r(), szA.r()], writes=[y_.r()])
                S.op("act", lambda e, y_=y_, yb_=yb_: e.copy(out=yb_[:], in_=y_[:]), reads=[y_.r()], writes=[yb_.r()])
                if c.ydbg is not None:
                    S.op("pool", lambda e, y_=y_, nsl=nsl: e.dma_start(out=c.ydbg[b, nsl, 0:256], in_=y_[:]), reads=[y_.r()], writes=[c.ydbg_r], dma=True)

                def tr(e, yb_=yb_):
                    for jx in range(2):
                        ins = e.transpose(out=pTr[:, 512 + jx * 64:512 + (jx + 1) * 64], in_=yb_[:, jx * 128:(jx + 1) * 128], identity=c.identb[0:64, 0:64])
                    return ins
                S.op("pe", tr, reads=[yb_.r()], writes=[pTr.r()])
                S.op("act", lambda e, nsl=nsl: e.copy(out=c.yT[:, 0:2, nsl], in_=pTr[:, 512:640].rearrange("p (j n) -> p j n", j=2)),
                     reads=[pTr.r()], writes=[c.yT.r((0, n))])


        def rec(fn, m):
            S.defer = []
            fn(m)
            lst, S.defer = S.defer, None
            return lst
        S.replay(rec(phase1, 0))
        for m in range(NPR + 1):
            lists = []
            if m < NPR:
                lists.append(rec(phase2, m))
            if m >= 1:
                lists.append(rec(post, m - 1))
            if m + 1 < NPR:
                lists.append(rec(phase1, m + 1))
            S.replay(*lists)
        S.emit()


def _neumann(S, NY, R, pNA, pNB, nprob=4, out_final=None):
    for s_ in range(6):
        ny, r_ = NY[s_ % 2], R[s_ % 2]
        ny2, r2 = NY[(s_ + 1) % 2], R[(s_ + 1) % 2]
        if s_ == 0:
            def mm0(e, ny=ny, r_=r_):
                for h in range(nprob):
                    e.matmul(pNA[:, h, 0:64], lhsT=r_[:, h, :], rhs=ny[:, h, 0:64], start=True, stop=True)
                    ins = e.matmul(pNB[:, h, 0:64], lhsT=ny[:, h, 0:64], rhs=r_[:, h, :], start=True, stop=True)
                return ins
            S.op("pe", mm0, reads=[ny.r(), r_.r()], writes=[pNA.r(), pNB.r()])
            S.op("act", lambda e, ny2=ny2: e.copy(out=ny2[:, :, 0:64], in_=pNA[:, :, 0:64]), reads=[pNA.r()], writes=[ny2.r()])
            S.op("dve", lambda e, ny=ny, ny2=ny2: e.tensor_copy(out=ny2[:, :, 64:128], in_=ny[:, :, 64:128]), reads=[ny.r()], writes=[ny2.r()])
            S.op("act", lambda e, r2=r2: e.copy(out=r2[:], in_=pNB[:, :, 0:64]), reads=[pNB.r()], writes=[r2.r()])
        else:
            last = (s_ == 5)

            def mms(e, ny=ny, r_=r_, last=last):
                for h in range(nprob):
                    if last:
                        ins = e.matmul(pNA[:, h, 64:128], lhsT=r_[:, h, :], rhs=ny[:, h, 64:128], start=True, stop=True)
                    else:
                        e.matmul(pNA[:, h, :], lhsT=r_[:, h, :], rhs=ny[:, h, :], start=True, stop=True)
                        ins = e.matmul(pNB[:, h, 0:64], lhsT=ny[:, h, 0:64], rhs=r_[:, h, :], start=True, stop=True)
                return ins
            S.op("pe", mms, reads=[ny.r(), r_.r()], writes=[pNA.r(), pNB.r()])
            if not last:
                S.op("act", lambda e, ny2=ny2: e.copy(out=ny2[:, :, 0:64], in_=pNA[:, :, 0:64]), reads=[pNA.r()], writes=[ny2.r()])
                S.op("act", lambda e, r2=r2: e.copy(out=r2[:], in_=pNB[:, :, 0:64]), reads=[pNB.r()], writes=[r2.r()])
            if last and out_final is not None:
                S.op("dve", lambda e, ny=ny: e.tensor_tensor(out=out_final[:], in0=ny[:, :, 64:128], in1=pNA[:, :, 64:128], op=ALU.add), reads=[ny.r(), pNA.r()], writes=[out_final.r()])
            else:
                S.op("dve", lambda e, ny=ny, ny2=ny2: e.tensor_tensor(out=ny2[:, :, 64:128], in0=ny[:, :, 64:128], in1=pNA[:, :, 64:128], op=ALU.add), reads=[ny.r(), pNA.r()], writes=[ny2.r()])


def stage3(c, l, b):
    nc, S = c.nc, c.S
    P = c.p
    last = (l == DEPTH - 1)
    xin = c.xres[l]
    xout = c.out if last else c.xres[l + 1]
    with ExitStack() as st:
        A = lambda n, s, d, psum=False: _alloc(nc, st, "s3_" + n, s, d, psum)
        TM = lambda n, s, d, k=2, psum=False: _tmp(nc, st, "s3_" + n, s, d, k, psum)
        wst = TM("wst", [128, 8, 256], F32, 2)
        wm = [A(f"wm{i}", [128, 8, 512], BF16) for i in range(4)]
        wbr = A("wbr", [128, 8, 512], BF16)
        wo = A("wo", [128, 8, 1024], BF16)
        mixedb = A("mixedb", [128, NTT, 1024], BF16)
        pg = TM("pg", [128, 512], F32, 2, psum=True)
        pb = TM("pb", [128, 512], F32, 2, psum=True)
        pTs = TM("pT", [128, 8, 128], BF16, 2, psum=True)
        po = TM("po", [128, 512], F32, 2, psum=True)
        sig = TM("sig", [128, 512], F32, 2)
        acc = TM("acc", [128, 512], F32, 2)
        tmp = TM("tmp", [128, 512], F32, 2)
        mT = TM("mT", [128, 8, 128], BF16, 2)
        xt = TM("xt", [128, 1024], F32, 2)
        sqt = A("sqt", [128, 1024], F32) if last else None
        ssm = TM("ssm", [128, 1], F32, 2)
        fg = A("fg", [128, 1024], F32) if last else None
        if last:
            S.op("sp", lambda e: e.dma_start(out=fg[:], in_=P["final_gain"][0:1, :].partition_broadcast(128)), writes=[fg.r()], dma=True)
        wcnt = [0]

        def load_w(src_ap, dst, scale_gain):
            for hf in range(2):
                j = wcnt[0] % 2
                wcnt[0] += 1
                ws = wst[j]
                cs = slice(hf * 256, (hf + 1) * 256)
                S.op("sp", lambda e, ws=ws, cs=cs: e.dma_start(out=ws[:], in_=src_ap[:, cs].rearrange("(k p) n -> p k n", p=128)), writes=[ws.r()], dma=True)
                for k in range(8):
                    if scale_gain:
                        if k % 2 == 0:
                            S.op("dve", lambda e, k=k, ws=ws, cs=cs: e.tensor_scalar(out=dst[:, k, cs], in0=ws[:, k, :], scalar1=c.gainT[:, l * 8 + k:l * 8 + k + 1], scalar2=None, op0=ALU.mult),
                                 reads=[ws.r()], writes=[dst.r(k)])
                        else:
                            S.op("act", lambda e, k=k, ws=ws, cs=cs: e.activation(out=dst[:, k, cs], in_=ws[:, k, :], func=AF.Copy, scale=c.gainT[:, l * 8 + k:l * 8 + k + 1]),
                                 reads=[ws.r()], writes=[dst.r(k)])
                    else:
                        if k % 2 == 0:
                            S.op("act", lambda e, k=k, ws=ws, cs=cs: e.copy(out=dst[:, k, cs], in_=ws[:, k, :]), reads=[ws.r()], writes=[dst.r(k)])
                        else:
                            S.op("dve", lambda e, k=k, ws=ws, cs=cs: e.tensor_copy(out=dst[:, k, cs], in_=ws[:, k, :]), reads=[ws.r()], writes=[dst.r(k)])
            return [dst.r(k) for k in range(8)]

        cnt = 0
        for nb in range(2):
            wm_r = [load_w(P["wm"][l, :, i * 1024 + nb * 512:i * 1024 + (nb + 1) * 512], wm[i], True) for i in range(4)]
            wbr_r = load_w(P["wbr"][l, :, nb * 512:(nb + 1) * 512], wbr, False)
            for tt in range(NTT):
                tsl = slice(tt * 128, (tt + 1) * 128)
                acc_ = acc[tt % 2]
                for i in range(4):
                    j = cnt % 2
                    cnt += 1
                    pg_, pb_, sig_, tmp_ = pg[j], pb[j], sig[j], tmp[j]

                    def mmg(e, pg_=pg_, i=i, tsl=tsl):
                        for k in range(8):
                            ins = e.matmul(pg_[:], lhsT=c.hT[:, k, tsl], rhs=wm[i][:, k, :], start=(k == 0), stop=(k == 7))
                        return ins
                    S.op("pe", mmg, reads=wm_r[i] + [c.hT.r(tt)], writes=[pg_.r()])

                    def mmb(e, pb_=pb_, i=i, tsl=tsl):
                        for kc in range(2):
                            ins = e.matmul(pb_[:], lhsT=c.yT[:, 2 * i + kc, tsl], rhs=wbr[:, 2 * i + kc, :], start=(kc == 0), stop=(kc == 1))
                        return ins
                    yr = [c.yT.r((i, tt))] if i != 0 else [c.yT.r((0, 2 * tt)), c.yT.r((0, 2 * tt + 1))]
                    S.op("pe", mmb, reads=[wbr_r[2 * i], wbr_r[2 * i + 1]] + yr, writes=[pb_.r()])
                    S.op("act", lambda e, pg_=pg_, sig_=sig_: e.activation(out=sig_[:], in_=pg_[:], func=AF.Sigmoid), reads=[pg_.r()], writes=[sig_.r()])
                    if i == 0:
                        S.op("dve", lambda e, acc_=acc_, sig_=sig_, pb_=pb_: e.tensor_tensor(out=acc_[:], in0=sig_[:], in1=pb_[:], op=ALU.mult), reads=[sig_.r(), pb_.r()], writes=[acc_.r()])
                    else:
                        S.op("dve", lambda e, tmp_=tmp_, sig_=sig_, pb_=pb_: e.tensor_tensor(out=tmp_[:], in0=sig_[:], in1=pb_[:], op=ALU.mult), reads=[sig_.r(), pb_.r()], writes=[tmp_.r()])
                        if i < 3:
                            S.op("dve", lambda e, acc_=acc_, tmp_=tmp_: e.tensor_tensor(out=acc_[:], in0=acc_[:], in1=tmp_[:], op=ALU.add), reads=[acc_.r(), tmp_.r()], writes=[acc_.r()])
                        else:
                            S.op("dve", lambda e, acc_=acc_, tmp_=tmp_, tt=tt, nb=nb: e.tensor_tensor(out=mixedb[:, tt, nb * 512:(nb + 1) * 512], in0=acc_[:], in1=tmp_[:], op=ALU.add),
                                 reads=[acc_.r(), tmp_.r()], writes=[mixedb.r((tt, nb))])
        for q4 in range(4):
            j = wcnt[0] % 2
            wcnt[0] += 1
            ws = wst[j]
            cs = slice(q4 * 256, (q4 + 1) * 256)
            S.op("sp", lambda e, ws=ws, cs=cs: e.dma_start(out=ws[:], in_=P["wo"][l, :, cs].rearrange("(k p) n -> p k n", p=128)), writes=[ws.r()], dma=True)
            S.op("act", lambda e, ws=ws, cs=cs: e.copy(out=wo[:, :, cs], in_=ws[:]), reads=[ws.r()], writes=[wo.r(q4 // 2)])
        def o_front(tt):
            tsl = slice(tt * 128, (tt + 1) * 128)
            i = tt % 2
            mT_, xt_, pT = mT[i], xt[i], pTs[i]
            S.op("sp", lambda e: e.dma_start(out=xt_[:], in_=xin[b, tsl, :]), reads=[c.xres_r[l]], writes=[xt_.r()], dma=True)

            def trm(e):
                for k in range(8):
                    ins = e.transpose(out=pT[:, k, :], in_=mixedb[:, tt, k * 128:(k + 1) * 128], identity=c.identb[:])
                return ins
            S.op("pe", trm, reads=[mixedb.r((tt, 0)), mixedb.r((tt, 1))], writes=[pT.r()])
            S.op("act", lambda e: e.copy(out=mT_[:], in_=pT[:]), reads=[pT.r()], writes=[mT_.r()])

        def o_back(tt):
            tsl = slice(tt * 128, (tt + 1) * 128)
            i = tt % 2
            mT_, xt_, ss_ = mT[i], xt[i], ssm[i]
            for nb in range(2):
                po_ = po[nb]

                def mmo(e, po_=po_, nb=nb):
                    for k in range(8):
                        ins = e.matmul(po_[:], lhsT=mT_[:, k, :], rhs=wo[:, k, nb * 512:(nb + 1) * 512], start=(k == 0), stop=(k == 7))
                    return ins
                S.op("pe", mmo, reads=[mT_.r(), wo.r(nb)], writes=[po_.r()])
                S.op("dve", lambda e, po_=po_, nb=nb: e.tensor_tensor(out=xt_[:, nb * 512:(nb + 1) * 512], in0=xt_[:, nb * 512:(nb + 1) * 512], in1=po_[:], op=ALU.add),
                     reads=[po_.r(), xt_.r()], writes=[xt_.r()])
            if last:
                S.op("act", lambda e: e.activation(out=sqt[:], in_=xt_[:], func=AF.Square, accum_out=ss_[:]), reads=[xt_.r()], writes=[sqt.r(), ss_.r()])
                S.op("act", lambda e: e.activation(out=ss_[:], in_=ss_[:], func=AF.Sqrt, scale=1.0 / D, bias=c.eps6[:, 0:1]), reads=[ss_.r()], writes=[ss_.r()])
                S.op("dve", lambda e: e.reciprocal(out=ss_[:], in_=ss_[:]), reads=[ss_.r()], writes=[ss_.r()])
                S.op("dve", lambda e: e.scalar_tensor_tensor(out=xt_[:], in0=xt_[:], scalar=ss_[:, 0:1], in1=fg[:], op0=ALU.mult, op1=ALU.mult),
                     reads=[xt_.r(), ss_.r(), fg.r()], writes=[xt_.r()])
            ev = S.op("pool", lambda e: e.dma_start(out=xout[b, tsl, :], in_=xt_[:]), reads=[xt_.r()], writes=[c.xres_r[l + 1]], dma=True)
            c.out_events.append(ev)
        o_front(0)
        for tt in range(NTT):
            if tt + 1 < NTT:
                o_front(tt + 1)
            o_back(tt)
        S.emit()


def host_params(inputs):
    f = lambda n: np.asarray(inputs[n], dtype=np.float32)
    fcols, tcols, mo = _col_perm()
    w_in = f("w_in")
    p = {}
    p["wf"] = np.ascontiguousarray(w_in[:, :, fcols])
    p["wt"] = np.ascontiguousarray(w_in[:, :, tcols])
    p["wm"] = np.ascontiguousarray(w_in[:, :, mo:mo + 4096])
    p["wbr"] = np.ascontiguousarray(f("w_branch").reshape(DEPTH, 1024, 1024))
    p["wo"] = f("w_out")
    p["final_gain"] = f("final_gain").reshape(1, D)
    p["gainT"] = np.ascontiguousarray(f("norm_gain").reshape(DEPTH, 8, 128).transpose(2, 0, 1).reshape(128, DEPTH * 8))
    p["sg_wsT"] = np.ascontiguousarray(f("sg_w_s").transpose(0, 3, 1, 2))
    p["sg_ln_gain"] = f("sg_ln_gain")
    p["sg_ln_bias"] = f("sg_ln_bias")
    p["sg_bsT"] = np.ascontiguousarray(f("sg_b_s").transpose(0, 2, 1))
    p["ident"] = np.eye(128, dtype=np.float32)
    i = np.arange(128)
    p["triT"] = (i[:, None] <= i[None, :]).astype(np.float32)
    p["cmp_posT"] = np.ascontiguousarray(f("nsa_cmp_pos").transpose(0, 3, 1, 2))
    p["nsa_cmp_w1"] = f("nsa_cmp_w1")
    p["nsa_cmp_w2"] = f("nsa_cmp_w2")
    p.update(nsa_consts())
    p["gdn_convT"] = np.ascontiguousarray(f("gdn_conv").transpose(0, 2, 1).reshape(DEPTH, 6, 128, 4).transpose(0, 2, 1, 3))
    p["gdn_a_log"] = f("gdn_a_log")
    p["gdn_dt_bias"] = f("gdn_dt_bias")
    p["gdn_norm"] = f("gdn_norm")
    p["rw_muT"] = np.ascontiguousarray(f("rw_mu").reshape(DEPTH, 7, 128).transpose(0, 2, 1))
    p["rw_lora"] = np.ascontiguousarray(np.concatenate([f("rw_w_up"), f("rw_a_up")], axis=1))
    colp = np.stack([f("rw_w0"), f("rw_a0"), f("rw_k_k"), f("rw_k_a"), f("rw_r_k").reshape(DEPTH, 256)], axis=-1)
    p["rw_cols"] = np.ascontiguousarray(colp.reshape(DEPTH, 2, 128, 5).transpose(0, 2, 1, 3))
    p["rw_gn_gain"] = f("rw_gn_gain")
    p["rw_gn_bias"] = f("rw_gn_bias")
    j64 = np.arange(64)
    p["m1cat"] = np.concatenate([(j64[:, None] < j64[None, :]), (j64[:, None] <= j64[None, :])], axis=1).astype(np.float32)
    p["maskL"] = (j64[None, :] < j64[:, None]).astype(np.float32)
    blk = np.zeros((128, 128), np.float32); blk[:64, :64] = 1; blk[64:, 64:] = 1
    p["blk"] = blk
    return p


def build(stages=("s1",), debug_out=(), pshapes=None, only=None):
    nc = bass.Bass("TRN2", target_bir_lowering=False)
    c = Ctx()
    c.nc = nc
    dt = lambda name, shape, kind="ExternalInput", dtype=F32: nc.dram_tensor(name, list(shape), dtype, kind=kind).ap()
    dbg = lambda name: "ExternalOutput" if name in debug_out else "Internal"
    x_in = dt("x", [NB, T, D])
    x1 = dt("x1", [NB, T, D], dbg("x1"))
    c.xres = [x_in, x1]
    c.out = dt("out", [NB, T, D], "ExternalOutput")
    c.p = {n: dt("p_" + n, shp) for n, shp in pshapes.items()}
    c.wf, c.wt = c.p["wf"], c.p["wt"]
    c.Fs = [dt(f"Fs{b}", [NF, T], dbg("Fs")) for b in range(NB)]
    c.Ts = [dt(f"Ts{b}", [T, NT], dbg("Ts")) for b in range(NB)]
    c.Fs_r = [DramRes() for _ in range(NB)]
    c.Ts_r = [DramRes() for _ in range(NB)]
    c.ydbg = dt("ydbg", [NB, T, 1024], "ExternalOutput") if "ydbg" in debug_out else None
    c.ydbg_r = DramRes()
    c.xres_r = [DramRes() for _ in range(DEPTH + 1)]
    c.out_events = []

    with ExitStack() as st:
        S = Sched(nc, st)
        c.S = S
        A = lambda n, s, d, psum=False: _alloc(nc, st, n, s, d, psum)
        c.hT = A("hT", [128, 8, T], BF16)
        c.yT = A("yT", [128, 8, T], BF16)
        c.ident = A("ident_f", [128, 128], F32)
        c.identb = A("ident_b", [128, 128], BF16)
        c.gainT = A("gainT_s", [128, DEPTH * 8], F32)
        c.eps6 = A("eps6", [128, 1], F32)
        c.eps5 = A("eps5", [128, 1], F32)
        S.op("sp", lambda e: e.dma_start(out=c.ident[:], in_=c.p["ident"][:, :]), writes=[c.ident.r()], dma=True)
        S.op("sp", lambda e: e.dma_start(out=c.gainT[:], in_=c.p["gainT"][:, :]), writes=[c.gainT.r()], dma=True)
        S.op("dve", lambda e: e.tensor_copy(out=c.identb[:], in_=c.ident[:]), reads=[c.ident.r()], writes=[c.identb.r()])
        S.op("dve", lambda e: e.memset(c.eps6[:], 1e-6), writes=[c.eps6.r()])
        S.op("dve", lambda e: e.memset(c.eps5[:], 1e-5), writes=[c.eps5.r()])
        c.epsgn = A("epsgn", [128, 1], F32)
        c.one = A("one", [128, 1], F32)
        S.op("dve", lambda e: e.memset(c.one[:], 1.0), writes=[c.one.r()])
        S.op("dve", lambda e: e.memset(c.epsgn[:], 64e-5), writes=[c.epsgn.r()])
        c.m1cat = A("m1cat", [64, 128], F32)
        c.maskL = A("maskL", [64, 64], F32)
        c.blk = A("blk", [128, 128], F32)
        c.hmask = A("hmask", [128, 2], F32)
        S.op("dve", lambda e: e.memset(c.hmask[:], 0.0), writes=[c.hmask.r()])
        S.op("dve", lambda e: e.memset(c.hmask[0:64, 0:1], 1.0), writes=[c.hmask.r()])
        S.op("dve", lambda e: e.memset(c.hmask[64:128, 1:2], 1.0), writes=[c.hmask.r()])
        for tl, nm in ((c.m1cat, "m1cat"), (c.maskL, "maskL"), (c.blk, "blk")):
            S.op("sp", lambda e, tl=tl, nm=nm: e.dma_start(out=tl[:], in_=c.p[nm]), writes=[tl.r()], dma=True)
        S.emit()
        for l in range(DEPTH):
            for b in range(NB):
                if only is not None and (l, b) not in only:
                    continue
                if "s1" in stages:
                    stage1(c, l, b)
                c.merge_B = ("A" in stages and "B" in stages)
                if "B" in stages and not c.merge_B:
                    mixerB(c, l, b)
                if "D" in stages:
                    mixerD(c, l, b)
                if "C" in stages:
                    mixerC(c, l, b)
                if "A" in stages:
                    mixerA(c, l, b)
                if "s3" in stages:
                    stage3(c, l, b)
    c.nops = S.nops
    return nc


def host_inputs(inputs):
    p = host_params(inputs)
    x = np.asarray(inputs["x"], dtype=np.float32)
    maps = []
    for i in range(NCORES):
        m = {"p_" + k: v for k, v in p.items()}
        m["x"] = np.ascontiguousarray(x[i * NB:(i + 1) * NB])
        maps.append(m)
    return maps, {n: a.shape for n, a in p.items()}


def kernel(**inputs):
    maps, pshapes = host_inputs(inputs)
    nc = build(stages=ALL_STAGES, pshapes=pshapes)
    res = run_bass_kernel_spmd(nc, maps, core_ids=list(range(NCORES)))
    return np.concatenate([r["out"] for r in res.results], axis=0)


ALL_STAGES = ("s1", "A", "B", "C", "D", "s3")
```

```python
import math
from contextlib import ExitStack

import numpy as np
import concourse.bass as bass
import concourse.mybir as mybir
from concourse.bass_utils import run_bass_kernel_spmd

F32 = mybir.dt.float32
BF16 = mybir.dt.bfloat16
AF = mybir.ActivationFunctionType
ALU = mybir.AluOpType
AX = mybir.AxisListType

NCORES = 8
DEPTH = 2
D = 1024
T = 2048
NB = 2
NTT = T // 128
D_IN = 7956
NF = 2176
NT = 1684

F_GQ, F_GK, F_GV = 0, 256, 512
F_RW = 768
F_NQ = 1664
F_KC, F_VC, F_KS, F_KW = 1920, 1984, 2048, 2112
T_GZ, T_SU, T_SV, T_SZ, T_RZ, T_NZ = 0, 256, 512, 768, 1024, 1280
T_VS, T_VW, T_GA, T_GB, T_NG = 1536, 1600, 1664, 1668, 1672


def _col_perm():
    o = {}
    off = 0
    names = ["gdn_qkv", "gdn_a", "gdn_b", "gdn_z", "sg_u", "sg_v", "sg_z", "rw", "rw_z",
             "nsa_q", "nsa_kv", "nsa_g", "nsa_z", "merge"]
    sizes = [768, 4, 4, 256, 256, 256, 256, 896, 256, 256, 384, 12, 256, 4096]
    for n, s in zip(names, sizes):
        o[n] = off
        off += s
    assert off == D_IN
    r = lambda a, n: list(range(a, a + n))
    kv = o["nsa_kv"]
    fcols = (r(o["gdn_qkv"], 768) + r(o["rw"], 896) + r(o["nsa_q"], 256)
             + r(kv, 64) + r(kv + 64, 64) + r(kv + 128, 64) + r(kv + 256, 64))
    tcols = (r(o["gdn_z"], 256) + r(o["sg_u"], 256) + r(o["sg_v"], 256) + r(o["sg_z"], 256)
             + r(o["rw_z"], 256) + r(o["nsa_z"], 256) + r(kv + 192, 64) + r(kv + 320, 64)
             + r(o["gdn_a"], 4) + r(o["gdn_b"], 4) + r(o["nsa_g"], 12))
    assert len(fcols) == NF and len(tcols) == NT
    return np.array(fcols), np.array(tcols), o["merge"]


class Res:
    __slots__ = ("last_w", "readers")

    def __init__(self):
        self.last_w = None
        self.readers = []


class DramRes(Res):
    __slots__ = ()


class Tile:
    def __init__(self, t):
        self.t = t
        self._r = {}

    def r(self, key=0):
        x = self._r.get(key)
        if x is None:
            x = self._r[key] = Res()
        return x

    def __getitem__(self, idx):
        return self.t[idx]


class Sched:
    ENGS = ("pe", "act", "dve", "pool", "sp")
    ROT = 30000

    def __init__(self, nc, stack, n_dma_ring=8):
        self.nc = nc
        self.stack = stack
        self.sems = []
        self.eng_sem = {}
        self.eng_cnt = {}
        self.ring = {}
        for e in self.ENGS:
            self.eng_sem[e] = self._new_sem(f"s_{e}")
            self.eng_cnt[e] = 0
            self.ring[e] = [[self._new_sem(f"d_{e}{i}"), 0] for i in range(n_dma_ring)]
        self.ring_pos = {e: 0 for e in self.ENGS}
        self.waited = {e: {} for e in self.ENGS}
        self.ops = {e: [] for e in self.ENGS}
        self.nops = 0
        self.defer = None
        self.last_ev = {e: None for e in self.ENGS}

    def _new_sem(self, name):
        s = self.stack.enter_context(self.nc.semaphore(name))
        self.sems.append(s)
        return len(self.sems) - 1

    def _need(self, eng, ev, waits):
        if ev is None:
            return
        si, val = ev[1], ev[2]
        if self.waited[eng].get(si, 0) >= val:
            return
        if val > waits.get(si, 0):
            waits[si] = val

    def op(self, eng, fn, reads=(), writes=(), dma=False, pe_acc=False):
        if self.defer is not None:
            self.defer.append((eng, fn, list(reads), list(writes), dma, pe_acc))
            return None
        reads = [r for r in reads if not isinstance(r, DramRes)]
        writes = [w for w in writes if not isinstance(w, DramRes)]
        waits = {}
        for r in reads:
            self._need(eng, r.last_w, waits)
        for w in writes:
            lw = w.last_w
            if not (pe_acc and lw is not None and lw[0] == "pe" and eng == "pe"):
                self._need(eng, lw, waits)
            for ev in w.readers:
                self._need(eng, ev, waits)
        if dma:
            pos = self.ring_pos[eng]
            self.ring_pos[eng] = (pos + 1) % len(self.ring[eng])
            slot = self.ring[eng][pos]
            if slot[1] > 0:
                self._need(eng, (eng, slot[0], slot[1]), waits)
            slot[1] += 16
            ev = (eng, slot[0], slot[1])
            inc = (slot[0], 16)
        else:
            if self.eng_cnt[eng] >= self.ROT:
                self.eng_sem[eng] = self._new_sem(f"s_{eng}_{len(self.sems)}")
                self.eng_cnt[eng] = 0
            self.eng_cnt[eng] += 1
            ev = (eng, self.eng_sem[eng], self.eng_cnt[eng])
            inc = (self.eng_sem[eng], 1)
        for si, val in waits.items():
            self.waited[eng][si] = val
        self.ops[eng].append((list(waits.items()), fn, inc))
        for r in reads:
            r.readers.append(ev)
        for w in writes:
            w.last_w = ev
            w.readers = []
        self.nops += 1
        self.last_ev[eng] = ev
        return ev

    def replay(self, *lists):
        assert self.defer is None
        pos = [0] * len(lists)
        total = sum(len(x) for x in lists)
        for _ in range(total):
            best, bf = None, None
            for i, x in enumerate(lists):
                if pos[i] < len(x):
                    f = pos[i] / len(x)
                    if bf is None or f < bf:
                        best, bf = i, f
            a = lists[best][pos[best]]
            pos[best] += 1
            self.op(a[0], a[1], a[2], a[3], a[4], a[5])

    def emit(self, final_events=()):
        nc = self.nc
        sems = self.sems
        ops = self.ops
        self.ops = {e: [] for e in self.ENGS}
        tail = [ev for ev in self.last_ev.values() if ev is not None]
        for e in self.ENGS:
            for slot in self.ring[e]:
                if slot[1] > 0:
                    tail.append((e, slot[0], slot[1]))
        tail += list(final_events)
        tw = {}
        for ev in tail:
            tw[ev[1]] = max(tw.get(ev[1], 0), ev[2])

        with nc.Block() as block:
            def run(engname, e):
                for waits, fn, inc in ops[engname]:
                    for si, val in waits:
                        e.wait_ge(sems[si], val)
                    ins = fn(e)
                    ins.then_inc(sems[inc[0]], inc[1])
                for si, val in tw.items():
                    if self.waited[engname].get(si, 0) < val:
                        e.wait_ge(sems[si], val)
                        self.waited[engname][si] = val

            @block.tensor
            def _(e):
                run("pe", e)

            @block.scalar
            def _(e):
                run("act", e)

            @block.vector
            def _(e):
                run("dve", e)

            @block.gpsimd
            def _(e):
                run("pool", e)

            @block.sync
            def _(e):
                run("sp", e)


class Ctx:
    pass


_UID = [0]


def _alloc(nc, st, name, shape, dtype, psum=False):
    _UID[0] += 1
    name = f"{name}_{_UID[0]}"
    if psum:
        return Tile(st.enter_context(nc.psum_tensor(name, shape, dtype)))
    return Tile(st.enter_context(nc.sbuf_tensor(name, shape, dtype)))


def stage1(c, l, b):
    nc, S = c.nc, c.S
    xin = c.xres[l]
    with ExitStack() as st:
        A = lambda n, s, d, psum=False: _alloc(nc, st, n, s, d, psum)
        xt = [A(f"s1_x{i}", [128, D], F32) for i in range(2)]
        sq = A("s1_sq", [128, D], F32)
        ssum = [A(f"s1_ss{i}", [128, 1], F32) for i in range(2)]
        hb = [A(f"s1_hb{i}", [128, D], BF16) for i in range(2)]
        pT = [A(f"s1_pT{i}", [128, 8, 128], BF16, psum=True) for i in range(2)]
        wf32 = [A(f"s1_wf{i}", [128, 8, 512], F32) for i in range(2)]
        wbf = [A(f"s1_wb{i}", [128, 8, 512], BF16) for i in range(2)]
        po = [A(f"s1_po{i}", [128, 512], F32, psum=True) for i in range(4)]
        ot = [A(f"s1_ot{i}", [128, 512], F32) for i in range(4)]

        def a_front(tt):
            i = tt % 2
            x_, ss_, hb_, pT_ = xt[i], ssum[i], hb[i], pT[i]
            S.op("sp", lambda e: e.dma_start(out=x_[:], in_=xin[b, tt * 128:(tt + 1) * 128, :]),
                 reads=[c.xres_r[l]], writes=[x_.r()], dma=True)
            S.op("act", lambda e: e.activation(out=sq[:], in_=x_[:], func=AF.Square, accum_out=ss_[:]),
                 reads=[x_.r()], writes=[sq.r(), ss_.r()])
            S.op("act", lambda e: e.activation(out=ss_[:], in_=ss_[:], func=AF.Sqrt, scale=1.0 / D, bias=c.eps6[:, 0:1]),
                 reads=[ss_.r()], writes=[ss_.r()])
            S.op("dve", lambda e: e.reciprocal(out=ss_[:], in_=ss_[:]), reads=[ss_.r()], writes=[ss_.r()])
            S.op("dve", lambda e: e.tensor_scalar(out=hb_[:], in0=x_[:], scalar1=ss_[:, 0:1], scalar2=None, op0=ALU.mult),
                 reads=[x_.r(), ss_.r()], writes=[hb_.r()])

            def tr(e):
                for k in range(8):
                    ins = e.transpose(out=pT_[:, k, :], in_=hb_[:, k * 128:(k + 1) * 128], identity=c.identb[:])
                return ins
            S.op("pe", tr, reads=[hb_.r()], writes=[pT_.r()])

        def a_back(tt):
            pT_ = pT[tt % 2]
            eng = "dve" if tt % 2 == 0 else "act"
            if eng == "dve":
                S.op("dve", lambda e: e.tensor_copy(out=c.hT[:, :, tt * 128:(tt + 1) * 128], in_=pT_[:]), reads=[pT_.r()], writes=[c.hT.r(tt)])
            else:
                S.op("act", lambda e: e.copy(out=c.hT[:, :, tt * 128:(tt + 1) * 128], in_=pT_[:]), reads=[pT_.r()], writes=[c.hT.r(tt)])
        a_front(0)
        for tt in range(NTT):
            if tt + 1 < NTT:
                a_front(tt + 1)
            a_back(tt)

        hT_all = [c.hT.r(tt) for tt in range(NTT)]

        def load_w(src, c0, n, j):
            wf_, wb_ = wf32[j], wbf[j]
            S.op("sp", lambda e: e.dma_start(out=wf_[:, :, 0:n], in_=src[l, :, c0:c0 + n].rearrange("(k p) n -> p k n", p=128)),
                 writes=[wf_.r()], dma=True)
            for k in range(8):
                eng = "pool" if k % 2 == 0 else "act"
                if eng == "pool":
                    S.op("dve", lambda e, k=k: e.tensor_scalar(out=wb_[:, k, 0:n], in0=wf_[:, k, 0:n], scalar1=c.gainT[:, l * 8 + k:l * 8 + k + 1], scalar2=None, op0=ALU.mult),
                         reads=[wf_.r()], writes=[wb_.r(k)])
                else:
                    S.op("act", lambda e, k=k: e.activation(out=wb_[:, k, 0:n], in_=wf_[:, k, 0:n], func=AF.Copy, scale=c.gainT[:, l * 8 + k:l * 8 + k + 1]),
                         reads=[wf_.r()], writes=[wb_.r(k)])
            return wb_, [wb_.r(k) for k in range(8)]

        cnt = 0
        for cc in range(0, NF, 512):
            n = min(512, NF - cc)
            wb_, wres = load_w(c.wf, cc, n, (cc // 512) % 2)
            for c1 in range(0, n, 128):
                for tq in range(4):
                    j = cnt % 4
                    cnt += 1
                    po_, ot_ = po[j], ot[j]

                    def mm(e, po_=po_, wb_=wb_, c1=c1, tq=tq):
                        for k in range(8):
                            ins = e.matmul(po_[:], lhsT=wb_[:, k, c1:c1 + 128], rhs=c.hT[:, k, tq * 512:(tq + 1) * 512],
                                           start=(k == 0), stop=(k == 7))
                        return ins
                    S.op("pe", mm, reads=wres + hT_all[tq * 4:tq * 4 + 4], writes=[po_.r()])
                    ev_eng = "dve" if cnt % 2 == 0 else "act"
                    if ev_eng == "dve":
                        S.op("dve", lambda e, po_=po_, ot_=ot_: e.tensor_copy(out=ot_[:], in_=po_[:]), reads=[po_.r()], writes=[ot_.r()])
                    else:
                        S.op("act", lambda e, po_=po_, ot_=ot_: e.copy(out=ot_[:], in_=po_[:]), reads=[po_.r()], writes=[ot_.r()])
                    row = cc + c1
                    S.op("pool", lambda e, ot_=ot_, row=row, tq=tq: e.dma_start(out=c.Fs[b][row:row + 128, tq * 512:(tq + 1) * 512], in_=ot_[:]),
                         reads=[ot_.r()], writes=[c.Fs_r[b]], dma=True)

        for ci, cc in enumerate(range(0, NT, 512)):
            n = min(512, NT - cc)
            wb_, wres = load_w(c.wt, cc, n, (ci + 1) % 2)
            for tt in range(NTT):
                j = cnt % 4
                cnt += 1
                po_, ot_ = po[j], ot[j]

                def mm(e, po_=po_, wb_=wb_, tt=tt, n=n):
                    for k in range(8):
                        ins = e.matmul(po_[:, 0:n], lhsT=c.hT[:, k, tt * 128:(tt + 1) * 128], rhs=wb_[:, k, 0:n],
                                       start=(k == 0), stop=(k == 7))
                    return ins
                S.op("pe", mm, reads=wres + [hT_all[tt]], writes=[po_.r()])
                if cnt % 2 == 0:
                    S.op("dve", lambda e, po_=po_, ot_=ot_, n=n: e.tensor_copy(out=ot_[:, 0:n], in_=po_[:, 0:n]), reads=[po_.r()], writes=[ot_.r()])
                else:
                    S.op("act", lambda e, po_=po_, ot_=ot_, n=n: e.copy(out=ot_[:, 0:n], in_=po_[:, 0:n]), reads=[po_.r()], writes=[ot_.r()])
                S.op("pool", lambda e, ot_=ot_, tt=tt, cc=cc, n=n: e.dma_start(out=c.Ts[b][tt * 128:(tt + 1) * 128, cc:cc + n], in_=ot_[:, 0:n]),
                     reads=[ot_.r()], writes=[c.Ts_r[b]], dma=True)
        S.emit()


def _tmp(nc, st, name, shape, dtype, n=2, psum=False):
    return [_alloc(nc, st, f"{name}{i}", shape, dtype, psum) for i in range(n)]


def _dbg_y(c, b, tt, col, yt, res, width=256):
    if c.ydbg is None:
        return
    c.S.op("pool", lambda e: e.dma_start(out=c.ydbg[b, tt * 128:(tt + 1) * 128, col:col + width], in_=yt),
           reads=[res], writes=[c.ydbg_r], dma=True)


def _y_to_yT(c, st_tiles, yb, yb_r, mixer, tt, i):
    S = c.S
    pT = st_tiles[i]

    def tr(e):
        for j in range(2):
            ins = e.transpose(out=pT[:, j, :], in_=yb[:, j * 128:(j + 1) * 128], identity=c.identb[:])
        return ins
    S.op("pe", tr, reads=[yb_r], writes=[pT.r()])
    S.op("act", lambda e: e.copy(out=c.yT[:, 2 * mixer:2 * mixer + 2, tt * 128:(tt + 1) * 128], in_=pT[:]),
         reads=[pT.r()], writes=[c.yT.r((mixer, tt))])


def mixerB(c, l, b, ext_st=None):
    nc, S = c.nc, c.S
    with ExitStack() as own_st:
        st = ext_st if ext_st is not None else own_st
        A = lambda n, s, d, psum=False: _alloc(nc, st, n, s, d, psum)
        nbuf = 2 if ext_st is None else 1
        TM = lambda n, s, d, k=2, psum=False: _tmp(nc, st, n, s, d, (k if psum else nbuf), psum)
        ws32 = A("mb_ws32", [128, 4, 128], F32)
        ws = A("mb_ws", [128, 4, 128], BF16)
        lng = A("mb_lng", [128, 256], F32)
        lnb = A("mb_lnb", [128, 256], F32)
        bsT = A("mb_bs", [128, 4], F32)
        triT = A("mb_triT", [128, 128], F32)
        S.op("sp", lambda e: e.dma_start(out=triT[:], in_=c.p["triT"][:, :]), writes=[triT.r()], dma=True)
        S.op("sp", lambda e: e.dma_start(out=ws32[:], in_=c.p["sg_wsT"][l]), writes=[ws32.r()], dma=True)
        S.op("sp", lambda e: e.dma_start(out=lng[:], in_=c.p["sg_ln_gain"][l:l + 1, :].partition_broadcast(128)), writes=[lng.r()], dma=True)
        S.op("sp", lambda e: e.dma_start(out=lnb[:], in_=c.p["sg_ln_bias"][l:l + 1, :].partition_broadcast(128)), writes=[lnb.r()], dma=True)
        S.op("sp", lambda e: e.dma_start(out=bsT[:], in_=c.p["sg_bsT"][l]), writes=[bsT.r()], dma=True)
        S.op("dve", lambda e: e.tensor_tensor(out=ws[:], in0=ws32[:], in1=triT[:].unsqueeze(1).to_broadcast([128, 4, 128]), op=ALU.mult),
             reads=[ws32.r(), triT.r()], writes=[ws.r()])
        xin = TM("mb_in", [128, 768], F32)
        gl = TM("mb_gl", [128, 512], F32)
        sz = TM("mb_sz", [128, 256], F32)
        m4 = TM("mb_m4", [128, 4], F32)
        v4 = TM("mb_v4", [128, 4], F32)
        xc = TM("mb_xc", [128, 256], F32)
        t1 = TM("mb_t1", [128, 256], F32)
        vb = TM("mb_vb", [128, 256], BF16)
        pm = TM("mb_pm", [128, 256], F32, psum=True)
        y32 = TM("mb_y", [128, 256], F32)
        yb = _tmp(nc, st, "mb_yb", [128, 256], BF16, 2)
        pT = TM("mb_pT", [128, 2, 128], BF16, psum=True)
        g3 = lambda ap: ap.rearrange("p (g c) -> p g c", g=4)
        bc = lambda ap: ap.unsqueeze(2).to_broadcast([128, 4, 64])
        for tt in range(NTT):
            i = tt % nbuf
            xi, gl_, sz_, m_, v_, xc_, t_, vb_, pm_, y_, yb_ = xin[i], gl[i], sz[i], m4[i], v4[i], xc[i], t1[i], vb[i], pm[i], y32[i], yb[tt % 2]
            S.op("sp", lambda e, xi=xi, tt=tt: e.dma_start(out=xi[:], in_=c.Ts[b][tt * 128:(tt + 1) * 128, T_SU:T_SU + 768]),
                 reads=[c.Ts_r[b]], writes=[xi.r()], dma=True)
            S.op("act", lambda e, xi=xi, gl_=gl_: e.activation(out=gl_[:], in_=xi[:, 0:512], func=AF.Gelu_apprx_tanh), reads=[xi.r()], writes=[gl_.r()])
            S.op("act", lambda e, xi=xi, sz_=sz_: e.activation(out=sz_[:], in_=xi[:, 512:768], func=AF.Silu), reads=[xi.r()], writes=[sz_.r()])
            S.op("dve", lambda e, gl_=gl_, m_=m_: e.tensor_reduce(out=m_[:], in_=g3(gl_[:, 256:512]), axis=AX.X, op=ALU.add), reads=[gl_.r()], writes=[m_.r()])
            S.op("dve", lambda e, gl_=gl_, m_=m_, xc_=xc_: e.scalar_tensor_tensor(out=g3(xc_[:]), in0=bc(m_[:]), scalar=-1.0 / 64, in1=g3(gl_[:, 256:512]), op0=ALU.mult, op1=ALU.add),
                 reads=[gl_.r(), m_.r()], writes=[xc_.r()])
            S.op("dve", lambda e, xc_=xc_, t_=t_: e.tensor_tensor(out=t_[:], in0=xc_[:], in1=xc_[:], op=ALU.mult), reads=[xc_.r()], writes=[t_.r()])
            S.op("dve", lambda e, t_=t_, v_=v_: e.tensor_reduce(out=v_[:], in_=g3(t_[:]), axis=AX.X, op=ALU.add), reads=[t_.r()], writes=[v_.r()])
            S.op("act", lambda e, v_=v_: e.activation(out=v_[:], in_=v_[:], func=AF.Sqrt, scale=1.0 / 64, bias=c.eps5[:, 0:1]), reads=[v_.r()], writes=[v_.r()])
            S.op("dve", lambda e, v_=v_: e.reciprocal(out=v_[:], in_=v_[:]), reads=[v_.r()], writes=[v_.r()])
            S.op("dve", lambda e, xc_=xc_, v_=v_, t_=t_: e.tensor_tensor(out=g3(t_[:]), in0=g3(xc_[:]), in1=bc(v_[:]), op=ALU.mult), reads=[xc_.r(), v_.r()], writes=[t_.r()])
            S.op("dve", lambda e, t_=t_: e.tensor_tensor(out=t_[:], in0=t_[:], in1=lng[:], op=ALU.mult), reads=[t_.r(), lng.r()], writes=[t_.r()])
            S.op("dve", lambda e, t_=t_, vb_=vb_: e.tensor_tensor(out=vb_[:], in0=t_[:], in1=lnb[:], op=ALU.add), reads=[t_.r(), lnb.r()], writes=[vb_.r()])

            def mm(e, vb_=vb_, pm_=pm_):
                for g in range(4):
                    ins = e.matmul(pm_[:, g * 64:(g + 1) * 64], lhsT=ws[:, g, :], rhs=vb_[:, g * 64:(g + 1) * 64], start=True, stop=True)
                return ins
            S.op("pe", mm, reads=[vb_.r(), ws.r()], writes=[pm_.r()])
            S.op("dve", lambda e, pm_=pm_, t_=t_: e.tensor_tensor(out=g3(t_[:]), in0=g3(pm_[:]), in1=bc(bsT[:]), op=ALU.add), reads=[pm_.r(), bsT.r()], writes=[t_.r()])
            S.op("dve", lambda e, t_=t_, gl_=gl_: e.tensor_tensor(out=t_[:], in0=t_[:], in1=gl_[:, 0:256], op=ALU.mult), reads=[t_.r(), gl_.r()], writes=[t_.r()])
            S.op("dve", lambda e, t_=t_, sz_=sz_, y_=y_: e.tensor_tensor(out=y_[:], in0=t_[:], in1=sz_[:], op=ALU.mult), reads=[t_.r(), sz_.r()], writes=[y_.r()])
            S.op("dve", lambda e, y_=y_, yb_=yb_: e.tensor_copy(out=yb_[:], in_=y_[:]), reads=[y_.r()], writes=[yb_.r()])
            _dbg_y(c, b, tt, 256, y_[:], y_.r())
            if tt >= 1:
                _y_to_yT(c, pT, yb[(tt - 1) % 2], yb[(tt - 1) % 2].r(), 1, tt - 1, (tt - 1) % 2)
        _y_to_yT(c, pT, yb[(NTT - 1) % 2], yb[(NTT - 1) % 2].r(), 1, NTT - 1, (NTT - 1) % 2)
        if ext_st is None:
            S.emit()


SLOPES = [2.0 ** (-8.0 * (h + 1) / 4) for h in range(4)]
NEG = -1e30


def nsa_consts():
    p = np.arange(128)[:, None, None]
    h_sl = np.array(SLOPES, dtype=np.float64)[None, :, None]
    m = (np.arange(248) - 120)[None, None, :]
    dist = p - 16 * m - 31
    wc = np.where(dist >= 0, -h_sl * dist, NEG).astype(np.float32)
    jp = (np.arange(62) - 30)[None, :]
    cur = (np.arange(128) >= 64).astype(np.int64)[:, None]
    sb = np.where(jp > cur, NEG, np.where((jp == cur) | (jp == cur - 1), 1e4, 0.0)).astype(np.float32)
    tq = np.arange(128)[None, None, None, :]
    pk = np.arange(128)[:, None, None, None]
    dl = np.arange(16)[None, :, None, None]
    hs = np.array(SLOPES, dtype=np.float64)[None, None, :, None]
    d4 = 128 * dl + tq - pk
    bb = np.where(d4 >= 0, -hs * d4, NEG).astype(np.float32)
    bw2 = np.where(d4[:, 2:3] < 256, bb[:, 2:3], NEG).astype(np.float32)[:, 0]
    j = np.arange(32)[:, None, None]
    kb = np.arange(16)[None, :, None]
    pp = (np.arange(128) >= 64).astype(np.int64)[None, None, :]
    esel = (j == 2 * kb + pp).astype(np.float32)
    return {"nsa_wc": wc, "nsa_sb": sb, "nsa_bb": bb.reshape(128, 16, 512), "nsa_bw2": bw2.reshape(128, 512), "nsa_esel": esel}


def mixerD(c, l, b):
    nc, S = c.nc, c.S
    P = c.p
    with ExitStack() as st:
        A = lambda n, s, d, psum=False: _alloc(nc, st, n, s, d, psum)
        TM = lambda n, s, d, k=2, psum=False: _tmp(nc, st, n, s, d, k, psum)
        wc = A("md_wc", [128, 4, 248], F32)
        sbt = A("md_sb", [128, 62], F32)
        bb = A("md_bb", [128, 16, 512], F32)
        bw2 = A("md_bw2", [128, 512], F32)
        esel = A("md_esel", [32, 16, 128], BF16)
        for tl, nm in ((wc, "nsa_wc"), (sbt, "nsa_sb"), (bb, "nsa_bb"), (bw2, "nsa_bw2")):
            S.op("sp", lambda e, tl=tl, nm=nm: e.dma_start(out=tl[:], in_=P[nm]), writes=[tl.r()], dma=True)
        qT = A("md_qT", [64, 4, T], BF16)
        ksT = A("md_ksT", [64, T], BF16)
        kwT = A("md_kwT", [64, T], BF16)
        vs = A("md_vs", [128, NTT, 65], BF16)
        vw = A("md_vw", [128, NTT, 65], BF16)
        selT = A("md_selT", [32, T], BF16)
        kcmpT = A("md_kcmpT", [64, 128], BF16)
        vcmp = A("md_vcmp", [128, 64], BF16)
        ps_c = A("md_ps_c", [128, 4, 128], F32, psum=True)
        pmisc = A("md_pmisc", [128, 1024], BF16, psum=True)
        ps_s = TM("md_ps_s", [128, 512], F32, 2, psum=True)
        pmask4 = A("md_pmask", [128, 4, 128], F32, psum=True)
        o_s = A("md_o_s", [128, 4, 65], F32, psum=True)
        o_w = A("md_o_w", [128, 4, 65], F32, psum=True)
        o_c = A("md_o_c", [128, 4, 64], F32, psum=True)

        pst = ExitStack()
        A0, TM0 = A, TM
        A = lambda n, s, d, psum=False: _alloc(nc, pst, n, s, d, psum)
        TM = lambda n, s, d, k=2, psum=False: _tmp(nc, pst, n, s, d, k, psum)
        stg = TM("md_stg", [64, T], F32)
        esel32 = A("md_esel32", [32, 16, 128], F32)
        S.op("sp", lambda e: e.dma_start(out=esel32[:], in_=P["nsa_esel"]), writes=[esel32.r()], dma=True)
        S.op("act", lambda e: e.copy(out=esel[:], in_=esel32[:]), reads=[esel32.r()], writes=[esel.r()])
        cl = [A(f"md_cl{i}", [64, T], BF16) for i in range(2)]
        ch = [A(f"md_ch{i}", [64, T], BF16) for i in range(2)]
        vstg = A("md_vstg", [128, NTT, 128], F32)
        posT = A("md_posT", [64, 2, 32], F32)
        w1s = A("md_w1s", [64, 32, 64], F32)
        w1 = [A(f"md_w1{i}", [64, 32, 64], BF16) for i in range(2)]
        w2s = A("md_w2s", [64, 2, 64], F32)
        w2 = A("md_w2", [64, 2, 64], BF16)
        h1T = [A(f"md_h1T{i}", [64, 128], BF16) for i in range(2)]
        Fsb, Tsb = c.Fs[b], c.Ts[b]
        for h in range(4):
            sg = stg[h % 2]
            S.op("sp", lambda e, sg=sg, h=h: e.dma_start(out=sg[:], in_=Fsb[F_NQ + h * 64:F_NQ + (h + 1) * 64, :]),
                 reads=[c.Fs_r[b]], writes=[sg.r()], dma=True)
            S.op("act", lambda e, sg=sg, h=h: e.activation(out=qT[:, h, :], in_=sg[:], func=AF.Copy, scale=0.125), reads=[sg.r()], writes=[qT.r(h)])
        qT_all = [qT.r(h) for h in range(4)]
        for i, (row, dst) in enumerate(((F_KS, ksT), (F_KW, kwT))):
            sg = stg[i % 2]
            S.op("sp", lambda e, sg=sg, row=row: e.dma_start(out=sg[:], in_=Fsb[row:row + 64, :]), reads=[c.Fs_r[b]], writes=[sg.r()], dma=True)
            S.op("act", lambda e, sg=sg, dst=dst: e.copy(out=dst[:], in_=sg[:]), reads=[sg.r()], writes=[dst.r()])
        S.op("sp", lambda e: e.dma_start(out=posT[:], in_=P["cmp_posT"][l]), writes=[posT.r()], dma=True)
        v3 = lambda ap: ap.rearrange("p (n s) -> p n s", s=16)
        for i, row in enumerate((F_KC, F_VC)):
            sg = stg[i % 2]
            S.op("sp", lambda e, sg=sg, row=row: e.dma_start(out=sg[:], in_=Fsb[row:row + 64, :]), reads=[c.Fs_r[b]], writes=[sg.r()], dma=True)
            S.op("dve", lambda e, sg=sg, i=i: e.tensor_tensor(out=v3(cl[i][:]), in0=v3(sg[:]), in1=posT[:, i, 0:16].unsqueeze(1).to_broadcast([64, 128, 16]), op=ALU.add),
                 reads=[sg.r(), posT.r()], writes=[cl[i].r()])
            S.op("dve", lambda e, sg=sg, i=i: e.tensor_tensor(out=v3(ch[i][:]), in0=v3(sg[:]), in1=posT[:, i, 16:32].unsqueeze(1).to_broadcast([64, 128, 16]), op=ALU.add),
                 reads=[sg.r(), posT.r()], writes=[ch[i].r()])
        S.op("sp", lambda e: e.dma_start(out=vstg[:], in_=Tsb[:, T_VS:T_VS + 128].rearrange("(n p) c -> p n c", p=128)),
             reads=[c.Ts_r[b]], writes=[vstg.r()], dma=True)
        for i, dst in enumerate((vs, vw)):
            S.op("pool", lambda e, dst=dst: e.memset(dst[:, :, 64:65], 1.0), writes=[dst.r()])
            S.op("act", lambda e, dst=dst, i=i: e.copy(out=dst[:, :, 0:64], in_=vstg[:, :, i * 64:(i + 1) * 64]), reads=[vstg.r()], writes=[dst.r()])
        S.op("sp", lambda e: e.dma_start(out=w2s[:], in_=P["nsa_cmp_w2"][l].rearrange("i h e -> h i e")), writes=[w2s.r()], dma=True)
        S.op("act", lambda e: e.copy(out=w2[:], in_=w2s[:]), reads=[w2s.r()], writes=[w2.r()])
        S.op("pool", lambda e: e.memset(kcmpT[:], 0.0), writes=[kcmpT.r()])
        S.op("pool", lambda e: e.memset(vcmp[:], 0.0), writes=[vcmp.r()])
        for i in range(2):
            S.op("sp", lambda e, i=i: e.dma_start(out=w1s[:], in_=P["nsa_cmp_w1"][l, i].rearrange("(p e) h -> e p h", e=64)), writes=[w1s.r()], dma=True)
            S.op("act", lambda e, i=i: e.copy(out=w1[i][:], in_=w1s[:]), reads=[w1s.r()], writes=[w1[i].r()])
            ph = ps_s[i]

            def mm1(e, i=i, ph=ph):
                for pp in range(32):
                    src = cl[i] if pp < 16 else ch[i]
                    if pp < 16:
                        rhs = v3(src[:])[:, 0:127, pp]
                    else:
                        rhs = v3(src[:])[:, 1:128, pp - 16]
                    ins = e.matmul(ph[0:64, 0:127], lhsT=w1[i][:, pp, :], rhs=rhs, start=(pp == 0), stop=(pp == 31))
                return ins
            S.op("pe", mm1, reads=[w1[i].r(), cl[i].r(), ch[i].r()], writes=[ph.r()])
            S.op("act", lambda e, i=i, ph=ph: e.activation(out=h1T[i][:, 0:127], in_=ph[0:64, 0:127], func=AF.Gelu_apprx_tanh), reads=[ph.r()], writes=[h1T[i].r()])
        S.op("pe", lambda e: e.matmul(ps_s[0][0:64, 0:127], lhsT=w2[:, 0, :], rhs=h1T[0][:, 0:127], start=True, stop=True),
             reads=[w2.r(), h1T[0].r()], writes=[ps_s[0].r()])
        S.op("dve", lambda e: e.tensor_copy(out=kcmpT[:, 0:127], in_=ps_s[0][0:64, 0:127]), reads=[ps_s[0].r()], writes=[kcmpT.r()])
        S.op("pe", lambda e: e.matmul(ps_s[1][0:127, 0:64], lhsT=h1T[1][:, 0:127], rhs=w2[:, 1, :], start=True, stop=True),
             reads=[w2.r(), h1T[1].r()], writes=[ps_s[1].r()])
        S.op("dve", lambda e: e.tensor_copy(out=vcmp[0:127, :], in_=ps_s[1][0:127, 0:64]), reads=[ps_s[1].r()], writes=[vcmp.r()])

        S.emit()
        pst.close()
        A, TM = A0, TM0
        sbuf2 = TM("md_sbuf", [128, 4, 128], F32)
        mx = TM("md_mx", [128, 4], F32)
        sm = TM("md_sm", [128, 4], F32)
        p32 = TM("md_p32", [128, 4, 128], F32)
        pb = TM("md_pb", [128, 4, 128], BF16)
        pTc = TM("md_pTc", [128, 4, 128], BF16)
        psum_h = TM("md_psh", [128, 128], F32)
        a32 = TM("md_a32", [128, 32], F32)
        imp = TM("md_imp", [128, 32], F32)
        top8 = TM("md_top8", [128, 8], F32)
        selb = TM("md_selb", [128, 32], BF16)
        tmp = TM("md_tmp", [128, 512], F32, 3)
        maskb = TM("md_maskb", [128, NTT, 128], BF16)
        eb = TM("md_eb", [128, 512], BF16, 3)
        pm = TM("md_pm", [128, 512], BF16, 3)
        oc = TM("md_oc", [128, 256], F32, 3)
        osn = TM("md_osn", [128, 256], F32)
        own = TM("md_own", [128, 256], F32)
        rsw = TM("md_rsw", [128, 8], F32)
        gz = TM("md_gz", [128, 12 + 256], F32)
        gsg = TM("md_gsg", [128, 12], F32)
        szz = TM("md_szz", [128, 256], F32)
        yy = TM("md_yy", [128, 256], F32)
        yb = TM("md_yb", [128, 256], BF16)
        pT_y = [Tile(pmisc.t) for _ in range(2)]
        g4 = lambda ap: ap.rearrange("p (h c) -> p h c", h=4)
        bc4 = lambda ap, n: ap.unsqueeze(2).to_broadcast([128, 4, n])
        cntl = [0]

        def cmp_part(tt):
            i = tt % 2
            tsl = slice(tt * 128, (tt + 1) * 128)
            def mmc(e):
                for h in range(4):
                    ins = e.matmul(ps_c[:, h, :], lhsT=qT[:, h, tsl], rhs=kcmpT[:], start=True, stop=True)
                return ins
            S.op("pe", mmc, reads=qT_all + [kcmpT.r()], writes=[ps_c.r()])
            sb_, mx_, sm_, p32_, pb_, pTc_ = sbuf2[i], mx[i], sm[i], p32[i], pb[i], pTc[i]
            o0 = 120 - 8 * tt
            S.op("dve", lambda e: e.tensor_tensor(out=sb_[:], in0=ps_c[:], in1=wc[:, :, o0:o0 + 128], op=ALU.add), reads=[ps_c.r(), wc.r()], writes=[sb_.r()])
            S.op("dve", lambda e: e.tensor_reduce(out=mx_[:], in_=sb_[:], axis=AX.X, op=ALU.max), reads=[sb_.r()], writes=[mx_.r()])
            S.op("dve", lambda e: e.tensor_scalar(out=mx_[:], in0=mx_[:], scalar1=-1e4, scalar2=-1.0, op0=ALU.max, op1=ALU.mult), reads=[mx_.r()], writes=[mx_.r()])
            S.op("dve", lambda e: e.tensor_tensor(out=sb_[:], in0=sb_[:], in1=bc4(mx_[:], 128), op=ALU.add), reads=[sb_.r(), mx_.r()], writes=[sb_.r()])
            S.op("act", lambda e: e.activation(out=sb_[:], in_=sb_[:], func=AF.Exp), reads=[sb_.r()], writes=[sb_.r()])
            S.op("dve", lambda e: e.tensor_reduce(out=sm_[:], in_=sb_[:], axis=AX.X, op=ALU.add), reads=[sb_.r()], writes=[sm_.r()])
            S.op("dve", lambda e: e.tensor_scalar(out=sm_[:], in0=sm_[:], scalar1=1e-30, scalar2=None, op0=ALU.add), reads=[sm_.r()], writes=[sm_.r()])
            S.op("dve", lambda e: e.reciprocal(out=sm_[:], in_=sm_[:]), reads=[sm_.r()], writes=[sm_.r()])
            S.op("dve", lambda e: e.tensor_tensor(out=p32_[:], in0=sb_[:], in1=bc4(sm_[:], 128), op=ALU.mult), reads=[sb_.r(), sm_.r()], writes=[p32_.r()])
            S.op("act", lambda e: e.copy(out=pb_[:], in_=p32_[:]), reads=[p32_.r()], writes=[pb_.r()])
            def trc(e):
                for h in range(4):
                    ins = e.transpose(out=pmisc[:, h * 128:(h + 1) * 128], in_=pb_[:, h, :], identity=c.identb[:])
                return ins
            S.op("pe", trc, reads=[pb_.r()], writes=[pmisc.r()])
            S.op("act", lambda e: e.copy(out=pTc_[:].rearrange("p h n -> p (h n)"), in_=pmisc[:, 0:512]), reads=[pmisc.r()], writes=[pTc_.r()])

            def mmoc(e):
                for h in range(4):
                    ins = e.matmul(o_c[:, h, :], lhsT=pTc_[:, h, :], rhs=vcmp[:], start=True, stop=True)
                return ins
            S.op("pe", mmoc, reads=[pTc_.r(), vcmp.r()], writes=[o_c.r()])
            oc_ = oc[tt % 3]
            S.op("act", lambda e: e.copy(out=oc_[:], in_=o_c[:].rearrange("p h c -> p (h c)")), reads=[o_c.r()], writes=[oc_.r()])
            ph_, a_, imp_, t8_, selb_ = psum_h[i], a32[i], imp[i], top8[i], selb[i]
            S.op("dve", lambda e: e.tensor_reduce(out=ph_[:], in_=p32_[:].rearrange("p h n -> p n h"), axis=AX.X, op=ALU.add), reads=[p32_.r()], writes=[ph_.r()])
            pv = lambda ap: ap.rearrange("p (j m) -> p j m", m=4)
            S.op("dve", lambda e: e.tensor_reduce(out=a_[:], in_=pv(ph_[:]), axis=AX.X, op=ALU.add), reads=[ph_.r()], writes=[a_.r()])
            S.op("dve", lambda e: e.scalar_tensor_tensor(out=imp_[:], in0=pv(ph_[:])[:, :, 3], scalar=-0.5, in1=a_[:], op0=ALU.mult, op1=ALU.add),
                 reads=[ph_.r(), a_.r()], writes=[imp_.r()])
            S.op("dve", lambda e: e.scalar_tensor_tensor(out=imp_[:, 1:32], in0=pv(ph_[:])[:, 0:31, 3], scalar=0.5, in1=imp_[:, 1:32], op0=ALU.mult, op1=ALU.add),
                 reads=[ph_.r(), imp_.r()], writes=[imp_.r()])
            j0 = 30 - 2 * tt
            S.op("dve", lambda e: e.tensor_tensor(out=imp_[:], in0=imp_[:], in1=sbt[:, j0:j0 + 32], op=ALU.add), reads=[imp_.r(), sbt.r()], writes=[imp_.r()])
            S.op("dve", lambda e: e.tensor_scalar(out=imp_[:, 0:1], in0=imp_[:, 0:1], scalar1=1e4, scalar2=None, op0=ALU.add), reads=[imp_.r()], writes=[imp_.r()])
            S.op("dve", lambda e: e.max(out=t8_[:], in_=imp_[:]), reads=[imp_.r()], writes=[t8_.r()])
            S.op("dve", lambda e: e.tensor_scalar(out=selb_[:], in0=imp_[:], scalar1=t8_[:, 7:8], scalar2=None, op0=ALU.is_ge),
                 reads=[imp_.r(), t8_.r()], writes=[selb_.r()])
            S.op("pe", lambda e: e.transpose(out=pmisc[0:32, 512:640], in_=selb_[:], identity=c.identb[:]), reads=[selb_.r()], writes=[pmisc.r()])
            S.op("act", lambda e: e.copy(out=selT[:, tsl], in_=pmisc[0:32, 512:640]), reads=[pmisc.r()], writes=[selT.r(tt)])

        def att_part(tt):
            i = tt % 2
            tsl = slice(tt * 128, (tt + 1) * 128)
            oc_, osn_, own_, rs_ = oc[tt % 3], osn[i], own[i], rsw[i]
            mk_ = maskb[i]
            for k0 in range(0, tt + 1, 4):
                k1 = min(k0 + 4, tt + 1)

                def mmm(e, k0=k0, k1=k1):
                    for kb in range(k0, k1):
                        ins = e.matmul(pmask4[:, kb - k0, :], lhsT=esel[:, kb, :], rhs=selT[:, tsl], start=True, stop=True)
                    return ins
                S.op("pe", mmm, reads=[esel.r(), selT.r(tt)], writes=[pmask4.r()])
                S.op("act", lambda e, k0=k0, k1=k1: e.copy(out=mk_[:, k0:k1, :], in_=pmask4[:, 0:k1 - k0, :]), reads=[pmask4.r()], writes=[mk_.r()])
            kbs = [kb for kb in (tt - 2, tt - 1, tt) if kb >= 0]
            its = [("s", kb) for kb in range(tt + 1)] + [("w", kb) for kb in kbs]
            bufs = []
            for _ in its:
                bufs.append((cntl[0] % 3, cntl[0] % 2))
                cntl[0] += 1

            def front(k):
                kind, kb = its[k]
                j, jj = bufs[k]
                pss, tmp_, eb_, pm_ = ps_s[jj], tmp[j], eb[j], pm[j]
                ksl = slice(kb * 128, (kb + 1) * 128)
                kT = ksT if kind == "s" else kwT
                S.op("pe", lambda e: e.matmul(pss[:], lhsT=kT[:, ksl], rhs=qT[:, :, tsl], start=True, stop=True), reads=qT_all + [kT.r()], writes=[pss.r()])
                dlt = tt - kb
                btab = bw2[:] if (kind == "w" and dlt == 2) else bb[:, dlt, :]
                S.op("dve", lambda e: e.tensor_tensor(out=tmp_[:], in0=pss[:], in1=btab, op=ALU.add), reads=[pss.r(), bb.r(), bw2.r()], writes=[tmp_.r()])
                S.op("act", lambda e: e.activation(out=eb_[:], in_=tmp_[:], func=AF.Exp), reads=[tmp_.r()], writes=[eb_.r()])
                if kind == "s":
                    S.op("pool", lambda e: e.tensor_tensor(out=g4(pm_[:]), in0=g4(eb_[:]), in1=mk_[:, kb, :].unsqueeze(1).to_broadcast([128, 4, 128]), op=ALU.mult),
                         reads=[eb_.r(), mk_.r()], writes=[pm_.r()])

            def back(k):
                kind, kb = its[k]
                j, jj = bufs[k]
                eb_, pm_ = eb[j], pm[j]
                if kind == "s":
                    def mmpv(e):
                        for h in range(4):
                            ins = e.matmul(o_s[:, h, :], lhsT=pm_[:, h * 128:(h + 1) * 128], rhs=vs[:, kb, :], start=(kb == 0 and h == 0), stop=(kb == tt and h == 3), skip_group_check=True)
                        return ins
                    S.op("pe", mmpv, reads=[pm_.r(), vs.r()], writes=[o_s.r()], pe_acc=(kb > 0))
                else:
                    first, last = (kb == kbs[0]), (kb == kbs[-1])

                    def mmpw(e):
                        for h in range(4):
                            ins = e.matmul(o_w[:, h, :], lhsT=eb_[:, h * 128:(h + 1) * 128], rhs=vw[:, kb, :], start=(first and h == 0), stop=(last and h == 3), skip_group_check=True)
                        return ins
                    S.op("pe", mmpw, reads=[eb_.r(), vw.r()], writes=[o_w.r()], pe_acc=(not first))
            nit = len(its)
            front(0)
            if nit > 1:
                front(1)
            for k in range(nit):
                back(k)
                if k + 2 < nit:
                    front(k + 2)

        def att_tail(tt):
            i = tt % 2
            tsl = slice(tt * 128, (tt + 1) * 128)
            oc_, osn_, own_, rs_ = oc[tt % 3], osn[i], own[i], rsw[i]
            S.op("dve", lambda e: e.reciprocal(out=rs_[:, 0:4], in_=o_s[:, :, 64]), reads=[o_s.r()], writes=[rs_.r()])
            S.op("dve", lambda e: e.tensor_tensor(out=g4(osn_[:]), in0=o_s[:, :, 0:64], in1=bc4(rs_[:, 0:4], 64), op=ALU.mult), reads=[o_s.r(), rs_.r()], writes=[osn_.r()])
            S.op("dve", lambda e: e.reciprocal(out=rs_[:, 4:8], in_=o_w[:, :, 64]), reads=[o_w.r()], writes=[rs_.r()])
            S.op("dve", lambda e: e.tensor_tensor(out=g4(own_[:]), in0=o_w[:, :, 0:64], in1=bc4(rs_[:, 4:8], 64), op=ALU.mult), reads=[o_w.r(), rs_.r()], writes=[own_.r()])
            gz_, gs_, sz_, y_, yb_ = gz[i], gsg[i], szz[i], yy[i], yb[i]
            S.op("sp", lambda e: e.dma_start(out=gz_[:, 0:12], in_=Tsb[tsl, T_NG:T_NG + 12]), reads=[c.Ts_r[b]], writes=[gz_.r()], dma=True)
            S.op("sp", lambda e: e.dma_start(out=gz_[:, 12:268], in_=Tsb[tsl, T_NZ:T_NZ + 256]), reads=[c.Ts_r[b]], writes=[gz_.r(1)], dma=True)
            S.op("act", lambda e: e.activation(out=gs_[:], in_=gz_[:, 0:12], func=AF.Sigmoid), reads=[gz_.r()], writes=[gs_.r()])
            S.op("act", lambda e: e.activation(out=sz_[:], in_=gz_[:, 12:268], func=AF.Silu), reads=[gz_.r(1)], writes=[sz_.r()])
            gv = lambda k: gs_[:].rearrange("p (h k) -> p h k", k=3)[:, :, k].unsqueeze(2).to_broadcast([128, 4, 64])
            S.op("dve", lambda e: e.tensor_tensor(out=g4(oc_[:]), in0=g4(oc_[:]), in1=gv(0), op=ALU.mult), reads=[oc_.r(), gs_.r()], writes=[oc_.r()])
            S.op("dve", lambda e: e.tensor_tensor(out=g4(osn_[:]), in0=g4(osn_[:]), in1=gv(1), op=ALU.mult), reads=[osn_.r(), gs_.r()], writes=[osn_.r()])
            S.op("dve", lambda e: e.tensor_tensor(out=g4(own_[:]), in0=g4(own_[:]), in1=gv(2), op=ALU.mult), reads=[own_.r(), gs_.r()], writes=[own_.r()])
            S.op("dve", lambda e: e.tensor_tensor(out=oc_[:], in0=oc_[:], in1=osn_[:], op=ALU.add), reads=[oc_.r(), osn_.r()], writes=[oc_.r()])
            S.op("dve", lambda e: e.tensor_tensor(out=oc_[:], in0=oc_[:], in1=own_[:], op=ALU.add), reads=[oc_.r(), own_.r()], writes=[oc_.r()])
            S.op("dve", lambda e: e.tensor_tensor(out=y_[:], in0=oc_[:], in1=sz_[:], op=ALU.mult), reads=[oc_.r(), sz_.r()], writes=[y_.r()])
            S.op("act", lambda e: e.copy(out=yb_[:], in_=y_[:]), reads=[y_.r()], writes=[yb_.r()])
            _dbg_y(c, b, tt, 768, y_[:], y_.r())

            def tr(e):
                for jx in range(2):
                    ins = e.transpose(out=pmisc[:, 640 + jx * 128:640 + (jx + 1) * 128], in_=yb_[:, jx * 128:(jx + 1) * 128], identity=c.identb[:])
                return ins
            S.op("pe", tr, reads=[yb_.r()], writes=[pmisc.r()])
            S.op("act", lambda e: e.copy(out=c.yT[:, 6:8, tsl], in_=pmisc[:, 640:896].rearrange("p (j n) -> p j n", j=2)),
                 reads=[pmisc.r()], writes=[c.yT.r((3, tt))])

        def rec(fn, t_):
            S.defer = []
            fn(t_)
            lst, S.defer = S.defer, None
            return lst
        S.replay(rec(cmp_part, 0))
        for tt in range(NTT + 1):
            lists = []
            if tt < NTT:
                lists.append(rec(att_part, tt))
            if tt >= 1:
                lists.append(rec(att_tail, tt - 1))
            if tt + 1 < NTT:
                lists.append(rec(cmp_part, tt + 1))
            S.replay(*lists)
        S.emit()


def _dplr_loop(c, st, b, pfx, AQ, Bt, Kt, BKV, Pend, bonT, l):
    nc, S = c.nc, c.S
    P = c.p
    Tsb = c.Ts[b]
    NCH = T // 64
    NPR = NCH // 2
    A = lambda n, s, d, psum=False: _alloc(nc, st, pfx + n, s, d, psum)
    TM = lambda n, s, d, k=2, psum=False: _tmp(nc, st, pfx + n, s, d, k, psum)
    pTr = A("_pTr", [128, 1024], BF16, psum=True)
    pA = A("_pA", [64, 8, 128], F32, psum=True)
    pB = A("_pB", [64, 8, 128], F32, psum=True)
    pXU = A("_pXU", [64, 2, 256], F32, psum=True)
    pO = A("_pO", [128, 512], F32, psum=True)
    pHd = A("_pHd", [128, 2, 256], F32, psum=True)
    H = A("_H", [128, 2, 64], F32)
    Hbs = TM("_Hb", [128, 2, 64], BF16)
    Hs = A("_Hs", [128, 2, 64], F32)
    tokm = TM("_tokm", [64, 2, 6, 128], BF16)
    NY = TM("_NY", [64, 8, 128], BF16, 2)
    R = TM("_R", [64, 8, 64], BF16, 2)
    TTs = TM("_TT", [64, 8, 64], BF16)
    MQ = TM("_MQ", [64, 8, 64], BF16)
    LM = TM("_LM", [64, 8, 128], BF16)
    XS = TM("_XS", [64, 256], BF16)
    Ub = TM("_Ub", [64, 256], BF16)
    gng = A("_gng", [128, 256], F32)
    gnb = A("_gnb", [128, 256], F32)
    S.op("sp", lambda e: e.dma_start(out=gng[:], in_=P["rw_gn_gain"][l:l + 1, :].partition_broadcast(128)), writes=[gng.r()], dma=True)
    S.op("sp", lambda e: e.dma_start(out=gnb[:], in_=P["rw_gn_bias"][l:l + 1, :].partition_broadcast(128)), writes=[gnb.r()], dma=True)
    S.op("dve", lambda e: e.memset(H[:], 0.0), writes=[H.r()])
    S.op("dve", lambda e: e.memset(Hbs[0][:], 0.0), writes=[Hbs[0].r()])
    o32 = TM("_o32", [128, 256], F32)
    m4 = TM("_m4", [128, 4], F32)
    v4 = TM("_v4", [128, 4], F32)
    xc = TM("_xc", [128, 256], F32)
    t1 = TM("_t1", [128, 256], F32)
    zin = TM("_zin", [128, 256], F32)
    bon = TM("_bon", [128, 256], BF16)
    y32 = TM("_y32", [128, 256], F32)
    yb = TM("_yb", [128, 256], BF16)
    g3 = lambda ap: ap.rearrange("p (g c) -> p g c", g=4)
    bc = lambda ap: ap.unsqueeze(2).to_broadcast([128, 4, 64])
    M1 = c.m1cat
    ML = c.maskL
    I64 = c.ident[0:64, 0:64]
    bc8 = lambda ap, n: ap.unsqueeze(1).to_broadcast([64, 8, n])
    AQm = AQ
    AQr = [a.r(k) for a in AQm for k in ((0, 0), (0, 1), (1, 0), (1, 1))]
    BKVr = [BKV.r((cc, i)) for cc in range(2) for i in range(3)]

    def phase1(m):
        i2 = m % 2
        tk, mq, lm, tts = tokm[i2], MQ[i2], LM[i2], TTs[i2]
        for ci in range(2):
            n = 2 * m + ci

            def tra(e, n=n):
                for cc in range(2):
                    for it in range(3):
                        ins = e.transpose(out=pTr[0:64, (cc * 3 + it) * 128:(cc * 3 + it + 1) * 128], in_=BKV[:, cc, n, it, :], identity=c.identb[:])
                return ins
            S.op("pe", tra, reads=BKVr, writes=[pTr.r()])
            S.op("act", lambda e, ci=ci: e.copy(out=tk[:, ci, :, :].rearrange("p a b -> p (a b)"), in_=pTr[0:64, 0:768]), reads=[pTr.r()], writes=[tk.r()])

        def mmx2(e):
            for ci in range(2):
                n = 2 * m + ci
                nsl = slice(n * 64, (n + 1) * 64)
                for h in range(4):
                    cc = h // 2
                    ins = e.matmul(pB[:, ci * 4 + h, :], lhsT=Kt[:, cc, nsl], rhs=AQm[h % 2][:, cc, n, :], start=True, stop=True)
            return ins
        S.op("pe", mmx2, reads=AQr + [Kt.r(0), Kt.r(1)], writes=[pB.r()])
        S.op("dve", lambda e: e.tensor_tensor(out=lm[:], in0=pB[:], in1=bc8(M1[:], 128), op=ALU.mult), reads=[pB.r(), M1.r()], writes=[lm.r()])

        def mmx1(e):
            for ci in range(2):
                n = 2 * m + ci
                nsl = slice(n * 64, (n + 1) * 64)
                for h in range(4):
                    cc = h // 2
                    aq = AQm[h % 2]
                    e.matmul(pA[:, ci * 4 + h, :], lhsT=Bt[:, cc, nsl], rhs=aq[:, cc, n, :], start=True, stop=True)
                    ins = e.matmul(pB[:, ci * 4 + h, 0:64], lhsT=aq[:, cc, n, 0:64], rhs=Bt[:, cc, nsl], start=True, stop=True)
            return ins
        S.op("pe", mmx1, reads=AQr + [Bt.r(0), Bt.r(1)], writes=[pA.r(), pB.r()])
        ny, r_ = NY[0], R[0]
        S.op("dve", lambda e: e.tensor_tensor(out=ny[:, :, 0:64], in0=pA[:, :, 0:64], in1=bc8(M1[:, 0:64], 64), op=ALU.mult), reads=[pA.r(), M1.r()], writes=[ny.r()])
        S.op("dve", lambda e: e.tensor_tensor(out=ny[:, :, 64:128], in0=ny[:, :, 0:64], in1=bc8(I64, 64), op=ALU.add), reads=[ny.r(), c.ident.r()], writes=[ny.r()])
        S.op("dve", lambda e: e.tensor_tensor(out=mq[:], in0=pA[:, :, 64:128], in1=bc8(M1[:, 64:128], 64), op=ALU.mult), reads=[pA.r(), M1.r()], writes=[mq.r()])
        S.op("dve", lambda e: e.tensor_tensor(out=r_[:], in0=pB[:, :, 0:64], in1=bc8(ML[:], 64), op=ALU.mult), reads=[pB.r(), ML.r()], writes=[r_.r()])
        _neumann(S, NY, R, pA, pB, 8, tts)

    def phase2(m):
        i2 = m % 2
        tk, mq, lm, tts = tokm[i2], MQ[i2], LM[i2], TTs[i2]
        for ci in range(2):
            n = 2 * m + ci
            p0 = ci * 4
            xs_, ub_ = XS[ci], Ub[ci]
            Hb, Hbn = Hbs[n % 2], Hbs[(n + 1) % 2]
            S.op("dve", lambda e, n=n: e.tensor_tensor(out=Hs[:], in0=H[:], in1=Pend[:, :, n].unsqueeze(2).to_broadcast([128, 2, 64]), op=ALU.mult),
                 reads=[H.r(), Pend.r(0), Pend.r(1)], writes=[Hs.r()])

            def mmd(e, n=n, ci=ci, p0=p0, Hb=Hb):
                for h in range(4):
                    cc, hp = h // 2, h % 2
                    e.matmul(pXU[:, 0, h * 64:(h + 1) * 64], lhsT=lm[:, p0 + h, 0:64], rhs=tk[:, ci, cc * 3 + 2, hp * 64:(hp + 1) * 64], start=True, stop=False)
                    ins = e.matmul(pXU[:, 0, h * 64:(h + 1) * 64], lhsT=AQm[hp][:, cc, n, 0:64], rhs=Hb[:, cc, :], start=False, stop=True)
                return ins
            S.op("pe", mmd, reads=[lm.r(), tk.r(), Hb.r()] + AQr, writes=[pXU.r()])
            S.op("act", lambda e, xs_=xs_: e.copy(out=xs_[:], in_=pXU[:, 0, :]), reads=[pXU.r()], writes=[xs_.r()])

            def mme(e, xs_=xs_, p0=p0):
                for h in range(4):
                    ins = e.matmul(pXU[:, 1, h * 64:(h + 1) * 64], lhsT=tts[:, p0 + h, :], rhs=xs_[:, h * 64:(h + 1) * 64], start=True, stop=True)
                return ins
            S.op("pe", mme, reads=[tts.r(), xs_.r()], writes=[pXU.r()])
            S.op("dve", lambda e, ub_=ub_: e.tensor_copy(out=ub_[:], in_=pXU[:, 1, :]), reads=[pXU.r()], writes=[ub_.r()])
            half = ci * 64

            def mmg(e, ci=ci, ub_=ub_):
                for h in range(4):
                    cc, hp = h // 2, h % 2
                    hs_ = slice(hp * 64, (hp + 1) * 64)
                    o_ = pHd[hs_, cc, 0:64]
                    e.matmul(o_, lhsT=tk[:, ci, cc * 3 + 0, hs_], rhs=ub_[:, h * 64:(h + 1) * 64], start=True, stop=False)
                    ins = e.matmul(o_, lhsT=tk[:, ci, cc * 3 + 1, hs_], rhs=tk[:, ci, cc * 3 + 2, hs_], start=False, stop=True)
                return ins
            S.op("pe", mmg, reads=[tk.r(), ub_.r()], writes=[pHd.r()])
            S.op("dve", lambda e, Hbn=Hbn: e.tensor_tensor(out=Hbn[:], in0=Hs[:], in1=pHd[:, :, 0:64], op=ALU.add), reads=[Hs.r(), pHd.r()], writes=[Hbn.r()])

            def mmf(e, n=n, ci=ci, p0=p0, ub_=ub_, half=half, Hb=Hb):
                for h in range(4):
                    cc, hp = h // 2, h % 2
                    o_ = pO[half:half + 64, h * 64:(h + 1) * 64]
                    e.matmul(o_, lhsT=AQm[hp][:, cc, n, 64:128], rhs=Hb[:, cc, :], start=True, stop=False)
                    e.matmul(o_, lhsT=mq[:, p0 + h, :], rhs=ub_[:, h * 64:(h + 1) * 64], start=False, stop=False)
                    ins = e.matmul(o_, lhsT=lm[:, p0 + h, 64:128], rhs=tk[:, ci, cc * 3 + 2, hp * 64:(hp + 1) * 64], start=False, stop=True)
                return ins
            S.op("pe", mmf, reads=[lm.r(), mq.r(), tk.r(), Hb.r(), ub_.r()] + AQr, writes=[pO.r(half)])
            S.op("dve", lambda e: e.tensor_tensor(out=H[:], in0=Hs[:], in1=pHd[:, :, 0:64], op=ALU.add), reads=[Hs.r(), pHd.r()], writes=[H.r()])

    def post(m):
        tt = m
        i = tt % 2
        tsl = slice(tt * 128, (tt + 1) * 128)
        o_, m_, v_, xc_, t_, z_, bo_, y_, yb_ = o32[i], m4[i], v4[i], xc[i], t1[i], zin[i], bon[i], y32[i], yb[i]
        S.op("sp", lambda e: e.dma_start(out=z_[:], in_=Tsb[tsl, T_RZ:T_RZ + 256]), reads=[c.Ts_r[b]], writes=[z_.r()], dma=True)
        S.op("act", lambda e: e.activation(out=z_[:], in_=z_[:], func=AF.Silu), reads=[z_.r()], writes=[z_.r()])
        S.op("act", lambda e: e.copy(out=o_[:], in_=pO[:, 0:256]), reads=[pO.r(0), pO.r(64)], writes=[o_.r()])

        def trb(e):
            for cc in range(2):
                ins = e.transpose(out=pTr[:, 768 + cc * 128:768 + (cc + 1) * 128], in_=bonT[:, cc, tsl], identity=c.identb[:])
            return ins
        S.op("pe", trb, reads=[bonT.r((cc, tq)) for cc in range(2) for tq in range(4)], writes=[pTr.r()])
        S.op("act", lambda e: e.copy(out=bo_[:], in_=pTr[:, 768:1024]), reads=[pTr.r()], writes=[bo_.r()])
        S.op("dve", lambda e: e.tensor_reduce(out=m_[:], in_=g3(o_[:]), axis=AX.X, op=ALU.add), reads=[o_.r()], writes=[m_.r()])
        S.op("dve", lambda e: e.scalar_tensor_tensor(out=g3(xc_[:]), in0=bc(m_[:]), scalar=-1.0 / 64, in1=g3(o_[:]), op0=ALU.mult, op1=ALU.add),
             reads=[o_.r(), m_.r()], writes=[xc_.r()])
        S.op("dve", lambda e: e.tensor_tensor(out=t_[:], in0=xc_[:], in1=xc_[:], op=ALU.mult), reads=[xc_.r()], writes=[t_.r()])
        S.op("dve", lambda e: e.tensor_reduce(out=v_[:], in_=g3(t_[:]), axis=AX.X, op=ALU.add), reads=[t_.r()], writes=[v_.r()])
        S.op("act", lambda e: e.activation(out=v_[:], in_=v_[:], func=AF.Sqrt, scale=1.0 / 64, bias=c.epsgn[:, 0:1]), reads=[v_.r()], writes=[v_.r()])
        S.op("dve", lambda e: e.reciprocal(out=v_[:], in_=v_[:]), reads=[v_.r()], writes=[v_.r()])
        S.op("dve", lambda e: e.tensor_tensor(out=g3(t_[:]), in0=g3(xc_[:]), in1=bc(v_[:]), op=ALU.mult), reads=[xc_.r(), v_.r()], writes=[t_.r()])
        S.op("dve", lambda e: e.tensor_tensor(out=t_[:], in0=t_[:], in1=gng[:], op=ALU.mult), reads=[t_.r(), gng.r()], writes=[t_.r()])
        S.op("dve", lambda e: e.tensor_tensor(out=t_[:], in0=t_[:], in1=gnb[:], op=ALU.add), reads=[t_.r(), gnb.r()], writes=[t_.r()])
        S.op("dve", lambda e: e.tensor_tensor(out=t_[:], in0=t_[:], in1=bo_[:], op=ALU.add), reads=[t_.r(), bo_.r()], writes=[t_.r()])
        S.op("dve", lambda e: e.tensor_tensor(out=y_[:], in0=t_[:], in1=z_[:], op=ALU.mult), reads=[t_.r(), z_.r()], writes=[y_.r()])
        S.op("act", lambda e: e.copy(out=yb_[:], in_=y_[:]), reads=[y_.r()], writes=[yb_.r()])
        _dbg_y(c, b, tt, 512, y_[:], y_.r())

        def tr(e):
            for jx in range(2):
                ins = e.transpose(out=pTr[:, 768 + jx * 128:768 + (jx + 1) * 128], in_=yb_[:, jx * 128:(jx + 1) * 128], identity=c.identb[:])
            return ins
        S.op("pe", tr, reads=[yb_.r()], writes=[pTr.r()])
        S.op("act", lambda e: e.copy(out=c.yT[:, 4:6, tsl], in_=pTr[:, 768:1024].rearrange("p (j n) -> p j n", j=2)),
             reads=[pTr.r()], writes=[c.yT.r((2, tt))])

    def rec(fn, m):
        S.defer = []
        fn(m)
        lst, S.defer = S.defer, None
        return lst

    S.replay(rec(phase1, 0))
    for m in range(NPR + 1):
        lists = []
        if m < NPR:
            lists.append(rec(phase2, m))
        if m >= 1:
            lists.append(rec(post, m - 1))
        if m + 1 < NPR:
            lists.append(rec(phase1, m + 1))
        S.replay(*lists)


def mixerC(c, l, b):
    nc, S = c.nc, c.S
    P = c.p
    Fsb, Tsb = c.Fs[b], c.Ts[b]
    NCH = T // 64
    with ExitStack() as st:
        A = lambda n, s, d, psum=False: _alloc(nc, st, n, s, d, psum)
        TM = lambda n, s, d, k=2, psum=False: _tmp(nc, st, n, s, d, k, psum)
        AQ0 = A("mc_AQ0", [128, 2, NCH, 128], BF16)
        AQ1 = A("mc_AQ1", [128, 2, NCH, 128], BF16)
        AQ = AQ0
        Bt = A("mc_Bt", [128, 2, T], BF16)
        Kt = A("mc_Kt", [128, 2, T], BF16)
        BKV = A("mc_BKV", [128, 2, NCH, 3, 64], BF16)
        bonT = A("mc_bonT", [128, 2, T], BF16)
        Pend = A("mc_Pend", [128, 2, NCH], F32)
        cols = A("mc_cols", [128, 2, 5], F32)
        S.op("sp", lambda e: e.dma_start(out=cols[:], in_=P["rw_cols"][l]), writes=[cols.r()], dma=True)
        pst = ExitStack()
        A1 = lambda n, s, d, psum=False: _alloc(nc, pst, n, s, d, psum)
        muT = A1("mc_muT", [128, 7], F32)
        lora = A1("mc_lora", [128, 256], F32)
        rst = A1("mc_rst", [128, T], BF16)
        wdad = A1("mc_wdad", [128, T], F32)
        t_lw = A1("mc_tlw", [128, T], F32)
        t_a = A1("mc_ta", [128, T], F32)
        t_gc = A1("mc_tgc", [128, T], F32)
        t0 = A1("mc_t0", [128, T], F32)
        t1 = A1("mc_t1", [128, T], F32)
        t_x = A1("mc_tx", [128, T], F32)
        xr = t1
        pp = [A1(f"mc_pp{i}", [128, 512], F32, psum=True) for i in range(4)]
        S.op("sp", lambda e: e.dma_start(out=muT[:], in_=P["rw_muT"][l]), writes=[muT.r()], dma=True)
        S.op("sp", lambda e: e.dma_start(out=lora[:], in_=P["rw_lora"][l]), writes=[lora.r()], dma=True)
        S.op("pool", lambda e: e.memset(rst[:], 1.0), writes=[rst.r()])
        S.op("pool", lambda e: e.memset(rst[:].rearrange("p (n s) -> p n s", s=64)[:, :, 0:1], 0.0), writes=[rst.r()])
        c3 = lambda ap: ap.rearrange("p (n s) -> p n s", s=64)

        def load_shift(dst, row, mcol, tmp):
            S.op("sp", lambda e: e.dma_start(out=dst[:], in_=Fsb[row:row + 128, :]), reads=[c.Fs_r[b]], writes=[dst.r()], dma=True)
            S.op("dve", lambda e: e.tensor_tensor(out=tmp[:, 1:T], in0=dst[:, 0:T - 1], in1=dst[:, 1:T], op=ALU.subtract), reads=[dst.r()], writes=[tmp.r()])
            S.op("dve", lambda e: e.tensor_scalar(out=tmp[:, 0:1], in0=dst[:, 0:1], scalar1=-1.0, scalar2=None, op0=ALU.mult), reads=[dst.r()], writes=[tmp.r()])
            S.op("dve", lambda e: e.scalar_tensor_tensor(out=dst[:], in0=tmp[:], scalar=muT[:, mcol:mcol + 1], in1=dst[:], op0=ALU.mult, op1=ALU.add),
                 reads=[tmp.r(), dst.r(), muT.r()], writes=[dst.r()])

        load_shift(wdad, F_RW + 768, 6, t0)
        S.op("act", lambda e: e.activation(out=wdad[0:64, :], in_=wdad[0:64, :], func=AF.Tanh), reads=[wdad.r()], writes=[wdad.r()])
        def prep_cc(cc):
            csl = slice(cc * 128, (cc + 1) * 128)
            for tq in range(4):
                qs = slice(tq * 512, (tq + 1) * 512)
                S.op("pe", lambda e, tq=tq, qs=qs: e.matmul(pp[tq][:], lhsT=lora[0:64, csl], rhs=wdad[0:64, qs], start=True, stop=True),
                     reads=[lora.r(), wdad.r()], writes=[pp[tq].r()])
                S.op("act", lambda e, tq=tq, qs=qs: e.activation(out=t_lw[:, qs], in_=pp[tq][:], func=AF.Sigmoid, bias=cols[:, cc, 0:1]),
                     reads=[pp[tq].r(), cols.r()], writes=[t_lw.r()])
            S.op("dve", lambda e: e.tensor_scalar(out=t_lw[:], in0=t_lw[:], scalar1=-0.6065306597126334, scalar2=None, op0=ALU.mult),
                 reads=[t_lw.r()], writes=[t_lw.r()])
            for tq in range(4):
                qs = slice(tq * 512, (tq + 1) * 512)
                S.op("pe", lambda e, tq=tq, qs=qs: e.matmul(pp[tq][:], lhsT=lora[64:128, csl], rhs=wdad[64:128, qs], start=True, stop=True),
                     reads=[lora.r(), wdad.r()], writes=[pp[tq].r()])
                S.op("act", lambda e, tq=tq, qs=qs: e.activation(out=t_a[:, qs], in_=pp[tq][:], func=AF.Sigmoid, bias=cols[:, cc, 1:2]),
                     reads=[pp[tq].r(), cols.r()], writes=[t_a.r()])
            ta_all = [t_a.r()]
            S.op("dve", lambda e: e.tensor_tensor_scan(out=t_gc[:], data0=rst[:], data1=t_lw[:], initial=0.0, op0=ALU.mult, op1=ALU.add),
                 reads=[rst.r(), t_lw.r()], writes=[t_gc.r()])
            S.op("dve", lambda e: e.tensor_tensor(out=t_lw[:], in0=t_gc[:], in1=t_lw[:], op=ALU.subtract), reads=[t_gc.r(), t_lw.r()], writes=[t_lw.r()])
            xk = t_x
            load_shift(xk, F_RW + 256 + cc * 128, 2 + cc, t0)
            S.op("dve", lambda e: e.tensor_scalar(out=t0[:], in0=xk[:], scalar1=cols[:, cc, 2:3], scalar2=None, op0=ALU.mult), reads=[xk.r(), cols.r()], writes=[t0.r()])
            S.op("dve", lambda e: e.tensor_tensor(out=t1[:], in0=t0[:], in1=t0[:], op=ALU.mult), reads=[t0.r()], writes=[t1.r()])
            for tq in range(4):
                qs = slice(tq * 512, (tq + 1) * 512)
                S.op("pe", lambda e, tq=tq, qs=qs: e.matmul(pp[tq][:], lhsT=c.blk[:], rhs=t1[:, qs], start=True, stop=True),
                     reads=[c.blk.r(), t1.r()], writes=[pp[tq].r()])
            for tq in range(4):
                qs = slice(tq * 512, (tq + 1) * 512)
                S.op("act", lambda e, tq=tq, qs=qs: e.activation(out=t1[:, qs], in_=pp[tq][:], func=AF.Sqrt, bias=c.eps6[:, 0:1]),
                     reads=[pp[tq].r()] + [pp[q].r() for q in range(4)], writes=[t1.r()])
            S.op("dve", lambda e: e.reciprocal(out=t1[:], in_=t1[:]), reads=[t1.r()], writes=[t1.r()])
            S.op("dve", lambda e: e.tensor_tensor(out=t0[:], in0=t0[:], in1=t1[:], op=ALU.mult), reads=[t0.r(), t1.r()], writes=[t0.r()])
            S.op("dve", lambda e: e.tensor_tensor(out=t1[:], in0=t0[:], in1=t_a[:], op=ALU.mult), reads=[t0.r()] + ta_all, writes=[t1.r()])
            S.op("dve", lambda e: e.tensor_scalar(out=t_a[:], in0=t_a[:], scalar1=-1.0, scalar2=cols[:, cc, 3:4], op0=ALU.add, op1=ALU.mult),
                 reads=ta_all + [t1.r(), cols.r()], writes=[t_a.r()])
            S.op("dve", lambda e: e.tensor_tensor(out=t_a[:], in0=t_a[:], in1=xk[:], op=ALU.mult), reads=[t_a.r(), xk.r()], writes=[t_a.r()])
            S.op("dve", lambda e: e.tensor_tensor(out=t_a[:], in0=t_a[:], in1=xk[:], op=ALU.add), reads=[t_a.r(), xk.r()], writes=[t_a.r()])
            S.op("act", lambda e: e.activation(out=t_x[:], in_=t_lw[:], func=AF.Exp), reads=[t_lw.r(), t_a.r()], writes=[t_x.r()])
            S.op("dve", lambda e: e.scalar_tensor_tensor(out=AQ[:, cc, :, 0:64], in0=c3(t0[:]), scalar=-1.0, in1=c3(t_x[:]), op0=ALU.mult, op1=ALU.mult),
                 reads=[t0.r(), t_x.r()], writes=[AQ.r((cc, 0))])
            S.op("dve", lambda e: e.tensor_scalar(out=AQ1[:, cc, :, 0:64], in0=AQ0[:, cc, :, 0:64], scalar1=c.hmask[:, 1:2], scalar2=None, op0=ALU.mult),
                 reads=[AQ.r((cc, 0)), c.hmask.r()], writes=[AQ1.r((cc, 0))])
            S.op("dve", lambda e: e.tensor_scalar(out=AQ0[:, cc, :, 0:64], in0=AQ0[:, cc, :, 0:64], scalar1=c.hmask[:, 0:1], scalar2=None, op0=ALU.mult),
                 reads=[AQ.r((cc, 0)), AQ1.r((cc, 0)), c.hmask.r()], writes=[AQ.r((cc, 0))])
            S.op("act", lambda e: e.activation(out=t_x[:], in_=t_gc[:], func=AF.Exp, scale=-1.0), reads=[t_gc.r(), AQ.r((cc, 0))], writes=[t_x.r()])
            S.op("dve", lambda e: e.tensor_tensor(out=Bt[:, cc, :], in0=t1[:], in1=t_x[:], op=ALU.mult), reads=[t1.r(), t_x.r()], writes=[Bt.r(cc)])
            S.op("dve", lambda e: e.tensor_tensor(out=Kt[:, cc, :], in0=t_a[:], in1=t_x[:], op=ALU.mult), reads=[t_a.r(), t_x.r()], writes=[Kt.r(cc)])
            S.op("dve", lambda e: e.tensor_tensor(out=c3(t_x[:]), in0=c3(t_gc[:])[:, :, 63:64].to_broadcast([128, NCH, 64]), in1=c3(t_gc[:]), op=ALU.subtract),
                 reads=[t_gc.r(), Bt.r(cc), Kt.r(cc)], writes=[t_x.r()])
            S.op("act", lambda e: e.activation(out=t_x[:], in_=t_x[:], func=AF.Exp), reads=[t_x.r()], writes=[t_x.r()])
            S.op("dve", lambda e: e.tensor_tensor(out=BKV[:, cc, :, 0, :], in0=c3(t1[:]), in1=c3(t_x[:]), op=ALU.mult), reads=[t1.r(), t_x.r()], writes=[BKV.r((cc, 0))])
            S.op("dve", lambda e: e.tensor_tensor(out=BKV[:, cc, :, 1, :], in0=c3(t_a[:]), in1=c3(t_x[:]), op=ALU.mult), reads=[t_a.r(), t_x.r()], writes=[BKV.r((cc, 1))])
            S.op("act", lambda e: e.activation(out=Pend[:, cc, :], in_=c3(t_gc[:])[:, :, 63], func=AF.Exp), reads=[t_gc.r()], writes=[Pend.r(cc)])
            S.op("act", lambda e: e.activation(out=t_x[:], in_=t_gc[:], func=AF.Exp), reads=[t_gc.r(), BKV.r((cc, 0)), BKV.r((cc, 1))], writes=[t_x.r()])
            load_shift(xr, F_RW + cc * 128, cc, t0)
            S.op("dve", lambda e: e.tensor_tensor(out=AQ[:, cc, :, 64:128], in0=c3(xr[:]), in1=c3(t_x[:]), op=ALU.mult), reads=[xr.r(), t_x.r()], writes=[AQ.r((cc, 1))])
            S.op("dve", lambda e: e.tensor_scalar(out=AQ1[:, cc, :, 64:128], in0=AQ0[:, cc, :, 64:128], scalar1=c.hmask[:, 1:2], scalar2=None, op0=ALU.mult),
                 reads=[AQ.r((cc, 1)), c.hmask.r()], writes=[AQ1.r((cc, 1))])
            S.op("dve", lambda e: e.tensor_scalar(out=AQ0[:, cc, :, 64:128], in0=AQ0[:, cc, :, 64:128], scalar1=c.hmask[:, 0:1], scalar2=None, op0=ALU.mult),
                 reads=[AQ.r((cc, 1)), AQ1.r((cc, 1)), c.hmask.r()], writes=[AQ.r((cc, 1))])
            S.op("dve", lambda e: e.scalar_tensor_tensor(out=t0[:], in0=xr[:], scalar=cols[:, cc, 4:5], in1=t_a[:], op0=ALU.mult, op1=ALU.mult),
                 reads=[xr.r(), t_a.r(), cols.r()], writes=[t0.r()])
            for tq in range(4):
                qs = slice(tq * 512, (tq + 1) * 512)
                S.op("pe", lambda e, tq=tq, qs=qs: e.matmul(pp[tq][:], lhsT=c.blk[:], rhs=t0[:, qs], start=True, stop=True),
                     reads=[c.blk.r(), t0.r()], writes=[pp[tq].r()])
            xv = t_x
            load_shift(xv, F_RW + 512 + cc * 128, 4 + cc, t1)
            for tq in range(4):
                qs = slice(tq * 512, (tq + 1) * 512)
                S.op("dve", lambda e, tq=tq, qs=qs: e.tensor_tensor(out=bonT[:, cc, qs], in0=pp[tq][:], in1=xv[:, qs], op=ALU.mult),
                     reads=[pp[tq].r(), xv.r()], writes=[bonT.r((cc, tq))])
            S.op("act", lambda e: e.copy(out=BKV[:, cc, :, 2, :], in_=c3(xv[:])), reads=[xv.r()], writes=[BKV.r((cc, 2))])
        for cc in range(2):
            prep_cc(cc)
        S.emit()
        pst.close()

        _dplr_loop(c, st, b, "mc", (AQ0, AQ1), Bt, Kt, BKV, Pend, bonT, l)
        S.emit()


def mixerA(c, l, b):
    nc, S = c.nc, c.S
    P = c.p
    Fsb, Tsb = c.Fs[b], c.Ts[b]
    NCH = T // 64
    with ExitStack() as st:
        A = lambda n, s, d, psum=False: _alloc(nc, st, "ma_" + n, s, d, psum)
        TM = lambda n, s, d, k=2, psum=False: _tmp(nc, st, "ma_" + n, s, d, k, psum)
        knT = A("knT", [128, 2, T], BF16)
        vT = A("vT", [128, 2, T], BF16)
        KQ0 = A("KQ0", [128, 2, NCH, 128], BF16)
        KQ1 = A("KQ1", [128, 2, NCH, 128], BF16)
        KQm = (KQ0, KQ1)
        szA = A("sz", [64, NCH, 256], F32)
        gA = A("g", [64, NCH, 4], F32)
        bA = A("b", [64, NCH, 4], F32)
        gcA = A("gc", [64, NCH, 4], F32)
        c3 = lambda ap: ap.rearrange("p (n s) -> p n s", s=64)
        pst = ExitStack()
        A1 = lambda n, s, d, psum=False: _alloc(nc, pst, "ma_" + n, s, d, psum)
        convT = A1("convT", [128, 6, 4], F32)
        xs2 = [A1(f"x{i}", [128, T], F32) for i in range(2)]
        accs2 = [A1(f"acc{i}", [128, T], F32) for i in range(2)]
        sqs2 = [A1(f"sq{i}", [128, T], BF16) for i in range(2)]
        blkb = A1("blkb", [128, 128], BF16)
        S.op("dve", lambda e: e.tensor_copy(out=blkb[:], in_=c.blk[:]), reads=[c.blk.r()], writes=[blkb.r()])
        ab = A1("ab", [64, NCH, 8], F32)
        dtb = A1("dtb", [64, 4], F32)
        nA = A1("nA", [64, 4], F32)
        pp = [A1(f"pp{i}", [128, 512], F32, psum=True) for i in range(4)]
        S.defer = []
        S.op("sp", lambda e: e.dma_start(out=convT[:], in_=P["gdn_convT"][l]), writes=[convT.r()], dma=True)
        S.op("sp", lambda e: e.dma_start(out=dtb[:], in_=P["gdn_dt_bias"][l:l + 1, :].partition_broadcast(64)), writes=[dtb.r()], dma=True)
        S.op("sp", lambda e: e.dma_start(out=nA[:], in_=P["gdn_a_log"][l:l + 1, :].partition_broadcast(64)), writes=[nA.r()], dma=True)
        S.op("sp", lambda e: e.dma_start(out=ab[:], in_=Tsb[:, T_GA:T_GA + 8].rearrange("(n p) c -> p n c", p=64)), reads=[c.Ts_r[b]], writes=[ab.r()], dma=True)
        for q4 in range(4):
            S.op("sp", lambda e, q4=q4: e.dma_start(out=szA[:, q4 * 8:(q4 + 1) * 8, :], in_=Tsb[q4 * 512:(q4 + 1) * 512, T_GZ:T_GZ + 256].rearrange("(n p) c -> p n c", p=64)),
                 reads=[c.Ts_r[b]], writes=[szA.r()], dma=True)
        S.op("act", lambda e: e.activation(out=szA[:], in_=szA[:], func=AF.Silu), reads=[szA.r()], writes=[szA.r()])
        S.op("act", lambda e: e.activation(out=nA[:], in_=nA[:], func=AF.Exp), reads=[nA.r()], writes=[nA.r()])
        S.op("dve", lambda e: e.tensor_tensor(out=gA[:], in0=ab[:, :, 0:4], in1=dtb[:].unsqueeze(1).to_broadcast([64, NCH, 4]), op=ALU.add), reads=[ab.r(), dtb.r()], writes=[gA.r()])
        S.op("act", lambda e: e.activation(out=gA[:], in_=gA[:], func=AF.Exp), reads=[gA.r()], writes=[gA.r()])
        S.op("act", lambda e: e.activation(out=gA[:], in_=gA[:], func=AF.Ln, bias=c.one[0:64, 0:1]), reads=[gA.r()], writes=[gA.r()])
        S.op("dve", lambda e: e.scalar_tensor_tensor(out=gA[:], in0=gA[:], scalar=-1.0, in1=nA[:].unsqueeze(1).to_broadcast([64, NCH, 4]), op0=ALU.mult, op1=ALU.mult),
             reads=[gA.r(), nA.r()], writes=[gA.r()])
        S.op("act", lambda e: e.activation(out=bA[:], in_=ab[:, :, 4:8], func=AF.Exp, scale=-1.0), reads=[ab.r()], writes=[bA.r()])
        S.op("dve", lambda e: e.tensor_scalar(out=bA[:], in0=bA[:], scalar1=1.0, scalar2=None, op0=ALU.add), reads=[bA.r()], writes=[bA.r()])
        S.op("dve", lambda e: e.reciprocal(out=bA[:], in_=bA[:]), reads=[bA.r()], writes=[bA.r()])
        S.op("pe", lambda e: e.matmul(pp[0][0:64, 0:NCH * 4], lhsT=c.m1cat[:, 64:128], rhs=gA[:].rearrange("p n h -> p (n h)"), start=True, stop=True),
             reads=[gA.r(), c.m1cat.r()], writes=[pp[0].r()])
        S.op("dve", lambda e: e.tensor_copy(out=gcA[:].rearrange("p n h -> p (n h)"), in_=pp[0][0:64, 0:NCH * 4]), reads=[pp[0].r()], writes=[gcA.r()])

        def conv_tile(ti, x, acc):
            S.op("sp", lambda e: e.dma_start(out=x[:], in_=Fsb[F_GQ + ti * 128:F_GQ + (ti + 1) * 128, :]), reads=[c.Fs_r[b]], writes=[x.r()], dma=True)
            S.op("dve", lambda e: e.tensor_scalar(out=acc[:], in0=x[:], scalar1=convT[:, ti, 3:4], scalar2=None, op0=ALU.mult), reads=[x.r(), convT.r()], writes=[acc.r()])
            for sh in (1, 2, 3):
                S.op("dve", lambda e, sh=sh: e.scalar_tensor_tensor(out=acc[:, sh:T], in0=x[:, 0:T - sh], scalar=convT[:, ti, 3 - sh:4 - sh], in1=acc[:, sh:T], op0=ALU.mult, op1=ALU.add),
                     reads=[x.r(), acc.r(), convT.r()], writes=[acc.r()])
            S.op("act", lambda e: e.activation(out=x[:], in_=acc[:], func=AF.Silu), reads=[acc.r()], writes=[x.r()])

        def l2n(scale, x, sq, pp, rt):
            S.op("dve", lambda e: e.tensor_tensor(out=sq[:], in0=x[:], in1=x[:], op=ALU.mult), reads=[x.r()], writes=[sq.r()])
            for tq in range(4):
                qs = slice(tq * 512, (tq + 1) * 512)
                S.op("pe", lambda e, tq=tq, qs=qs: e.matmul(pp[tq % 2][:], lhsT=blkb[:], rhs=sq[:, qs], start=True, stop=True), reads=[blkb.r(), sq.r()], writes=[pp[tq % 2].r()])
                S.op("act", lambda e, tq=tq, qs=qs: e.activation(out=rt[:, qs], in_=pp[tq % 2][:], func=AF.Sqrt, bias=c.eps6[:, 0:1]), reads=[pp[tq % 2].r()], writes=[rt.r()])
            S.op("dve", lambda e: e.reciprocal(out=rt[:], in_=rt[:]), reads=[rt.r()], writes=[rt.r()])
            S.op("dve", lambda e: e.scalar_tensor_tensor(out=x[:], in0=x[:], scalar=scale, in1=rt[:], op0=ALU.mult, op1=ALU.mult), reads=[x.r(), rt.r()], writes=[x.r()])

        def prep_cc(cc):
            x, acc, sq, pp2 = xs2[cc], accs2[cc], sqs2[cc], pp[2 * cc:2 * cc + 2]
            conv_tile(2 + cc, x, acc)
            l2n(1.0, x, sq, pp2, acc)
            S.op("act", lambda e: e.copy(out=knT[:, cc, :], in_=x[:]), reads=[x.r()], writes=[knT.r(cc)])
            for hp in range(2):
                S.op("dve", lambda e, hp=hp: e.tensor_scalar(out=KQm[hp][:, cc, :, 0:64], in0=c3(x[:]), scalar1=c.hmask[:, hp:hp + 1], scalar2=None, op0=ALU.mult),
                     reads=[x.r(), c.hmask.r()], writes=[KQm[hp].r((cc, 0))])
            conv_tile(cc, x, acc)
            l2n(0.125, x, sq, pp2, acc)
            for hp in range(2):
                S.op("dve", lambda e, hp=hp: e.tensor_scalar(out=KQm[hp][:, cc, :, 64:128], in0=c3(x[:]), scalar1=c.hmask[:, hp:hp + 1], scalar2=None, op0=ALU.mult),
                     reads=[x.r(), c.hmask.r()], writes=[KQm[hp].r((cc, 1))])
            conv_tile(4 + cc, x, acc)
            S.op("act", lambda e: e.copy(out=vT[:, cc, :], in_=x[:]), reads=[x.r()], writes=[vT.r(cc)])
        lst0, S.defer = S.defer, []
        prep_cc(0)
        lstc0, S.defer = S.defer, []
        prep_cc(1)
        lstc1, S.defer = S.defer, None
        S.replay(lst0)
        lists = [lstc0, lstc1]
        if c.merge_B:
            S.defer = []
            mixerB(c, l, b, ext_st=pst)
            lstB, S.defer = S.defer, None
            lists.append(lstB)
        S.replay(*lists)
        S.emit()
        pst.close()

        NPR = NCH // 2
        pTr = A("pTr", [128, 1024], BF16, psum=True)
        pX1 = A("pX1", [64, 8, 128], F32, psum=True)
        pGN = A("pGN", [128, 2, 512], F32, psum=True)

        class _NBView:
            def r(self, key=0):
                return pGN.r()

            def __getitem__(self, idx):
                p, h, cs = idx
                if isinstance(h, slice):
                    return pGN.t[0:64, 1, :].rearrange("p (q i) -> p q i", i=64)
                return pGN.t[0:64, 1, h * 64:(h + 1) * 64]
        pNB = _NBView()
        pXab = A("pXab", [64, 2, 256], F32, psum=True)
        pO = A("pO", [64, 2, 256], F32, psum=True)
        pHU = A("pHU", [128, 512], F32, psum=True)
        H = A("H", [128, 2, 64], F32)
        Hbs = TM("Hb", [128, 2, 64], BF16)
        Hs = A("Hs", [128, 2, 64], F32)
        ones = A("ones", [64, 128], F32)
        nw = A("nw", [64, 64], F32)
        S.op("sp", lambda e: e.dma_start(out=nw[:], in_=P["gdn_norm"][l:l + 1, :].partition_broadcast(64)), writes=[nw.r()], dma=True)
        S.op("dve", lambda e: e.memset(ones[:], 1.0), writes=[ones.r()])
        S.op("dve", lambda e: e.memset(H[:], 0.0), writes=[H.r()])
        S.op("dve", lambda e: e.memset(Hbs[0][:], 0.0), writes=[Hbs[0].r()])
        tokm = TM("tokm", [64, 2, 4, 128], BF16)
        MQ = TM("MQ", [64, 8, 64], BF16)
        NL = TM("NL", [64, 8, 64], BF16)
        TTp = TM("TTp", [64, 8, 64], BF16)
        scp = TM("scp", [64, 2, 8], F32)
        PCr = TM("PCr", [64, 2, 4], F32)
        PeT = TM("PeT", [128, 2, 2], F32)
        Vb = TM("Vb", [64, 8, 64], F32)
        Vbb = TM("Vbb", [64, 8, 64], BF16)
        R2 = A("R2", [64, 2, 8, 64], F32)
        Dm = A("Dm", [64, 8, 64], F32)
        E1 = A("E1", [64, 8, 64], F32)
        E2 = A("E2", [64, 8, 64], F32)
        G1 = A("G1", [64, 8, 64], F32)
        G2 = A("G2", [64, 8, 64], F32)
        GL = A("GL", [64, 8, 64], F32)
        NY = TM("NY", [64, 8, 128], BF16, 2)
        R = TM("R", [64, 8, 64], BF16, 2)
        XS = TM("XS", [64, 256], BF16)
        XSf = TM("XSf", [64, 256], F32)
        Wb = TM("Wb", [64, 256], BF16)
        Wp = TM("Wp", [64, 256], BF16)
        o32 = TM("o32", [64, 256], F32, 4)
        t1 = TM("t1", [64, 256], F32)
        s4 = TM("s4", [64, 4], F32)
        y32 = TM("y32", [64, 256], F32)
        yb = TM("yb", [64, 256], BF16)
        M1 = c.m1cat
        ML = c.maskL
        I64 = c.ident[0:64, 0:64]
        b4 = lambda ap, n: ap.unsqueeze(1).to_broadcast([64, 4, n])
        b8 = lambda ap, n: ap.unsqueeze(1).to_broadcast([64, 8, n])
        s4b = lambda ap, n: ap.unsqueeze(2).to_broadcast([64, 4, n])
        g4 = lambda ap: ap.rearrange("p (h c) -> p h c", h=4)
        KQr = [a.r(k) for a in KQm for k in ((0, 0), (0, 1), (1, 0), (1, 1))]

        def phase1(m):
            i2 = m % 2
            tk, mq, nl, tts, sc_, pcr, pe_, vb, vbb = tokm[i2], MQ[i2], NL[i2], TTp[i2], scp[i2], PCr[i2], PeT[i2], Vb[i2], Vbb[i2]
            ny, r_ = NY[0], R[0]
            n0 = 2 * m
            r2, dm, e1, e2, g1, g2, gl = R2, Dm, E1, E2, G1, G2, GL
            f8 = lambda t_: t_[:, n0:n0 + 2, :].rearrange("p n h -> p (n h)")
            s8b = lambda ap, n: ap.unsqueeze(2).to_broadcast([64, 8, n])
            for ci in range(2):
                nsl = slice((n0 + ci) * 64, (n0 + ci + 1) * 64)

                def tra(e, nsl=nsl):
                    for cc in range(2):
                        e.transpose(out=pTr[0:64, cc * 128:(cc + 1) * 128], in_=knT[:, cc, nsl], identity=c.identb[:])
                        ins = e.transpose(out=pTr[0:64, (2 + cc) * 128:(3 + cc) * 128], in_=vT[:, cc, nsl], identity=c.identb[:])
                    return ins
                S.op("pe", tra, reads=[knT.r(0), knT.r(1), vT.r(0), vT.r(1)], writes=[pTr.r()])
                S.op("act", lambda e, ci=ci: e.copy(out=tk[:, ci, :, :].rearrange("p a b -> p (a b)"), in_=pTr[0:64, 0:512]), reads=[pTr.r()], writes=[tk.r()])
            S.op("dve", lambda e: e.tensor_tensor(out=r2[:, 0, :, :], in0=s8b(f8(gA), 64), in1=b8(M1[:, 64:128], 64), op=ALU.mult), reads=[gA.r(), M1.r()], writes=[r2.r()])
            S.op("dve", lambda e: e.tensor_tensor(out=r2[:, 1, :, :], in0=s8b(f8(bA), 64), in1=b8(I64, 64), op=ALU.mult), reads=[bA.r(), c.ident.r()], writes=[r2.r()])

            def mmgb(e):
                e.matmul(pGN[:, 0, :], lhsT=ones[:], rhs=r2[:, 0, :, :].rearrange("p q i -> p (q i)"), start=True, stop=True)
                return e.matmul(pGN[:, 1, :], lhsT=ones[:], rhs=r2[:, 1, :, :].rearrange("p q i -> p (q i)"), start=True, stop=True)
            S.op("pe", mmgb, reads=[ones.r(), r2.r()], writes=[pGN.r()])
            GR = lambda: pGN[0:64, 0, :].rearrange("p (q i) -> p q i", i=64)
            BR = lambda: pGN[0:64, 1, :].rearrange("p (q i) -> p q i", i=64)
            S.op("dve", lambda e: e.tensor_tensor(out=dm[:], in0=GR(), in1=s8b(f8(gcA), 64), op=ALU.subtract), reads=[pGN.r(), gcA.r()], writes=[dm.r()])
            S.op("dve", lambda e: e.tensor_scalar(out=e1[:], in0=dm[:], scalar1=0.0, scalar2=None, op0=ALU.min), reads=[dm.r()], writes=[e1.r()])
            S.op("dve", lambda e: e.tensor_scalar(out=e2[:], in0=dm[:], scalar1=-1.0, scalar2=0.0, op0=ALU.mult, op1=ALU.min), reads=[dm.r()], writes=[e2.r()])
            S.op("act", lambda e: e.activation(out=e1[:], in_=e1[:], func=AF.Exp), reads=[e1.r()], writes=[e1.r()])
            S.op("act", lambda e: e.activation(out=e2[:], in_=e2[:], func=AF.Exp), reads=[e2.r()], writes=[e2.r()])
            S.op("dve", lambda e: e.tensor_tensor(out=g2[:], in0=e1[:], in1=b8(M1[:, 64:128], 64), op=ALU.mult), reads=[e1.r(), M1.r()], writes=[g2.r()])
            S.op("dve", lambda e: e.tensor_tensor(out=g1[:], in0=e1[:], in1=b8(M1[:, 0:64], 64), op=ALU.mult), reads=[e1.r(), M1.r()], writes=[g1.r()])
            S.op("dve", lambda e: e.scalar_tensor_tensor(out=g1[:], in0=g1[:], scalar=-1.0, in1=BR(), op0=ALU.mult, op1=ALU.mult), reads=[g1.r(), pGN.r()], writes=[g1.r()])
            S.op("dve", lambda e: e.tensor_tensor(out=gl[:], in0=e2[:], in1=b8(ML[:], 64), op=ALU.mult), reads=[e2.r(), ML.r()], writes=[gl.r()])
            S.op("dve", lambda e: e.scalar_tensor_tensor(out=gl[:], in0=gl[:], scalar=-1.0, in1=s8b(f8(bA), 64), op0=ALU.mult, op1=ALU.mult), reads=[gl.r(), bA.r()], writes=[gl.r()])
            S.op("act", lambda e: e.activation(out=sc_[:, :, 0:4], in_=gcA[:, n0:n0 + 2, :], func=AF.Exp), reads=[gcA.r()], writes=[sc_.r()])
            S.op("dve", lambda e: e.scalar_tensor_tensor(out=sc_[:, :, 4:8], in0=sc_[:, :, 0:4], scalar=-1.0, in1=bA[:, n0:n0 + 2, :], op0=ALU.mult, op1=ALU.mult), reads=[sc_.r(), bA.r()], writes=[sc_.r()])
            S.op("dve", lambda e: e.tensor_copy(out=pcr[:].rearrange("p n h -> p (n h)"), in_=e1[:, :, 63]), reads=[e1.r()], writes=[pcr.r()])
            for hp in range(2):
                S.op("act", lambda e, hp=hp: e.activation(out=pe_[hp * 64:(hp + 1) * 64, :, :], in_=pGN[hp * 64:(hp + 1) * 64, 0, :].rearrange("p (ci cc hp i) -> p ci cc hp i", ci=2, cc=2, hp=2)[:, :, :, hp, 63], func=AF.Exp),
                     reads=[pGN.r()], writes=[pe_.r()])
            for ci in range(2):
                S.op("dve", lambda e, ci=ci: e.tensor_tensor(out=vb[:, ci * 4:ci * 4 + 4, :], in0=tk[:, ci, 2:4, :].rearrange("p a (hp v) -> p (a hp) v", hp=2), in1=s4b(bA[:, n0 + ci, :], 64), op=ALU.mult),
                     reads=[tk.r(), bA.r()], writes=[vb.r()])
            S.op("act", lambda e: e.copy(out=vbb[:], in_=vb[:]), reads=[vb.r()], writes=[vbb.r()])
            def mmx(e):
                for ci in range(2):
                    n = n0 + ci
                    nsl = slice(n * 64, (n + 1) * 64)
                    for h in range(4):
                        cc = h // 2
                        ins = e.matmul(pX1[:, ci * 4 + h, :], lhsT=knT[:, cc, nsl], rhs=KQm[h % 2][:, cc, n, :], start=True, stop=True)
                return ins
            S.op("pe", mmx, reads=KQr + [knT.r(0), knT.r(1)], writes=[pX1.r()])
            S.op("dve", lambda e: e.tensor_tensor(out=ny[:, :, 0:64], in0=pX1[:, :, 0:64], in1=g1[:], op=ALU.mult), reads=[pX1.r(), g1.r()], writes=[ny.r()])
            S.op("dve", lambda e: e.tensor_tensor(out=mq[:], in0=pX1[:, :, 64:128], in1=g2[:], op=ALU.mult), reads=[pX1.r(), g2.r()], writes=[mq.r()])
            S.op("dve", lambda e: e.tensor_tensor(out=r_[:], in0=pX1[:, :, 0:64], in1=gl[:], op=ALU.mult), reads=[pX1.r(), gl.r()], writes=[r_.r()])
            S.op("dve", lambda e: e.tensor_tensor(out=ny[:, :, 64:128], in0=ny[:, :, 0:64], in1=b8(I64, 64), op=ALU.add), reads=[ny.r(), c.ident.r()], writes=[ny.r()])
            S.op("act", lambda e: e.copy(out=nl[:], in_=ny[:, :, 0:64]), reads=[ny.r()], writes=[nl.r()])
            _neumann(S, NY, R, pX1, pNB, 8, tts)

        def phase2(m):
            i2 = m % 2
            tk, mq, nl, tts, sc_, pcr, pe_, vb, vbb = tokm[i2], MQ[i2], NL[i2], TTp[i2], scp[i2], PCr[i2], PeT[i2], Vb[i2], Vbb[i2]
            for ci in range(2):
                n = 2 * m + ci
                nsl = slice(n * 64, (n + 1) * 64)
                p0 = ci * 4
                xs_, xf_, wb, wp = XS[ci], XSf[ci], Wb[ci], Wp[ci]
                Hb, Hbn = Hbs[n % 2], Hbs[(n + 1) % 2]
                S.op("dve", lambda e, ci=ci: e.tensor_tensor(out=Hs[:], in0=H[:], in1=pe_[:, ci, :].unsqueeze(2).to_broadcast([128, 2, 64]), op=ALU.mult), reads=[H.r(), pe_.r()], writes=[Hs.r()])
                def mmd(e, n=n, p0=p0, Hb=Hb):
                    for h in range(4):
                        cc = h // 2
                        e.matmul(pXab[:, 0, h * 64:(h + 1) * 64], lhsT=KQm[h % 2][:, cc, n, 0:64], rhs=Hb[:, cc, :], start=True, stop=True)
                        ins = e.matmul(pXab[:, 1, h * 64:(h + 1) * 64], lhsT=nl[:, p0 + h, :], rhs=vbb[:, p0 + h, :], start=True, stop=True)
                    return ins
                S.op("pe", mmd, reads=KQr + [Hb.r(), nl.r(), vbb.r()], writes=[pXab.r()])
                S.op("dve", lambda e, xf_=xf_, ci=ci: e.tensor_tensor(out=g4(xf_[:]), in0=g4(pXab[:, 0, :]), in1=s4b(sc_[:, ci, 4:8], 64), op=ALU.mult), reads=[pXab.r(), sc_.r()], writes=[xf_.r()])
                S.op("dve", lambda e, xs_=xs_, xf_=xf_: e.tensor_tensor(out=xs_[:], in0=xf_[:], in1=pXab[:, 1, :], op=ALU.add), reads=[xf_.r(), pXab.r()], writes=[xs_.r()])
                def mme(e, xs_=xs_, p0=p0):
                    for h in range(4):
                        ins = e.matmul(pHU[0:64, 256 + h * 64:256 + (h + 1) * 64], lhsT=tts[:, p0 + h, :], rhs=xs_[:, h * 64:(h + 1) * 64], start=True, stop=True)
                    return ins
                S.op("pe", mme, reads=[tts.r(), xs_.r()], writes=[pHU.r()])
                S.op("dve", lambda e, wb=wb, p0=p0: e.tensor_tensor(out=g4(wb[:]), in0=g4(pHU[0:64, 256:512]), in1=vb[:, p0:p0 + 4, :], op=ALU.add), reads=[pHU.r(), vb.r()], writes=[wb.r()])
                S.op("dve", lambda e, wb=wb, wp=wp, ci=ci: e.tensor_tensor(out=g4(wp[:]), in0=g4(wb[:]), in1=s4b(pcr[:, ci, :], 64), op=ALU.mult), reads=[wb.r(), pcr.r()], writes=[wp.r()])
                def mmg(e, ci=ci, wp=wp):
                    for h in range(4):
                        cc, hp = h // 2, h % 2
                        hs_ = slice(hp * 64, (hp + 1) * 64)
                        ins = e.matmul(pHU[hs_, cc * 64:(cc + 1) * 64], lhsT=tk[:, ci, cc, hs_], rhs=wp[:, h * 64:(h + 1) * 64], start=True, stop=True)
                    return ins
                S.op("pe", mmg, reads=[tk.r(), wp.r()], writes=[pHU.r()])
                S.op("dve", lambda e, Hbn=Hbn: e.tensor_tensor(out=Hbn[:], in0=Hs[:], in1=pHU[:, 0:128].rearrange("p (cc v) -> p cc v", cc=2), op=ALU.add), reads=[Hs.r(), pHU.r()], writes=[Hbn.r()])
                def mmf(e, n=n, p0=p0, wb=wb, Hb=Hb):
                    for h in range(4):
                        cc = h // 2
                        e.matmul(pO[:, 0, h * 64:(h + 1) * 64], lhsT=KQm[h % 2][:, cc, n, 64:128], rhs=Hb[:, cc, :], start=True, stop=True)
                        ins = e.matmul(pO[:, 1, h * 64:(h + 1) * 64], lhsT=mq[:, p0 + h, :], rhs=wb[:, h * 64:(h + 1) * 64], start=True, stop=True)
                    return ins
                S.op("pe", mmf, reads=KQr + [Hb.r(), mq.r(), wb.r()], writes=[pO.r()])
                S.op("dve", lambda e: e.tensor_tensor(out=H[:], in0=Hs[:], in1=pHU[:, 0:128].rearrange("p (cc v) -> p cc v", cc=2), op=ALU.add), reads=[Hs.r(), pHU.r()], writes=[H.r()])
                o_ = o32[(m % 2) * 2 + ci]
                S.op("dve", lambda e, o_=o_, ci=ci: e.tensor_tensor(out=g4(o_[:]), in0=g4(pO[:, 0, :]), in1=s4b(sc_[:, ci, 0:4], 64), op=ALU.mult), reads=[pO.r(), sc_.r()], writes=[o_.r()])
                S.op("dve", lambda e, o_=o_: e.tensor_tensor(out=o_[:], in0=o_[:], in1=pO[:, 1, :], op=ALU.add), reads=[o_.r(), pO.r()], writes=[o_.r()])

        def post(m):
            for ci in range(2):
                n = 2 * m + ci
                nsl = slice(n * 64, (n + 1) * 64)
                o_, t_, s_, y_, yb_ = o32[(m % 2) * 2 + ci], t1[ci], s4[ci], y32[ci], yb[ci]
                S.op("dve", lambda e, o_=o_, t_=t_: e.tensor_tensor(out=t_[:], in0=o_[:], in1=o_[:], op=ALU.mult), reads=[o_.r(), t_.r()], writes=[t_.r()])
                S.op("dve", lambda e, t_=t_, s_=s_: e.tensor_reduce(out=s_[:], in_=g4(t_[:]), axis=AX.X, op=ALU.add), reads=[t_.r()], writes=[s_.r()])
                S.op("act", lambda e, s_=s_: e.activation(out=s_[:], in_=s_[:], func=AF.Ln, scale=1.0 / 64, bias=c.eps6[0:64, 0:1]), reads=[s_.r()], writes=[s_.r()])
                S.op("act", lambda e, s_=s_: e.activation(out=s_[:], in_=s_[:], func=AF.Exp, scale=-0.5), reads=[s_.r()], writes=[s_.r()])
                S.op("dve", lambda e, o_=o_, s_=s_, t_=t_: e.tensor_tensor(out=g4(t_[:]), in0=g4(o_[:]), in1=s4b(s_[:], 64), op=ALU.mult), reads=[o_.r(), s_.r()], writes=[t_.r()])
                S.op("dve", lambda e, t_=t_: e.tensor_tensor(out=g4(t_[:]), in0=g4(t_[:]), in1=b4(nw[:], 64), op=ALU.mult), reads=[t_.r(), nw.r()], writes=[t_.r()])
                S.op("dve", lambda e, t_=t_, y_=y_, n=n: e.tensor_tensor(out=y_[:], in0=t_[:], in1=szA[:, n, :], op=ALU.mult), reads=[t_.r(), szA.r()], writes=[y_.r()])
                S.op("act", lambda e, y_=y_, yb_=yb_: e.copy(out=yb_[:], in_=y_[:]), reads=[y_.r()], writes=[yb_.r()])
                if c.ydbg is not None:
                    S.op("pool", lambda e, y_=y_, nsl=nsl: e.dma_start(out=c.ydbg[b, nsl, 0:256], in_=y_[:]), reads=[y_.r()], writes=[c.ydbg_r], dma=True)

                def tr(e, yb_=yb_):
                    for jx in range(2):
                        ins = e.transpose(out=pTr[:, 512 + jx * 64:512 + (jx + 1) * 64], in_=yb_[:, jx * 128:(jx + 1) * 128], identity=c.identb[0:64, 0:64])
                    return ins
                S.op("pe", tr, reads=[yb_.r()], writes=[pTr.r()])
                S.op("act", lambda e, nsl=nsl: e.copy(out=c.yT[:, 0:2, nsl], in_=pTr[:, 512:640].rearrange("p (j n) -> p j n", j=2)),
                     reads=[pTr.r()], writes=[c.yT.r((0, n))])


        def rec(fn, m):
            S.defer = []
            fn(m)
            lst, S.defer = S.defer, None
            return lst
        S.replay(rec(phase1, 0))
        for m in range(NPR + 1):
            lists = []
            if m < NPR:
                lists.append(rec(phase2, m))
            if m >= 1:
                lists.append(rec(post, m - 1))
            if m + 1 < NPR:
                lists.append(rec(phase1, m + 1))
            S.replay(*lists)
        S.emit()


def _neumann(S, NY, R, pNA, pNB, nprob=4, out_final=None):
    for s_ in range(6):
        ny, r_ = NY[s_ % 2], R[s_ % 2]
        ny2, r2 = NY[(s_ + 1) % 2], R[(s_ + 1) % 2]
        if s_ == 0:
            def mm0(e, ny=ny, r_=r_):
                for h in range(nprob):
                    e.matmul(pNA[:, h, 0:64], lhsT=r_[:, h, :], rhs=ny[:, h, 0:64], start=True, stop=True)
                    ins = e.matmul(pNB[:, h, 0:64], lhsT=ny[:, h, 0:64], rhs=r_[:, h, :], start=True, stop=True)
                return ins
            S.op("pe", mm0, reads=[ny.r(), r_.r()], writes=[pNA.r(), pNB.r()])
            S.op("act", lambda e, ny2=ny2: e.copy(out=ny2[:, :, 0:64], in_=pNA[:, :, 0:64]), reads=[pNA.r()], writes=[ny2.r()])
            S.op("dve", lambda e, ny=ny, ny2=ny2: e.tensor_copy(out=ny2[:, :, 64:128], in_=ny[:, :, 64:128]), reads=[ny.r()], writes=[ny2.r()])
            S.op("act", lambda e, r2=r2: e.copy(out=r2[:], in_=pNB[:, :, 0:64]), reads=[pNB.r()], writes=[r2.r()])
        else:
            last = (s_ == 5)

            def mms(e, ny=ny, r_=r_, last=last):
                for h in range(nprob):
                    if last:
                        ins = e.matmul(pNA[:, h, 64:128], lhsT=r_[:, h, :], rhs=ny[:, h, 64:128], start=True, stop=True)
                    else:
                        e.matmul(pNA[:, h, :], lhsT=r_[:, h, :], rhs=ny[:, h, :], start=True, stop=True)
                        ins = e.matmul(pNB[:, h, 0:64], lhsT=ny[:, h, 0:64], rhs=r_[:, h, :], start=True, stop=True)
                return ins
            S.op("pe", mms, reads=[ny.r(), r_.r()], writes=[pNA.r(), pNB.r()])
            if not last:
                S.op("act", lambda e, ny2=ny2: e.copy(out=ny2[:, :, 0:64], in_=pNA[:, :, 0:64]), reads=[pNA.r()], writes=[ny2.r()])
                S.op("act", lambda e, r2=r2: e.copy(out=r2[:], in_=pNB[:, :, 0:64]), reads=[pNB.r()], writes=[r2.r()])
            if last and out_final is not None:
                S.op("dve", lambda e, ny=ny: e.tensor_tensor(out=out_final[:], in0=ny[:, :, 64:128], in1=pNA[:, :, 64:128], op=ALU.add), reads=[ny.r(), pNA.r()], writes=[out_final.r()])
            else:
                S.op("dve", lambda e, ny=ny, ny2=ny2: e.tensor_tensor(out=ny2[:, :, 64:128], in0=ny[:, :, 64:128], in1=pNA[:, :, 64:128], op=ALU.add), reads=[ny.r(), pNA.r()], writes=[ny2.r()])


def stage3(c, l, b):
    nc, S = c.nc, c.S
    P = c.p
    last = (l == DEPTH - 1)
    xin = c.xres[l]
    xout = c.out if last else c.xres[l + 1]
    with ExitStack() as st:
        A = lambda n, s, d, psum=False: _alloc(nc, st, "s3_" + n, s, d, psum)
        TM = lambda n, s, d, k=2, psum=False: _tmp(nc, st, "s3_" + n, s, d, k, psum)
        wst = TM("wst", [128, 8, 256], F32, 2)
        wm = [A(f"wm{i}", [128, 8, 512], BF16) for i in range(4)]
        wbr = A("wbr", [128, 8, 512], BF16)
        wo = A("wo", [128, 8, 1024], BF16)
        mixedb = A("mixedb", [128, NTT, 1024], BF16)
        pg = TM("pg", [128, 512], F32, 2, psum=True)
        pb = TM("pb", [128, 512], F32, 2, psum=True)
        pTs = TM("pT", [128, 8, 128], BF16, 2, psum=True)
        po = TM("po", [128, 512], F32, 2, psum=True)
        sig = TM("sig", [128, 512], F32, 2)
        acc = TM("acc", [128, 512], F32, 2)
        tmp = TM("tmp", [128, 512], F32, 2)
        mT = TM("mT", [128, 8, 128], BF16, 2)
        xt = TM("xt", [128, 1024], F32, 2)
        sqt = A("sqt", [128, 1024], F32) if last else None
        ssm = TM("ssm", [128, 1], F32, 2)
        fg = A("fg", [128, 1024], F32) if last else None
        if last:
            S.op("sp", lambda e: e.dma_start(out=fg[:], in_=P["final_gain"][0:1, :].partition_broadcast(128)), writes=[fg.r()], dma=True)
        wcnt = [0]

        def load_w(src_ap, dst, scale_gain):
            for hf in range(2):
                j = wcnt[0] % 2
                wcnt[0] += 1
                ws = wst[j]
                cs = slice(hf * 256, (hf + 1) * 256)
                S.op("sp", lambda e, ws=ws, cs=cs: e.dma_start(out=ws[:], in_=src_ap[:, cs].rearrange("(k p) n -> p k n", p=128)), writes=[ws.r()], dma=True)
                for k in range(8):
                    if scale_gain:
                        if k % 2 == 0:
                            S.op("dve", lambda e, k=k, ws=ws, cs=cs: e.tensor_scalar(out=dst[:, k, cs], in0=ws[:, k, :], scalar1=c.gainT[:, l * 8 + k:l * 8 + k + 1], scalar2=None, op0=ALU.mult),
                                 reads=[ws.r()], writes=[dst.r(k)])
                        else:
                            S.op("act", lambda e, k=k, ws=ws, cs=cs: e.activation(out=dst[:, k, cs], in_=ws[:, k, :], func=AF.Copy, scale=c.gainT[:, l * 8 + k:l * 8 + k + 1]),
                                 reads=[ws.r()], writes=[dst.r(k)])
                    else:
                        if k % 2 == 0:
                            S.op("act", lambda e, k=k, ws=ws, cs=cs: e.copy(out=dst[:, k, cs], in_=ws[:, k, :]), reads=[ws.r()], writes=[dst.r(k)])
                        else:
                            S.op("dve", lambda e, k=k, ws=ws, cs=cs: e.tensor_copy(out=dst[:, k, cs], in_=ws[:, k, :]), reads=[ws.r()], writes=[dst.r(k)])
            return [dst.r(k) for k in range(8)]

        cnt = 0
        for nb in range(2):
            wm_r = [load_w(P["wm"][l, :, i * 1024 + nb * 512:i * 1024 + (nb + 1) * 512], wm[i], True) for i in range(4)]
            wbr_r = load_w(P["wbr"][l, :, nb * 512:(nb + 1) * 512], wbr, False)
            for tt in range(NTT):
                tsl = slice(tt * 128, (tt + 1) * 128)
                acc_ = acc[tt % 2]
                for i in range(4):
                    j = cnt % 2
                    cnt += 1
                    pg_, pb_, sig_, tmp_ = pg[j], pb[j], sig[j], tmp[j]

                    def mmg(e, pg_=pg_, i=i, tsl=tsl):
                        for k in range(8):
                            ins = e.matmul(pg_[:], lhsT=c.hT[:, k, tsl], rhs=wm[i][:, k, :], start=(k == 0), stop=(k == 7))
                        return ins
                    S.op("pe", mmg, reads=wm_r[i] + [c.hT.r(tt)], writes=[pg_.r()])

                    def mmb(e, pb_=pb_, i=i, tsl=tsl):
                        for kc in range(2):
                            ins = e.matmul(pb_[:], lhsT=c.yT[:, 2 * i + kc, tsl], rhs=wbr[:, 2 * i + kc, :], start=(kc == 0), stop=(kc == 1))
                        return ins
                    yr = [c.yT.r((i, tt))] if i != 0 else [c.yT.r((0, 2 * tt)), c.yT.r((0, 2 * tt + 1))]
                    S.op("pe", mmb, reads=[wbr_r[2 * i], wbr_r[2 * i + 1]] + yr, writes=[pb_.r()])
                    S.op("act", lambda e, pg_=pg_, sig_=sig_: e.activation(out=sig_[:], in_=pg_[:], func=AF.Sigmoid), reads=[pg_.r()], writes=[sig_.r()])
                    if i == 0:
                        S.op("dve", lambda e, acc_=acc_, sig_=sig_, pb_=pb_: e.tensor_tensor(out=acc_[:], in0=sig_[:], in1=pb_[:], op=ALU.mult), reads=[sig_.r(), pb_.r()], writes=[acc_.r()])
                    else:
                        S.op("dve", lambda e, tmp_=tmp_, sig_=sig_, pb_=pb_: e.tensor_tensor(out=tmp_[:], in0=sig_[:], in1=pb_[:], op=ALU.mult), reads=[sig_.r(), pb_.r()], writes=[tmp_.r()])
                        if i < 3:
                            S.op("dve", lambda e, acc_=acc_, tmp_=tmp_: e.tensor_tensor(out=acc_[:], in0=acc_[:], in1=tmp_[:], op=ALU.add), reads=[acc_.r(), tmp_.r()], writes=[acc_.r()])
                        else:
                            S.op("dve", lambda e, acc_=acc_, tmp_=tmp_, tt=tt, nb=nb: e.tensor_tensor(out=mixedb[:, tt, nb * 512:(nb + 1) * 512], in0=acc_[:], in1=tmp_[:], op=ALU.add),
                                 reads=[acc_.r(), tmp_.r()], writes=[mixedb.r((tt, nb))])
        for q4 in range(4):
            j = wcnt[0] % 2
            wcnt[0] += 1
            ws = wst[j]
            cs = slice(q4 * 256, (q4 + 1) * 256)
            S.op("sp", lambda e, ws=ws, cs=cs: e.dma_start(out=ws[:], in_=P["wo"][l, :, cs].rearrange("(k p) n -> p k n", p=128)), writes=[ws.r()], dma=True)
            S.op("act", lambda e, ws=ws, cs=cs: e.copy(out=wo[:, :, cs], in_=ws[:]), reads=[ws.r()], writes=[wo.r(q4 // 2)])
        def o_front(tt):
            tsl = slice(tt * 128, (tt + 1) * 128)
            i = tt % 2
            mT_, xt_, pT = mT[i], xt[i], pTs[i]
            S.op("sp", lambda e: e.dma_start(out=xt_[:], in_=xin[b, tsl, :]), reads=[c.xres_r[l]], writes=[xt_.r()], dma=True)

            def trm(e):
                for k in range(8):
                    ins = e.transpose(out=pT[:, k, :], in_=mixedb[:, tt, k * 128:(k + 1) * 128], identity=c.identb[:])
                return ins
            S.op("pe", trm, reads=[mixedb.r((tt, 0)), mixedb.r((tt, 1))], writes=[pT.r()])
            S.op("act", lambda e: e.copy(out=mT_[:], in_=pT[:]), reads=[pT.r()], writes=[mT_.r()])

        def o_back(tt):
            tsl = slice(tt * 128, (tt + 1) * 128)
            i = tt % 2
            mT_, xt_, ss_ = mT[i], xt[i], ssm[i]
            for nb in range(2):
                po_ = po[nb]

                def mmo(e, po_=po_, nb=nb):
                    for k in range(8):
                        ins = e.matmul(po_[:], lhsT=mT_[:, k, :], rhs=wo[:, k, nb * 512:(nb + 1) * 512], start=(k == 0), stop=(k == 7))
                    return ins
                S.op("pe", mmo, reads=[mT_.r(), wo.r(nb)], writes=[po_.r()])
                S.op("dve", lambda e, po_=po_, nb=nb: e.tensor_tensor(out=xt_[:, nb * 512:(nb + 1) * 512], in0=xt_[:, nb * 512:(nb + 1) * 512], in1=po_[:], op=ALU.add),
                     reads=[po_.r(), xt_.r()], writes=[xt_.r()])
            if last:
                S.op("act", lambda e: e.activation(out=sqt[:], in_=xt_[:], func=AF.Square, accum_out=ss_[:]), reads=[xt_.r()], writes=[sqt.r(), ss_.r()])
                S.op("act", lambda e: e.activation(out=ss_[:], in_=ss_[:], func=AF.Sqrt, scale=1.0 / D, bias=c.eps6[:, 0:1]), reads=[ss_.r()], writes=[ss_.r()])
                S.op("dve", lambda e: e.reciprocal(out=ss_[:], in_=ss_[:]), reads=[ss_.r()], writes=[ss_.r()])
                S.op("dve", lambda e: e.scalar_tensor_tensor(out=xt_[:], in0=xt_[:], scalar=ss_[:, 0:1], in1=fg[:], op0=ALU.mult, op1=ALU.mult),
                     reads=[xt_.r(), ss_.r(), fg.r()], writes=[xt_.r()])
            ev = S.op("pool", lambda e: e.dma_start(out=xout[b, tsl, :], in_=xt_[:]), reads=[xt_.r()], writes=[c.xres_r[l + 1]], dma=True)
            c.out_events.append(ev)
        o_front(0)
        for tt in range(NTT):
            if tt + 1 < NTT:
                o_front(tt + 1)
            o_back(tt)
        S.emit()


def host_params(inputs):
    f = lambda n: np.asarray(inputs[n], dtype=np.float32)
    fcols, tcols, mo = _col_perm()
    w_in = f("w_in")
    p = {}
    p["wf"] = np.ascontiguousarray(w_in[:, :, fcols])
    p["wt"] = np.ascontiguousarray(w_in[:, :, tcols])
    p["wm"] = np.ascontiguousarray(w_in[:, :, mo:mo + 4096])
    p["wbr"] = np.ascontiguousarray(f("w_branch").reshape(DEPTH, 1024, 1024))
    p["wo"] = f("w_out")
    p["final_gain"] = f("final_gain").reshape(1, D)
    p["gainT"] = np.ascontiguousarray(f("norm_gain").reshape(DEPTH, 8, 128).transpose(2, 0, 1).reshape(128, DEPTH * 8))
    p["sg_wsT"] = np.ascontiguousarray(f("sg_w_s").transpose(0, 3, 1, 2))
    p["sg_ln_gain"] = f("sg_ln_gain")
    p["sg_ln_bias"] = f("sg_ln_bias")
    p["sg_bsT"] = np.ascontiguousarray(f("sg_b_s").transpose(0, 2, 1))
    p["ident"] = np.eye(128, dtype=np.float32)
    i = np.arange(128)
    p["triT"] = (i[:, None] <= i[None, :]).astype(np.float32)
    p["cmp_posT"] = np.ascontiguousarray(f("nsa_cmp_pos").transpose(0, 3, 1, 2))
    p["nsa_cmp_w1"] = f("nsa_cmp_w1")
    p["nsa_cmp_w2"] = f("nsa_cmp_w2")
    p.update(nsa_consts())
    p["gdn_convT"] = np.ascontiguousarray(f("gdn_conv").transpose(0, 2, 1).reshape(DEPTH, 6, 128, 4).transpose(0, 2, 1, 3))
    p["gdn_a_log"] = f("gdn_a_log")
    p["gdn_dt_bias"] = f("gdn_dt_bias")
    p["gdn_norm"] = f("gdn_norm")
    p["rw_muT"] = np.ascontiguousarray(f("rw_mu").reshape(DEPTH, 7, 128).transpose(0, 2, 1))
    p["rw_lora"] = np.ascontiguousarray(np.concatenate([f("rw_w_up"), f("rw_a_up")], axis=1))
    colp = np.stack([f("rw_w0"), f("rw_a0"), f("rw_k_k"), f("rw_k_a"), f("rw_r_k").reshape(DEPTH, 256)], axis=-1)
    p["rw_cols"] = np.ascontiguousarray(colp.reshape(DEPTH, 2, 128, 5).transpose(0, 2, 1, 3))
    p["rw_gn_gain"] = f("rw_gn_gain")
    p["rw_gn_bias"] = f("rw_gn_bias")
    j64 = np.arange(64)
    p["m1cat"] = np.concatenate([(j64[:, None] < j64[None, :]), (j64[:, None] <= j64[None, :])], axis=1).astype(np.float32)
    p["maskL"] = (j64[None, :] < j64[:, None]).astype(np.float32)
    blk = np.zeros((128, 128), np.float32); blk[:64, :64] = 1; blk[64:, 64:] = 1
    p["blk"] = blk
    return p


def build(stages=("s1",), debug_out=(), pshapes=None, only=None):
    nc = bass.Bass("TRN2", target_bir_lowering=False)
    c = Ctx()
    c.nc = nc
    dt = lambda name, shape, kind="ExternalInput", dtype=F32: nc.dram_tensor(name, list(shape), dtype, kind=kind).ap()
    dbg = lambda name: "ExternalOutput" if name in debug_out else "Internal"
    x_in = dt("x", [NB, T, D])
    x1 = dt("x1", [NB, T, D], dbg("x1"))
    c.xres = [x_in, x1]
    c.out = dt("out", [NB, T, D], "ExternalOutput")
    c.p = {n: dt("p_" + n, shp) for n, shp in pshapes.items()}
    c.wf, c.wt = c.p["wf"], c.p["wt"]
    c.Fs = [dt(f"Fs{b}", [NF, T], dbg("Fs")) for b in range(NB)]
    c.Ts = [dt(f"Ts{b}", [T, NT], dbg("Ts")) for b in range(NB)]
    c.Fs_r = [DramRes() for _ in range(NB)]
    c.Ts_r = [DramRes() for _ in range(NB)]
    c.ydbg = dt("ydbg", [NB, T, 1024], "ExternalOutput") if "ydbg" in debug_out else None
    c.ydbg_r = DramRes()
    c.xres_r = [DramRes() for _ in range(DEPTH + 1)]
    c.out_events = []

    with ExitStack() as st:
        S = Sched(nc, st)
        c.S = S
        A = lambda n, s, d, psum=False: _alloc(nc, st, n, s, d, psum)
        c.hT = A("hT", [128, 8, T], BF16)
        c.yT = A("yT", [128, 8, T], BF16)
        c.ident = A("ident_f", [128, 128], F32)
        c.identb = A("ident_b", [128, 128], BF16)
        c.gainT = A("gainT_s", [128, DEPTH * 8], F32)
        c.eps6 = A("eps6", [128, 1], F32)
        c.eps5 = A("eps5", [128, 1], F32)
        S.op("sp", lambda e: e.dma_start(out=c.ident[:], in_=c.p["ident"][:, :]), writes=[c.ident.r()], dma=True)
        S.op("sp", lambda e: e.dma_start(out=c.gainT[:], in_=c.p["gainT"][:, :]), writes=[c.gainT.r()], dma=True)
        S.op("dve", lambda e: e.tensor_copy(out=c.identb[:], in_=c.ident[:]), reads=[c.ident.r()], writes=[c.identb.r()])
        S.op("dve", lambda e: e.memset(c.eps6[:], 1e-6), writes=[c.eps6.r()])
        S.op("dve", lambda e: e.memset(c.eps5[:], 1e-5), writes=[c.eps5.r()])
        c.epsgn = A("epsgn", [128, 1], F32)
        c.one = A("one", [128, 1], F32)
        S.op("dve", lambda e: e.memset(c.one[:], 1.0), writes=[c.one.r()])
        S.op("dve", lambda e: e.memset(c.epsgn[:], 64e-5), writes=[c.epsgn.r()])
        c.m1cat = A("m1cat", [64, 128], F32)
        c.maskL = A("maskL", [64, 64], F32)
        c.blk = A("blk", [128, 128], F32)
        c.hmask = A("hmask", [128, 2], F32)
        S.op("dve", lambda e: e.memset(c.hmask[:], 0.0), writes=[c.hmask.r()])
        S.op("dve", lambda e: e.memset(c.hmask[0:64, 0:1], 1.0), writes=[c.hmask.r()])
        S.op("dve", lambda e: e.memset(c.hmask[64:128, 1:2], 1.0), writes=[c.hmask.r()])
        for tl, nm in ((c.m1cat, "m1cat"), (c.maskL, "maskL"), (c.blk, "blk")):
            S.op("sp", lambda e, tl=tl, nm=nm: e.dma_start(out=tl[:], in_=c.p[nm]), writes=[tl.r()], dma=True)
        S.emit()
        for l in range(DEPTH):
            for b in range(NB):
                if only is not None and (l, b) not in only:
                    continue
                if "s1" in stages:
                    stage1(c, l, b)
                c.merge_B = ("A" in stages and "B" in stages)
                if "B" in stages and not c.merge_B:
                    mixerB(c, l, b)
                if "D" in stages:
                    mixerD(c, l, b)
                if "C" in stages:
                    mixerC(c, l, b)
                if "A" in stages:
                    mixerA(c, l, b)
                if "s3" in stages:
                    stage3(c, l, b)
    c.nops = S.nops
    return nc


def host_inputs(inputs):
    p = host_params(inputs)
    x = np.asarray(inputs["x"], dtype=np.float32)
    maps = []
    for i in range(NCORES):
        m = {"p_" + k: v for k, v in p.items()}
        m["x"] = np.ascontiguousarray(x[i * NB:(i + 1) * NB])
        maps.append(m)
    return maps, {n: a.shape for n, a in p.items()}


def kernel(**inputs):
    maps, pshapes = host_inputs(inputs)
    nc = build(stages=ALL_STAGES, pshapes=pshapes)
    res = run_bass_kernel_spmd(nc, maps, core_ids=list(range(NCORES)))
    return np.concatenate([r["out"] for r in res.results], axis=0)


ALL_STAGES = ("s1", "A", "B", "C", "D", "s3")
```

```python
import math
from contextlib import ExitStack

import numpy as np
import concourse.bass as bass
import concourse.mybir as mybir
from concourse.bass_utils import run_bass_kernel_spmd

F32 = mybir.dt.float32
BF16 = mybir.dt.bfloat16
AF = mybir.ActivationFunctionType
ALU = mybir.AluOpType
AX = mybir.AxisListType

NCORES = 8
DEPTH = 2
D = 1024
T = 2048
NB = 2
NTT = T // 128
D_IN = 7956
NF = 2176
NT = 1684

F_GQ, F_GK, F_GV = 0, 256, 512
F_RW = 768
F_NQ = 1664
F_KC, F_VC, F_KS, F_KW = 1920, 1984, 2048, 2112
T_GZ, T_SU, T_SV, T_SZ, T_RZ, T_NZ = 0, 256, 512, 768, 1024, 1280
T_VS, T_VW, T_GA, T_GB, T_NG = 1536, 1600, 1664, 1668, 1672


def _col_perm():
    o = {}
    off = 0
    names = ["gdn_qkv", "gdn_a", "gdn_b", "gdn_z", "sg_u", "sg_v", "sg_z", "rw", "rw_z",
             "nsa_q", "nsa_kv", "nsa_g", "nsa_z", "merge"]
    sizes = [768, 4, 4, 256, 256, 256, 256, 896, 256, 256, 384, 12, 256, 4096]
    for n, s in zip(names, sizes):
        o[n] = off
        off += s
    assert off == D_IN
    r = lambda a, n: list(range(a, a + n))
    kv = o["nsa_kv"]
    fcols = (r(o["gdn_qkv"], 768) + r(o["rw"], 896) + r(o["nsa_q"], 256)
             + r(kv, 64) + r(kv + 64, 64) + r(kv + 128, 64) + r(kv + 256, 64))
    tcols = (r(o["gdn_z"], 256) + r(o["sg_u"], 256) + r(o["sg_v"], 256) + r(o["sg_z"], 256)
             + r(o["rw_z"], 256) + r(o["nsa_z"], 256) + r(kv + 192, 64) + r(kv + 320, 64)
             + r(o["gdn_a"], 4) + r(o["gdn_b"], 4) + r(o["nsa_g"], 12))
    assert len(fcols) == NF and len(tcols) == NT
    return np.array(fcols), np.array(tcols), o["merge"]


class Res:
    __slots__ = ("last_w", "readers")

    def __init__(self):
        self.last_w = None
        self.readers = []


class DramRes(Res):
    __slots__ = ()


class Tile:
    def __init__(self, t):
        self.t = t
        self._r = {}

    def r(self, key=0):
        x = self._r.get(key)
        if x is None:
            x = self._r[key] = Res()
        return x

    def __getitem__(self, idx):
        return self.t[idx]


class Sched:
    ENGS = ("pe", "act", "dve", "pool", "sp")
    ROT = 30000

    def __init__(self, nc, stack, n_dma_ring=8):
        self.nc = nc
        self.stack = stack
        self.sems = []
        self.eng_sem = {}
        self.eng_cnt = {}
        self.ring = {}
        for e in self.ENGS:
            self.eng_sem[e] = self._new_sem(f"s_{e}")
            self.eng_cnt[e] = 0
            self.ring[e] = [[self._new_sem(f"d_{e}{i}"), 0] for i in range(n_dma_ring)]
        self.ring_pos = {e: 0 for e in self.ENGS}
        self.waited = {e: {} for e in self.ENGS}
        self.ops = {e: [] for e in self.ENGS}
        self.nops = 0
        self.defer = None
        self.last_ev = {e: None for e in self.ENGS}

    def _new_sem(self, name):
        s = self.stack.enter_context(self.nc.semaphore(name))
        self.sems.append(s)
        return len(self.sems) - 1

    def _need(self, eng, ev, waits):
        if ev is None:
            return
        si, val = ev[1], ev[2]
        if self.waited[eng].get(si, 0) >= val:
            return
        if val > waits.get(si, 0):
            waits[si] = val

    def op(self, eng, fn, reads=(), writes=(), dma=False, pe_acc=False):
        if self.defer is not None:
            self.defer.append((eng, fn, list(reads), list(writes), dma, pe_acc))
            return None
        reads = [r for r in reads if not isinstance(r, DramRes)]
        writes = [w for w in writes if not isinstance(w, DramRes)]
        waits = {}
        for r in reads:
            self._need(eng, r.last_w, waits)
        for w in writes:
            lw = w.last_w
            if not (pe_acc and lw is not None and lw[0] == "pe" and eng == "pe"):
                self._need(eng, lw, waits)
            for ev in w.readers:
                self._need(eng, ev, waits)
        if dma:
            pos = self.ring_pos[eng]
            self.ring_pos[eng] = (pos + 1) % len(self.ring[eng])
            slot = self.ring[eng][pos]
            if slot[1] > 0:
                self._need(eng, (eng, slot[0], slot[1]), waits)
            slot[1] += 16
            ev = (eng, slot[0], slot[1])
            inc = (slot[0], 16)
        else:
            if self.eng_cnt[eng] >= self.ROT:
                self.eng_sem[eng] = self._new_sem(f"s_{eng}_{len(self.sems)}")
                self.eng_cnt[eng] = 0
            self.eng_cnt[eng] += 1
            ev = (eng, self.eng_sem[eng], self.eng_cnt[eng])
            inc = (self.eng_sem[eng], 1)
        for si, val in waits.items():
            self.waited[eng][si] = val
        self.ops[eng].append((list(waits.items()), fn, inc))
        for r in reads:
            r.readers.append(ev)
        for w in writes:
            w.last_w = ev
            w.readers = []
        self.nops += 1
        self.last_ev[eng] = ev
        return ev

    def replay(self, *lists):
        assert self.defer is None
        pos = [0] * len(lists)
        total = sum(len(x) for x in lists)
        for _ in range(total):
            best, bf = None, None
            for i, x in enumerate(lists):
                if pos[i] < len(x):
                    f = pos[i] / len(x)
                    if bf is None or f < bf:
                        best, bf = i, f
            a = lists[best][pos[best]]
            pos[best] += 1
            self.op(a[0], a[1], a[2], a[3], a[4], a[5])

    def emit(self, final_events=()):
        nc = self.nc
        sems = self.sems
        ops = self.ops
        self.ops = {e: [] for e in self.ENGS}
        tail = [ev for ev in self.last_ev.values() if ev is not None]
        for e in self.ENGS:
            for slot in self.ring[e]:
                if slot[1] > 0:
                    tail.append((e, slot[0], slot[1]))
        tail += list(final_events)
        tw = {}
        for ev in tail:
            tw[ev[1]] = max(tw.get(ev[1], 0), ev[2])

        with nc.Block() as block:
            def run(engname, e):
                for waits, fn, inc in ops[engname]:
                    for si, val in waits:
                        e.wait_ge(sems[si], val)
                    ins = fn(e)
                    ins.then_inc(sems[inc[0]], inc[1])
                for si, val in tw.items():
                    if self.waited[engname].get(si, 0) < val:
                        e.wait_ge(sems[si], val)
                        self.waited[engname][si] = val

            @block.tensor
            def _(e):
                run("pe", e)

            @block.scalar
            def _(e):
                run("act", e)

            @block.vector
            def _(e):
                run("dve", e)

            @block.gpsimd
            def _(e):
                run("pool", e)

            @block.sync
            def _(e):
                run("sp", e)


class Ctx:
    pass


_UID = [0]


def _alloc(nc, st, name, shape, dtype, psum=False):
    _UID[0] += 1
    name = f"{name}_{_UID[0]}"
    if psum:
        return Tile(st.enter_context(nc.psum_tensor(name, shape, dtype)))
    return Tile(st.enter_context(nc.sbuf_tensor(name, shape, dtype)))


def stage1(c, l, b):
    nc, S = c.nc, c.S
    xin = c.xres[l]
    with ExitStack() as st:
        A = lambda n, s, d, psum=False: _alloc(nc, st, n, s, d, psum)
        xt = [A(f"s1_x{i}", [128, D], F32) for i in range(2)]
        sq = A("s1_sq", [128, D], F32)
        ssum = [A(f"s1_ss{i}", [128, 1], F32) for i in range(2)]
        hb = [A(f"s1_hb{i}", [128, D], BF16) for i in range(2)]
        pT = [A(f"s1_pT{i}", [128, 8, 128], BF16, psum=True) for i in range(2)]
        wf32 = [A(f"s1_wf{i}", [128, 8, 512], F32) for i in range(2)]
        wbf = [A(f"s1_wb{i}", [128, 8, 512], BF16) for i in range(2)]
        po = [A(f"s1_po{i}", [128, 512], F32, psum=True) for i in range(4)]
        ot = [A(f"s1_ot{i}", [128, 512], F32) for i in range(4)]

        def a_front(tt):
            i = tt % 2
            x_, ss_, hb_, pT_ = xt[i], ssum[i], hb[i], pT[i]
            S.op("sp", lambda e: e.dma_start(out=x_[:], in_=xin[b, tt * 128:(tt + 1) * 128, :]),
                 reads=[c.xres_r[l]], writes=[x_.r()], dma=True)
            S.op("act", lambda e: e.activation(out=sq[:], in_=x_[:], func=AF.Square, accum_out=ss_[:]),
                 reads=[x_.r()], writes=[sq.r(), ss_.r()])
            S.op("act", lambda e: e.activation(out=ss_[:], in_=ss_[:], func=AF.Sqrt, scale=1.0 / D, bias=c.eps6[:, 0:1]),
                 reads=[ss_.r()], writes=[ss_.r()])
            S.op("dve", lambda e: e.reciprocal(out=ss_[:], in_=ss_[:]), reads=[ss_.r()], writes=[ss_.r()])
            S.op("dve", lambda e: e.tensor_scalar(out=hb_[:], in0=x_[:], scalar1=ss_[:, 0:1], scalar2=None, op0=ALU.mult),
                 reads=[x_.r(), ss_.r()], writes=[hb_.r()])

            def tr(e):
                for k in range(8):
                    ins = e.transpose(out=pT_[:, k, :], in_=hb_[:, k * 128:(k + 1) * 128], identity=c.identb[:])
                return ins
            S.op("pe", tr, reads=[hb_.r()], writes=[pT_.r()])

        def a_back(tt):
            pT_ = pT[tt % 2]
            eng = "dve" if tt % 2 == 0 else "act"
            if eng == "dve":
                S.op("dve", lambda e: e.tensor_copy(out=c.hT[:, :, tt * 128:(tt + 1) * 128], in_=pT_[:]), reads=[pT_.r()], writes=[c.hT.r(tt)])
            else:
                S.op("act", lambda e: e.copy(out=c.hT[:, :, tt * 128:(tt + 1) * 128], in_=pT_[:]), reads=[pT_.r()], writes=[c.hT.r(tt)])
        a_front(0)
        for tt in range(NTT):
            if tt + 1 < NTT:
                a_front(tt + 1)
            a_back(tt)

        hT_all = [c.hT.r(tt) for tt in range(NTT)]

        def load_w(src, c0, n, j):
            wf_, wb_ = wf32[j], wbf[j]
            S.op("sp", lambda e: e.dma_start(out=wf_[:, :, 0:n], in_=src[l, :, c0:c0 + n].rearrange("(k p) n -> p k n", p=128)),
                 writes=[wf_.r()], dma=True)
            for k in range(8):
                eng = "pool" if k % 2 == 0 else "act"
                if eng == "pool":
                    S.op("dve", lambda e, k=k: e.tensor_scalar(out=wb_[:, k, 0:n], in0=wf_[:, k, 0:n], scalar1=c.gainT[:, l * 8 + k:l * 8 + k + 1], scalar2=None, op0=ALU.mult),
                         reads=[wf_.r()], writes=[wb_.r(k)])
                else:
                    S.op("act", lambda e, k=k: e.activation(out=wb_[:, k, 0:n], in_=wf_[:, k, 0:n], func=AF.Copy, scale=c.gainT[:, l * 8 + k:l * 8 + k + 1]),
                         reads=[wf_.r()], writes=[wb_.r(k)])
            return wb_, [wb_.r(k) for k in range(8)]

        cnt = 0
        for cc in range(0, NF, 512):
            n = min(512, NF - cc)
            wb_, wres = load_w(c.wf, cc, n, (cc // 512) % 2)
            for c1 in range(0, n, 128):
                for tq in range(4):
                    j = cnt % 4
                    cnt += 1
                    po_, ot_ = po[j], ot[j]

                    def mm(e, po_=po_, wb_=wb_, c1=c1, tq=tq):
                        for k in range(8):
                            ins = e.matmul(po_[:], lhsT=wb_[:, k, c1:c1 + 128], rhs=c.hT[:, k, tq * 512:(tq + 1) * 512],
                                           start=(k == 0), stop=(k == 7))
                        return ins
                    S.op("pe", mm, reads=wres + hT_all[tq * 4:tq * 4 + 4], writes=[po_.r()])
                    ev_eng = "dve" if cnt % 2 == 0 else "act"
                    if ev_eng == "dve":
                        S.op("dve", lambda e, po_=po_, ot_=ot_: e.tensor_copy(out=ot_[:], in_=po_[:]), reads=[po_.r()], writes=[ot_.r()])
                    else:
                        S.op("act", lambda e, po_=po_, ot_=ot_: e.copy(out=ot_[:], in_=po_[:]), reads=[po_.r()], writes=[ot_.r()])
                    row = cc + c1
                    S.op("pool", lambda e, ot_=ot_, row=row, tq=tq: e.dma_start(out=c.Fs[b][row:row + 128, tq * 512:(tq + 1) * 512], in_=ot_[:]),
                         reads=[ot_.r()], writes=[c.Fs_r[b]], dma=True)

        for ci, cc in enumerate(range(0, NT, 512)):
            n = min(512, NT - cc)
            wb_, wres = load_w(c.wt, cc, n, (ci + 1) % 2)
            for tt in range(NTT):
                j = cnt % 4
                cnt += 1
                po_, ot_ = po[j], ot[j]

                def mm(e, po_=po_, wb_=wb_, tt=tt, n=n):
                    for k in range(8):
                        ins = e.matmul(po_[:, 0:n], lhsT=c.hT[:, k, tt * 128:(tt + 1) * 128], rhs=wb_[:, k, 0:n],
                                       start=(k == 0), stop=(k == 7))
                    return ins
                S.op("pe", mm, reads=wres + [hT_all[tt]], writes=[po_.r()])
                if cnt % 2 == 0:
                    S.op("dve", lambda e, po_=po_, ot_=ot_, n=n: e.tensor_copy(out=ot_[:, 0:n], in_=po_[:, 0:n]), reads=[po_.r()], writes=[ot_.r()])
                else:
                    S.op("act", lambda e, po_=po_, ot_=ot_, n=n: e.copy(out=ot_[:, 0:n], in_=po_[:, 0:n]), reads=[po_.r()], writes=[ot_.r()])
                S.op("pool", lambda e, ot_=ot_, tt=tt, cc=cc, n=n: e.dma_start(out=c.Ts[b][tt * 128:(tt + 1) * 128, cc:cc + n], in_=ot_[:, 0:n]),
                     reads=[ot_.r()], writes=[c.Ts_r[b]], dma=True)
        S.emit()


def _tmp(nc, st, name, shape, dtype, n=2, psum=False):
    return [_alloc(nc, st, f"{name}{i}", shape, dtype, psum) for i in range(n)]


def _dbg_y(c, b, tt, col, yt, res, width=256):
    if c.ydbg is None:
        return
    c.S.op("pool", lambda e: e.dma_start(out=c.ydbg[b, tt * 128:(tt + 1) * 128, col:col + width], in_=yt),
           reads=[res], writes=[c.ydbg_r], dma=True)


def _y_to_yT(c, st_tiles, yb, yb_r, mixer, tt, i):
    S = c.S
    pT = st_tiles[i]

    def tr(e):
        for j in range(2):
            ins = e.transpose(out=pT[:, j, :], in_=yb[:, j * 128:(j + 1) * 128], identity=c.identb[:])
        return ins
    S.op("pe", tr, reads=[yb_r], writes=[pT.r()])
    S.op("act", lambda e: e.copy(out=c.yT[:, 2 * mixer:2 * mixer + 2, tt * 128:(tt + 1) * 128], in_=pT[:]),
         reads=[pT.r()], writes=[c.yT.r((mixer, tt))])


def mixerB(c, l, b, ext_st=None):
    nc, S = c.nc, c.S
    with ExitStack() as own_st:
        st = ext_st if ext_st is not None else own_st
        A = lambda n, s, d, psum=False: _alloc(nc, st, n, s, d, psum)
        nbuf = 2 if ext_st is None else 1
        TM = lambda n, s, d, k=2, psum=False: _tmp(nc, st, n, s, d, (k if psum else nbuf), psum)
        ws32 = A("mb_ws32", [128, 4, 128], F32)
        ws = A("mb_ws", [128, 4, 128], BF16)
        lng = A("mb_lng", [128, 256], F32)
        lnb = A("mb_lnb", [128, 256], F32)
        bsT = A("mb_bs", [128, 4], F32)
        triT = A("mb_triT", [128, 128], F32)
        S.op("sp", lambda e: e.dma_start(out=triT[:], in_=c.p["triT"][:, :]), writes=[triT.r()], dma=True)
        S.op("sp", lambda e: e.dma_start(out=ws32[:], in_=c.p["sg_wsT"][l]), writes=[ws32.r()], dma=True)
        S.op("sp", lambda e: e.dma_start(out=lng[:], in_=c.p["sg_ln_gain"][l:l + 1, :].partition_broadcast(128)), writes=[lng.r()], dma=True)
        S.op("sp", lambda e: e.dma_start(out=lnb[:], in_=c.p["sg_ln_bias"][l:l + 1, :].partition_broadcast(128)), writes=[lnb.r()], dma=True)
        S.op("sp", lambda e: e.dma_start(out=bsT[:], in_=c.p["sg_bsT"][l]), writes=[bsT.r()], dma=True)
        S.op("dve", lambda e: e.tensor_tensor(out=ws[:], in0=ws32[:], in1=triT[:].unsqueeze(1).to_broadcast([128, 4, 128]), op=ALU.mult),
             reads=[ws32.r(), triT.r()], writes=[ws.r()])
        xin = TM("mb_in", [128, 768], F32)
        gl = TM("mb_gl", [128, 512], F32)
        sz = TM("mb_sz", [128, 256], F32)
        m4 = TM("mb_m4", [128, 4], F32)
        v4 = TM("mb_v4", [128, 4], F32)
        xc = TM("mb_xc", [128, 256], F32)
        t1 = TM("mb_t1", [128, 256], F32)
        vb = TM("mb_vb", [128, 256], BF16)
        pm = TM("mb_pm", [128, 256], F32, psum=True)
        y32 = TM("mb_y", [128, 256], F32)
        yb = _tmp(nc, st, "mb_yb", [128, 256], BF16, 2)
        pT = TM("mb_pT", [128, 2, 128], BF16, psum=True)
        g3 = lambda ap: ap.rearrange("p (g c) -> p g c", g=4)
        bc = lambda ap: ap.unsqueeze(2).to_broadcast([128, 4, 64])
        for tt in range(NTT):
            i = tt % nbuf
            xi, gl_, sz_, m_, v_, xc_, t_, vb_, pm_, y_, yb_ = xin[i], gl[i], sz[i], m4[i], v4[i], xc[i], t1[i], vb[i], pm[i], y32[i], yb[tt % 2]
            S.op("sp", lambda e, xi=xi, tt=tt: e.dma_start(out=xi[:], in_=c.Ts[b][tt * 128:(tt + 1) * 128, T_SU:T_SU + 768]),
                 reads=[c.Ts_r[b]], writes=[xi.r()], dma=True)
            S.op("act", lambda e, xi=xi, gl_=gl_: e.activation(out=gl_[:], in_=xi[:, 0:512], func=AF.Gelu_apprx_tanh), reads=[xi.r()], writes=[gl_.r()])
            S.op("act", lambda e, xi=xi, sz_=sz_: e.activation(out=sz_[:], in_=xi[:, 512:768], func=AF.Silu), reads=[xi.r()], writes=[sz_.r()])
            S.op("dve", lambda e, gl_=gl_, m_=m_: e.tensor_reduce(out=m_[:], in_=g3(gl_[:, 256:512]), axis=AX.X, op=ALU.add), reads=[gl_.r()], writes=[m_.r()])
            S.op("dve", lambda e, gl_=gl_, m_=m_, xc_=xc_: e.scalar_tensor_tensor(out=g3(xc_[:]), in0=bc(m_[:]), scalar=-1.0 / 64, in1=g3(gl_[:, 256:512]), op0=ALU.mult, op1=ALU.add),
                 reads=[gl_.r(), m_.r()], writes=[xc_.r()])
            S.op("dve", lambda e, xc_=xc_, t_=t_: e.tensor_tensor(out=t_[:], in0=xc_[:], in1=xc_[:], op=ALU.mult), reads=[xc_.r()], writes=[t_.r()])
            S.op("dve", lambda e, t_=t_, v_=v_: e.tensor_reduce(out=v_[:], in_=g3(t_[:]), axis=AX.X, op=ALU.add), reads=[t_.r()], writes=[v_.r()])
            S.op("act", lambda e, v_=v_: e.activation(out=v_[:], in_=v_[:], func=AF.Sqrt, scale=1.0 / 64, bias=c.eps5[:, 0:1]), reads=[v_.r()], writes=[v_.r()])
            S.op("dve", lambda e, v_=v_: e.reciprocal(out=v_[:], in_=v_[:]), reads=[v_.r()], writes=[v_.r()])
            S.op("dve", lambda e, xc_=xc_, v_=v_, t_=t_: e.tensor_tensor(out=g3(t_[:]), in0=g3(xc_[:]), in1=bc(v_[:]), op=ALU.mult), reads=[xc_.r(), v_.r()], writes=[t_.r()])
            S.op("dve", lambda e, t_=t_: e.tensor_tensor(out=t_[:], in0=t_[:], in1=lng[:], op=ALU.mult), reads=[t_.r(), lng.r()], writes=[t_.r()])
            S.op("dve", lambda e, t_=t_, vb_=vb_: e.tensor_tensor(out=vb_[:], in0=t_[:], in1=lnb[:], op=ALU.add), reads=[t_.r(), lnb.r()], writes=[vb_.r()])

            def mm(e, vb_=vb_, pm_=pm_):
                for g in range(4):
                    ins = e.matmul(pm_[:, g * 64:(g + 1) * 64], lhsT=ws[:, g, :], rhs=vb_[:, g * 64:(g + 1) * 64], start=True, stop=True)
                return ins
            S.op("pe", mm, reads=[vb_.r(), ws.r()], writes=[pm_.r()])
            S.op("dve", lambda e, pm_=pm_, t_=t_: e.tensor_tensor(out=g3(t_[:]), in0=g3(pm_[:]), in1=bc(bsT[:]), op=ALU.add), reads=[pm_.r(), bsT.r()], writes=[t_.r()])
            S.op("dve", lambda e, t_=t_, gl_=gl_: e.tensor_tensor(out=t_[:], in0=t_[:], in1=gl_[:, 0:256], op=ALU.mult), reads=[t_.r(), gl_.r()], writes=[t_.r()])
            S.op("dve", lambda e, t_=t_, sz_=sz_, y_=y_: e.tensor_tensor(out=y_[:], in0=t_[:], in1=sz_[:], op=ALU.mult), reads=[t_.r(), sz_.r()], writes=[y_.r()])
            S.op("dve", lambda e, y_=y_, yb_=yb_: e.tensor_copy(out=yb_[:], in_=y_[:]), reads=[y_.r()], writes=[yb_.r()])
            _dbg_y(c, b, tt, 256, y_[:], y_.r())
            if tt >= 1:
                _y_to_yT(c, pT, yb[(tt - 1) % 2], yb[(tt - 1) % 2].r(), 1, tt - 1, (tt - 1) % 2)
        _y_to_yT(c, pT, yb[(NTT - 1) % 2], yb[(NTT - 1) % 2].r(), 1, NTT - 1, (NTT - 1) % 2)
        if ext_st is None:
            S.emit()


SLOPES = [2.0 ** (-8.0 * (h + 1) / 4) for h in range(4)]
NEG = -1e30


def nsa_consts():
    p = np.arange(128)[:, None, None]
    h_sl = np.array(SLOPES, dtype=np.float64)[None, :, None]
    m = (np.arange(248) - 120)[None, None, :]
    dist = p - 16 * m - 31
    wc = np.where(dist >= 0, -h_sl * dist, NEG).astype(np.float32)
    jp = (np.arange(62) - 30)[None, :]
    cur = (np.arange(128) >= 64).astype(np.int64)[:, None]
    sb = np.where(jp > cur, NEG, np.where((jp == cur) | (jp == cur - 1), 1e4, 0.0)).astype(np.float32)
    tq = np.arange(128)[None, None, None, :]
    pk = np.arange(128)[:, None, None, None]
    dl = np.arange(16)[None, :, None, None]
    hs = np.array(SLOPES, dtype=np.float64)[None, None, :, None]
    d4 = 128 * dl + tq - pk
    bb = np.where(d4 >= 0, -hs * d4, NEG).astype(np.float32)
    bw2 = np.where(d4[:, 2:3] < 256, bb[:, 2:3], NEG).astype(np.float32)[:, 0]
    j = np.arange(32)[:, None, None]
    kb = np.arange(16)[None, :, None]
    pp = (np.arange(128) >= 64).astype(np.int64)[None, None, :]
    esel = (j == 2 * kb + pp).astype(np.float32)
    return {"nsa_wc": wc, "nsa_sb": sb, "nsa_bb": bb.reshape(128, 16, 512), "nsa_bw2": bw2.reshape(128, 512), "nsa_esel": esel}


def mixerD(c, l, b):
    nc, S = c.nc, c.S
    P = c.p
    with ExitStack() as st:
        A = lambda n, s, d, psum=False: _alloc(nc, st, n, s, d, psum)
        TM = lambda n, s, d, k=2, psum=False: _tmp(nc, st, n, s, d, k, psum)
        wc = A("md_wc", [128, 4, 248], F32)
        sbt = A("md_sb", [128, 62], F32)
        bb = A("md_bb", [128, 16, 512], F32)
        bw2 = A("md_bw2", [128, 512], F32)
        esel = A("md_esel", [32, 16, 128], BF16)
        for tl, nm in ((wc, "nsa_wc"), (sbt, "nsa_sb"), (bb, "nsa_bb"), (bw2, "nsa_bw2")):
            S.op("sp", lambda e, tl=tl, nm=nm: e.dma_start(out=tl[:], in_=P[nm]), writes=[tl.r()], dma=True)
        qT = A("md_qT", [64, 4, T], BF16)
        ksT = A("md_ksT", [64, T], BF16)
        kwT = A("md_kwT", [64, T], BF16)
        vs = A("md_vs", [128, NTT, 65], BF16)
        vw = A("md_vw", [128, NTT, 65], BF16)
        selT = A("md_selT", [32, T], BF16)
        kcmpT = A("md_kcmpT", [64, 128], BF16)
        vcmp = A("md_vcmp", [128, 64], BF16)
        ps_c = A("md_ps_c", [128, 4, 128], F32, psum=True)
        pmisc = A("md_pmisc", [128, 1024], BF16, psum=True)
        ps_s = TM("md_ps_s", [128, 512], F32, 2, psum=True)
        pmask4 = A("md_pmask", [128, 4, 128], F32, psum=True)
        o_s = A("md_o_s", [128, 4, 65], F32, psum=True)
        o_w = A("md_o_w", [128, 4, 65], F32, psum=True)
        o_c = A("md_o_c", [128, 4, 64], F32, psum=True)

        pst = ExitStack()
        A0, TM0 = A, TM
        A = lambda n, s, d, psum=False: _alloc(nc, pst, n, s, d, psum)
        TM = lambda n, s, d, k=2, psum=False: _tmp(nc, pst, n, s, d, k, psum)
        stg = TM("md_stg", [64, T], F32)
        esel32 = A("md_esel32", [32, 16, 128], F32)
        S.op("sp", lambda e: e.dma_start(out=esel32[:], in_=P["nsa_esel"]), writes=[esel32.r()], dma=True)
        S.op("act", lambda e: e.copy(out=esel[:], in_=esel32[:]), reads=[esel32.r()], writes=[esel.r()])
        cl = [A(f"md_cl{i}", [64, T], BF16) for i in range(2)]
        ch = [A(f"md_ch{i}", [64, T], BF16) for i in range(2)]
        vstg = A("md_vstg", [128, NTT, 128], F32)
        posT = A("md_posT", [64, 2, 32], F32)
        w1s = A("md_w1s", [64, 32, 64], F32)
        w1 = [A(f"md_w1{i}", [64, 32, 64], BF16) for i in range(2)]
        w2s = A("md_w2s", [64, 2, 64], F32)
        w2 = A("md_w2", [64, 2, 64], BF16)
        h1T = [A(f"md_h1T{i}", [64, 128], BF16) for i in range(2)]
        Fsb, Tsb = c.Fs[b], c.Ts[b]
        for h in range(4):
            sg = stg[h % 2]
            S.op("sp", lambda e, sg=sg, h=h: e.dma_start(out=sg[:], in_=Fsb[F_NQ + h * 64:F_NQ + (h + 1) * 64, :]),
                 reads=[c.Fs_r[b]], writes=[sg.r()], dma=True)
            S.op("act", lambda e, sg=sg, h=h: e.activation(out=qT[:, h, :], in_=sg[:], func=AF.Copy, scale=0.125), reads=[sg.r()], writes=[qT.r(h)])
        qT_all = [qT.r(h) for h in range(4)]
        for i, (row, dst) in enumerate(((F_KS, ksT), (F_KW, kwT))):
            sg = stg[i % 2]
            S.op("sp", lambda e, sg=sg, row=row: e.dma_start(out=sg[:], in_=Fsb[row:row + 64, :]), reads=[c.Fs_r[b]], writes=[sg.r()], dma=True)
            S.op("act", lambda e, sg=sg, dst=dst: e.copy(out=dst[:], in_=sg[:]), reads=[sg.r()], writes=[dst.r()])
        S.op("sp", lambda e: e.dma_start(out=posT[:], in_=P["cmp_posT"][l]), writes=[posT.r()], dma=True)
        v3 = lambda ap: ap.rearrange("p (n s) -> p n s", s=16)
        for i, row in enumerate((F_KC, F_VC)):
            sg = stg[i % 2]
            S.op("sp", lambda e, sg=sg, row=row: e.dma_start(out=sg[:], in_=Fsb[row:row + 64, :]), reads=[c.Fs_r[b]], writes=[sg.r()], dma=True)
            S.op("dve", lambda e, sg=sg, i=i: e.tensor_tensor(out=v3(cl[i][:]), in0=v3(sg[:]), in1=posT[:, i, 0:16].unsqueeze(1).to_broadcast([64, 128, 16]), op=ALU.add),
                 reads=[sg.r(), posT.r()], writes=[cl[i].r()])
            S.op("dve", lambda e, sg=sg, i=i: e.tensor_tensor(out=v3(ch[i][:]), in0=v3(sg[:]), in1=posT[:, i, 16:32].unsqueeze(1).to_broadcast([64, 128, 16]), op=ALU.add),
                 reads=[sg.r(), posT.r()], writes=[ch[i].r()])
        S.op("sp", lambda e: e.dma_start(out=vstg[:], in_=Tsb[:, T_VS:T_VS + 128].rearrange("(n p) c -> p n c", p=128)),
             reads=[c.Ts_r[b]], writes=[vstg.r()], dma=True)
        for i, dst in enumerate((vs, vw)):
            S.op("pool", lambda e, dst=dst: e.memset(dst[:, :, 64:65], 1.0), writes=[dst.r()])
            S.op("act", lambda e, dst=dst, i=i: e.copy(out=dst[:, :, 0:64], in_=vstg[:, :, i * 64:(i + 1) * 64]), reads=[vstg.r()], writes=[dst.r()])
        S.op("sp", lambda e: e.dma_start(out=w2s[:], in_=P["nsa_cmp_w2"][l].rearrange("i h e -> h i e")), writes=[w2s.r()], dma=True)
        S.op("act", lambda e: e.copy(out=w2[:], in_=w2s[:]), reads=[w2s.r()], writes=[w2.r()])
        S.op("pool", lambda e: e.memset(kcmpT[:], 0.0), writes=[kcmpT.r()])
        S.op("pool", lambda e: e.memset(vcmp[:], 0.0), writes=[vcmp.r()])
        for i in range(2):
            S.op("sp", lambda e, i=i: e.dma_start(out=w1s[:], in_=P["nsa_cmp_w1"][l, i].rearrange("(p e) h -> e p h", e=64)), writes=[w1s.r()], dma=True)
            S.op("act", lambda e, i=i: e.copy(out=w1[i][:], in_=w1s[:]), reads=[w1s.r()], writes=[w1[i].r()])
            ph = ps_s[i]

            def mm1(e, i=i, ph=ph):
                for pp in range(32):
                    src = cl[i] if pp < 16 else ch[i]
                    if pp < 16:
                        rhs = v3(src[:])[:, 0:127, pp]
                    else:
                        rhs = v3(src[:])[:, 1:128, pp - 16]
                    ins = e.matmul(ph[0:64, 0:127], lhsT=w1[i][:, pp, :], rhs=rhs, start=(pp == 0), stop=(pp == 31))
                return ins
            S.op("pe", mm1, reads=[w1[i].r(), cl[i].r(), ch[i].r()], writes=[ph.r()])
            S.op("act", lambda e, i=i, ph=ph: e.activation(out=h1T[i][:, 0:127], in_=ph[0:64, 0:127], func=AF.Gelu_apprx_tanh), reads=[ph.r()], writes=[h1T[i].r()])
        S.op("pe", lambda e: e.matmul(ps_s[0][0:64, 0:127], lhsT=w2[:, 0, :], rhs=h1T[0][:, 0:127], start=True, stop=True),
             reads=[w2.r(), h1T[0].r()], writes=[ps_s[0].r()])
        S.op("dve", lambda e: e.tensor_copy(out=kcmpT[:, 0:127], in_=ps_s[0][0:64, 0:127]), reads=[ps_s[0].r()], writes=[kcmpT.r()])
        S.op("pe", lambda e: e.matmul(ps_s[1][0:127, 0:64], lhsT=h1T[1][:, 0:127], rhs=w2[:, 1, :], start=True, stop=True),
             reads=[w2.r(), h1T[1].r()], writes=[ps_s[1].r()])
        S.op("dve", lambda e: e.tensor_copy(out=vcmp[0:127, :], in_=ps_s[1][0:127, 0:64]), reads=[ps_s[1].r()], writes=[vcmp.r()])

        S.emit()
        pst.close()
        A, TM = A0, TM0
        sbuf2 = TM("md_sbuf", [128, 4, 128], F32)
        mx = TM("md_mx", [128, 4], F32)
        sm = TM("md_sm", [128, 4], F32)
        p32 = TM("md_p32", [128, 4, 128], F32)
        pb = TM("md_pb", [128, 4, 128], BF16)
        pTc = TM("md_pTc", [128, 4, 128], BF16)
        psum_h = TM("md_psh", [128, 128], F32)
        a32 = TM("md_a32", [128, 32], F32)
        imp = TM("md_imp", [128, 32], F32)
        top8 = TM("md_top8", [128, 8], F32)
        selb = TM("md_selb", [128, 32], BF16)
        tmp = TM("md_tmp", [128, 512], F32, 3)
        maskb = TM("md_maskb", [128, NTT, 128], BF16)
        eb = TM("md_eb", [128, 512], BF16, 3)
        pm = TM("md_pm", [128, 512], BF16, 3)
        oc = TM("md_oc", [128, 256], F32, 3)
        osn = TM("md_osn", [128, 256], F32)
        own = TM("md_own", [128, 256], F32)
        rsw = TM("md_rsw", [128, 8], F32)
        gz = TM("md_gz", [128, 12 + 256], F32)
        gsg = TM("md_gsg", [128, 12], F32)
        szz = TM("md_szz", [128, 256], F32)
        yy = TM("md_yy", [128, 256], F32)
        yb = TM("md_yb", [128, 256], BF16)
        pT_y = [Tile(pmisc.t) for _ in range(2)]
        g4 = lambda ap: ap.rearrange("p (h c) -> p h c", h=4)
        bc4 = lambda ap, n: ap.unsqueeze(2).to_broadcast([128, 4, n])
        cntl = [0]

        def cmp_part(tt):
            i = tt % 2
            tsl = slice(tt * 128, (tt + 1) * 128)
            def mmc(e):
                for h in range(4):
                    ins = e.matmul(ps_c[:, h, :], lhsT=qT[:, h, tsl], rhs=kcmpT[:], start=True, stop=True)
                return ins
            S.op("pe", mmc, reads=qT_all + [kcmpT.r()], writes=[ps_c.r()])
            sb_, mx_, sm_, p32_, pb_, pTc_ = sbuf2[i], mx[i], sm[i], p32[i], pb[i], pTc[i]
            o0 = 120 - 8 * tt
            S.op("dve", lambda e: e.tensor_tensor(out=sb_[:], in0=ps_c[:], in1=wc[:, :, o0:o0 + 128], op=ALU.add), reads=[ps_c.r(), wc.r()], writes=[sb_.r()])
            S.op("dve", lambda e: e.tensor_reduce(out=mx_[:], in_=sb_[:], axis=AX.X, op=ALU.max), reads=[sb_.r()], writes=[mx_.r()])
            S.op("dve", lambda e: e.tensor_scalar(out=mx_[:], in0=mx_[:], scalar1=-1e4, scalar2=-1.0, op0=ALU.max, op1=ALU.mult), reads=[mx_.r()], writes=[mx_.r()])
            S.op("dve", lambda e: e.tensor_tensor(out=sb_[:], in0=sb_[:], in1=bc4(mx_[:], 128), op=ALU.add), reads=[sb_.r(), mx_.r()], writes=[sb_.r()])
            S.op("act", lambda e: e.activation(out=sb_[:], in_=sb_[:], func=AF.Exp), reads=[sb_.r()], writes=[sb_.r()])
            S.op("dve", lambda e: e.tensor_reduce(out=sm_[:], in_=sb_[:], axis=AX.X, op=ALU.add), reads=[sb_.r()], writes=[sm_.r()])
            S.op("dve", lambda e: e.tensor_scalar(out=sm_[:], in0=sm_[:], scalar1=1e-30, scalar2=None, op0=ALU.add), reads=[sm_.r()], writes=[sm_.r()])
            S.op("dve", lambda e: e.reciprocal(out=sm_[:], in_=sm_[:]), reads=[sm_.r()], writes=[sm_.r()])
            S.op("dve", lambda e: e.tensor_tensor(out=p32_[:], in0=sb_[:], in1=bc4(sm_[:], 128), op=ALU.mult), reads=[sb_.r(), sm_.r()], writes=[p32_.r()])
            S.op("act", lambda e: e.copy(out=pb_[:], in_=p32_[:]), reads=[p32_.r()], writes=[pb_.r()])
            def trc(e):
                for h in range(4):
                    ins = e.transpose(out=pmisc[:, h * 128:(h + 1) * 128], in_=pb_[:, h, :], identity=c.identb[:])
                return ins
            S.op("pe", trc, reads=[pb_.r()], writes=[pmisc.r()])
            S.op("act", lambda e: e.copy(out=pTc_[:].rearrange("p h n -> p (h n)"), in_=pmisc[:, 0:512]), reads=[pmisc.r()], writes=[pTc_.r()])

            def mmoc(e):
                for h in range(4):
                    ins = e.matmul(o_c[:, h, :], lhsT=pTc_[:, h, :], rhs=vcmp[:], start=True, stop=True)
                return ins
            S.op("pe", mmoc, reads=[pTc_.r(), vcmp.r()], writes=[o_c.r()])
            oc_ = oc[tt % 3]
            S.op("act", lambda e: e.copy(out=oc_[:], in_=o_c[:].rearrange("p h c -> p (h c)")), reads=[o_c.r()], writes=[oc_.r()])
            ph_, a_, imp_, t8_, selb_ = psum_h[i], a32[i], imp[i], top8[i], selb[i]
            S.op("dve", lambda e: e.tensor_reduce(out=ph_[:], in_=p32_[:].rearrange("p h n -> p n h"), axis=AX.X, op=ALU.add), reads=[p32_.r()], writes=[ph_.r()])
            pv = lambda ap: ap.rearrange("p (j m) -> p j m", m=4)
            S.op("dve", lambda e: e.tensor_reduce(out=a_[:], in_=pv(ph_[:]), axis=AX.X, op=ALU.add), reads=[ph_.r()], writes=[a_.r()])
            S.op("dve", lambda e: e.scalar_tensor_tensor(out=imp_[:], in0=pv(ph_[:])[:, :, 3], scalar=-0.5, in1=a_[:], op0=ALU.mult, op1=ALU.add),
                 reads=[ph_.r(), a_.r()], writes=[imp_.r()])
            S.op("dve", lambda e: e.scalar_tensor_tensor(out=imp_[:, 1:32], in0=pv(ph_[:])[:, 0:31, 3], scalar=0.5, in1=imp_[:, 1:32], op0=ALU.mult, op1=ALU.add),
                 reads=[ph_.r(), imp_.r()], writes=[imp_.r()])
            j0 = 30 - 2 * tt
            S.op("dve", lambda e: e.tensor_tensor(out=imp_[:], in0=imp_[:], in1=sbt[:, j0:j0 + 32], op=ALU.add), reads=[imp_.r(), sbt.r()], writes=[imp_.r()])
            S.op("dve", lambda e: e.tensor_scalar(out=imp_[:, 0:1], in0=imp_[:, 0:1], scalar1=1e4, scalar2=None, op0=ALU.add), reads=[imp_.r()], writes=[imp_.r()])
            S.op("dve", lambda e: e.max(out=t8_[:], in_=imp_[:]), reads=[imp_.r()], writes=[t8_.r()])
            S.op("dve", lambda e: e.tensor_scalar(out=selb_[:], in0=imp_[:], scalar1=t8_[:, 7:8], scalar2=None, op0=ALU.is_ge),
                 reads=[imp_.r(), t8_.r()], writes=[selb_.r()])
            S.op("pe", lambda e: e.transpose(out=pmisc[0:32, 512:640], in_=selb_[:], identity=c.identb[:]), reads=[selb_.r()], writes=[pmisc.r()])
            S.op("act", lambda e: e.copy(out=selT[:, tsl], in_=pmisc[0:32, 512:640]), reads=[pmisc.r()], writes=[selT.r(tt)])
            mk_ = maskb[i]
            for k0 in range(0, tt + 1, 4):
                k1 = min(k0 + 4, tt + 1)

                def mmm(e, k0=k0, k1=k1):
                    for kb in range(k0, k1):
                        ins = e.matmul(pmask4[:, kb - k0, :], lhsT=esel[:, kb, :], rhs=selT[:, tsl], start=True, stop=True)
                    return ins
                S.op("pe", mmm, reads=[esel.r(), selT.r(tt)], writes=[pmask4.r()])
                S.op("act", lambda e, k0=k0, k1=k1: e.copy(out=mk_[:, k0:k1, :], in_=pmask4[:, 0:k1 - k0, :]), reads=[pmask4.r()], writes=[mk_.r()])

        def att_part(tt):
            i = tt % 2
            tsl = slice(tt * 128, (tt + 1) * 128)
            oc_, osn_, own_, rs_ = oc[tt % 3], osn[i], own[i], rsw[i]
            mk_ = maskb[i]
            kbs = [kb for kb in (tt - 2, tt - 1, tt) if kb >= 0]
            its = [("s", kb) for kb in range(tt + 1)] + [("w", kb) for kb in kbs]
            bufs = []
            for _ in its:
                bufs.append((cntl[0] % 3, cntl[0] % 2))
                cntl[0] += 1

            def front(k):
                kind, kb = its[k]
                j, jj = bufs[k]
                pss, tmp_, eb_, pm_ = ps_s[jj], tmp[j], eb[j], pm[j]
                ksl = slice(kb * 128, (kb + 1) * 128)
                kT = ksT if kind == "s" else kwT
                S.op("pe", lambda e: e.matmul(pss[:], lhsT=kT[:, ksl], rhs=qT[:, :, tsl], start=True, stop=True), reads=qT_all + [kT.r()], writes=[pss.r()])
                dlt = tt - kb
                btab = bw2[:] if (kind == "w" and dlt == 2) else bb[:, dlt, :]
                S.op("dve", lambda e: e.tensor_tensor(out=tmp_[:], in0=pss[:], in1=btab, op=ALU.add), reads=[pss.r(), bb.r(), bw2.r()], writes=[tmp_.r()])
                S.op("act", lambda e: e.activation(out=eb_[:], in_=tmp_[:], func=AF.Exp), reads=[tmp_.r()], writes=[eb_.r()])
                if kind == "s":
                    S.op("pool", lambda e: e.tensor_tensor(out=g4(pm_[:]), in0=g4(eb_[:]), in1=mk_[:, kb, :].unsqueeze(1).to_broadcast([128, 4, 128]), op=ALU.mult),
                         reads=[eb_.r(), mk_.r()], writes=[pm_.r()])

            def back(k):
                kind, kb = its[k]
                j, jj = bufs[k]
                eb_, pm_ = eb[j], pm[j]
                if kind == "s":
                    def mmpv(e):
                        for h in range(4):
                            ins = e.matmul(o_s[:, h, :], lhsT=pm_[:, h * 128:(h + 1) * 128], rhs=vs[:, kb, :], start=(kb == 0 and h == 0), stop=(kb == tt and h == 3), skip_group_check=True)
                        return ins
                    S.op("pe", mmpv, reads=[pm_.r(), vs.r()], writes=[o_s.r()], pe_acc=(kb > 0))
                else:
                    first, last = (kb == kbs[0]), (kb == kbs[-1])

                    def mmpw(e):
                        for h in range(4):
                            ins = e.matmul(o_w[:, h, :], lhsT=eb_[:, h * 128:(h + 1) * 128], rhs=vw[:, kb, :], start=(first and h == 0), stop=(last and h == 3), skip_group_check=True)
                        return ins
                    S.op("pe", mmpw, reads=[eb_.r(), vw.r()], writes=[o_w.r()], pe_acc=(not first))
            nit = len(its)
            front(0)
            if nit > 1:
                front(1)
            for k in range(nit):
                back(k)
                if k + 2 < nit:
                    front(k + 2)

        def att_tail(tt):
            i = tt % 2
            tsl = slice(tt * 128, (tt + 1) * 128)
            oc_, osn_, own_, rs_ = oc[tt % 3], osn[i], own[i], rsw[i]
            S.op("dve", lambda e: e.reciprocal(out=rs_[:, 0:4], in_=o_s[:, :, 64]), reads=[o_s.r()], writes=[rs_.r()])
            S.op("dve", lambda e: e.tensor_tensor(out=g4(osn_[:]), in0=o_s[:, :, 0:64], in1=bc4(rs_[:, 0:4], 64), op=ALU.mult), reads=[o_s.r(), rs_.r()], writes=[osn_.r()])
            S.op("dve", lambda e: e.reciprocal(out=rs_[:, 4:8], in_=o_w[:, :, 64]), reads=[o_w.r()], writes=[rs_.r()])
            S.op("dve", lambda e: e.tensor_tensor(out=g4(own_[:]), in0=o_w[:, :, 0:64], in1=bc4(rs_[:, 4:8], 64), op=ALU.mult), reads=[o_w.r(), rs_.r()], writes=[own_.r()])
            gz_, gs_, sz_, y_, yb_ = gz[i], gsg[i], szz[i], yy[i], yb[i]
            S.op("sp", lambda e: e.dma_start(out=gz_[:, 0:12], in_=Tsb[tsl, T_NG:T_NG + 12]), reads=[c.Ts_r[b]], writes=[gz_.r()], dma=True)
            S.op("sp", lambda e: e.dma_start(out=gz_[:, 12:268], in_=Tsb[tsl, T_NZ:T_NZ + 256]), reads=[c.Ts_r[b]], writes=[gz_.r(1)], dma=True)
            S.op("act", lambda e: e.activation(out=gs_[:], in_=gz_[:, 0:12], func=AF.Sigmoid), reads=[gz_.r()], writes=[gs_.r()])
            S.op("act", lambda e: e.activation(out=sz_[:], in_=gz_[:, 12:268], func=AF.Silu), reads=[gz_.r(1)], writes=[sz_.r()])
            gv = lambda k: gs_[:].rearrange("p (h k) -> p h k", k=3)[:, :, k].unsqueeze(2).to_broadcast([128, 4, 64])
            S.op("dve", lambda e: e.tensor_tensor(out=g4(oc_[:]), in0=g4(oc_[:]), in1=gv(0), op=ALU.mult), reads=[oc_.r(), gs_.r()], writes=[oc_.r()])
            S.op("dve", lambda e: e.tensor_tensor(out=g4(osn_[:]), in0=g4(osn_[:]), in1=gv(1), op=ALU.mult), reads=[osn_.r(), gs_.r()], writes=[osn_.r()])
            S.op("dve", lambda e: e.tensor_tensor(out=g4(own_[:]), in0=g4(own_[:]), in1=gv(2), op=ALU.mult), reads=[own_.r(), gs_.r()], writes=[own_.r()])
            S.op("dve", lambda e: e.tensor_tensor(out=oc_[:], in0=oc_[:], in1=osn_[:], op=ALU.add), reads=[oc_.r(), osn_.r()], writes=[oc_.r()])
            S.op("dve", lambda e: e.tensor_tensor(out=oc_[:], in0=oc_[:], in1=own_[:], op=ALU.add), reads=[oc_.r(), own_.r()], writes=[oc_.r()])
            S.op("dve", lambda e: e.tensor_tensor(out=y_[:], in0=oc_[:], in1=sz_[:], op=ALU.mult), reads=[oc_.r(), sz_.r()], writes=[y_.r()])
            S.op("act", lambda e: e.copy(out=yb_[:], in_=y_[:]), reads=[y_.r()], writes=[yb_.r()])
            _dbg_y(c, b, tt, 768, y_[:], y_.r())

            def tr(e):
                for jx in range(2):
                    ins = e.transpose(out=pmisc[:, 640 + jx * 128:640 + (jx + 1) * 128], in_=yb_[:, jx * 128:(jx + 1) * 128], identity=c.identb[:])
                return ins
            S.op("pe", tr, reads=[yb_.r()], writes=[pmisc.r()])
            S.op("act", lambda e: e.copy(out=c.yT[:, 6:8, tsl], in_=pmisc[:, 640:896].rearrange("p (j n) -> p j n", j=2)),
                 reads=[pmisc.r()], writes=[c.yT.r((3, tt))])

        def rec(fn, t_):
            S.defer = []
            fn(t_)
            lst, S.defer = S.defer, None
            return lst
        S.replay(rec(cmp_part, 0))
        for tt in range(NTT + 1):
            lists = []
            if tt < NTT:
                lists.append(rec(att_part, tt))
            if tt >= 1:
                lists.append(rec(att_tail, tt - 1))
            if tt + 1 < NTT:
                lists.append(rec(cmp_part, tt + 1))
            S.replay(*lists)
        S.emit()


def _dplr_loop(c, st, b, pfx, AQ, Bt, Kt, BKV, Pend, bonT, l):
    nc, S = c.nc, c.S
    P = c.p
    Tsb = c.Ts[b]
    NCH = T // 64
    NPR = NCH // 2
    A = lambda n, s, d, psum=False: _alloc(nc, st, pfx + n, s, d, psum)
    TM = lambda n, s, d, k=2, psum=False: _tmp(nc, st, pfx + n, s, d, k, psum)
    pTr = A("_pTr", [128, 1024], BF16, psum=True)
    pA = A("_pA", [64, 8, 128], F32, psum=True)
    pB = A("_pB", [64, 8, 128], F32, psum=True)
    pXU = A("_pXU", [64, 2, 256], F32, psum=True)
    pO = A("_pO", [128, 512], F32, psum=True)
    pHd = A("_pHd", [128, 2, 256], F32, psum=True)
    H = A("_H", [128, 2, 64], F32)
    Hbs = TM("_Hb", [128, 2, 64], BF16)
    Hs = A("_Hs", [128, 2, 64], F32)
    tokm = TM("_tokm", [64, 2, 6, 128], BF16)
    NY = TM("_NY", [64, 8, 128], BF16, 2)
    R = TM("_R", [64, 8, 64], BF16, 2)
    TTs = TM("_TT", [64, 8, 64], BF16)
    MQ = TM("_MQ", [64, 8, 64], BF16)
    LM = TM("_LM", [64, 8, 128], BF16)
    XS = TM("_XS", [64, 256], BF16)
    Ub = TM("_Ub", [64, 256], BF16)
    gng = A("_gng", [128, 256], F32)
    gnb = A("_gnb", [128, 256], F32)
    S.op("sp", lambda e: e.dma_start(out=gng[:], in_=P["rw_gn_gain"][l:l + 1, :].partition_broadcast(128)), writes=[gng.r()], dma=True)
    S.op("sp", lambda e: e.dma_start(out=gnb[:], in_=P["rw_gn_bias"][l:l + 1, :].partition_broadcast(128)), writes=[gnb.r()], dma=True)
    S.op("dve", lambda e: e.memset(H[:], 0.0), writes=[H.r()])
    S.op("dve", lambda e: e.memset(Hbs[0][:], 0.0), writes=[Hbs[0].r()])
    o32 = TM("_o32", [128, 256], F32)
    m4 = TM("_m4", [128, 4], F32)
    v4 = TM("_v4", [128, 4], F32)
    xc = TM("_xc", [128, 256], F32)
    t1 = TM("_t1", [128, 256], F32)
    zin = TM("_zin", [128, 256], F32)
    bon = TM("_bon", [128, 256], BF16)
    y32 = TM("_y32", [128, 256], F32)
    yb = TM("_yb", [128, 256], BF16)
    g3 = lambda ap: ap.rearrange("p (g c) -> p g c", g=4)
    bc = lambda ap: ap.unsqueeze(2).to_broadcast([128, 4, 64])
    M1 = c.m1cat
    ML = c.maskL
    I64 = c.ident[0:64, 0:64]
    bc8 = lambda ap, n: ap.unsqueeze(1).to_broadcast([64, 8, n])
    AQm = AQ
    AQr = [a.r(k) for a in AQm for k in ((0, 0), (0, 1), (1, 0), (1, 1))]
    BKVr = [BKV.r((cc, i)) for cc in range(2) for i in range(3)]

    def phase1(m):
        i2 = m % 2
        tk, mq, lm, tts = tokm[i2], MQ[i2], LM[i2], TTs[i2]
        for ci in range(2):
            n = 2 * m + ci

            def tra(e, n=n):
                for cc in range(2):
                    for it in range(3):
                        ins = e.transpose(out=pTr[0:64, (cc * 3 + it) * 128:(cc * 3 + it + 1) * 128], in_=BKV[:, cc, n, it, :], identity=c.identb[:])
                return ins
            S.op("pe", tra, reads=BKVr, writes=[pTr.r()])
            S.op("act", lambda e, ci=ci: e.copy(out=tk[:, ci, :, :].rearrange("p a b -> p (a b)"), in_=pTr[0:64, 0:768]), reads=[pTr.r()], writes=[tk.r()])

        def mmx2(e):
            for ci in range(2):
                n = 2 * m + ci
                nsl = slice(n * 64, (n + 1) * 64)
                for h in range(4):
                    cc = h // 2
                    ins = e.matmul(pB[:, ci * 4 + h, :], lhsT=Kt[:, cc, nsl], rhs=AQm[h % 2][:, cc, n, :], start=True, stop=True)
            return ins
        S.op("pe", mmx2, reads=AQr + [Kt.r(0), Kt.r(1)], writes=[pB.r()])
        S.op("dve", lambda e: e.tensor_tensor(out=lm[:], in0=pB[:], in1=bc8(M1[:], 128), op=ALU.mult), reads=[pB.r(), M1.r()], writes=[lm.r()])

        def mmx1(e):
            for ci in range(2):
                n = 2 * m + ci
                nsl = slice(n * 64, (n + 1) * 64)
                for h in range(4):
                    cc = h // 2
                    aq = AQm[h % 2]
                    e.matmul(pA[:, ci * 4 + h, :], lhsT=Bt[:, cc, nsl], rhs=aq[:, cc, n, :], start=True, stop=True)
                    ins = e.matmul(pB[:, ci * 4 + h, 0:64], lhsT=aq[:, cc, n, 0:64], rhs=Bt[:, cc, nsl], start=True, stop=True)
            return ins
        S.op("pe", mmx1, reads=AQr + [Bt.r(0), Bt.r(1)], writes=[pA.r(), pB.r()])
        ny, r_ = NY[0], R[0]
        S.op("dve", lambda e: e.tensor_tensor(out=ny[:, :, 0:64], in0=pA[:, :, 0:64], in1=bc8(M1[:, 0:64], 64), op=ALU.mult), reads=[pA.r(), M1.r()], writes=[ny.r()])
        S.op("dve", lambda e: e.tensor_tensor(out=ny[:, :, 64:128], in0=ny[:, :, 0:64], in1=bc8(I64, 64), op=ALU.add), reads=[ny.r(), c.ident.r()], writes=[ny.r()])
        S.op("dve", lambda e: e.tensor_tensor(out=mq[:], in0=pA[:, :, 64:128], in1=bc8(M1[:, 64:128], 64), op=ALU.mult), reads=[pA.r(), M1.r()], writes=[mq.r()])
        S.op("dve", lambda e: e.tensor_tensor(out=r_[:], in0=pB[:, :, 0:64], in1=bc8(ML[:], 64), op=ALU.mult), reads=[pB.r(), ML.r()], writes=[r_.r()])
        _neumann(S, NY, R, pA, pB, 8, tts)

    def phase2(m):
        i2 = m % 2
        tk, mq, lm, tts = tokm[i2], MQ[i2], LM[i2], TTs[i2]
        for ci in range(2):
            n = 2 * m + ci
            p0 = ci * 4
            xs_, ub_ = XS[ci], Ub[ci]
            Hb, Hbn = Hbs[n % 2], Hbs[(n + 1) % 2]
            S.op("dve", lambda e, n=n: e.tensor_tensor(out=Hs[:], in0=H[:], in1=Pend[:, :, n].unsqueeze(2).to_broadcast([128, 2, 64]), op=ALU.mult),
                 reads=[H.r(), Pend.r(0), Pend.r(1)], writes=[Hs.r()])

            def mmd(e, n=n, ci=ci, p0=p0, Hb=Hb):
                for h in range(4):
                    cc, hp = h // 2, h % 2
                    e.matmul(pXU[:, 0, h * 64:(h + 1) * 64], lhsT=lm[:, p0 + h, 0:64], rhs=tk[:, ci, cc * 3 + 2, hp * 64:(hp + 1) * 64], start=True, stop=False)
                    ins = e.matmul(pXU[:, 0, h * 64:(h + 1) * 64], lhsT=AQm[hp][:, cc, n, 0:64], rhs=Hb[:, cc, :], start=False, stop=True)
                return ins
            S.op("pe", mmd, reads=[lm.r(), tk.r(), Hb.r()] + AQr, writes=[pXU.r()])
            S.op("act", lambda e, xs_=xs_: e.copy(out=xs_[:], in_=pXU[:, 0, :]), reads=[pXU.r()], writes=[xs_.r()])

            def mme(e, xs_=xs_, p0=p0):
                for h in range(4):
                    ins = e.matmul(pXU[:, 1, h * 64:(h + 1) * 64], lhsT=tts[:, p0 + h, :], rhs=xs_[:, h * 64:(h + 1) * 64], start=True, stop=True)
                return ins
            S.op("pe", mme, reads=[tts.r(), xs_.r()], writes=[pXU.r()])
            S.op("dve", lambda e, ub_=ub_: e.tensor_copy(out=ub_[:], in_=pXU[:, 1, :]), reads=[pXU.r()], writes=[ub_.r()])
            half = ci * 64

            def mmg(e, ci=ci, ub_=ub_):
                for h in range(4):
                    cc, hp = h // 2, h % 2
                    hs_ = slice(hp * 64, (hp + 1) * 64)
                    o_ = pHd[hs_, cc, 0:64]
                    e.matmul(o_, lhsT=tk[:, ci, cc * 3 + 0, hs_], rhs=ub_[:, h * 64:(h + 1) * 64], start=True, stop=False)
                    ins = e.matmul(o_, lhsT=tk[:, ci, cc * 3 + 1, hs_], rhs=tk[:, ci, cc * 3 + 2, hs_], start=False, stop=True)
                return ins
            S.op("pe", mmg, reads=[tk.r(), ub_.r()], writes=[pHd.r()])
            S.op("dve", lambda e, Hbn=Hbn: e.tensor_tensor(out=Hbn[:], in0=Hs[:], in1=pHd[:, :, 0:64], op=ALU.add), reads=[Hs.r(), pHd.r()], writes=[Hbn.r()])

            def mmf(e, n=n, ci=ci, p0=p0, ub_=ub_, half=half, Hb=Hb):
                for h in range(4):
                    cc, hp = h // 2, h % 2
                    o_ = pO[half:half + 64, h * 64:(h + 1) * 64]
                    e.matmul(o_, lhsT=AQm[hp][:, cc, n, 64:128], rhs=Hb[:, cc, :], start=True, stop=False)
                    e.matmul(o_, lhsT=mq[:, p0 + h, :], rhs=ub_[:, h * 64:(h + 1) * 64], start=False, stop=False)
                    ins = e.matmul(o_, lhsT=lm[:, p0 + h, 64:128], rhs=tk[:, ci, cc * 3 + 2, hp * 64:(hp + 1) * 64], start=False, stop=True)
                return ins
            S.op("pe", mmf, reads=[lm.r(), mq.r(), tk.r(), Hb.r(), ub_.r()] + AQr, writes=[pO.r(half)])
            S.op("dve", lambda e: e.tensor_tensor(out=H[:], in0=Hs[:], in1=pHd[:, :, 0:64], op=ALU.add), reads=[Hs.r(), pHd.r()], writes=[H.r()])

    def post(m):
        tt = m
        i = tt % 2
        tsl = slice(tt * 128, (tt + 1) * 128)
        o_, m_, v_, xc_, t_, z_, bo_, y_, yb_ = o32[i], m4[i], v4[i], xc[i], t1[i], zin[i], bon[i], y32[i], yb[i]
        S.op("sp", lambda e: e.dma_start(out=z_[:], in_=Tsb[tsl, T_RZ:T_RZ + 256]), reads=[c.Ts_r[b]], writes=[z_.r()], dma=True)
        S.op("act", lambda e: e.activation(out=z_[:], in_=z_[:], func=AF.Silu), reads=[z_.r()], writes=[z_.r()])
        S.op("act", lambda e: e.copy(out=o_[:], in_=pO[:, 0:256]), reads=[pO.r(0), pO.r(64)], writes=[o_.r()])

        def trb(e):
            for cc in range(2):
                ins = e.transpose(out=pTr[:, 768 + cc * 128:768 + (cc + 1) * 128], in_=bonT[:, cc, tsl], identity=c.identb[:])
            return ins
        S.op("pe", trb, reads=[bonT.r((cc, tq)) for cc in range(2) for tq in range(4)], writes=[pTr.r()])
        S.op("act", lambda e: e.copy(out=bo_[:], in_=pTr[:, 768:1024]), reads=[pTr.r()], writes=[bo_.r()])
        S.op("dve", lambda e: e.tensor_reduce(out=m_[:], in_=g3(o_[:]), axis=AX.X, op=ALU.add), reads=[o_.r()], writes=[m_.r()])
        S.op("dve", lambda e: e.scalar_tensor_tensor(out=g3(xc_[:]), in0=bc(m_[:]), scalar=-1.0 / 64, in1=g3(o_[:]), op0=ALU.mult, op1=ALU.add),
             reads=[o_.r(), m_.r()], writes=[xc_.r()])
        S.op("dve", lambda e: e.tensor_tensor(out=t_[:], in0=xc_[:], in1=xc_[:], op=ALU.mult), reads=[xc_.r()], writes=[t_.r()])
        S.op("dve", lambda e: e.tensor_reduce(out=v_[:], in_=g3(t_[:]), axis=AX.X, op=ALU.add), reads=[t_.r()], writes=[v_.r()])
        S.op("act", lambda e: e.activation(out=v_[:], in_=v_[:], func=AF.Sqrt, scale=1.0 / 64, bias=c.epsgn[:, 0:1]), reads=[v_.r()], writes=[v_.r()])
        S.op("dve", lambda e: e.reciprocal(out=v_[:], in_=v_[:]), reads=[v_.r()], writes=[v_.r()])
        S.op("dve", lambda e: e.tensor_tensor(out=g3(t_[:]), in0=g3(xc_[:]), in1=bc(v_[:]), op=ALU.mult), reads=[xc_.r(), v_.r()], writes=[t_.r()])
        S.op("dve", lambda e: e.tensor_tensor(out=t_[:], in0=t_[:], in1=gng[:], op=ALU.mult), reads=[t_.r(), gng.r()], writes=[t_.r()])
        S.op("dve", lambda e: e.tensor_tensor(out=t_[:], in0=t_[:], in1=gnb[:], op=ALU.add), reads=[t_.r(), gnb.r()], writes=[t_.r()])
        S.op("dve", lambda e: e.tensor_tensor(out=t_[:], in0=t_[:], in1=bo_[:], op=ALU.add), reads=[t_.r(), bo_.r()], writes=[t_.r()])
        S.op("dve", lambda e: e.tensor_tensor(out=y_[:], in0=t_[:], in1=z_[:], op=ALU.mult), reads=[t_.r(), z_.r()], writes=[y_.r()])
        S.op("act", lambda e: e.copy(out=yb_[:], in_=y_[:]), reads=[y_.r()], writes=[yb_.r()])
        _dbg_y(c, b, tt, 512, y_[:], y_.r())

        def tr(e):
            for jx in range(2):
                ins = e.transpose(out=pTr[:, 768 + jx * 128:768 + (jx + 1) * 128], in_=yb_[:, jx * 128:(jx + 1) * 128], identity=c.identb[:])
            return ins
        S.op("pe", tr, reads=[yb_.r()], writes=[pTr.r()])
        S.op("act", lambda e: e.copy(out=c.yT[:, 4:6, tsl], in_=pTr[:, 768:1024].rearrange("p (j n) -> p j n", j=2)),
             reads=[pTr.r()], writes=[c.yT.r((2, tt))])

    def rec(fn, m):
        S.defer = []
        fn(m)
        lst, S.defer = S.defer, None
        return lst

    S.replay(rec(phase1, 0))
    for m in range(NPR + 1):
        lists = []
        if m < NPR:
            lists.append(rec(phase2, m))
        if m >= 1:
            lists.append(rec(post, m - 1))
        if m + 1 < NPR:
            lists.append(rec(phase1, m + 1))
        S.replay(*lists)


def mixerC(c, l, b):
    nc, S = c.nc, c.S
    P = c.p
    Fsb, Tsb = c.Fs[b], c.Ts[b]
    NCH = T // 64
    with ExitStack() as st:
        A = lambda n, s, d, psum=False: _alloc(nc, st, n, s, d, psum)
        TM = lambda n, s, d, k=2, psum=False: _tmp(nc, st, n, s, d, k, psum)
        AQ0 = A("mc_AQ0", [128, 2, NCH, 128], BF16)
        AQ1 = A("mc_AQ1", [128, 2, NCH, 128], BF16)
        AQ = AQ0
        Bt = A("mc_Bt", [128, 2, T], BF16)
        Kt = A("mc_Kt", [128, 2, T], BF16)
        BKV = A("mc_BKV", [128, 2, NCH, 3, 64], BF16)
        bonT = A("mc_bonT", [128, 2, T], BF16)
        Pend = A("mc_Pend", [128, 2, NCH], F32)
        cols = A("mc_cols", [128, 2, 5], F32)
        S.op("sp", lambda e: e.dma_start(out=cols[:], in_=P["rw_cols"][l]), writes=[cols.r()], dma=True)
        pst = ExitStack()
        A1 = lambda n, s, d, psum=False: _alloc(nc, pst, n, s, d, psum)
        muT = A1("mc_muT", [128, 7], F32)
        lora = A1("mc_lora", [128, 256], F32)
        rst = A1("mc_rst", [128, T], BF16)
        wdad = A1("mc_wdad", [128, T], F32)
        t_lw = A1("mc_tlw", [128, T], F32)
        t_a = A1("mc_ta", [128, T], F32)
        t_gc = A1("mc_tgc", [128, T], F32)
        t0 = A1("mc_t0", [128, T], F32)
        t1 = A1("mc_t1", [128, T], F32)
        t_x = A1("mc_tx", [128, T], F32)
        xr = t1
        pp = [A1(f"mc_pp{i}", [128, 512], F32, psum=True) for i in range(4)]
        S.op("sp", lambda e: e.dma_start(out=muT[:], in_=P["rw_muT"][l]), writes=[muT.r()], dma=True)
        S.op("sp", lambda e: e.dma_start(out=lora[:], in_=P["rw_lora"][l]), writes=[lora.r()], dma=True)
        S.op("pool", lambda e: e.memset(rst[:], 1.0), writes=[rst.r()])
        S.op("pool", lambda e: e.memset(rst[:].rearrange("p (n s) -> p n s", s=64)[:, :, 0:1], 0.0), writes=[rst.r()])
        c3 = lambda ap: ap.rearrange("p (n s) -> p n s", s=64)

        def load_shift(dst, row, mcol, tmp):
            S.op("sp", lambda e: e.dma_start(out=dst[:], in_=Fsb[row:row + 128, :]), reads=[c.Fs_r[b]], writes=[dst.r()], dma=True)
            S.op("dve", lambda e: e.tensor_tensor(out=tmp[:, 1:T], in0=dst[:, 0:T - 1], in1=dst[:, 1:T], op=ALU.subtract), reads=[dst.r()], writes=[tmp.r()])
            S.op("dve", lambda e: e.tensor_scalar(out=tmp[:, 0:1], in0=dst[:, 0:1], scalar1=-1.0, scalar2=None, op0=ALU.mult), reads=[dst.r()], writes=[tmp.r()])
            S.op("dve", lambda e: e.scalar_tensor_tensor(out=dst[:], in0=tmp[:], scalar=muT[:, mcol:mcol + 1], in1=dst[:], op0=ALU.mult, op1=ALU.add),
                 reads=[tmp.r(), dst.r(), muT.r()], writes=[dst.r()])

        load_shift(wdad, F_RW + 768, 6, t0)
        S.op("act", lambda e: e.activation(out=wdad[0:64, :], in_=wdad[0:64, :], func=AF.Tanh), reads=[wdad.r()], writes=[wdad.r()])
        def prep_cc(cc):
            csl = slice(cc * 128, (cc + 1) * 128)
            for tq in range(4):
                qs = slice(tq * 512, (tq + 1) * 512)
                S.op("pe", lambda e, tq=tq, qs=qs: e.matmul(pp[tq][:], lhsT=lora[0:64, csl], rhs=wdad[0:64, qs], start=True, stop=True),
                     reads=[lora.r(), wdad.r()], writes=[pp[tq].r()])
                S.op("act", lambda e, tq=tq, qs=qs: e.activation(out=t_lw[:, qs], in_=pp[tq][:], func=AF.Sigmoid, bias=cols[:, cc, 0:1]),
                     reads=[pp[tq].r(), cols.r()], writes=[t_lw.r()])
            S.op("dve", lambda e: e.tensor_scalar(out=t_lw[:], in0=t_lw[:], scalar1=-0.6065306597126334, scalar2=None, op0=ALU.mult),
                 reads=[t_lw.r()], writes=[t_lw.r()])
            for tq in range(4):
                qs = slice(tq * 512, (tq + 1) * 512)
                S.op("pe", lambda e, tq=tq, qs=qs: e.matmul(pp[tq][:], lhsT=lora[64:128, csl], rhs=wdad[64:128, qs], start=True, stop=True),
                     reads=[lora.r(), wdad.r()], writes=[pp[tq].r()])
                S.op("act", lambda e, tq=tq, qs=qs: e.activation(out=t_a[:, qs], in_=pp[tq][:], func=AF.Sigmoid, bias=cols[:, cc, 1:2]),
                     reads=[pp[tq].r(), cols.r()], writes=[t_a.r()])
            ta_all = [t_a.r()]
            S.op("dve", lambda e: e.tensor_tensor_scan(out=t_gc[:], data0=rst[:], data1=t_lw[:], initial=0.0, op0=ALU.mult, op1=ALU.add),
                 reads=[rst.r(), t_lw.r()], writes=[t_gc.r()])
            S.op("dve", lambda e: e.tensor_tensor(out=t_lw[:], in0=t_gc[:], in1=t_lw[:], op=ALU.subtract), reads=[t_gc.r(), t_lw.r()], writes=[t_lw.r()])
            xk = t_x
            load_shift(xk, F_RW + 256 + cc * 128, 2 + cc, t0)
            S.op("dve", lambda e: e.tensor_scalar(out=t0[:], in0=xk[:], scalar1=cols[:, cc, 2:3], scalar2=None, op0=ALU.mult), reads=[xk.r(), cols.r()], writes=[t0.r()])
            S.op("dve", lambda e: e.tensor_tensor(out=t1[:], in0=t0[:], in1=t0[:], op=ALU.mult), reads=[t0.r()], writes=[t1.r()])
            for tq in range(4):
                qs = slice(tq * 512, (tq + 1) * 512)
                S.op("pe", lambda e, tq=tq, qs=qs: e.matmul(pp[tq][:], lhsT=c.blk[:], rhs=t1[:, qs], start=True, stop=True),
                     reads=[c.blk.r(), t1.r()], writes=[pp[tq].r()])
            for tq in range(4):
                qs = slice(tq * 512, (tq + 1) * 512)
                S.op("act", lambda e, tq=tq, qs=qs: e.activation(out=t1[:, qs], in_=pp[tq][:], func=AF.Sqrt, bias=c.eps6[:, 0:1]),
                     reads=[pp[tq].r()] + [pp[q].r() for q in range(4)], writes=[t1.r()])
            S.op("dve", lambda e: e.reciprocal(out=t1[:], in_=t1[:]), reads=[t1.r()], writes=[t1.r()])
            S.op("dve", lambda e: e.tensor_tensor(out=t0[:], in0=t0[:], in1=t1[:], op=ALU.mult), reads=[t0.r(), t1.r()], writes=[t0.r()])
            S.op("dve", lambda e: e.tensor_tensor(out=t1[:], in0=t0[:], in1=t_a[:], op=ALU.mult), reads=[t0.r()] + ta_all, writes=[t1.r()])
            S.op("dve", lambda e: e.tensor_scalar(out=t_a[:], in0=t_a[:], scalar1=-1.0, scalar2=cols[:, cc, 3:4], op0=ALU.add, op1=ALU.mult),
                 reads=ta_all + [t1.r(), cols.r()], writes=[t_a.r()])
            S.op("dve", lambda e: e.tensor_tensor(out=t_a[:], in0=t_a[:], in1=xk[:], op=ALU.mult), reads=[t_a.r(), xk.r()], writes=[t_a.r()])
            S.op("dve", lambda e: e.tensor_tensor(out=t_a[:], in0=t_a[:], in1=xk[:], op=ALU.add), reads=[t_a.r(), xk.r()], writes=[t_a.r()])
            S.op("act", lambda e: e.activation(out=t_x[:], in_=t_lw[:], func=AF.Exp), reads=[t_lw.r(), t_a.r()], writes=[t_x.r()])
            S.op("dve", lambda e: e.scalar_tensor_tensor(out=AQ[:, cc, :, 0:64], in0=c3(t0[:]), scalar=-1.0, in1=c3(t_x[:]), op0=ALU.mult, op1=ALU.mult),
                 reads=[t0.r(), t_x.r()], writes=[AQ.r((cc, 0))])
            S.op("dve", lambda e: e.tensor_scalar(out=AQ1[:, cc, :, 0:64], in0=AQ0[:, cc, :, 0:64], scalar1=c.hmask[:, 1:2], scalar2=None, op0=ALU.mult),
                 reads=[AQ.r((cc, 0)), c.hmask.r()], writes=[AQ1.r((cc, 0))])
            S.op("dve", lambda e: e.tensor_scalar(out=AQ0[:, cc, :, 0:64], in0=AQ0[:, cc, :, 0:64], scalar1=c.hmask[:, 0:1], scalar2=None, op0=ALU.mult),
                 reads=[AQ.r((cc, 0)), AQ1.r((cc, 0)), c.hmask.r()], writes=[AQ.r((cc, 0))])
            S.op("act", lambda e: e.activation(out=t_x[:], in_=t_gc[:], func=AF.Exp, scale=-1.0), reads=[t_gc.r(), AQ.r((cc, 0))], writes=[t_x.r()])
            S.op("dve", lambda e: e.tensor_tensor(out=Bt[:, cc, :], in0=t1[:], in1=t_x[:], op=ALU.mult), reads=[t1.r(), t_x.r()], writes=[Bt.r(cc)])
            S.op("dve", lambda e: e.tensor_tensor(out=Kt[:, cc, :], in0=t_a[:], in1=t_x[:], op=ALU.mult), reads=[t_a.r(), t_x.r()], writes=[Kt.r(cc)])
            S.op("dve", lambda e: e.tensor_tensor(out=c3(t_x[:]), in0=c3(t_gc[:])[:, :, 63:64].to_broadcast([128, NCH, 64]), in1=c3(t_gc[:]), op=ALU.subtract),
                 reads=[t_gc.r(), Bt.r(cc), Kt.r(cc)], writes=[t_x.r()])
            S.op("act", lambda e: e.activation(out=t_x[:], in_=t_x[:], func=AF.Exp), reads=[t_x.r()], writes=[t_x.r()])
            S.op("dve", lambda e: e.tensor_tensor(out=BKV[:, cc, :, 0, :], in0=c3(t1[:]), in1=c3(t_x[:]), op=ALU.mult), reads=[t1.r(), t_x.r()], writes=[BKV.r((cc, 0))])
            S.op("dve", lambda e: e.tensor_tensor(out=BKV[:, cc, :, 1, :], in0=c3(t_a[:]), in1=c3(t_x[:]), op=ALU.mult), reads=[t_a.r(), t_x.r()], writes=[BKV.r((cc, 1))])
            S.op("act", lambda e: e.activation(out=Pend[:, cc, :], in_=c3(t_gc[:])[:, :, 63], func=AF.Exp), reads=[t_gc.r()], writes=[Pend.r(cc)])
            S.op("act", lambda e: e.activation(out=t_x[:], in_=t_gc[:], func=AF.Exp), reads=[t_gc.r(), BKV.r((cc, 0)), BKV.r((cc, 1))], writes=[t_x.r()])
            load_shift(xr, F_RW + cc * 128, cc, t0)
            S.op("dve", lambda e: e.tensor_tensor(out=AQ[:, cc, :, 64:128], in0=c3(xr[:]), in1=c3(t_x[:]), op=ALU.mult), reads=[xr.r(), t_x.r()], writes=[AQ.r((cc, 1))])
            S.op("dve", lambda e: e.tensor_scalar(out=AQ1[:, cc, :, 64:128], in0=AQ0[:, cc, :, 64:128], scalar1=c.hmask[:, 1:2], scalar2=None, op0=ALU.mult),
                 reads=[AQ.r((cc, 1)), c.hmask.r()], writes=[AQ1.r((cc, 1))])
            S.op("dve", lambda e: e.tensor_scalar(out=AQ0[:, cc, :, 64:128], in0=AQ0[:, cc, :, 64:128], scalar1=c.hmask[:, 0:1], scalar2=None, op0=ALU.mult),
                 reads=[AQ.r((cc, 1)), AQ1.r((cc, 1)), c.hmask.r()], writes=[AQ.r((cc, 1))])
            S.op("dve", lambda e: e.scalar_tensor_tensor(out=t0[:], in0=xr[:], scalar=cols[:, cc, 4:5], in1=t_a[:], op0=ALU.mult, op1=ALU.mult),
                 reads=[xr.r(), t_a.r(), cols.r()], writes=[t0.r()])
            for tq in range(4):
                qs = slice(tq * 512, (tq + 1) * 512)
                S.op("pe", lambda e, tq=tq, qs=qs: e.matmul(pp[tq][:], lhsT=c.blk[:], rhs=t0[:, qs], start=True, stop=True),
                     reads=[c.blk.r(), t0.r()], writes=[pp[tq].r()])
            xv = t_x
            load_shift(xv, F_RW + 512 + cc * 128, 4 + cc, t1)
            for tq in range(4):
                qs = slice(tq * 512, (tq + 1) * 512)
                S.op("dve", lambda e, tq=tq, qs=qs: e.tensor_tensor(out=bonT[:, cc, qs], in0=pp[tq][:], in1=xv[:, qs], op=ALU.mult),
                     reads=[pp[tq].r(), xv.r()], writes=[bonT.r((cc, tq))])
            S.op("act", lambda e: e.copy(out=BKV[:, cc, :, 2, :], in_=c3(xv[:])), reads=[xv.r()], writes=[BKV.r((cc, 2))])
        for cc in range(2):
            prep_cc(cc)
        S.emit()
        pst.close()

        _dplr_loop(c, st, b, "mc", (AQ0, AQ1), Bt, Kt, BKV, Pend, bonT, l)
        S.emit()


def mixerA(c, l, b):
    nc, S = c.nc, c.S
    P = c.p
    Fsb, Tsb = c.Fs[b], c.Ts[b]
    NCH = T // 64
    with ExitStack() as st:
        A = lambda n, s, d, psum=False: _alloc(nc, st, "ma_" + n, s, d, psum)
        TM = lambda n, s, d, k=2, psum=False: _tmp(nc, st, "ma_" + n, s, d, k, psum)
        knT = A("knT", [128, 2, T], BF16)
        vT = A("vT", [128, 2, T], BF16)
        KQ0 = A("KQ0", [128, 2, NCH, 128], BF16)
        KQ1 = A("KQ1", [128, 2, NCH, 128], BF16)
        KQm = (KQ0, KQ1)
        szA = A("sz", [64, NCH, 256], F32)
        gA = A("g", [64, NCH, 4], F32)
        bA = A("b", [64, NCH, 4], F32)
        gcA = A("gc", [64, NCH, 4], F32)
        c3 = lambda ap: ap.rearrange("p (n s) -> p n s", s=64)
        pst = ExitStack()
        A1 = lambda n, s, d, psum=False: _alloc(nc, pst, "ma_" + n, s, d, psum)
        convT = A1("convT", [128, 6, 4], F32)
        xs2 = [A1(f"x{i}", [128, T], F32) for i in range(2)]
        accs2 = [A1(f"acc{i}", [128, T], F32) for i in range(2)]
        sqs2 = [A1(f"sq{i}", [128, T], BF16) for i in range(2)]
        blkb = A1("blkb", [128, 128], BF16)
        S.op("dve", lambda e: e.tensor_copy(out=blkb[:], in_=c.blk[:]), reads=[c.blk.r()], writes=[blkb.r()])
        ab = A1("ab", [64, NCH, 8], F32)
        dtb = A1("dtb", [64, 4], F32)
        nA = A1("nA", [64, 4], F32)
        pp = [A1(f"pp{i}", [128, 512], F32, psum=True) for i in range(4)]
        S.defer = []
        S.op("sp", lambda e: e.dma_start(out=convT[:], in_=P["gdn_convT"][l]), writes=[convT.r()], dma=True)
        S.op("sp", lambda e: e.dma_start(out=dtb[:], in_=P["gdn_dt_bias"][l:l + 1, :].partition_broadcast(64)), writes=[dtb.r()], dma=True)
        S.op("sp", lambda e: e.dma_start(out=nA[:], in_=P["gdn_a_log"][l:l + 1, :].partition_broadcast(64)), writes=[nA.r()], dma=True)
        S.op("sp", lambda e: e.dma_start(out=ab[:], in_=Tsb[:, T_GA:T_GA + 8].rearrange("(n p) c -> p n c", p=64)), reads=[c.Ts_r[b]], writes=[ab.r()], dma=True)
        for q4 in range(4):
            S.op("sp", lambda e, q4=q4: e.dma_start(out=szA[:, q4 * 8:(q4 + 1) * 8, :], in_=Tsb[q4 * 512:(q4 + 1) * 512, T_GZ:T_GZ + 256].rearrange("(n p) c -> p n c", p=64)),
                 reads=[c.Ts_r[b]], writes=[szA.r()], dma=True)
        S.op("act", lambda e: e.activation(out=szA[:], in_=szA[:], func=AF.Silu), reads=[szA.r()], writes=[szA.r()])
        S.op("act", lambda e: e.activation(out=nA[:], in_=nA[:], func=AF.Exp), reads=[nA.r()], writes=[nA.r()])
        S.op("dve", lambda e: e.tensor_tensor(out=gA[:], in0=ab[:, :, 0:4], in1=dtb[:].unsqueeze(1).to_broadcast([64, NCH, 4]), op=ALU.add), reads=[ab.r(), dtb.r()], writes=[gA.r()])
        S.op("act", lambda e: e.activation(out=gA[:], in_=gA[:], func=AF.Exp), reads=[gA.r()], writes=[gA.r()])
        S.op("act", lambda e: e.activation(out=gA[:], in_=gA[:], func=AF.Ln, bias=c.one[0:64, 0:1]), reads=[gA.r()], writes=[gA.r()])
        S.op("dve", lambda e: e.scalar_tensor_tensor(out=gA[:], in0=gA[:], scalar=-1.0, in1=nA[:].unsqueeze(1).to_broadcast([64, NCH, 4]), op0=ALU.mult, op1=ALU.mult),
             reads=[gA.r(), nA.r()], writes=[gA.r()])
        S.op("act", lambda e: e.activation(out=bA[:], in_=ab[:, :, 4:8], func=AF.Exp, scale=-1.0), reads=[ab.r()], writes=[bA.r()])
        S.op("dve", lambda e: e.tensor_scalar(out=bA[:], in0=bA[:], scalar1=1.0, scalar2=None, op0=ALU.add), reads=[bA.r()], writes=[bA.r()])
        S.op("dve", lambda e: e.reciprocal(out=bA[:], in_=bA[:]), reads=[bA.r()], writes=[bA.r()])
        S.op("pe", lambda e: e.matmul(pp[0][0:64, 0:NCH * 4], lhsT=c.m1cat[:, 64:128], rhs=gA[:].rearrange("p n h -> p (n h)"), start=True, stop=True),
             reads=[gA.r(), c.m1cat.r()], writes=[pp[0].r()])
        S.op("dve", lambda e: e.tensor_copy(out=gcA[:].rearrange("p n h -> p (n h)"), in_=pp[0][0:64, 0:NCH * 4]), reads=[pp[0].r()], writes=[gcA.r()])

        def conv_tile(ti, x, acc):
            S.op("sp", lambda e: e.dma_start(out=x[:], in_=Fsb[F_GQ + ti * 128:F_GQ + (ti + 1) * 128, :]), reads=[c.Fs_r[b]], writes=[x.r()], dma=True)
            S.op("dve", lambda e: e.tensor_scalar(out=acc[:], in0=x[:], scalar1=convT[:, ti, 3:4], scalar2=None, op0=ALU.mult), reads=[x.r(), convT.r()], writes=[acc.r()])
            for sh in (1, 2, 3):
                S.op("dve", lambda e, sh=sh: e.scalar_tensor_tensor(out=acc[:, sh:T], in0=x[:, 0:T - sh], scalar=convT[:, ti, 3 - sh:4 - sh], in1=acc[:, sh:T], op0=ALU.mult, op1=ALU.add),
                     reads=[x.r(), acc.r(), convT.r()], writes=[acc.r()])
            S.op("act", lambda e: e.activation(out=x[:], in_=acc[:], func=AF.Silu), reads=[acc.r()], writes=[x.r()])

        def l2n(scale, x, sq, pp, rt):
            S.op("dve", lambda e: e.tensor_tensor(out=sq[:], in0=x[:], in1=x[:], op=ALU.mult), reads=[x.r()], writes=[sq.r()])
            for tq in range(4):
                qs = slice(tq * 512, (tq + 1) * 512)
                S.op("pe", lambda e, tq=tq, qs=qs: e.matmul(pp[tq % 2][:], lhsT=blkb[:], rhs=sq[:, qs], start=True, stop=True), reads=[blkb.r(), sq.r()], writes=[pp[tq % 2].r()])
                S.op("act", lambda e, tq=tq, qs=qs: e.activation(out=rt[:, qs], in_=pp[tq % 2][:], func=AF.Sqrt, bias=c.eps6[:, 0:1]), reads=[pp[tq % 2].r()], writes=[rt.r()])
            S.op("dve", lambda e: e.reciprocal(out=rt[:], in_=rt[:]), reads=[rt.r()], writes=[rt.r()])
            S.op("dve", lambda e: e.scalar_tensor_tensor(out=x[:], in0=x[:], scalar=scale, in1=rt[:], op0=ALU.mult, op1=ALU.mult), reads=[x.r(), rt.r()], writes=[x.r()])

        def prep_cc(cc):
            x, acc, sq, pp2 = xs2[cc], accs2[cc], sqs2[cc], pp[2 * cc:2 * cc + 2]
            conv_tile(2 + cc, x, acc)
            l2n(1.0, x, sq, pp2, acc)
            S.op("act", lambda e: e.copy(out=knT[:, cc, :], in_=x[:]), reads=[x.r()], writes=[knT.r(cc)])
            for hp in range(2):
                S.op("dve", lambda e, hp=hp: e.tensor_scalar(out=KQm[hp][:, cc, :, 0:64], in0=c3(x[:]), scalar1=c.hmask[:, hp:hp + 1], scalar2=None, op0=ALU.mult),
                     reads=[x.r(), c.hmask.r()], writes=[KQm[hp].r((cc, 0))])
            conv_tile(cc, x, acc)
            l2n(0.125, x, sq, pp2, acc)
            for hp in range(2):
                S.op("dve", lambda e, hp=hp: e.tensor_scalar(out=KQm[hp][:, cc, :, 64:128], in0=c3(x[:]), scalar1=c.hmask[:, hp:hp + 1], scalar2=None, op0=ALU.mult),
                     reads=[x.r(), c.hmask.r()], writes=[KQm[hp].r((cc, 1))])
            conv_tile(4 + cc, x, acc)
            S.op("act", lambda e: e.copy(out=vT[:, cc, :], in_=x[:]), reads=[x.r()], writes=[vT.r(cc)])
        lst0, S.defer = S.defer, []
        prep_cc(0)
        lstc0, S.defer = S.defer, []
        prep_cc(1)
        lstc1, S.defer = S.defer, None
        S.replay(lst0)
        lists = [lstc0, lstc1]
        if c.merge_B:
            S.defer = []
            mixerB(c, l, b, ext_st=pst)
            lstB, S.defer = S.defer, None
            lists.append(lstB)
        S.replay(*lists)
        S.emit()
        pst.close()

        NPR = NCH // 2
        pTr = A("pTr", [128, 1024], BF16, psum=True)
        pX1 = A("pX1", [64, 8, 128], F32, psum=True)
        pGN = A("pGN", [128, 2, 512], F32, psum=True)

        class _NBView:
            def r(self, key=0):
                return pGN.r()

            def __getitem__(self, idx):
                p, h, cs = idx
                if isinstance(h, slice):
                    return pGN.t[0:64, 1, :].rearrange("p (q i) -> p q i", i=64)
                return pGN.t[0:64, 1, h * 64:(h + 1) * 64]
        pNB = _NBView()
        pXab = A("pXab", [64, 2, 256], F32, psum=True)
        pO = A("pO", [64, 2, 256], F32, psum=True)
        pHU = A("pHU", [128, 512], F32, psum=True)
        H = A("H", [128, 2, 64], F32)
        Hbs = TM("Hb", [128, 2, 64], BF16)
        Hs = A("Hs", [128, 2, 64], F32)
        ones = A("ones", [64, 128], F32)
        nw = A("nw", [64, 64], F32)
        S.op("sp", lambda e: e.dma_start(out=nw[:], in_=P["gdn_norm"][l:l + 1, :].partition_broadcast(64)), writes=[nw.r()], dma=True)
        S.op("dve", lambda e: e.memset(ones[:], 1.0), writes=[ones.r()])
        S.op("dve", lambda e: e.memset(H[:], 0.0), writes=[H.r()])
        S.op("dve", lambda e: e.memset(Hbs[0][:], 0.0), writes=[Hbs[0].r()])
        tokm = TM("tokm", [64, 2, 4, 128], BF16)
        MQ = TM("MQ", [64, 8, 64], BF16)
        NL = TM("NL", [64, 8, 64], BF16)
        TTp = TM("TTp", [64, 8, 64], BF16)
        scp = TM("scp", [64, 2, 8], F32)
        PCr = TM("PCr", [64, 2, 4], F32)
        PeT = TM("PeT", [128, 2, 2], F32)
        Vb = TM("Vb", [64, 8, 64], F32)
        Vbb = TM("Vbb", [64, 8, 64], BF16)
        R2 = A("R2", [64, 2, 8, 64], F32)
        Dm = A("Dm", [64, 8, 64], F32)
        E1 = A("E1", [64, 8, 64], F32)
        E2 = A("E2", [64, 8, 64], F32)
        G1 = A("G1", [64, 8, 64], F32)
        G2 = A("G2", [64, 8, 64], F32)
        GL = A("GL", [64, 8, 64], F32)
        NY = TM("NY", [64, 8, 128], BF16, 2)
        R = TM("R", [64, 8, 64], BF16, 2)
        XS = TM("XS", [64, 256], BF16)
        XSf = TM("XSf", [64, 256], F32)
        Wb = TM("Wb", [64, 256], BF16)
        Wp = TM("Wp", [64, 256], BF16)
        o32 = TM("o32", [64, 256], F32, 4)
        t1 = TM("t1", [64, 256], F32)
        s4 = TM("s4", [64, 4], F32)
        y32 = TM("y32", [64, 256], F32)
        yb = TM("yb", [64, 256], BF16)
        M1 = c.m1cat
        ML = c.maskL
        I64 = c.ident[0:64, 0:64]
        b4 = lambda ap, n: ap.unsqueeze(1).to_broadcast([64, 4, n])
        b8 = lambda ap, n: ap.unsqueeze(1).to_broadcast([64, 8, n])
        s4b = lambda ap, n: ap.unsqueeze(2).to_broadcast([64, 4, n])
        g4 = lambda ap: ap.rearrange("p (h c) -> p h c", h=4)
        KQr = [a.r(k) for a in KQm for k in ((0, 0), (0, 1), (1, 0), (1, 1))]

        def phase1(m):
            i2 = m % 2
            tk, mq, nl, tts, sc_, pcr, pe_, vb, vbb = tokm[i2], MQ[i2], NL[i2], TTp[i2], scp[i2], PCr[i2], PeT[i2], Vb[i2], Vbb[i2]
            ny, r_ = NY[0], R[0]
            n0 = 2 * m
            r2, dm, e1, e2, g1, g2, gl = R2, Dm, E1, E2, G1, G2, GL
            f8 = lambda t_: t_[:, n0:n0 + 2, :].rearrange("p n h -> p (n h)")
            s8b = lambda ap, n: ap.unsqueeze(2).to_broadcast([64, 8, n])
            for ci in range(2):
                nsl = slice((n0 + ci) * 64, (n0 + ci + 1) * 64)

                def tra(e, nsl=nsl):
                    for cc in range(2):
                        e.transpose(out=pTr[0:64, cc * 128:(cc + 1) * 128], in_=knT[:, cc, nsl], identity=c.identb[:])
                        ins = e.transpose(out=pTr[0:64, (2 + cc) * 128:(3 + cc) * 128], in_=vT[:, cc, nsl], identity=c.identb[:])
                    return ins
                S.op("pe", tra, reads=[knT.r(0), knT.r(1), vT.r(0), vT.r(1)], writes=[pTr.r()])
                S.op("act", lambda e, ci=ci: e.copy(out=tk[:, ci, :, :].rearrange("p a b -> p (a b)"), in_=pTr[0:64, 0:512]), reads=[pTr.r()], writes=[tk.r()])
            S.op("dve", lambda e: e.tensor_tensor(out=r2[:, 0, :, :], in0=s8b(f8(gA), 64), in1=b8(M1[:, 64:128], 64), op=ALU.mult), reads=[gA.r(), M1.r()], writes=[r2.r()])
            S.op("dve", lambda e: e.tensor_tensor(out=r2[:, 1, :, :], in0=s8b(f8(bA), 64), in1=b8(I64, 64), op=ALU.mult), reads=[bA.r(), c.ident.r()], writes=[r2.r()])

            def mmgb(e):
                e.matmul(pGN[:, 0, :], lhsT=ones[:], rhs=r2[:, 0, :, :].rearrange("p q i -> p (q i)"), start=True, stop=True)
                return e.matmul(pGN[:, 1, :], lhsT=ones[:], rhs=r2[:, 1, :, :].rearrange("p q i -> p (q i)"), start=True, stop=True)
            S.op("pe", mmgb, reads=[ones.r(), r2.r()], writes=[pGN.r()])
            GR = lambda: pGN[0:64, 0, :].rearrange("p (q i) -> p q i", i=64)
            BR = lambda: pGN[0:64, 1, :].rearrange("p (q i) -> p q i", i=64)
            S.op("dve", lambda e: e.tensor_tensor(out=dm[:], in0=GR(), in1=s8b(f8(gcA), 64), op=ALU.subtract), reads=[pGN.r(), gcA.r()], writes=[dm.r()])
            S.op("dve", lambda e: e.tensor_scalar(out=e1[:], in0=dm[:], scalar1=0.0, scalar2=None, op0=ALU.min), reads=[dm.r()], writes=[e1.r()])
            S.op("dve", lambda e: e.tensor_scalar(out=e2[:], in0=dm[:], scalar1=-1.0, scalar2=0.0, op0=ALU.mult, op1=ALU.min), reads=[dm.r()], writes=[e2.r()])
            S.op("act", lambda e: e.activation(out=e1[:], in_=e1[:], func=AF.Exp), reads=[e1.r()], writes=[e1.r()])
            S.op("act", lambda e: e.activation(out=e2[:], in_=e2[:], func=AF.Exp), reads=[e2.r()], writes=[e2.r()])
            S.op("dve", lambda e: e.tensor_tensor(out=g2[:], in0=e1[:], in1=b8(M1[:, 64:128], 64), op=ALU.mult), reads=[e1.r(), M1.r()], writes=[g2.r()])
            S.op("dve", lambda e: e.tensor_tensor(out=g1[:], in0=e1[:], in1=b8(M1[:, 0:64], 64), op=ALU.mult), reads=[e1.r(), M1.r()], writes=[g1.r()])
            S.op("dve", lambda e: e.scalar_tensor_tensor(out=g1[:], in0=g1[:], scalar=-1.0, in1=BR(), op0=ALU.mult, op1=ALU.mult), reads=[g1.r(), pGN.r()], writes=[g1.r()])
            S.op("dve", lambda e: e.tensor_tensor(out=gl[:], in0=e2[:], in1=b8(ML[:], 64), op=ALU.mult), reads=[e2.r(), ML.r()], writes=[gl.r()])
            S.op("dve", lambda e: e.scalar_tensor_tensor(out=gl[:], in0=gl[:], scalar=-1.0, in1=s8b(f8(bA), 64), op0=ALU.mult, op1=ALU.mult), reads=[gl.r(), bA.r()], writes=[gl.r()])
            S.op("act", lambda e: e.activation(out=sc_[:, :, 0:4], in_=gcA[:, n0:n0 + 2, :], func=AF.Exp), reads=[gcA.r()], writes=[sc_.r()])
            S.op("dve", lambda e: e.scalar_tensor_tensor(out=sc_[:, :, 4:8], in0=sc_[:, :, 0:4], scalar=-1.0, in1=bA[:, n0:n0 + 2, :], op0=ALU.mult, op1=ALU.mult), reads=[sc_.r(), bA.r()], writes=[sc_.r()])
            S.op("dve", lambda e: e.tensor_copy(out=pcr[:].rearrange("p n h -> p (n h)"), in_=e1[:, :, 63]), reads=[e1.r()], writes=[pcr.r()])
            for hp in range(2):
                S.op("act", lambda e, hp=hp: e.activation(out=pe_[hp * 64:(hp + 1) * 64, :, :], in_=pGN[hp * 64:(hp + 1) * 64, 0, :].rearrange("p (ci cc hp i) -> p ci cc hp i", ci=2, cc=2, hp=2)[:, :, :, hp, 63], func=AF.Exp),
                     reads=[pGN.r()], writes=[pe_.r()])
            for ci in range(2):
                S.op("dve", lambda e, ci=ci: e.tensor_tensor(out=vb[:, ci * 4:ci * 4 + 4, :], in0=tk[:, ci, 2:4, :].rearrange("p a (hp v) -> p (a hp) v", hp=2), in1=s4b(bA[:, n0 + ci, :], 64), op=ALU.mult),
                     reads=[tk.r(), bA.r()], writes=[vb.r()])
            S.op("act", lambda e: e.copy(out=vbb[:], in_=vb[:]), reads=[vb.r()], writes=[vbb.r()])
            def mmx(e):
                for ci in range(2):
                    n = n0 + ci
                    nsl = slice(n * 64, (n + 1) * 64)
                    for h in range(4):
                        cc = h // 2
                        ins = e.matmul(pX1[:, ci * 4 + h, :], lhsT=knT[:, cc, nsl], rhs=KQm[h % 2][:, cc, n, :], start=True, stop=True)
                return ins
            S.op("pe", mmx, reads=KQr + [knT.r(0), knT.r(1)], writes=[pX1.r()])
            S.op("dve", lambda e: e.tensor_tensor(out=ny[:, :, 0:64], in0=pX1[:, :, 0:64], in1=g1[:], op=ALU.mult), reads=[pX1.r(), g1.r()], writes=[ny.r()])
            S.op("dve", lambda e: e.tensor_tensor(out=mq[:], in0=pX1[:, :, 64:128], in1=g2[:], op=ALU.mult), reads=[pX1.r(), g2.r()], writes=[mq.r()])
            S.op("dve", lambda e: e.tensor_tensor(out=r_[:], in0=pX1[:, :, 0:64], in1=gl[:], op=ALU.mult), reads=[pX1.r(), gl.r()], writes=[r_.r()])
            S.op("dve", lambda e: e.tensor_tensor(out=ny[:, :, 64:128], in0=ny[:, :, 0:64], in1=b8(I64, 64), op=ALU.add), reads=[ny.r(), c.ident.r()], writes=[ny.r()])
            S.op("act", lambda e: e.copy(out=nl[:], in_=ny[:, :, 0:64]), reads=[ny.r()], writes=[nl.r()])
            _neumann(S, NY, R, pX1, pNB, 8, tts)

        def phase2(m):
            i2 = m % 2
            tk, mq, nl, tts, sc_, pcr, pe_, vb, vbb = tokm[i2], MQ[i2], NL[i2], TTp[i2], scp[i2], PCr[i2], PeT[i2], Vb[i2], Vbb[i2]
            for ci in range(2):
                n = 2 * m + ci
                nsl = slice(n * 64, (n + 1) * 64)
                p0 = ci * 4
                xs_, xf_, wb, wp = XS[ci], XSf[ci], Wb[ci], Wp[ci]
                Hb, Hbn = Hbs[n % 2], Hbs[(n + 1) % 2]
                S.op("dve", lambda e, ci=ci: e.tensor_tensor(out=Hs[:], in0=H[:], in1=pe_[:, ci, :].unsqueeze(2).to_broadcast([128, 2, 64]), op=ALU.mult), reads=[H.r(), pe_.r()], writes=[Hs.r()])
                def mmd(e, n=n, p0=p0, Hb=Hb):
                    for h in range(4):
                        cc = h // 2
                        e.matmul(pXab[:, 0, h * 64:(h + 1) * 64], lhsT=KQm[h % 2][:, cc, n, 0:64], rhs=Hb[:, cc, :], start=True, stop=True)
                        ins = e.matmul(pXab[:, 1, h * 64:(h + 1) * 64], lhsT=nl[:, p0 + h, :], rhs=vbb[:, p0 + h, :], start=True, stop=True)
                    return ins
                S.op("pe", mmd, reads=KQr + [Hb.r(), nl.r(), vbb.r()], writes=[pXab.r()])
                S.op("dve", lambda e, xf_=xf_, ci=ci: e.tensor_tensor(out=g4(xf_[:]), in0=g4(pXab[:, 0, :]), in1=s4b(sc_[:, ci, 4:8], 64), op=ALU.mult), reads=[pXab.r(), sc_.r()], writes=[xf_.r()])
                S.op("dve", lambda e, xs_=xs_, xf_=xf_: e.tensor_tensor(out=xs_[:], in0=xf_[:], in1=pXab[:, 1, :], op=ALU.add), reads=[xf_.r(), pXab.r()], writes=[xs_.r()])
                def mme(e, xs_=xs_, p0=p0):
                    for h in range(4):
                        ins = e.matmul(pHU[0:64, 256 + h * 64:256 + (h + 1) * 64], lhsT=tts[:, p0 + h, :], rhs=xs_[:, h * 64:(h + 1) * 64], start=True, stop=True)
                    return ins
                S.op("pe", mme, reads=[tts.r(), xs_.r()], writes=[pHU.r()])
                S.op("dve", lambda e, wb=wb, p0=p0: e.tensor_tensor(out=g4(wb[:]), in0=g4(pHU[0:64, 256:512]), in1=vb[:, p0:p0 + 4, :], op=ALU.add), reads=[pHU.r(), vb.r()], writes=[wb.r()])
                S.op("dve", lambda e, wb=wb, wp=wp, ci=ci: e.tensor_tensor(out=g4(wp[:]), in0=g4(wb[:]), in1=s4b(pcr[:, ci, :], 64), op=ALU.mult), reads=[wb.r(), pcr.r()], writes=[wp.r()])
                def mmg(e, ci=ci, wp=wp):
                    for h in range(4):
                        cc, hp = h // 2, h % 2
                        hs_ = slice(hp * 64, (hp + 1) * 64)
                        ins = e.matmul(pHU[hs_, cc * 64:(cc + 1) * 64], lhsT=tk[:, ci, cc, hs_], rhs=wp[:, h * 64:(h + 1) * 64], start=True, stop=True)
                    return ins
                S.op("pe", mmg, reads=[tk.r(), wp.r()], writes=[pHU.r()])
                S.op("dve", lambda e, Hbn=Hbn: e.tensor_tensor(out=Hbn[:], in0=Hs[:], in1=pHU[:, 0:128].rearrange("p (cc v) -> p cc v", cc=2), op=ALU.add), reads=[Hs.r(), pHU.r()], writes=[Hbn.r()])
                def mmf(e, n=n, p0=p0, wb=wb, Hb=Hb):
                    for h in range(4):
                        cc = h // 2
                        e.matmul(pO[:, 0, h * 64:(h + 1) * 64], lhsT=KQm[h % 2][:, cc, n, 64:128], rhs=Hb[:, cc, :], start=True, stop=True)
                        ins = e.matmul(pO[:, 1, h * 64:(h + 1) * 64], lhsT=mq[:, p0 + h, :], rhs=wb[:, h * 64:(h + 1) * 64], start=True, stop=True)
                    return ins
                S.op("pe", mmf, reads=KQr + [Hb.r(), mq.r(), wb.r()], writes=[pO.r()])
                S.op("dve", lambda e: e.tensor_tensor(out=H[:], in0=Hs[:], in1=pHU[:, 0:128].rearrange("p (cc v) -> p cc v", cc=2), op=ALU.add), reads=[Hs.r(), pHU.r()], writes=[H.r()])
                o_ = o32[(m % 2) * 2 + ci]
                S.op("dve", lambda e, o_=o_, ci=ci: e.tensor_tensor(out=g4(o_[:]), in0=g4(pO[:, 0, :]), in1=s4b(sc_[:, ci, 0:4], 64), op=ALU.mult), reads=[pO.r(), sc_.r()], writes=[o_.r()])
                S.op("dve", lambda e, o_=o_: e.tensor_tensor(out=o_[:], in0=o_[:], in1=pO[:, 1, :], op=ALU.add), reads=[o_.r(), pO.r()], writes=[o_.r()])

        def post(m):
            for ci in range(2):
                n = 2 * m + ci
                nsl = slice(n * 64, (n + 1) * 64)
                o_, t_, s_, y_, yb_ = o32[(m % 2) * 2 + ci], t1[ci], s4[ci], y32[ci], yb[ci]
                S.op("dve", lambda e, o_=o_, t_=t_: e.tensor_tensor(out=t_[:], in0=o_[:], in1=o_[:], op=ALU.mult), reads=[o_.r(), t_.r()], writes=[t_.r()])
                S.op("dve", lambda e, t_=t_, s_=s_: e.tensor_reduce(out=s_[:], in_=g4(t_[:]), axis=AX.X, op=ALU.add), reads=[t_.r()], writes=[s_.r()])
                S.op("act", lambda e, s_=s_: e.activation(out=s_[:], in_=s_[:], func=AF.Ln, scale=1.0 / 64, bias=c.eps6[0:64, 0:1]), reads=[s_.r()], writes=[s_.r()])
                S.op("act", lambda e, s_=s_: e.activation(out=s_[:], in_=s_[:], func=AF.Exp, scale=-0.5), reads=[s_.r()], writes=[s_.r()])
                S.op("dve", lambda e, o_=o_, s_=s_, t_=t_: e.tensor_tensor(out=g4(t_[:]), in0=g4(o_[:]), in1=s4b(s_[:], 64), op=ALU.mult), reads=[o_.r(), s_.r()], writes=[t_.r()])
                S.op("dve", lambda e, t_=t_: e.tensor_tensor(out=g4(t_[:]), in0=g4(t_[:]), in1=b4(nw[:], 64), op=ALU.mult), reads=[t_.r(), nw.r()], writes=[t_.r()])
                S.op("dve", lambda e, t_=t_, y_=y_, n=n: e.tensor_tensor(out=y_[:], in0=t_[:], in1=szA[:, n, :], op=ALU.mult), reads=[t_.r(), szA.r()], writes=[y_.r()])
                S.op("act", lambda e, y_=y_, yb_=yb_: e.copy(out=yb_[:], in_=y_[:]), reads=[y_.r()], writes=[yb_.r()])
                if c.ydbg is not None:
                    S.op("pool", lambda e, y_=y_, nsl=nsl: e.dma_start(out=c.ydbg[b, nsl, 0:256], in_=y_[:]), reads=[y_.r()], writes=[c.ydbg_r], dma=True)

                def tr(e, yb_=yb_):
                    for jx in range(2):
                        ins = e.transpose(out=pTr[:, 512 + jx * 64:512 + (jx + 1) * 64], in_=yb_[:, jx * 128:(jx + 1) * 128], identity=c.identb[0:64, 0:64])
                    return ins
                S.op("pe", tr, reads=[yb_.r()], writes=[pTr.r()])
                S.op("act", lambda e, nsl=nsl: e.copy(out=c.yT[:, 0:2, nsl], in_=pTr[:, 512:640].rearrange("p (j n) -> p j n", j=2)),
                     reads=[pTr.r()], writes=[c.yT.r((0, n))])


        def rec(fn, m):
            S.defer = []
            fn(m)
            lst, S.defer = S.defer, None
            return lst
        S.replay(rec(phase1, 0))
        for m in range(NPR + 1):
            lists = []
            if m < NPR:
                lists.append(rec(phase2, m))
            if m >= 1:
                lists.append(rec(post, m - 1))
            if m + 1 < NPR:
                lists.append(rec(phase1, m + 1))
            S.replay(*lists)
        S.emit()


def _neumann(S, NY, R, pNA, pNB, nprob=4, out_final=None):
    for s_ in range(6):
        ny, r_ = NY[s_ % 2], R[s_ % 2]
        ny2, r2 = NY[(s_ + 1) % 2], R[(s_ + 1) % 2]
        if s_ == 0:
            def mm0(e, ny=ny, r_=r_):
                for h in range(nprob):
                    e.matmul(pNA[:, h, 0:64], lhsT=r_[:, h, :], rhs=ny[:, h, 0:64], start=True, stop=True)
                    ins = e.matmul(pNB[:, h, 0:64], lhsT=ny[:, h, 0:64], rhs=r_[:, h, :], start=True, stop=True)
                return ins
            S.op("pe", mm0, reads=[ny.r(), r_.r()], writes=[pNA.r(), pNB.r()])
            S.op("act", lambda e, ny2=ny2: e.copy(out=ny2[:, :, 0:64], in_=pNA[:, :, 0:64]), reads=[pNA.r()], writes=[ny2.r()])
            S.op("dve", lambda e, ny=ny, ny2=ny2: e.tensor_copy(out=ny2[:, :, 64:128], in_=ny[:, :, 64:128]), reads=[ny.r()], writes=[ny2.r()])
            S.op("act", lambda e, r2=r2: e.copy(out=r2[:], in_=pNB[:, :, 0:64]), reads=[pNB.r()], writes=[r2.r()])
        else:
            last = (s_ == 5)

            def mms(e, ny=ny, r_=r_, last=last):
                for h in range(nprob):
                    if last:
                        ins = e.matmul(pNA[:, h, 64:128], lhsT=r_[:, h, :], rhs=ny[:, h, 64:128], start=True, stop=True)
                    else:
                        e.matmul(pNA[:, h, :], lhsT=r_[:, h, :], rhs=ny[:, h, :], start=True, stop=True)
                        ins = e.matmul(pNB[:, h, 0:64], lhsT=ny[:, h, 0:64], rhs=r_[:, h, :], start=True, stop=True)
                return ins
            S.op("pe", mms, reads=[ny.r(), r_.r()], writes=[pNA.r(), pNB.r()])
            if not last:
                S.op("act", lambda e, ny2=ny2: e.copy(out=ny2[:, :, 0:64], in_=pNA[:, :, 0:64]), reads=[pNA.r()], writes=[ny2.r()])
                S.op("act", lambda e, r2=r2: e.copy(out=r2[:], in_=pNB[:, :, 0:64]), reads=[pNB.r()], writes=[r2.r()])
            if last and out_final is not None:
                S.op("dve", lambda e, ny=ny: e.tensor_tensor(out=out_final[:], in0=ny[:, :, 64:128], in1=pNA[:, :, 64:128], op=ALU.add), reads=[ny.r(), pNA.r()], writes=[out_final.r()])
            else:
                S.op("dve", lambda e, ny=ny, ny2=ny2: e.tensor_tensor(out=ny2[:, :, 64:128], in0=ny[:, :, 64:128], in1=pNA[:, :, 64:128], op=ALU.add), reads=[ny.r(), pNA.r()], writes=[ny2.r()])


def stage3(c, l, b):
    nc, S = c.nc, c.S
    P = c.p
    last = (l == DEPTH - 1)
    xin = c.xres[l]
    xout = c.out if last else c.xres[l + 1]
    with ExitStack() as st:
        A = lambda n, s, d, psum=False: _alloc(nc, st, "s3_" + n, s, d, psum)
        TM = lambda n, s, d, k=2, psum=False: _tmp(nc, st, "s3_" + n, s, d, k, psum)
        wst = TM("wst", [128, 8, 256], F32, 2)
        wm = [A(f"wm{i}", [128, 8, 512], BF16) for i in range(4)]
        wbr = A("wbr", [128, 8, 512], BF16)
        wo = A("wo", [128, 8, 1024], BF16)
        mixedb = A("mixedb", [128, NTT, 1024], BF16)
        pg = TM("pg", [128, 512], F32, 2, psum=True)
        pb = TM("pb", [128, 512], F32, 2, psum=True)
        pTs = TM("pT", [128, 8, 128], BF16, 2, psum=True)
        po = TM("po", [128, 512], F32, 2, psum=True)
        sig = TM("sig", [128, 512], F32, 2)
        acc = TM("acc", [128, 512], F32, 2)
        tmp = TM("tmp", [128, 512], F32, 2)
        mT = TM("mT", [128, 8, 128], BF16, 2)
        xt = TM("xt", [128, 1024], F32, 2)
        sqt = A("sqt", [128, 1024], F32) if last else None
        ssm = TM("ssm", [128, 1], F32, 2)
        fg = A("fg", [128, 1024], F32) if last else None
        if last:
            S.op("sp", lambda e: e.dma_start(out=fg[:], in_=P["final_gain"][0:1, :].partition_broadcast(128)), writes=[fg.r()], dma=True)
        wcnt = [0]

        def load_w(src_ap, dst, scale_gain):
            for hf in range(2):
                j = wcnt[0] % 2
                wcnt[0] += 1
                ws = wst[j]
                cs = slice(hf * 256, (hf + 1) * 256)
                S.op("sp", lambda e, ws=ws, cs=cs: e.dma_start(out=ws[:], in_=src_ap[:, cs].rearrange("(k p) n -> p k n", p=128)), writes=[ws.r()], dma=True)
                for k in range(8):
                    if scale_gain:
                        if k % 2 == 0:
                            S.op("dve", lambda e, k=k, ws=ws, cs=cs: e.tensor_scalar(out=dst[:, k, cs], in0=ws[:, k, :], scalar1=c.gainT[:, l * 8 + k:l * 8 + k + 1], scalar2=None, op0=ALU.mult),
                                 reads=[ws.r()], writes=[dst.r(k)])
                        else:
                            S.op("act", lambda e, k=k, ws=ws, cs=cs: e.activation(out=dst[:, k, cs], in_=ws[:, k, :], func=AF.Copy, scale=c.gainT[:, l * 8 + k:l * 8 + k + 1]),
                                 reads=[ws.r()], writes=[dst.r(k)])
                    else:
                        if k % 2 == 0:
                            S.op("act", lambda e, k=k, ws=ws, cs=cs: e.copy(out=dst[:, k, cs], in_=ws[:, k, :]), reads=[ws.r()], writes=[dst.r(k)])
                        else:
                            S.op("dve", lambda e, k=k, ws=ws, cs=cs: e.tensor_copy(out=dst[:, k, cs], in_=ws[:, k, :]), reads=[ws.r()], writes=[dst.r(k)])
            return [dst.r(k) for k in range(8)]

        cnt = 0
        for nb in range(2):
            wm_r = [load_w(P["wm"][l, :, i * 1024 + nb * 512:i * 1024 + (nb + 1) * 512], wm[i], True) for i in range(4)]
            wbr_r = load_w(P["wbr"][l, :, nb * 512:(nb + 1) * 512], wbr, False)
            for tt in range(NTT):
                tsl = slice(tt * 128, (tt + 1) * 128)
                acc_ = acc[tt % 2]
                for i in range(4):
                    j = cnt % 2
                    cnt += 1
                    pg_, pb_, sig_, tmp_ = pg[j], pb[j], sig[j], tmp[j]

                    def mmg(e, pg_=pg_, i=i, tsl=tsl):
                        for k in range(8):
                            ins = e.matmul(pg_[:], lhsT=c.hT[:, k, tsl], rhs=wm[i][:, k, :], start=(k == 0), stop=(k == 7))
                        return ins
                    S.op("pe", mmg, reads=wm_r[i] + [c.hT.r(tt)], writes=[pg_.r()])

                    def mmb(e, pb_=pb_, i=i, tsl=tsl):
                        for kc in range(2):
                            ins = e.matmul(pb_[:], lhsT=c.yT[:, 2 * i + kc, tsl], rhs=wbr[:, 2 * i + kc, :], start=(kc == 0), stop=(kc == 1))
                        return ins
                    yr = [c.yT.r((i, tt))] if i != 0 else [c.yT.r((0, 2 * tt)), c.yT.r((0, 2 * tt + 1))]
                    S.op("pe", mmb, reads=[wbr_r[2 * i], wbr_r[2 * i + 1]] + yr, writes=[pb_.r()])
                    S.op("act", lambda e, pg_=pg_, sig_=sig_: e.activation(out=sig_[:], in_=pg_[:], func=AF.Sigmoid), reads=[pg_.r()], writes=[sig_.r()])
                    if i == 0:
                        S.op("dve", lambda e, acc_=acc_, sig_=sig_, pb_=pb_: e.tensor_tensor(out=acc_[:], in0=sig_[:], in1=pb_[:], op=ALU.mult), reads=[sig_.r(), pb_.r()], writes=[acc_.r()])
                    else:
                        S.op("dve", lambda e, tmp_=tmp_, sig_=sig_, pb_=pb_: e.tensor_tensor(out=tmp_[:], in0=sig_[:], in1=pb_[:], op=ALU.mult), reads=[sig_.r(), pb_.r()], writes=[tmp_.r()])
                        if i < 3:
                            S.op("dve", lambda e, acc_=acc_, tmp_=tmp_: e.tensor_tensor(out=acc_[:], in0=acc_[:], in1=tmp_[:], op=ALU.add), reads=[acc_.r(), tmp_.r()], writes=[acc_.r()])
                        else:
                            S.op("dve", lambda e, acc_=acc_, tmp_=tmp_, tt=tt, nb=nb: e.tensor_tensor(out=mixedb[:, tt, nb * 512:(nb + 1) * 512], in0=acc_[:], in1=tmp_[:], op=ALU.add),
                                 reads=[acc_.r(), tmp_.r()], writes=[mixedb.r((tt, nb))])
        for q4 in range(4):
            j = wcnt[0] % 2
            wcnt[0] += 1
            ws = wst[j]
            cs = slice(q4 * 256, (q4 + 1) * 256)
            S.op("sp", lambda e, ws=ws, cs=cs: e.dma_start(out=ws[:], in_=P["wo"][l, :, cs].rearrange("(k p) n -> p k n", p=128)), writes=[ws.r()], dma=True)
            S.op("act", lambda e, ws=ws, cs=cs: e.copy(out=wo[:, :, cs], in_=ws[:]), reads=[ws.r()], writes=[wo.r(q4 // 2)])
        def o_front(tt):
            tsl = slice(tt * 128, (tt + 1) * 128)
            i = tt % 2
            mT_, xt_, pT = mT[i], xt[i], pTs[i]
            S.op("sp", lambda e: e.dma_start(out=xt_[:], in_=xin[b, tsl, :]), reads=[c.xres_r[l]], writes=[xt_.r()], dma=True)

            def trm(e):
                for k in range(8):
                    ins = e.transpose(out=pT[:, k, :], in_=mixedb[:, tt, k * 128:(k + 1) * 128], identity=c.identb[:])
                return ins
            S.op("pe", trm, reads=[mixedb.r((tt, 0)), mixedb.r((tt, 1))], writes=[pT.r()])
            S.op("act", lambda e: e.copy(out=mT_[:], in_=pT[:]), reads=[pT.r()], writes=[mT_.r()])

        def o_back(tt):
            tsl = slice(tt * 128, (tt + 1) * 128)
            i = tt % 2
            mT_, xt_, ss_ = mT[i], xt[i], ssm[i]
            for nb in range(2):
                po_ = po[nb]

                def mmo(e, po_=po_, nb=nb):
                    for k in range(8):
                        ins = e.matmul(po_[:], lhsT=mT_[:, k, :], rhs=wo[:, k, nb * 512:(nb + 1) * 512], start=(k == 0), stop=(k == 7))
                    return ins
                S.op("pe", mmo, reads=[mT_.r(), wo.r(nb)], writes=[po_.r()])
                S.op("dve", lambda e, po_=po_, nb=nb: e.tensor_tensor(out=xt_[:, nb * 512:(nb + 1) * 512], in0=xt_[:, nb * 512:(nb + 1) * 512], in1=po_[:], op=ALU.add),
                     reads=[po_.r(), xt_.r()], writes=[xt_.r()])
            if last:
                S.op("act", lambda e: e.activation(out=sqt[:], in_=xt_[:], func=AF.Square, accum_out=ss_[:]), reads=[xt_.r()], writes=[sqt.r(), ss_.r()])
                S.op("act", lambda e: e.activation(out=ss_[:], in_=ss_[:], func=AF.Sqrt, scale=1.0 / D, bias=c.eps6[:, 0:1]), reads=[ss_.r()], writes=[ss_.r()])
                S.op("dve", lambda e: e.reciprocal(out=ss_[:], in_=ss_[:]), reads=[ss_.r()], writes=[ss_.r()])
                S.op("dve", lambda e: e.scalar_tensor_tensor(out=xt_[:], in0=xt_[:], scalar=ss_[:, 0:1], in1=fg[:], op0=ALU.mult, op1=ALU.mult),
                     reads=[xt_.r(), ss_.r(), fg.r()], writes=[xt_.r()])
            ev = S.op("pool", lambda e: e.dma_start(out=xout[b, tsl, :], in_=xt_[:]), reads=[xt_.r()], writes=[c.xres_r[l + 1]], dma=True)
            c.out_events.append(ev)
        o_front(0)
        for tt in range(NTT):
            if tt + 1 < NTT:
                o_front(tt + 1)
            o_back(tt)
        S.emit()


def host_params(inputs):
    f = lambda n: np.asarray(inputs[n], dtype=np.float32)
    fcols, tcols, mo = _col_perm()
    w_in = f("w_in")
    p = {}
    p["wf"] = np.ascontiguousarray(w_in[:, :, fcols])
    p["wt"] = np.ascontiguousarray(w_in[:, :, tcols])
    p["wm"] = np.ascontiguousarray(w_in[:, :, mo:mo + 4096])
    p["wbr"] = np.ascontiguousarray(f("w_branch").reshape(DEPTH, 1024, 1024))
    p["wo"] = f("w_out")
    p["final_gain"] = f("final_gain").reshape(1, D)
    p["gainT"] = np.ascontiguousarray(f("norm_gain").reshape(DEPTH, 8, 128).transpose(2, 0, 1).reshape(128, DEPTH * 8))
    p["sg_wsT"] = np.ascontiguousarray(f("sg_w_s").transpose(0, 3, 1, 2))
    p["sg_ln_gain"] = f("sg_ln_gain")
    p["sg_ln_bias"] = f("sg_ln_bias")
    p["sg_bsT"] = np.ascontiguousarray(f("sg_b_s").transpose(0, 2, 1))
    p["ident"] = np.eye(128, dtype=np.float32)
    i = np.arange(128)
    p["triT"] = (i[:, None] <= i[None, :]).astype(np.float32)
    p["cmp_posT"] = np.ascontiguousarray(f("nsa_cmp_pos").transpose(0, 3, 1, 2))
    p["nsa_cmp_w1"] = f("nsa_cmp_w1")
    p["nsa_cmp_w2"] = f("nsa_cmp_w2")
    p.update(nsa_consts())
    p["gdn_convT"] = np.ascontiguousarray(f("gdn_conv").transpose(0, 2, 1).reshape(DEPTH, 6, 128, 4).transpose(0, 2, 1, 3))
    p["gdn_a_log"] = f("gdn_a_log")
    p["gdn_dt_bias"] = f("gdn_dt_bias")
    p["gdn_norm"] = f("gdn_norm")
    p["rw_muT"] = np.ascontiguousarray(f("rw_mu").reshape(DEPTH, 7, 128).transpose(0, 2, 1))
    p["rw_lora"] = np.ascontiguousarray(np.concatenate([f("rw_w_up"), f("rw_a_up")], axis=1))
    colp = np.stack([f("rw_w0"), f("rw_a0"), f("rw_k_k"), f("rw_k_a"), f("rw_r_k").reshape(DEPTH, 256)], axis=-1)
    p["rw_cols"] = np.ascontiguousarray(colp.reshape(DEPTH, 2, 128, 5).transpose(0, 2, 1, 3))
    p["rw_gn_gain"] = f("rw_gn_gain")
    p["rw_gn_bias"] = f("rw_gn_bias")
    j64 = np.arange(64)
    p["m1cat"] = np.concatenate([(j64[:, None] < j64[None, :]), (j64[:, None] <= j64[None, :])], axis=1).astype(np.float32)
    p["maskL"] = (j64[None, :] < j64[:, None]).astype(np.float32)
    blk = np.zeros((128, 128), np.float32); blk[:64, :64] = 1; blk[64:, 64:] = 1
    p["blk"] = blk
    return p


def build(stages=("s1",), debug_out=(), pshapes=None, only=None):
    nc = bass.Bass("TRN2", target_bir_lowering=False)
    c = Ctx()
    c.nc = nc
    dt = lambda name, shape, kind="ExternalInput", dtype=F32: nc.dram_tensor(name, list(shape), dtype, kind=kind).ap()
    dbg = lambda name: "ExternalOutput" if name in debug_out else "Internal"
    x_in = dt("x", [NB, T, D])
    x1 = dt("x1", [NB, T, D], dbg("x1"))
    c.xres = [x_in, x1]
    c.out = dt("out", [NB, T, D], "ExternalOutput")
    c.p = {n: dt("p_" + n, shp) for n, shp in pshapes.items()}
    c.wf, c.wt = c.p["wf"], c.p["wt"]
    c.Fs = [dt(f"Fs{b}", [NF, T], dbg("Fs")) for b in range(NB)]
    c.Ts = [dt(f"Ts{b}", [T, NT], dbg("Ts")) for b in range(NB)]
    c.Fs_r = [DramRes() for _ in range(NB)]
    c.Ts_r = [DramRes() for _ in range(NB)]
    c.ydbg = dt("ydbg", [NB, T, 1024], "ExternalOutput") if "ydbg" in debug_out else None
    c.ydbg_r = DramRes()
    c.xres_r = [DramRes() for _ in range(DEPTH + 1)]
    c.out_events = []

    with ExitStack() as st:
        S = Sched(nc, st)
        c.S = S
        A = lambda n, s, d, psum=False: _alloc(nc, st, n, s, d, psum)
        c.hT = A("hT", [128, 8, T], BF16)
        c.yT = A("yT", [128, 8, T], BF16)
        c.ident = A("ident_f", [128, 128], F32)
        c.identb = A("ident_b", [128, 128], BF16)
        c.gainT = A("gainT_s", [128, DEPTH * 8], F32)
        c.eps6 = A("eps6", [128, 1], F32)
        c.eps5 = A("eps5", [128, 1], F32)
        S.op("sp", lambda e: e.dma_start(out=c.ident[:], in_=c.p["ident"][:, :]), writes=[c.ident.r()], dma=True)
        S.op("sp", lambda e: e.dma_start(out=c.gainT[:], in_=c.p["gainT"][:, :]), writes=[c.gainT.r()], dma=True)
        S.op("dve", lambda e: e.tensor_copy(out=c.identb[:], in_=c.ident[:]), reads=[c.ident.r()], writes=[c.identb.r()])
        S.op("dve", lambda e: e.memset(c.eps6[:], 1e-6), writes=[c.eps6.r()])
        S.op("dve", lambda e: e.memset(c.eps5[:], 1e-5), writes=[c.eps5.r()])
        c.epsgn = A("epsgn", [128, 1], F32)
        c.one = A("one", [128, 1], F32)
        S.op("dve", lambda e: e.memset(c.one[:], 1.0), writes=[c.one.r()])
        S.op("dve", lambda e: e.memset(c.epsgn[:], 64e-5), writes=[c.epsgn.r()])
        c.m1cat = A("m1cat", [64, 128], F32)
        c.maskL = A("maskL", [64, 64], F32)
        c.blk = A("blk", [128, 128], F32)
        c.hmask = A("hmask", [128, 2], F32)
        S.op("dve", lambda e: e.memset(c.hmask[:], 0.0), writes=[c.hmask.r()])
        S.op("dve", lambda e: e.memset(c.hmask[0:64, 0:1], 1.0), writes=[c.hmask.r()])
        S.op("dve", lambda e: e.memset(c.hmask[64:128, 1:2], 1.0), writes=[c.hmask.r()])
        for tl, nm in ((c.m1cat, "m1cat"), (c.maskL, "maskL"), (c.blk, "blk")):
            S.op("sp", lambda e, tl=tl, nm=nm: e.dma_start(out=tl[:], in_=c.p[nm]), writes=[tl.r()], dma=True)
        S.emit()
        for l in range(DEPTH):
            for b in range(NB):
                if only is not None and (l, b) not in only:
                    continue
                if "s1" in stages:
                    stage1(c, l, b)
                c.merge_B = ("A" in stages and "B" in stages)
                if "B" in stages and not c.merge_B:
                    mixerB(c, l, b)
                if "D" in stages:
                    mixerD(c, l, b)
                if "C" in stages:
                    mixerC(c, l, b)
                if "A" in stages:
                    mixerA(c, l, b)
                if "s3" in stages:
                    stage3(c, l, b)
    c.nops = S.nops
    return nc


def host_inputs(inputs):
    p = host_params(inputs)
    x = np.asarray(inputs["x"], dtype=np.float32)
    maps = []
    for i in range(NCORES):
        m = {"p_" + k: v for k, v in p.items()}
        m["x"] = np.ascontiguousarray(x[i * NB:(i + 1) * NB])
        maps.append(m)
    return maps, {n: a.shape for n, a in p.items()}


def kernel(**inputs):
    maps, pshapes = host_inputs(inputs)
    nc = build(stages=ALL_STAGES, pshapes=pshapes)
    res = run_bass_kernel_spmd(nc, maps, core_ids=list(range(NCORES)))
    return np.concatenate([r["out"] for r in res.results], axis=0)


ALL_STAGES = ("s1", "A", "B", "C", "D", "s3")
```
